# Optimizing a Trainium2 kernel written in Bass

```python
import jax
import jax.numpy as jnp
from jax import lax
import numpy as np

D_MODEL = 1024
BATCH = 2
SEQ = 8192
DEPTH = 2

CTX_LEN = 256
GRID_W = 64
CONV_DIM = 512
CONV_K = 31
NA_HEADS = 8
HEAD_DIM = 64
NA_DIM = NA_HEADS * HEAD_DIM
NA_KH_MAX = 8
NA_KW = 16
ROPE_THETA = 10000.0
FFN_DIM = 2816
FFN_CONV_K = 3
N_BRANCH = 2
EPS = 1e-6
GLU_END = 2 * CONV_DIM
QKV_END = GLU_END + 3 * NA_DIM
IN_DIM = QKV_END + N_BRANCH * D_MODEL

kernel_name = "hybrid_conformer_natten_dit_trunk"


def rmsnorm(x, g):
    xf = x.astype(jnp.float32)
    y = xf * lax.rsqrt(jnp.mean(xf * xf, axis=-1, keepdims=True) + EPS)
    return (y * g.astype(jnp.float32)).astype(x.dtype)


def layernorm(x, g, b):
    xf = x.astype(jnp.float32)
    mu = jnp.mean(xf, axis=-1, keepdims=True)
    var = jnp.mean(jnp.square(xf - mu), axis=-1, keepdims=True)
    y = (xf - mu) * lax.rsqrt(var + EPS)
    return (y * g.astype(jnp.float32) + b.astype(jnp.float32)).astype(x.dtype)


def modulate(h, shift, scale):
    return h * (1 + scale) + shift


def dwconv1d(x, w, b):
    k, ch = w.shape
    y = lax.conv_general_dilated(
        x, w.astype(x.dtype)[:, None, :], window_strides=(1,),
        padding=[(k // 2, k // 2)],
        dimension_numbers=("NWC", "WIO", "NWC"),
        feature_group_count=ch)
    return y + b


def split_heads(t):
    b, l, _ = t.shape
    return t.reshape(b, l, NA_HEADS, HEAD_DIM).transpose(0, 2, 1, 3)


def merge_heads(t):
    b, h, l, d = t.shape
    return t.transpose(0, 2, 1, 3).reshape(b, l, h * d)


def rope_2d(x, pos_row, pos_col):
    half = HEAD_DIM // 2
    inv_freq = ROPE_THETA ** (-jnp.arange(0, half, 2, dtype=jnp.float32) / half)

    def rot(xa, pos):
        ang = pos[:, None] * inv_freq[None, :]
        cos, sin = jnp.cos(ang), jnp.sin(ang)
        x1, x2 = jnp.split(xa.astype(jnp.float32), 2, axis=-1)
        return jnp.concatenate([x1 * cos - x2 * sin, x1 * sin + x2 * cos], axis=-1)

    out = jnp.concatenate([rot(x[..., :half], pos_row), rot(x[..., half:], pos_col)], axis=-1)
    return out.astype(x.dtype)


def conv_branch(p_glu, dw, dw_b, ln_g, ln_b, w_o):
    a, g = jnp.split(p_glu, 2, axis=-1)
    u = a * jax.nn.sigmoid(g)
    u = dwconv1d(u, dw, dw_b)
    u = jax.nn.silu(layernorm(u, ln_g, ln_b))
    return u @ w_o


def neighbourhood_attention(q, k, v, k_ctx, v_ctx, rpb):
    b, h, l, dh = q.shape
    rows = l // GRID_W
    kh = min(NA_KH_MAX, rows)
    n_win = kh * NA_KW
    qg = q.reshape(b, h, rows, GRID_W, dh)
    kg = k.reshape(b, h, rows, GRID_W, dh)
    vg = v.reshape(b, h, rows, GRID_W, dh)
    cols = jnp.arange(GRID_W)
    col_start = jnp.clip(cols - NA_KW // 2, 0, GRID_W - NA_KW)
    col_idx = col_start[:, None] + jnp.arange(NA_KW)[None, :]
    dc_idx = col_idx - cols[:, None] + (NA_KW - 1)
    scale = HEAD_DIM ** -0.5

    def row_block(r):
        row_start = jnp.clip(r - kh // 2, 0, rows - kh)
        q_r = lax.dynamic_index_in_dim(qg, r, axis=2, keepdims=False)
        k_band = lax.dynamic_slice_in_dim(kg, row_start, kh, axis=2)
        v_band = lax.dynamic_slice_in_dim(vg, row_start, kh, axis=2)
        k_win = k_band[:, :, :, col_idx]
        v_win = v_band[:, :, :, col_idx]
        s_win = jnp.einsum("bhqd,bhiqjd->bhqij", q_r, k_win,
                           preferred_element_type=jnp.float32) * scale
        dr_idx = row_start + jnp.arange(kh) - r + (NA_KH_MAX - 1)
        bias = rpb[:, dr_idx[None, :, None], dc_idx[:, None, :]]
        s_win = s_win + bias[None].astype(jnp.float32)
        s_ctx = jnp.einsum("bhqd,bhcd->bhqc", q_r, k_ctx,
                           preferred_element_type=jnp.float32) * scale
        s = jnp.concatenate([s_win.reshape(b, h, GRID_W, n_win), s_ctx], axis=-1)
        p = jax.nn.softmax(s, axis=-1).astype(v.dtype)
        p_win = p[..., :n_win].reshape(b, h, GRID_W, kh, NA_KW)
        p_ctx = p[..., n_win:]
        return (jnp.einsum("bhqij,bhiqjd->bhqd", p_win, v_win)
                + jnp.einsum("bhqc,bhcd->bhqd", p_ctx, v_ctx))

    out = lax.map(row_block, jnp.arange(rows))
    return out.transpose(1, 2, 0, 3, 4).reshape(b, h, l, dh)


def context_attention(q, k, v):
    s = jnp.einsum("bhqd,bhkd->bhqk", q, k, preferred_element_type=jnp.float32) * (HEAD_DIM ** -0.5)
    p = jax.nn.softmax(s, axis=-1).astype(v.dtype)
    return jnp.einsum("bhqk,bhkd->bhqd", p, v)


def gated_merge(p_gate, y_conv, y_attn, w_o):
    g_conv, g_attn = jnp.split(jax.nn.sigmoid(p_gate), 2, axis=-1)
    return (g_conv * y_conv + g_attn * y_attn) @ w_o


def conv_ffn(h, w_up, dw, dw_b, w_down):
    u = dwconv1d(h @ w_up, dw, dw_b)
    a, g = jnp.split(u, 2, axis=-1)
    return (jax.nn.silu(g) * a) @ w_down


def setup_inputs(seed: int = 0) -> dict:
    key = jax.random.key(seed)
    ks = jax.random.split(key, 24)
    n = jax.random.normal
    f32 = jnp.float32
    D = D_MODEL
    return {
        "x": n(ks[0], (BATCH, SEQ, D), f32),
        "c": n(ks[1], (BATCH, D), f32),
        "ctx": n(ks[2], (BATCH, CTX_LEN, D), f32),
        "c_ctx": n(ks[3], (D,), f32),
        "w_ada": n(ks[4], (DEPTH, D, 6 * D), f32) * (0.5 * D ** -0.5),
        "b_ada": n(ks[5], (DEPTH, 6 * D), f32) * 0.02,
        "norm1_g": 1.0 + 0.02 * n(ks[6], (DEPTH, D), f32),
        "w_in": n(ks[7], (DEPTH, D, IN_DIM), f32) * D ** -0.5,
        "conv_dw": n(ks[8], (DEPTH, CONV_K, CONV_DIM), f32) * CONV_K ** -0.5,
        "conv_dw_b": n(ks[9], (DEPTH, CONV_DIM), f32) * 0.02,
        "conv_ln_g": 1.0 + 0.02 * n(ks[10], (DEPTH, CONV_DIM), f32),
        "conv_ln_b": n(ks[11], (DEPTH, CONV_DIM), f32) * 0.02,
        "w_conv_out": n(ks[12], (DEPTH, CONV_DIM, D), f32) * CONV_DIM ** -0.5,
        "na_rpb": n(ks[13], (DEPTH, NA_HEADS, 2 * NA_KH_MAX - 1, 2 * NA_KW - 1), f32) * 0.1,
        "w_na_out": n(ks[14], (DEPTH, NA_DIM, D), f32) * NA_DIM ** -0.5,
        "w_out": n(ks[15], (DEPTH, D, D), f32) * D ** -0.5,
        "norm2_g": 1.0 + 0.02 * n(ks[16], (DEPTH, D), f32),
        "w_up": n(ks[17], (DEPTH, D, 2 * FFN_DIM), f32) * D ** -0.5,
        "ffn_dw": n(ks[18], (DEPTH, FFN_CONV_K, 2 * FFN_DIM), f32) * FFN_CONV_K ** -0.5,
        "ffn_dw_b": n(ks[19], (DEPTH, 2 * FFN_DIM), f32) * 0.02,
        "w_down": n(ks[20], (DEPTH, FFN_DIM, D), f32) * FFN_DIM ** -0.5,
        "final_norm_g": 1.0 + 0.02 * n(ks[21], (D,), f32),
    }


def reference(x, c, ctx, c_ctx, w_ada, b_ada, norm1_g, w_in, conv_dw, conv_dw_b, conv_ln_g,
              conv_ln_b, w_conv_out, na_rpb, w_na_out, w_out, norm2_g, w_up, ffn_dw, ffn_dw_b,
              w_down, final_norm_g):
    L = x.shape[1]
    t = jnp.arange(L)
    pos_row = (t // GRID_W).astype(jnp.float32)
    pos_col = (t % GRID_W).astype(jnp.float32)
    xc = ctx
    s_lat = jax.nn.silu(c)
    s_ctx = jax.nn.silu(c_ctx)
    for l in range(DEPTH):
        last = l == DEPTH - 1
        mod = (s_lat @ w_ada[l] + b_ada[l])[:, None, :]
        sh1, sc1, g1, sh2, sc2, g2 = jnp.split(mod, 6, axis=-1)
        mod_c = s_ctx @ w_ada[l] + b_ada[l]
        csh1, csc1, cg1, csh2, csc2, cg2 = jnp.split(mod_c, 6, axis=-1)

        h = modulate(rmsnorm(x, norm1_g[l]), sh1, sc1)
        hc = modulate(rmsnorm(xc, norm1_g[l]), csh1, csc1)
        p = h @ w_in[l]
        if last:
            kv_c = hc @ w_in[l][:, GLU_END + NA_DIM:QKV_END]
            k_c, v_c = (split_heads(u) for u in jnp.split(kv_c, 2, axis=-1))
        else:
            pc = hc @ w_in[l]
            q_c, k_c, v_c = (split_heads(u) for u in jnp.split(pc[..., GLU_END:QKV_END], 3, axis=-1))

        q, k, v = (split_heads(u) for u in jnp.split(p[..., GLU_END:QKV_END], 3, axis=-1))
        q = rope_2d(q, pos_row, pos_col)
        k = rope_2d(k, pos_row, pos_col)
        y_conv = conv_branch(p[..., :GLU_END], conv_dw[l], conv_dw_b[l], conv_ln_g[l],
                             conv_ln_b[l], w_conv_out[l])
        y_attn = merge_heads(neighbourhood_attention(q, k, v, k_c, v_c, na_rpb[l])) @ w_na_out[l]
        x = x + g1 * gated_merge(p[..., QKV_END:], y_conv, y_attn, w_out[l])

        h2 = modulate(rmsnorm(x, norm2_g[l]), sh2, sc2)
        x = x + g2 * conv_ffn(h2, w_up[l], ffn_dw[l], ffn_dw_b[l], w_down[l])

        if not last:
            yc_conv = conv_branch(pc[..., :GLU_END], conv_dw[l], conv_dw_b[l], conv_ln_g[l],
                                  conv_ln_b[l], w_conv_out[l])
            yc_attn = merge_heads(context_attention(q_c, k_c, v_c)) @ w_na_out[l]
            xc = xc + cg1 * gated_merge(pc[..., QKV_END:], yc_conv, yc_attn, w_out[l])
            hc2 = modulate(rmsnorm(xc, norm2_g[l]), csh2, csc2)
            xc = xc + cg2 * conv_ffn(hc2, w_up[l], ffn_dw[l], ffn_dw_b[l], w_down[l])
    return rmsnorm(x, final_norm_g)
```

```python
import contextlib
import numpy as np
import concourse.bass as bass
import concourse.mybir as mybir
from concourse.bass_utils import run_bass_kernel_spmd

F32 = mybir.dt.float32
BF16 = mybir.dt.bfloat16
AF = mybir.ActivationFunctionType
ALU = mybir.AluOpType

D = 1024
NCH = 8
SEQ = 8192
GW = 64
NH = 8
DH = 64
CDIM = 512
CK = 31
FFN = 2816
NJ = 22
CTX = 256
IN_DIM = 4608
EPS = 1e-6
NEG = -30000.0
OWN = 16
TPB = 2
RING = 6
URING = 3
NWS = 4
WSW = 1024
FBLK = 510
POOL_CONV_CHUNKS = 2

V_BADA = 0
V_N1G = 48
V_N2G = 56
V_CDW = 64
V_CDB = V_CDW + 4 * CK
V_LNG = V_CDB + 4
V_LNB = V_LNG + 4
V_FDW = V_LNB + 4
V_FDB = V_FDW + 44 * 3
V_FNG = V_FDB + 44
NV = V_FNG + 8

ENGS = ("pe", "act", "dve", "pool", "sp")


class _Op:
    __slots__ = ("eng", "fn", "deps", "signal", "ticket", "dma_key", "idx")


class Prog:
    def __init__(self, nc):
        self.nc = nc
        self.ops = []
        self.last_w = {}
        self.readers = {}
        self.barrier_deps = set()

    def add(self, eng, fn, reads=(), writes=(), dma_key=None):
        op = _Op()
        op.eng, op.fn, op.dma_key = eng, fn, dma_key
        op.signal, op.ticket = False, None
        op.idx = len(self.ops)
        deps = set(self.barrier_deps)
        for r in reads:
            w = self.last_w.get(r)
            if w is not None:
                deps.add(w)
        for w_ in writes:
            w = self.last_w.get(w_)
            if w is not None:
                deps.add(w)
            deps.update(self.readers.get(w_, ()))
        if dma_key is not None:
            k = ("__dk", dma_key)
            w = self.last_w.get(k)
            if w is not None:
                deps.add(w)
            self.last_w[k] = op.idx
        for r in reads:
            self.readers.setdefault(r, []).append(op.idx)
        for w_ in writes:
            self.last_w[w_] = op.idx
            self.readers[w_] = []
        fin = set()
        for d in deps:
            dop = self.ops[d]
            if eng == "pe" and dop.eng == "pe" and dop.dma_key is None and dma_key is None:
                continue
            fin.add(d)
        op.deps = fin
        self.ops.append(op)
        return op.idx

    def barrier(self):
        last = {}
        for op in self.ops:
            key = ("d", op.dma_key) if op.dma_key is not None else ("e", op.eng)
            last[key] = op.idx
        self.barrier_deps = set(last.values())

    def emit(self, final_wait_keys=()):
        nc, ops = self.nc, self.ops
        for op in ops:
            for d in op.deps:
                ops[d].signal = True
        dma_keys = []
        seen = set()
        for op in ops:
            if op.dma_key is not None and op.dma_key not in seen:
                seen.add(op.dma_key)
                dma_keys.append(op.dma_key)
        cnt = {e: 0 for e in ENGS}
        dcnt = {k: 0 for k in dma_keys}
        for op in ops:
            if op.dma_key is not None:
                dcnt[op.dma_key] += 16
                op.ticket = ("d", op.dma_key, dcnt[op.dma_key])
            elif op.signal:
                cnt[op.eng] += 1
                op.ticket = ("e", op.eng, cnt[op.eng])
        per_eng = {e: [op for op in ops if op.eng == e] for e in ENGS}
        with contextlib.ExitStack() as st:
            esem = {e: st.enter_context(nc.semaphore("s_" + e)) for e in ENGS}
            dsem = {k: st.enter_context(nc.semaphore("d_%d" % i)) for i, k in enumerate(dma_keys)}
            block = st.enter_context(nc.Block())

            def run(name, e):
                waited = {}
                for op in per_eng[name]:
                    need = {}
                    for d in op.deps:
                        t = ops[d].ticket
                        key = (t[0], t[1])
                        if waited.get(key, 0) >= t[2]:
                            continue
                        if need.get(key, 0) < t[2]:
                            need[key] = t[2]
                    for key, v in need.items():
                        e.wait_ge(esem[key[1]] if key[0] == "e" else dsem[key[1]], v)
                        waited[key] = v
                    ins = op.fn(e)
                    if op.dma_key is not None:
                        ins.then_inc(dsem[op.dma_key], 16)
                    elif op.signal:
                        ins.then_inc(esem[name], 1)
                if name == "sp":
                    for k in final_wait_keys:
                        e.wait_ge(dsem[k], dcnt[k])

            block.tensor(lambda e: run("pe", e))
            block.scalar(lambda e: run("act", e))
            block.vector(lambda e: run("dve", e))
            block.gpsimd(lambda e: run("pool", e))
            block.sync(lambda e: run("sp", e))


def layer_cfg(big, l, last):
    if big:
        return dict(l=l, last=last, KA=-5, KB=20, TA=-3, TB=19, F0=-3 * 128 + 64, F1=18 * 128)
    return dict(l=l, last=last, KA=-3, KB=18, TA=-1, TB=17, F0=0, F1=OWN * 128)


def make_cfg(mode):
    if mode == "fused":
        layers = [layer_cfg(True, 0, False), layer_cfg(False, 1, True)]
    elif mode == "l0":
        layers = [layer_cfg(False, 0, False)]
    else:
        layers = [layer_cfg(False, 1, True)]
    XA = min(c["TA"] for c in layers)
    XB = max(c["TB"] for c in layers)
    XIA = layers[0]["KA"]
    XIB = layers[0]["KB"]
    import os
    return dict(mode=mode, layers=layers, XA=XA, XB=XB, XIA=XIA, XIB=XIB, final=layers[-1]["last"],
                ndbg=int(os.environ.get("KDBG", "0")))


def build(cfg):
    nc = bass.Bass("TRN2", target_bir_lowering=False)
    layers = cfg["layers"]
    XA, XB, XIA, XIB = cfg["XA"], cfg["XB"], cfg["XIA"], cfg["XIB"]
    NXT = (XB - XA) * 128
    NIT = (XIB - XIA) * 128

    def din(name, shape, dt=F32):
        return nc.dram_tensor(name, list(shape), dt, kind="ExternalInput").ap()

    xT_d = din("xT", [NCH, 128, NIT])
    ctxT_d = din("ctxT", [NCH, 128, CTX])
    cin_d = din("cin", [128, NCH * 2])
    tokm_d = din("tokm", [128, NIT])
    ropeC_d = din("ropeC", [128, NIT])
    ropeS_d = din("ropeS", [128, NIT])
    rm_d = din("rm", [128, 48])
    W = {}
    for c in layers:
        l = c["l"]
        W[l] = dict(
            w_in=din("w_in%d" % l, [D, IN_DIM]), w_rot=din("w_rot%d" % l, [D, 1024]),
            w_co=din("w_co%d" % l, [CDIM, D]), w_no=din("w_no%d" % l, [CDIM, D]),
            w_out=din("w_out%d" % l, [D, D]), w_up=din("w_up%d" % l, [D, 2 * FFN]),
            w_down=din("w_down%d" % l, [FFN, D]), w_ada=din("w_ada%d" % l, [D, 6 * D]),
            vec=din("vec%d" % l, [128, NV]), bias=din("bias%d" % l, [128, NH * 8 * 64]))
        for k in ("w_in", "w_rot", "w_co", "w_no", "w_out", "w_up", "w_down"):
            W[l][k + "_b"] = nc.dram_tensor("%s%d_bf" % (k, l), list(W[l][k].shape), BF16).ap()
    outT_d = nc.dram_tensor("outT", [NCH, 128, OWN * 128], F32, kind="ExternalOutput").ap()
    xcT_d = None
    if not cfg["final"]:
        xcT_d = nc.dram_tensor("xcT", [NCH, 128, CTX], F32, kind="ExternalOutput").ap()

    NDBG = cfg.get("ndbg", 0)
    dbg_d = nc.dram_tensor("dbg", [max(NDBG, 1), 128, 512], F32, kind="ExternalOutput").ap() if NDBG else None
    dbgc = {"n": 0}
    st = contextlib.ExitStack()
    with st:
        ASZ = 53200
        arena_t = st.enter_context(nc.sbuf_tensor("arena", [128, ASZ], F32))
        psum_t = st.enter_context(nc.psum_tensor("psum", [128, 4096], F32))
        AR = {"top": 0}

        def af(n):
            o = AR["top"]
            AR["top"] += n
            assert AR["top"] <= ASZ, ("SBUF arena overflow", AR["top"])
            return arena_t[:, o:o + n]

        def ab(n):
            return af((n + 1) // 2).bitcast(BF16)

        P = Prog(nc)
        add = P.add

        def A(eng, method, *args, reads=(), writes=(), dma_key=None, **kw):
            return add(eng, lambda e: getattr(e, method)(*args, **kw), reads=reads, writes=writes, dma_key=dma_key)

        def bank(k):
            return psum_t[:, 512 * k:512 * (k + 1)]

        def dbg(ap, reads, label):
            if not NDBG or dbgc["n"] >= NDBG:
                return
            i = dbgc["n"]
            dbgc["n"] += 1
            pr, w = ap.shape[0], ap.shape[1]
            print("DBG", i, label, ap.shape)
            A("pool", "dma_start", out=dbg_d[i, 0:pr, 0:w], in_=ap, reads=reads, dma_key="dbg")

        x_res = af(NCH * NXT).rearrange("p (c t) -> p c t", c=NCH)
        xc_res = af(NCH * CTX).rearrange("p (c t) -> p c t", c=NCH)
        wslot = [af(WSW) for _ in range(NWS)]
        ones_f = af(128)
        ident_f = af(128)
        ident = ab(128)
        epsT = af(1)
        sT = af(NCH * 2)
        cinT = af(NCH * 2)
        rmT = af(48)
        vecT = {c["l"]: af(NV) for c in layers}
        modT = {c["l"]: af(96).rearrange("p (c s) -> p c s", s=2) for c in layers}
        modv = {c["l"]: af(2 * 6 * 8).rearrange("p (s k c) -> p s k c", s=2, k=6) for c in layers}
        Etab = ab(NH * 8 * 64).rearrange("p (h i q) -> p h i q", h=NH, i=8)
        KcT = ab(4 * CTX).rearrange("p (c t) -> p c t", c=4)
        Vc = ab(2 * NH * 65).rearrange("p (t h d) -> p t h d", t=2, h=NH)
        mark_persist = AR["top"]

        HW_ = 64 + TPB * 128
        hbuf = [ab(NCH * HW_).rearrange("p (c t) -> p c t", c=NCH) for _ in range(2)]
        hbuf.append(ab(NCH * (64 + 128)).rearrange("p (c t) -> p c t", c=NCH))
        KW_ = 64 + RING * 128
        Kring = ab(4 * KW_).rearrange("p (c t) -> p c t", c=4)
        Vev = ab(RING * NH * 65).rearrange("p (t h d) -> p t h d", t=RING, h=NH)
        Vod = ab(RING * NH * 65).rearrange("p (t h d) -> p t h d", t=RING, h=NH)
        UW_ = 16 + URING * TPB * 128 + 16
        uring = ab(4 * UW_).rearrange("p (c t) -> p c t", c=4)
        ucx = ab(4 * (16 + CTX + 16)).rearrange("p (c t) -> p c t", c=4)
        NB = TPB * 128
        sq = [af(NB) for _ in range(2)]
        rstd = af(NB)
        tt = [af(NB) for _ in range(2)]
        rC = [af(NB) for _ in range(2)] + [af(128)]
        rS = [af(NB) for _ in range(2)] + [af(128)]
        tkm = af(NB)
        r1 = [af(NB) for _ in range(2)]
        QT = ab(4 * NB).rearrange("p (c t) -> p c t", c=4)
        Pb = [ab(512) for _ in range(3)]
        Otok = [ab(512) for _ in range(2)]
        rcp = [af(8) for _ in range(2)]
        OT = ab(4 * NB).rearrange("p (c t) -> p c t", c=4)
        cacc_raw = af(4 * NB)
        cacc = cacc_raw.rearrange("p (c t) -> p c t", c=4)
        xk = cacc_raw.rearrange("p (c t) -> p c t", c=NCH)
        stg = cacc_raw[:, 0:512]
        CACC = ["cacc0", "cacc1", "cacc2", "cacc3"]
        lsq = sq
        lmu = rstd
        lrs = af(NB)
        cT = ab(4 * NB).rearrange("p (c t) -> p c t", c=4)
        ptmp = af(NB)
        gt = [ab(NB) for _ in range(2)]
        m12 = [af(NB) for _ in range(2)]
        sig = m12
        mrg = ab(NCH * NB).rearrange("p (c t) -> p c t", c=NCH)
        mark_tm = AR["top"]

        AR["top"] = mark_persist
        FW_ = FBLK + 2
        h2 = ab(NCH * FW_).rearrange("p (c t) -> p c t", c=NCH)
        hid = ab(NJ * FBLK).rearrange("p (c t) -> p c t", c=NJ)
        fsq = [af(FW_) for _ in range(2)]
        frs = af(FW_)
        ftt = [af(FW_) for _ in range(2)]
        fta = [af(FBLK) for _ in range(2)]
        ftg = [af(FBLK) for _ in range(2)]
        fsg = [af(FBLK) for _ in range(2)]
        ftm = af(FW_)
        hstash = ab(16).rearrange("p (c t) -> p c t", c=NCH)
        fo = [af(512) for _ in range(2)]
        mark_ffn = AR["top"]
        AR["top"] = max(mark_tm, mark_ffn)
        print("ARENA words: persist", mark_persist, "tm", mark_tm, "ffn", mark_ffn, "of", ASZ, "(%.1f KB)" % (AR["top"] * 4 / 1024))

        wctr = {"n": 0}

        def wload(dram_ap, view_fn, dt_bf=True, reads=()):
            s = wctr["n"] % NWS
            wctr["n"] += 1
            raw = wslot[s]
            v = view_fn(raw.bitcast(BF16) if dt_bf else raw)
            res = "ws%d" % s
            A("sp", "dma_start", out=v, in_=dram_ap, reads=list(reads), writes=[res], dma_key=res)
            return v, res

        def kgroup(wb, c0, ncols, kchunks):
            src = wb[:, c0:c0 + ncols].rearrange("(kc p) n -> p kc n", p=128)
            return wload(src, lambda raw: raw[:, 0:kchunks * ncols].rearrange("p (kc n) -> p kc n", kc=kchunks),
                         reads=CASTRES[id(wb)])

        if NDBG == 20:
            pass
            dbg(Kring.rearrange("p c t -> p (c t)")[:, 0:512], ["ARENA0"], "Kring")
            dbg(Vod.rearrange("p t h d -> p (t h d)")[:, 0:512], ["ARENA0"], "Vod")
            dbg(Pb[2][:, 0:512], ["ARENA0"], "Pb2")
            dbg(Otok[1][:, 0:512], ["ARENA0"], "Otok1")
            dbg(mrg.rearrange("p c t -> p (c t)")[:, 0:512], ["ARENA0"], "mrg")
            dbg(stg[:, 0:512], ["ARENA0"], "stg")
            dbg(x_res[:, 0, 0:512], ["ARENA0"], "xres(poison expected)")
        A("pool", "memset", ones_f, 1.0, writes=["ones"])
        A("pool", "memset", epsT, EPS, writes=["eps"])
        A("pool", "memset", ident_f, 0.0, writes=["identf"])
        A("pool", "affine_select", out=ident_f, in_=ident_f, pattern=[[-1, 128]], compare_op=ALU.not_equal,
                                              fill=1.0, base=0, channel_multiplier=1, reads=["identf"], writes=["identf"])
        A("dve", "tensor_copy", out=ident, in_=ident_f, reads=["identf"], writes=["ident"])
        A("pool", "memset", Vc[:, :, :, 64:65], 1.0, writes=["Vc"])
        CASTRES = {}
        for c in layers:
            l = c["l"]
            for k in ("w_in", "w_rot", "w_co", "w_no", "w_out", "w_up", "w_down"):
                src, dst = W[l][k], W[l][k + "_b"]
                rows = src.shape[0]
                step = 512
                lst = []
                for r0 in range(0, rows, step):
                    r1_ = min(rows, r0 + step)
                    res = "cast_%s%d_%d" % (k, l, r0 // step)
                    A("pool", "dma_start", out=dst[r0:r1_, :], in_=src[r0:r1_, :], writes=[res], dma_key=res)
                    lst.append(res)
                CASTRES[id(dst)] = lst

        A("sp", "dma_start", out=cinT, in_=cin_d, writes=["cin"], dma_key="misc")
        A("sp", "dma_start", out=rmT, in_=rm_d, writes=["rm"], dma_key="misc")
        for c in layers:
            l = c["l"]
            A("sp", "dma_start", out=vecT[l], in_=W[l]["vec"], writes=["vec%d" % l], dma_key="misc")
        for ch in range(NCH):
            A("sp", "dma_start", out=x_res[:, ch, :], in_=xT_d[ch, :, (XA - XIA) * 128:(XB - XIA) * 128],
                writes=["x%d" % ch], dma_key="xin%d" % (ch % 2))
        A("sp", "dma_start", out=xc_res, in_=ctxT_d.rearrange("c p t -> p c t"), writes=["xc"], dma_key="misc")
        A("act", "activation", out=sT, in_=cinT, func=AF.Silu, reads=["cin"], writes=["sT"])

        def emit_mod(l):
            wa = W[l]["w_ada"]
            pm = bank(7)
            for oc in range(48):
                src = wa[:, oc * 128:(oc + 1) * 128].rearrange("(kc p) n -> p kc n", p=128)
                v, res = wload(src, lambda raw: raw[:, 0:1024].rearrange("p (kc n) -> p kc n", kc=8), dt_bf=False)
                for kc in range(NCH):
                    A("pe", "matmul", pm[:, oc * 2:oc * 2 + 2], lhsT=v[:, kc, :],
                                                                    rhs=sT[:, kc * 2:kc * 2 + 2], start=(kc == 0), stop=(kc == 7),
                        reads=[res, "sT"], writes=["B7"])
            pmv = pm[:, 0:96].rearrange("p (c s) -> p c s", s=2)
            for s in range(2):
                A("dve", "tensor_tensor", out=modT[l][:, :, s], in0=pmv[:, :, s], in1=vecT[l][:, V_BADA:V_BADA + 48],
                                                          op=ALU.add, reads=["B7", "vec%d" % l], writes=["modT%d" % l])
            for s in range(2):
                m = modT[l]
                A("dve", "scalar_tensor_tensor", out=modv[l][:, s, 0, :], in0=m[:, 8:16, s], scalar=1.0,
                                                                      in1=vecT[l][:, V_N1G:V_N1G + 8], op0=ALU.add, op1=ALU.mult,
                    reads=["modT%d" % l, "vec%d" % l], writes=["modv%d" % l])
                A("dve", "scalar_tensor_tensor", out=modv[l][:, s, 3, :], in0=m[:, 32:40, s], scalar=1.0,
                                                                      in1=vecT[l][:, V_N2G:V_N2G + 8], op0=ALU.add, op1=ALU.mult,
                    reads=["modT%d" % l, "vec%d" % l], writes=["modv%d" % l])
                for kind, c0 in ((1, 0), (2, 16), (4, 24), (5, 40)):
                    A("dve", "tensor_copy", out=modv[l][:, s, kind, :], in_=m[:, c0:c0 + 8, s],
                        reads=["modT%d" % l], writes=["modv%d" % l])

        def emit_etab(l):
            bsrc = W[l]["bias"].rearrange("p (h n) -> p h n", h=NH)
            for h in range(NH):
                A("sp", "dma_start", out=stg, in_=bsrc[:, h, :], writes=["stg"] + CACC, dma_key="stg")
                A("act", "activation", out=Etab[:, h, :, :].rearrange("p i q -> p (i q)"), in_=stg, func=AF.Exp,
                    reads=["stg"] + CACC, writes=["Etab"])

        def emit_norm_mod(xsrc, n, mv, kinds, sqb, rsb, ttb, out_fn, psb, xres, tag):
            ps = bank(psb)[:, 0:n]
            for c in range(NCH):
                b = sqb[c % 2][:, 0:n]
                A("pool", "tensor_tensor", out=b, in0=xsrc[:, c, :], in1=xsrc[:, c, :], op=ALU.mult,
                    reads=xres(c), writes=[tag + "sq%d" % (c % 2)])
                A("pe", "matmul", ps, lhsT=ones_f, rhs=b, start=(c == 0), stop=(c == 7),
                    reads=[tag + "sq%d" % (c % 2), "ones"], writes=["B%d" % psb])
            rs = rsb[:, 0:n]
            A("act", "activation", out=rs, in_=ps, func=AF.Sqrt, bias=epsT, scale=1.0 / D,
                reads=["B%d" % psb, "eps"], writes=[tag + "rs"])
            A("dve", "reciprocal", out=rs, in_=rs, reads=[tag + "rs"], writes=[tag + "rs"])
            for c in range(NCH):
                t = ttb[c % 2][:, 0:n]
                A("dve", "tensor_tensor", out=t, in0=xsrc[:, c, :], in1=rs, op=ALU.mult,
                    reads=xres(c) + [tag + "rs"], writes=[tag + "tt%d" % (c % 2)])
                o, ores = out_fn(c)
                A("act", "activation", out=o, in_=t, func=AF.Identity, bias=mv[:, kinds[1], c:c + 1],
                                                                 scale=mv[:, kinds[0], c:c + 1],
                    reads=[tag + "tt%d" % (c % 2), "modv"], writes=ores)

        gctr = {"n": 0}

        def gslot(n):
            k = (5, 6, 0, 1, 2)[gctr["n"] % 5]
            gctr["n"] += 1
            return bank(k)[:, 0:n], "B%d" % k

        def proj(wview, wres, col0, ncol, rhs_fn, nk, n, rres):
            ps, pres = gslot(n)
            for k in range(nk):
                A("pe", "matmul", ps[0:ncol, :], lhsT=wview[:, k, col0:col0 + ncol], rhs=rhs_fn(k),
                                                  start=(k == 0), stop=(k == nk - 1),
                    reads=[wres] + rres, writes=[pres])
            return ps, pres

        def phase_A(c, stream, blk):
            l = c["l"]
            mv = modv[l][:, stream]
            lat = stream == 0
            if lat:
                lt0, nt, hs, xsrc, xres, need_mask = blk["lt0"], blk["nt"], blk["hs"], blk["xsrc"], blk["xres"], blk["mask"]
                n = nt * 128
                hb = hbuf[hs]
                hview = hb[:, :, 64:64 + n]
                if blk["prev_hs"] is not None:
                    pb = hbuf[blk["prev_hs"]]
                    pn = blk["prev_n"]
                    A("pool", "tensor_copy", out=hb[:, :, 0:64], in_=pb[:, :, 64 + pn - 64:64 + pn],
                        reads=["h%d" % blk["prev_hs"]], writes=["h%d" % hs])
                hres = "h%d" % hs
            else:
                n = CTX
                xsrc, xres = xc_res, (lambda cc: ["xc"])
                hb = hbuf[0]
                hview = hb[:, :, 64:64 + n]
                hres = "h0"
                need_mask = False
            emit_norm_mod(xsrc, n, mv, (0, 1), sq, rstd, tt, lambda cc: (hview[:, cc, :], [hres]), 7, xres, "A")
            wb = W[l]
            if lat:
                col = (lt0 - XIA) * 128
                bi = blk["hs"]
                A("sp", "dma_start", out=rC[bi][:, 0:n], in_=ropeC_d[:, col:col + n], writes=["rC%d" % bi], dma_key="rC%d" % bi)
                A("sp", "dma_start", out=rS[bi][:, 0:n], in_=ropeS_d[:, col:col + n], writes=["rS%d" % bi], dma_key="rS%d" % bi)
                if need_mask:
                    A("sp", "dma_start", out=tkm[:, 0:n], in_=tokm_d[:, col:col + n], writes=["tkm"], dma_key="tkm")
            for g in range(2):
                wv, wr = kgroup(wb["w_in_b"], 1536 + g * 256, 256, 8)
                if lat:
                    wv2, wr2 = kgroup(wb["w_rot_b"], 512 + g * 256, 256, 8)
                for j in range(2):
                    hp = g * 2 + j
                    ps, pres = proj(wv, wr, j * 128, 128, lambda k: hview[:, k, :], 8, n, [hres])
                    if lat:
                        ps2, pres2 = proj(wv2, wr2, j * 128, 128, lambda k: hview[:, k, :], 8, n, [hres])
                        a, b = r1[0][:, 0:n], r1[1][:, 0:n]
                        A("dve", "tensor_tensor", out=a, in0=ps, in1=rC[bi][:, 0:n], op=ALU.mult,
                            reads=[pres, "rC%d" % bi], writes=["r1a"])
                        A("dve", "tensor_tensor", out=b, in0=ps2, in1=rS[bi][:, 0:n], op=ALU.mult,
                            reads=[pres2, "rS%d" % bi], writes=["r1b"])
                        for t in range(nt):
                            sl = (lt0 + t - c["KA"]) % RING
                            A("pool", "tensor_tensor",
                                out=Kring[:, hp, 64 + sl * 128:64 + (sl + 1) * 128], in0=a[:, t * 128:(t + 1) * 128],
                                in1=b[:, t * 128:(t + 1) * 128], op=ALU.add, reads=["r1a", "r1b"], writes=["K%d" % sl])
                            if sl == RING - 1:
                                A("pool", "tensor_copy", out=Kring[:, hp, 0:64],
                                                                                 in_=Kring[:, hp, 64 + sl * 128 + 64:64 + (sl + 1) * 128],
                                    reads=["K%d" % sl], writes=["Kmar"])
                    else:
                        A("act", "copy", out=KcT[:, hp, :], in_=ps, reads=[pres], writes=["KcT"])
            wv0, wr0 = kgroup(wb["w_in_b"], 2048, 256, 8)
            wv1, wr1 = kgroup(wb["w_in_b"], 2304, 256, 8)

            def vtile(col_lo, dst, dres):
                for half, (wv, wr) in enumerate(((wv0, wr0), (wv1, wr1))):
                    ps, pres = gslot(256)
                    for k in range(NCH):
                        A("pe", "matmul", ps, lhsT=hb[:, k, col_lo:col_lo + 128], rhs=wv[:, k, :],
                                                                        start=(k == 0), stop=(k == 7),
                            reads=[wr, hres], writes=[pres])
                    A("act", "copy", out=dst[:, half * 4:(half + 1) * 4, 0:64],
                                                                  in_=ps.rearrange("p (h d) -> p h d", h=4),
                        reads=[pres], writes=[dres])
            if lat:
                for t in range(nt):
                    lt = lt0 + t
                    sl = (lt - c["KA"]) % RING
                    vtile(64 + t * 128, Vev[:, sl], "Ve%d" % sl)
                    if lt - 1 >= c["KA"]:
                        so = (lt - 1 - c["KA"]) % RING
                        vtile(64 + t * 128 - 64, Vod[:, so], "Vo%d" % so)
            else:
                for t in range(2):
                    vtile(64 + t * 128, Vc[:, t], "Vc")
            if lat or not c["last"]:
                for g in range(2):
                    wa_, wra = kgroup(wb["w_in_b"], g * 256, 256, 8)
                    wg_, wrg = kgroup(wb["w_in_b"], 512 + g * 256, 256, 8)
                    for j in range(2):
                        cc = g * 2 + j
                        pa, pra = proj(wa_, wra, j * 128, 128, lambda k: hview[:, k, :], 8, n, [hres])
                        pg, prg = proj(wg_, wrg, j * 128, 128, lambda k: hview[:, k, :], 8, n, [hres])
                        sg = sig[cc % 2][:, 0:n]
                        A("act", "activation", out=sg, in_=pg, func=AF.Sigmoid, reads=[prg],
                            writes=["m%d" % (cc % 2)])
                        if need_mask:
                            A("pool", "tensor_tensor", out=sg, in0=sg, in1=tkm[:, 0:n], op=ALU.mult,
                                reads=["m%d" % (cc % 2), "tkm"], writes=["m%d" % (cc % 2)])
                        if lat:
                            up = blk["upos"]
                            dst = uring[:, cc, 16 + up:16 + up + n]
                            ures = ["u%d" % (up // 128 + q_) for q_ in range(nt)]
                        else:
                            dst = ucx[:, cc, 16:16 + CTX]
                            ures = ["ucx"]
                        A("dve", "tensor_tensor", out=dst, in0=pa, in1=sg, op=ALU.mult,
                            reads=[pra, "m%d" % (cc % 2)], writes=ures)
                if lat:
                    up = blk["upos"]
                    URT = URING * NB
                    if up + n == URT:
                        A("pool", "tensor_copy", out=uring[:, :, 0:16], in_=uring[:, :, 16 + URT - 16:16 + URT],
                            reads=["u%d" % (URT // 128 - 1)], writes=["umarF"])
                    if up == 0:
                        A("pool", "tensor_copy", out=uring[:, :, 16 + URT:16 + URT + 16], in_=uring[:, :, 16:32],
                            reads=["u0"], writes=["umarB"])

        sctr = {"n": 0}
        actr = {"n": 0}
        dscr = af(512) if NDBG == 30 else None

        def attention(c, qT_fn, nq, chunks, ores, out_rows, fill=None):
            import os
            SER = ["ATTSER"] if os.environ.get("KSER") else []
            actr["n"] += 1
            DBGA = NDBG == 30 and actr["n"] == 1
            nch = len(chunks)
            width = nch * nq
            obank = psum_t[:, 3 * 512:5 * 512]
            ov = obank[0:nq, :].rearrange("p (h d) -> p h d", h=NH)
            pvq = []
            for h in range(NH):
                sb = sctr["n"] % 3
                sctr["n"] += 1
                sps = bank(sb)[:, 0:width]
                pb = Pb[sb][:, 0:width]
                hp, po = h // 2, (h % 2) * 64
                for i, ch in enumerate(chunks):
                    A("pe", "matmul",
                        sps[:, i * nq:(i + 1) * nq], lhsT=ch["k"](hp, po), rhs=qT_fn(hp, po), start=True, stop=True,
                        reads=ch["kr"] + ["QT"], writes=["B%d" % sb] + SER)
                A("act", "activation", out=pb, in_=sps, func=AF.Exp, scale=DH ** -0.5,
                    reads=["B%d" % sb], writes=["P%d" % sb] + SER)
                if DBGA and h < 4:
                    dbg(pb, ["P%d" % sb], "Pexp h%d" % h)
                i = 0
                while i < nch:
                    ch = chunks[i]
                    if ch["e"] is None:
                        i += 1
                        continue
                    if ch["rm"] is None:
                        j = i
                        while j + 1 < nch and chunks[j + 1]["e"] is not None and chunks[j + 1]["rm"] is None \
                                and chunks[j + 1]["ei"] == chunks[j]["ei"] + 1:
                            j += 1
                        e0 = ch["ei"]
                        ev = Etab[:, h, e0:e0 + (j - i + 1), :].rearrange("p i q -> p (i q)")
                        A("dve", "tensor_tensor", out=pb[:, i * nq:(j + 1) * nq],
                                                                                   in0=pb[:, i * nq:(j + 1) * nq], in1=ev, op=ALU.mult,
                            reads=["P%d" % sb, "Etab"], writes=["P%d" % sb] + SER)
                        i = j + 1
                    else:
                        A("dve", "scalar_tensor_tensor",
                            out=pb[:, i * nq:(i + 1) * nq], in0=pb[:, i * nq:(i + 1) * nq], scalar=ch["rm"],
                            in1=Etab[:, h, ch["ei"], :], op0=ALU.mult, op1=ALU.mult,
                            reads=["P%d" % sb, "Etab", "rm"], writes=["P%d" % sb] + SER)
                        i += 1
                if fill is not None:
                    fill["f"](fill["k"])

                def _pv(h=h, pb=pb, sb=sb):
                    for i, ch in enumerate(chunks):
                        A("pe", "matmul", ov[:, h, 0:65], lhsT=pb[:, i * nq:(i + 1) * nq],
                          rhs=ch["v"](h), start=(i == 0), stop=(i == nch - 1),
                          reads=["P%d" % sb] + ch["vr"], writes=["OB"] + SER)
                if pvq:
                    pvq.pop(0)()
                pvq.append(_pv)
            while pvq:
                pvq.pop(0)()
            ob = out_rows["otok"]
            rc = rcp[ob]
            A("dve", "reciprocal", out=rc[0:nq, :], in_=ov[:, :, 64], reads=["OB"], writes=["rcp%d" % ob] + SER)
            ot = Otok[ob][0:nq, :].rearrange("p (h d) -> p h d", h=NH)
            A("dve", "tensor_tensor", out=ot, in0=ov[:, :, 0:64], in1=rc[0:nq, :].unsqueeze(2).broadcast_to([nq, NH, 64]),
                                                 op=ALU.mult, reads=["OB", "rcp%d" % ob], writes=["Otok%d" % ob] + SER)
            if DBGA:
                dbg(rc[0:nq, :], ["rcp%d" % ob], "rc")
                dbg(Otok[ob][0:nq, :], ["Otok%d" % ob], "Otok")
            tp = bank(7).bitcast(BF16)
            for f in range(4):
                A("pe", "transpose", out=tp[:, f * nq:(f + 1) * nq], in_=Otok[ob][0:nq, f * 128:(f + 1) * 128],
                                                     identity=ident[0:nq, 0:nq], reads=["Otok%d" % ob, "ident"], writes=["B7"] + SER)
            c0 = out_rows["col"]
            A("act", "copy", out=OT[:, :, c0:c0 + nq], in_=tp[:, 0:4 * nq].rearrange("p (f q) -> p f q", f=4),
                reads=["B7"], writes=[ores] + SER)

        def phase_B(c, stream, blk):
            l = c["l"]
            wb = W[l]
            mv = modv[l][:, stream]
            lat = stream == 0
            if lat:
                lt0, nt, hs = blk["lt0"], blk["nt"], blk["hs"]
                n = nt * 128
                hview = hbuf[hs][:, :, 64:64 + n]
                hres = "h%d" % hs
                bi = blk["hs"]
                xcol = (lt0 - XA) * 128
                xv = x_res[:, :, xcol:xcol + n]
                xres = lambda cc: ["x%d" % cc]
            else:
                n = CTX
                hview = hbuf[0][:, :, 64:64 + n]
                hres = "h0"
                xv = xc_res
                xres = lambda cc: ["xc"]
            for g in range(2):
                wv, wr = kgroup(wb["w_in_b"], 1024 + g * 256, 256, 8)
                if lat:
                    wv2, wr2 = kgroup(wb["w_rot_b"], g * 256, 256, 8)
                for j in range(2):
                    hp = g * 2 + j
                    ps, pres = proj(wv, wr, j * 128, 128, lambda k: hview[:, k, :], 8, n, [hres])
                    if lat:
                        ps2, pres2 = proj(wv2, wr2, j * 128, 128, lambda k: hview[:, k, :], 8, n, [hres])
                        a, b = r1[0][:, 0:n], r1[1][:, 0:n]
                        A("dve", "tensor_tensor", out=a, in0=ps, in1=rC[bi][:, 0:n], op=ALU.mult,
                            reads=[pres, "rC%d" % bi], writes=["r1a"])
                        A("dve", "tensor_tensor", out=b, in0=ps2, in1=rS[bi][:, 0:n], op=ALU.mult,
                            reads=[pres2, "rS%d" % bi], writes=["r1b"])
                        A("pool", "tensor_tensor", out=QT[:, hp, 0:n], in0=a, in1=b, op=ALU.add,
                            reads=["r1a", "r1b"], writes=["QT"])
                    else:
                        A("act", "copy", out=QT[:, hp, 0:n], in_=ps, reads=[pres], writes=["QT"])
            if not lat and NDBG == 50:
                for hp_ in range(4):
                    dbg(QT[:, hp_, 0:n], ["QT"], "QT%d" % hp_)
                for hp_ in range(4):
                    dbg(KcT[:, hp_, :], ["KcT"], "KcT%d" % hp_)
                dbg(hview[:, 0, :], [hres], "hc0")
                dbg(hview[:, 7, :], [hres], "hc7")
            if not lat and False:
                dbg(hview[:, 0, :], [hres], "hc")
                dbg(KcT[:, 0, :], ["KcT"], "KcT")
                dbg(Vc[:, 0].rearrange("p h d -> p (h d)")[:, 0:512], ["Vc"], "Vc")
                dbg(QT[:, 0, 0:n], ["QT"], "QT")
            vec = vecT[l]

            def conv_gen():
              for cc in (2, 3, 0, 1):
                  if lat:
                      up = blk["upos"]
                      base = 16 + up
                      usrc = uring
                      nsl = URING * TPB
                      ur = ["u%d" % ((up // 128 + q_) % nsl) for q_ in (-1, 0, 1, 2)] + ["umarF", "umarB"]
                  else:
                      base = 16
                      usrc = ucx
                      ur = ["ucx"]
                  acc = cacc[:, cc, 0:n]
                  if cc >= 4 - POOL_CONV_CHUNKS:
                      for k in range(CK):
                          src = usrc[:, cc, base + k - 15:base + k - 15 + n]
                          wk = vec[:, V_CDW + cc * CK + k:V_CDW + cc * CK + k + 1]
                          if k == 0:
                              A("pool", "tensor_scalar", out=acc, in0=src, scalar1=wk, scalar2=vec[:, V_CDB + cc:V_CDB + cc + 1],
                                op0=ALU.mult, op1=ALU.add, reads=ur + ["vec%d" % l], writes=["cacc%d" % cc])
                          else:
                              A("pool", "tensor_scalar", out=ptmp[:, 0:n], in0=src, scalar1=wk, scalar2=None, op0=ALU.mult,
                                reads=ur + ["vec%d" % l], writes=["ptmp"])
                              A("pool", "tensor_tensor", out=acc, in0=acc, in1=ptmp[:, 0:n], op=ALU.add,
                                reads=["ptmp", "cacc%d" % cc], writes=["cacc%d" % cc])
                      continue
                  for k in range(CK):
                      src = usrc[:, cc, base + k - 15:base + k - 15 + n]
                      wk = vec[:, V_CDW + cc * CK + k:V_CDW + cc * CK + k + 1]
                      if k == 0:
                          A("dve", "tensor_scalar",
                              out=acc, in0=src, scalar1=wk, scalar2=vec[:, V_CDB + cc:V_CDB + cc + 1], op0=ALU.mult, op1=ALU.add,
                              reads=ur + ["vec%d" % l], writes=["cacc%d" % cc])
                      else:
                          A("dve", "scalar_tensor_tensor", out=acc, in0=src, scalar=wk, in1=acc,
                                                                                             op0=ALU.mult, op1=ALU.add,
                              reads=ur + ["vec%d" % l, "cacc%d" % cc], writes=["cacc%d" % cc])
                      yield

            cgen = conv_gen()

            def filler(k):
                for _ in range(k):
                    try:
                        next(cgen)
                    except StopIteration:
                        return
            nheads_total = (nt * 2 * NH) if lat else (4 * NH)
            per_head = -(-(4 - POOL_CONV_CHUNKS) * CK // nheads_total)
            FILL = {"f": filler, "k": per_head}
            if lat:
                for t in range(nt):
                    lt = lt0 + t
                    for rr in range(2):
                        r = 2 * lt + rr
                        qc0 = t * 128 + rr * 64
                        special = None
                        if lt in (0, 1):
                            special = ("top", r)
                        elif lt in (OWN - 2, OWN - 1):
                            special = ("bot", r - (2 * OWN - 4))
                        if special is None:
                            cis = [0, 1, 2, 3]
                        elif special[0] == "top":
                            cis = [0, 1, 2, 3, 4, 5]
                        else:
                            cis = [-2, -1, 0, 1, 2, 3]
                        chunks = []
                        for ii, ci in enumerate(cis):
                            kr0 = r - 4 + 2 * ci
                            pos = (kr0 * 64 - c["KA"] * 128)
                            rp = pos % (RING * 128)
                            if rp + 128 <= RING * 128:
                                ka = 64 + rp
                                kres = ["K%d" % (rp // 128)] + (["K%d" % ((rp // 128 + 1) % RING)] if rp % 128 else [])
                            else:
                                ka = 0
                                kres = ["Kmar", "K0"]
                            if kr0 % 2 == 0:
                                vs = ((kr0 // 2) - c["KA"]) % RING
                                vfn = (lambda h, vs=vs: Vev[:, vs, h, :])
                                vres = ["Ve%d" % vs]
                            else:
                                vs = (((kr0 - 1) // 2) - c["KA"]) % RING
                                vfn = (lambda h, vs=vs: Vod[:, vs, h, :])
                                vres = ["Vo%d" % vs]
                            rmap = None
                            if special is not None:
                                sidx = (special[1] + (0 if special[0] == "top" else 4)) * 6 + ii
                                rmap = rmT[:, sidx:sidx + 1]
                            chunks.append(dict(k=(lambda hp, po, ka=ka: Kring[po:po + 64, hp, ka:ka + 128]), kr=kres,
                                               v=vfn, vr=vres, e=True, ei=ci + 2, rm=rmap))
                        for t2 in range(2):
                            chunks.append(dict(k=(lambda hp, po, t2=t2: KcT[po:po + 64, hp, t2 * 128:(t2 + 1) * 128]), kr=["KcT"],
                                               v=(lambda h, t2=t2: Vc[:, t2, h, :]), vr=["Vc"], e=None, ei=None, rm=None))
                        attention(c, lambda hp, po, qc0=qc0: QT[po:po + 64, hp, qc0:qc0 + 64], 64, chunks, "OT",
                                  dict(otok=(t * 2 + rr) % 2, col=qc0), fill=FILL)
            else:
                for t in range(2):
                    for hh in range(2):
                        qc0 = t * 128 + hh * 64
                        chunks = [dict(k=(lambda hp, po, t2=t2: KcT[po:po + 64, hp, t2 * 128:(t2 + 1) * 128]), kr=["KcT"],
                                       v=(lambda h, t2=t2: Vc[:, t2, h, :]), vr=["Vc"], e=None, ei=None, rm=None) for t2 in range(2)]
                        attention(c, lambda hp, po, qc0=qc0: QT[po:po + 64, hp, qc0:qc0 + 64], 64, chunks, "OT",
                                  dict(otok=(t * 2 + hh) % 2, col=qc0), fill=FILL)
            filler(10 ** 6)
            pmu = bank(5)[:, 0:n]

            pm2 = bank(6)[:, 0:n]
            for cc in range(4):
                b = lsq[cc % 2][:, 0:n]
                A("pool", "tensor_tensor", out=b, in0=cacc[:, cc, 0:n], in1=cacc[:, cc, 0:n], op=ALU.mult,
                    reads=["cacc%d" % cc], writes=["Asq%d" % (cc % 2)])
                A("pe", "matmul", pmu, lhsT=ones_f, rhs=cacc[:, cc, 0:n], start=(cc == 0), stop=(cc == 3),
                    reads=["cacc%d" % cc, "ones"], writes=["B5"])
                A("pe", "matmul", pm2, lhsT=ones_f, rhs=b, start=(cc == 0), stop=(cc == 3),
                    reads=["Asq%d" % (cc % 2), "ones"], writes=["B6"])
            gctr["n"] = 0
            mu, rs = lmu[:, 0:n], lrs[:, 0:n]
            A("act", "activation", out=mu, in_=pmu, func=AF.Identity, scale=1.0 / CDIM, reads=["B5"], writes=["Ars"])
            A("dve", "tensor_tensor", out=rs, in0=mu, in1=mu, op=ALU.mult, reads=["Ars"], writes=["lrs"])
            A("dve", "scalar_tensor_tensor", out=rs, in0=pm2, scalar=1.0 / CDIM, in1=rs, op0=ALU.mult, op1=ALU.subtract,
                reads=["B6", "lrs"], writes=["lrs"])
            A("act", "activation", out=rs, in_=rs, func=AF.Sqrt, bias=epsT, scale=1.0, reads=["lrs", "eps"], writes=["lrs"])
            A("dve", "reciprocal", out=rs, in_=rs, reads=["lrs"], writes=["lrs"])
            for cc in range(4):
                acc = cacc[:, cc, 0:n]
                A("dve", "tensor_tensor", out=acc, in0=acc, in1=mu, op=ALU.subtract,
                    reads=["cacc%d" % cc, "Ars"], writes=["cacc%d" % cc])
                A("pool", "tensor_tensor", out=acc, in0=acc, in1=rs, op=ALU.mult,
                    reads=["cacc%d" % cc, "lrs"], writes=["cacc%d" % cc])
                A("act", "activation", out=cT[:, cc, 0:n], in_=acc, func=AF.Silu,
                                                                  bias=vec[:, V_LNB + cc:V_LNB + cc + 1], scale=vec[:, V_LNG + cc:V_LNG + cc + 1],
                    reads=["cacc%d" % cc, "vec%d" % l], writes=["cT"])
            gctr["n"] = 0
            if NDBG == 40 and not lat:
                for f_ in range(4):
                    dbg(OT[:, f_, 0:n], ["OT"], "cOT%d" % f_)
            if NDBG == 10 and lat and blk["lt0"] == -1:
                dbg(QT[:, 0, 0:n], ["QT"], "QT")
                dbg(cT[:, 0, 0:n], ["cT"], "cT")
                for f_ in range(4):
                    dbg(OT[:, f_, 0:n], ["OT"], "OT%d" % f_)
                dbg(Kring[:, 0, 0:512], ["K0"], "Kring0")
                dbg(Vev.rearrange("p t h d -> p (t h d)")[:, 0:512], ["Ve0"], "Vev0")
                dbg(Vod.rearrange("p t h d -> p (t h d)")[:, 0:512], ["Vo0"], "Vod0")
            if not lat and False:
                dbg(cT[:, 0, 0:n], ["cT"], "cT")
                dbg(OT[:, 0, 0:n], ["OT"], "OT")
                dbg(Otok[0][0:64, :], ["Otok0"], "Otok0")
                dbg(Pb[0][:, 0:128], ["P0"], "P0")
            for g in range(4):
                wcv, wcr = kgroup(wb["w_co_b"], g * 256, 256, 4)
                wnv, wnr = kgroup(wb["w_no_b"], g * 256, 256, 4)
                wgc, wgcr = kgroup(wb["w_in_b"], 2560 + g * 256, 256, 8)
                wga, wgar = kgroup(wb["w_in_b"], 3584 + g * 256, 256, 8)
                for j in range(2):
                    oc = g * 2 + j
                    pg1, pg1r = proj(wgc, wgcr, j * 128, 128, lambda k: hview[:, k, :], 8, n, [hres])
                    g1_ = gt[0][:, 0:n]
                    A("act", "activation", out=g1_, in_=pg1, func=AF.Sigmoid, reads=[pg1r], writes=["gt0"])
                    py1, py1r = proj(wcv, wcr, j * 128, 128, lambda k: cT[:, k, 0:n], 4, n, ["cT"])
                    ma = m12[0][:, 0:n]
                    A("dve", "tensor_tensor", out=ma, in0=py1, in1=g1_, op=ALU.mult,
                        reads=[py1r, "gt0"], writes=["m0"])
                    pg2, pg2r = proj(wga, wgar, j * 128, 128, lambda k: hview[:, k, :], 8, n, [hres])
                    g2_ = gt[1][:, 0:n]
                    A("act", "activation", out=g2_, in_=pg2, func=AF.Sigmoid, reads=[pg2r], writes=["gt1"])
                    py2, py2r = proj(wnv, wnr, j * 128, 128, lambda k: OT[:, k, 0:n], 4, n, ["OT"])
                    mb = m12[1][:, 0:n]
                    A("dve", "tensor_tensor", out=mb, in0=py2, in1=g2_, op=ALU.mult,
                        reads=[py2r, "gt1"], writes=["m1"])
                    A("pool", "tensor_tensor", out=mrg[:, oc, 0:n], in0=ma, in1=mb, op=ALU.add,
                        reads=["m0", "m1"], writes=["mrg"])
            if lat and blk["lt0"] == -1:
                dbg(mrg[:, 0, 0:n], ["mrg"], "mrg")
            for g in range(4):
                wov, wor = kgroup(wb["w_out_b"], g * 256, 256, 8)
                for j in range(2):
                    oc = g * 2 + j
                    po, por = proj(wov, wor, j * 128, 128, lambda k: mrg[:, k, 0:n], 8, n, ["mrg"])
                    A("dve", "scalar_tensor_tensor", out=xv[:, oc, :], in0=po, scalar=mv[:, 2, oc:oc + 1],
                                                                            in1=xv[:, oc, :], op0=ALU.mult, op1=ALU.add,
                        reads=[por, "modv"] + xres(oc), writes=xres(oc))

        fprev = {"n": None}

        def ffn_block(c, stream, t0, n, need_mask):
            l = c["l"]
            wb = W[l]
            vec = vecT[l]
            mv = modv[l][:, stream]
            lat = stream == 0
            if lat:
                xin = x_res[:, :, t0 - 1:t0 + n + 1]
                xo = x_res[:, :, t0:t0 + n]
                xres = lambda cc: ["x%d" % cc]
                hv = h2[:, :, 0:n + 2]
                nprev = fprev["n"]
                if nprev is not None:
                    A("pool", "tensor_copy", out=hstash[:, :, 0:1], in_=h2[:, :, nprev:nprev + 1], reads=["h2"], writes=["hstash"])
                emit_norm_mod(xin, n + 2, mv, (3, 4), fsq, frs, ftt, lambda cc: (hv[:, cc, :], ["h2"]), 7, xres, "F")
                if nprev is not None:
                    A("pool", "tensor_copy", out=h2[:, :, 0:1], in_=hstash[:, :, 0:1], reads=["hstash"], writes=["h2"])
                fprev["n"] = n
                if need_mask:
                    col = t0 - 1 + (XA - XIA) * 128
                    A("sp", "dma_start", out=ftm[:, 0:n + 2], in_=tokm_d[:, col:col + n + 2], writes=["ftm"], dma_key="ftm")
                    for cc in range(NCH):
                        A("pool", "tensor_tensor", out=hv[:, cc, :], in0=hv[:, cc, :], in1=ftm[:, 0:n + 2], op=ALU.mult,
                            reads=["h2", "ftm"], writes=["h2"])
            else:
                xo = xc_res
                xres = lambda cc: ["xc"]
                hv = h2[:, :, 0:n + 2]
                A("pool", "memset", h2[:, :, 0:1], 0.0, writes=["h2"])
                A("pool", "memset", h2[:, :, n + 1:n + 2], 0.0, writes=["h2"])
                emit_norm_mod(xc_res, n, mv, (3, 4), fsq, frs, ftt, lambda cc: (h2[:, cc, 1:n + 1], ["h2"]), 7, xres, "F")
            for j in range(NJ):
                wv, wr = kgroup(wb["w_up_b"], j * 256, 256, 8)
                outs = []
                for half in range(2):
                    k_ = 1 + 2 * (j % 2) + half
                    ps = bank(k_)[:, 0:n + 2]
                    pres = "B%d" % k_
                    for k in range(NCH):
                        A("pe", "matmul", ps, lhsT=wv[:, k, half * 128:(half + 1) * 128],
                                                                                 rhs=hv[:, k, :], start=(k == 0), stop=(k == 7),
                            reads=[wr, "h2"], writes=[pres])
                    ch = 2 * j + half
                    w0 = vec[:, V_FDW + ch * 3 + 0:V_FDW + ch * 3 + 1]
                    w1 = vec[:, V_FDW + ch * 3 + 1:V_FDW + ch * 3 + 2]
                    w2 = vec[:, V_FDW + ch * 3 + 2:V_FDW + ch * 3 + 3]
                    bb = vec[:, V_FDB + ch:V_FDB + ch + 1]
                    tb = (fta if half == 0 else ftg)[j % 2][:, 0:n]
                    tres = "ft%d%d" % (half, j % 2)
                    A("act", "activation", out=tb, in_=ps[:, 1:n + 1], func=AF.Identity, bias=bb, scale=w1,
                        reads=[pres, "vec%d" % l], writes=[tres])
                    A("dve", "scalar_tensor_tensor", out=tb, in0=ps[:, 0:n], scalar=w0, in1=tb,
                                                                                   op0=ALU.mult, op1=ALU.add,
                        reads=[pres, tres, "vec%d" % l], writes=[tres])
                    A("dve", "scalar_tensor_tensor", out=tb, in0=ps[:, 2:n + 2], scalar=w2, in1=tb,
                                                                                   op0=ALU.mult, op1=ALU.add,
                        reads=[pres, tres, "vec%d" % l], writes=[tres])
                    outs.append((tb, tres))
                (ta, tar), (tg, tgr) = outs
                sg = fsg[j % 2][:, 0:n]
                A("act", "activation", out=sg, in_=tg, func=AF.Silu, reads=[tgr], writes=["fsg%d" % (j % 2)])
                A("pool", "tensor_tensor", out=hid[:, j, 0:n], in0=ta, in1=sg, op=ALU.mult,
                    reads=[tar, "fsg%d" % (j % 2)], writes=["hid"])
            for oc in range(NCH):
                halves = []
                for hf in range(2):
                    src = wb["w_down_b"][hf * 11 * 128:(hf + 1) * 11 * 128, oc * 128:(oc + 1) * 128].rearrange("(j p) n -> p j n", p=128)
                    halves.append(wload(src, lambda raw: raw[:, 0:11 * 128].rearrange("p (j n) -> p j n", j=11),
                                        reads=CASTRES[id(wb["w_down_b"])]))
                k_ = 5 + (oc % 2)
                ps = bank(k_)[:, 0:n]
                for j in range(NJ):
                    wv, wr = halves[j // 11]
                    A("pe", "matmul", ps, lhsT=wv[:, j % 11, :], rhs=hid[:, j, 0:n], start=(j == 0), stop=(j == NJ - 1),
                        reads=[wr, "hid"], writes=["B%d" % k_])
                A("dve", "scalar_tensor_tensor", out=xo[:, oc, :], in0=ps, scalar=mv[:, 5, oc:oc + 1], in1=xo[:, oc, :],
                                                                        op0=ALU.mult, op1=ALU.add,
                    reads=["B%d" % k_, "modv"] + xres(oc), writes=xres(oc))

        def final_out(c):
            l = c["l"]
            vec = vecT[l]
            col0 = (0 - XA) * 128
            for bi in range(OWN * 128 // 512):
                t0 = col0 + bi * 512
                n = 512
                xin = x_res[:, :, t0:t0 + n]
                ps = bank(7)[:, 0:n]
                for cc in range(NCH):
                    b = fsq[cc % 2][:, 0:n]
                    A("pool", "tensor_tensor", out=b, in0=xin[:, cc, :], in1=xin[:, cc, :], op=ALU.mult,
                        reads=["x%d" % cc], writes=["Fsq%d" % (cc % 2)])
                    A("pe", "matmul", ps, lhsT=ones_f, rhs=b, start=(cc == 0), stop=(cc == 7),
                        reads=["Fsq%d" % (cc % 2), "ones"], writes=["B7"])
                rs = frs[:, 0:n]
                A("act", "activation", out=rs, in_=ps, func=AF.Sqrt, bias=epsT, scale=1.0 / D, reads=["B7", "eps"], writes=["Frs"])
                A("dve", "reciprocal", out=rs, in_=rs, reads=["Frs"], writes=["Frs"])
                for cc in range(NCH):
                    o = fo[cc % 2][:, 0:n]
                    A("dve", "scalar_tensor_tensor", out=o, in0=xin[:, cc, :], scalar=vec[:, V_FNG + cc:V_FNG + cc + 1],
                                                                          in1=rs, op0=ALU.mult, op1=ALU.mult,
                        reads=["x%d" % cc, "Frs", "vec%d" % l], writes=["fo%d" % (cc % 2)])
                    A("sp", "dma_start", out=outT_d[cc, :, bi * 512:(bi + 1) * 512], in_=o,
                        reads=["fo%d" % (cc % 2)], dma_key="out%d" % (cc % 2))

        UR = URING * NB
        for c in layers:
            l = c["l"]
            first_layer = c is layers[0]
            half = (mark_persist + AR["top"]) // 2
            A("pool", "memset", arena_t[:, mark_persist:half], 0.0, writes=["ARENA0"])
            A("dve", "memset", arena_t[:, half:AR["top"]], 0.0, writes=["ARENA1"])
            P.barrier()
            A("pool", "memset", Vev[:, :, :, 64:65], 1.0, writes=["Vev"])
            A("pool", "memset", Vod[:, :, :, 64:65], 1.0, writes=["Vod"])
            emit_mod(l)
            A("dve", "tensor_copy", out=modv[l][:, 0, 0, 0:1], in_=modv[l][:, 0, 0, 0:1],
                reads=["modv%d" % l], writes=["modv"])
            emit_etab(l)
            if NDBG == 60 and not first_layer:
                dbg(x_res[:, 0, 0:512], ["x0"], "x1 cols0-512")
                dbg(x_res[:, 0, 1000:1512], ["x0"], "x1 cols1000-1512")
                dbg(x_res[:, 7, NXT - 512:NXT], ["x7"], "x1 last512 ch7")
                dbg(xc_res[:, 0, :], ["xc"], "xc1")
                dbg(modv[l].rearrange("p s k c -> p (s k c)"), ["modv"], "modv1")
                dbg(Etab[:, 0, :, :].rearrange("p i q -> p (i q)"), ["Etab"], "Etab1 h0")
            if NDBG == 40:
                for h_ in range(NH):
                    dbg(Etab[:, h_, :, :].rearrange("p i q -> p (i q)"), ["Etab"], "Etab%d" % h_)
            phase_A(c, 1, None)
            if not c["last"]:
                phase_B(c, 1, None)
            blocks = []
            lt, idx = c["KA"], 0
            while lt < c["KB"]:
                tm = c["TA"] <= lt < c["TB"]
                nt = 1
                if tm and lt % 2 == 0 and lt + 1 < c["TB"]:
                    nt = 2
                blocks.append(dict(lt0=lt, nt=nt, tm=tm, idx=idx))
                lt += nt
                idx += 1
            KA0 = c["KA"] - (c["KA"] % 2)
            prev, tmc = None, 0
            for b in blocks:
                if b["tm"]:
                    b["hs"] = tmc % 2
                    tmc += 1
                else:
                    b["hs"] = 2
                b["upos"] = ((b["lt0"] - KA0) * 128) % UR
                b["prev_hs"] = prev["hs"] if prev else None
                b["prev_n"] = prev["nt"] * 128 if prev else None
                b["mask"] = (b["lt0"] < 0) or (b["lt0"] + b["nt"] > OWN)
                b["need"] = min(b["lt0"] + b["nt"] - 1 + 2, c["KB"] - 1)
                prev = b
            pend = []
            for b in blocks:
                resident = XA <= b["lt0"] and b["lt0"] + b["nt"] <= XB
                if resident:
                    xcol = (b["lt0"] - XA) * 128
                    b["xsrc"] = x_res[:, :, xcol:xcol + b["nt"] * 128]
                    b["xres"] = lambda cc: ["x%d" % cc]
                else:
                    assert first_layer and b["nt"] == 1 and not b["tm"]
                    col = (b["lt0"] - XIA) * 128
                    A("sp", "dma_start", out=xk, in_=xT_d[:, :, col:col + 128].rearrange("c p t -> p c t"), writes=["xk"] + CACC, dma_key="xk")
                    b["xsrc"] = xk
                    b["xres"] = lambda cc: ["xk"] + CACC
                phase_A(c, 0, b)
                covered = b["lt0"] + b["nt"] - 1
                if b["tm"]:
                    pend.append(b)
                while pend and pend[0]["need"] <= covered:
                    phase_B(c, 0, pend.pop(0))
            assert not pend
            if NDBG == 61 and not first_layer:
                for q_ in range(4):
                    dbg(x_res[:, 0, 384 + q_ * 512:384 + (q_ + 1) * 512], ["x0"], "xmid own q%d" % q_)
            P.barrier()
            if not c["last"]:
                ffn_block(c, 1, 0, CTX, False)
            f0 = c["F0"] - XA * 128
            f1 = c["F1"] - XA * 128
            fprev["n"] = None
            t0 = f0
            while t0 < f1:
                n = min(FBLK, f1 - t0)
                lo_t = (t0 - 1) // 128 + XA
                hi_t = (t0 + n) // 128 + XA
                ffn_block(c, 0, t0, n, lo_t < 0 or hi_t >= OWN)
                t0 += n
            if NDBG == 61 and not first_layer:
                for q_ in range(4):
                    dbg(x_res[:, 0, 384 + q_ * 512:384 + (q_ + 1) * 512], ["x0"], "x2 own q%d" % q_)
            if c["last"]:
                final_out(c)
            P.barrier()
        outs = ["out0", "out1"]
        if not cfg["final"]:
            col0 = (0 - XA) * 128
            for cc in range(NCH):
                A("sp", "dma_start", out=outT_d[cc], in_=x_res[:, cc, col0:col0 + OWN * 128], reads=["x%d" % cc],
                    dma_key="out%d" % (cc % 2))
            A("sp", "dma_start", out=xcT_d.rearrange("c p t -> p c t"), in_=xc_res, reads=["xc"], dma_key="out0")
        P.emit(final_wait_keys=outs + (["dbg"] if dbgc["n"] else []))
    return nc


ROPE_THETA = 10000.0


def _rope_tables(g_tiles):
    half = DH // 2
    inv_freq = (ROPE_THETA ** (-np.arange(0, half, 2, dtype=np.float32) / half)).astype(np.float32)
    p = np.arange(128)
    d = p % 64
    f = d % 16
    first = (d % 32) < 16
    use_row = d < 32
    C = np.zeros((128, len(g_tiles) * 128), np.float32)
    S = np.zeros_like(C)
    i = np.arange(128)
    for k, g in enumerate(g_tiles):
        row = (2 * g + i // 64).astype(np.float32)
        col = (i % 64).astype(np.float32)
        pos = np.where(use_row[:, None], row[None, :], col[None, :]).astype(np.float32)
        ang = (pos * inv_freq[f][:, None]).astype(np.float32)
        C[:, k * 128:(k + 1) * 128] = np.cos(ang)
        sn = np.sin(ang)
        S[:, k * 128:(k + 1) * 128] = np.where(first[:, None], -sn, sn)
    return C, S


def _bias_table(rpb_l):
    p = np.arange(128)
    kr2 = p // 64
    kc = p % 64
    qc = np.arange(64)
    cs = np.clip(qc - 8, 0, 48)
    out = np.full((128, NH, 8, 64), NEG, np.float32)
    for ei in range(8):
        ci = ei - 2
        dr = -4 + 2 * ci + kr2
        dc = kc[:, None] - qc[None, :]
        ok = (kc[:, None] >= cs[None, :]) & (kc[:, None] < cs[None, :] + 16) & (np.abs(dr)[:, None] <= 7)
        dri = np.clip(dr + 7, 0, 14)
        dci = np.clip(dc + 15, 0, 30)
        vals = rpb_l[:, dri[:, None], dci]
        out[:, :, ei, :] = np.where(ok[:, None, :], vals.transpose(1, 0, 2), np.float32(NEG))
    return out.reshape(128, NH * 8 * 64)


def _row_masks(ci_core):
    p = np.arange(128)
    kr2 = p // 64
    rm = np.zeros((128, 48), np.float32)
    for sidx in range(8):
        if sidx < 4:
            r = sidx
            cis = range(0, 6)
        else:
            r = 28 + (sidx - 4)
            cis = range(-2, 4)
        R = ci_core * 32 + r
        rs = min(max(R - 4, 0), 120)
        for ii, ci in enumerate(cis):
            kr = R - 4 + 2 * ci + kr2
            rm[:, sidx * 6 + ii] = ((kr >= rs) & (kr < rs + 8)).astype(np.float32)
    return rm


def _pack_vec(inp, l):
    v = np.zeros((128, NV), np.float32)
    fm = lambda a: np.ascontiguousarray(np.asarray(a, np.float32).reshape(-1, 128).T)
    v[:, V_BADA:V_BADA + 48] = fm(inp["b_ada"][l])
    v[:, V_N1G:V_N1G + 8] = fm(inp["norm1_g"][l])
    v[:, V_N2G:V_N2G + 8] = fm(inp["norm2_g"][l])
    cdw = np.asarray(inp["conv_dw"][l], np.float32)
    for cc in range(4):
        v[:, V_CDW + cc * CK:V_CDW + (cc + 1) * CK] = cdw[:, cc * 128:(cc + 1) * 128].T
    v[:, V_CDB:V_CDB + 4] = fm(inp["conv_dw_b"][l])
    v[:, V_LNG:V_LNG + 4] = fm(inp["conv_ln_g"][l])
    v[:, V_LNB:V_LNB + 4] = fm(inp["conv_ln_b"][l])
    fdw = np.asarray(inp["ffn_dw"][l], np.float32)
    fdb = np.asarray(inp["ffn_dw_b"][l], np.float32)
    for j in range(NJ):
        for half in range(2):
            ch = 2 * j + half
            c0 = half * FFN + j * 128
            v[:, V_FDW + ch * 3:V_FDW + ch * 3 + 3] = fdw[:, c0:c0 + 128].T
            v[:, V_FDB + ch] = fdb[c0:c0 + 128]
    v[:, V_FNG:V_FNG + 8] = fm(inp["final_norm_g"])
    return v


def _weights_for_layer(inp, l):
    w_in = np.asarray(inp["w_in"][l], np.float32)
    d = np.arange(64)
    partner = np.where((d % 32) < 16, d + 16, d - 16)
    qcols = np.concatenate([1024 + h * 64 + partner for h in range(NH)])
    kcols = np.concatenate([1536 + h * 64 + partner for h in range(NH)])
    w_rot = np.ascontiguousarray(w_in[:, np.concatenate([qcols, kcols])])
    w_up = np.asarray(inp["w_up"][l], np.float32)
    perm = np.concatenate([np.concatenate([np.arange(j * 128, (j + 1) * 128), FFN + np.arange(j * 128, (j + 1) * 128)])
                           for j in range(NJ)])
    return {
        "w_in%d" % l: np.ascontiguousarray(w_in), "w_rot%d" % l: w_rot,
        "w_co%d" % l: np.ascontiguousarray(np.asarray(inp["w_conv_out"][l], np.float32)),
        "w_no%d" % l: np.ascontiguousarray(np.asarray(inp["w_na_out"][l], np.float32)),
        "w_out%d" % l: np.ascontiguousarray(np.asarray(inp["w_out"][l], np.float32)),
        "w_up%d" % l: np.ascontiguousarray(w_up[:, perm]),
        "w_down%d" % l: np.ascontiguousarray(np.asarray(inp["w_down"][l], np.float32)),
        "w_ada%d" % l: np.ascontiguousarray(np.asarray(inp["w_ada"][l], np.float32)),
        "vec%d" % l: _pack_vec(inp, l),
        "bias%d" % l: _bias_table(np.asarray(inp["na_rpb"][l], np.float32)),
    }


def _core_inputs(cfg, inp, x, ctx, shared):
    XIA, XIB = cfg["XIA"], cfg["XIB"]
    maps = []
    for core in range(8):
        b, ci = core // 4, core % 4
        tiles = [ci * OWN + lt for lt in range(XIA, XIB)]
        nit = len(tiles) * 128
        xr = np.zeros((nit, D), np.float32)
        tm = np.zeros((nit,), np.float32)
        for k, g in enumerate(tiles):
            if 0 <= g < SEQ // 128:
                xr[k * 128:(k + 1) * 128] = x[b, g * 128:(g + 1) * 128]
                tm[k * 128:(k + 1) * 128] = 1.0
        C, S = _rope_tables(tiles)
        cin = np.zeros((128, NCH * 2), np.float32)
        cin[:, 0::2] = np.asarray(inp["c"], np.float32)[b].reshape(NCH, 128).T
        cin[:, 1::2] = np.asarray(inp["c_ctx"], np.float32).reshape(NCH, 128).T
        m = {
            "xT": np.ascontiguousarray(xr.T.reshape(NCH, 128, nit)),
            "ctxT": np.ascontiguousarray(ctx[b].T.reshape(NCH, 128, CTX)),
            "cin": cin,
            "tokm": np.ascontiguousarray(np.broadcast_to(tm[None, :], (128, nit))),
            "ropeC": C, "ropeS": S,
            "rm": _row_masks(ci),
        }
        m.update(shared)
        maps.append(m)
    return maps


_NC_CACHE = {}
MODE = "fused"


def _run(mode, inp, x, ctx):
    cfg = make_cfg(mode)
    if mode not in _NC_CACHE:
        _NC_CACHE[mode] = build(cfg)
    nc = _NC_CACHE[mode]
    shared = {}
    for c in cfg["layers"]:
        shared.update(_weights_for_layer(inp, c["l"]))
    maps = _core_inputs(cfg, inp, x, ctx, shared)
    res = run_bass_kernel_spmd(nc, maps, core_ids=list(range(8)))
    if cfg.get("ndbg"):
        np.save("_dbg.npy", np.asarray(res.results[0]["dbg"]))
    xo = np.zeros((2, SEQ, D), np.float32)
    xc = None if cfg["final"] else np.zeros((2, CTX, D), np.float32)
    for core in range(8):
        b, ci = core // 4, core % 4
        r = res.results[core]
        xo[b, ci * OWN * 128:(ci + 1) * OWN * 128] = np.asarray(r["outT"]).reshape(D, OWN * 128).T
        if xc is not None:
            xc[b] = np.asarray(r["xcT"]).reshape(D, CTX).T
    return xo, xc


def kernel(**inputs):
    x = np.asarray(inputs["x"], np.float32)
    ctx = np.asarray(inputs["ctx"], np.float32)
    if MODE == "fused":
        out, _ = _run("fused", inputs, x, ctx)
        return out
    x1, xc1 = _run("l0", inputs, x, ctx)
    out, _ = _run("l1", inputs, x1, xc1)
    return out
```

```python
import contextlib
import numpy as np
import concourse.bass as bass
import concourse.mybir as mybir
from concourse.bass_utils import run_bass_kernel_spmd

F32 = mybir.dt.float32
BF16 = mybir.dt.bfloat16
AF = mybir.ActivationFunctionType
ALU = mybir.AluOpType

D = 1024
NCH = 8
SEQ = 8192
GW = 64
NH = 8
DH = 64
CDIM = 512
CK = 31
FFN = 2816
NJ = 22
CTX = 256
IN_DIM = 4608
EPS = 1e-6
NEG = -30000.0
OWN = 16
TPB = 2
RING = 6
URING = 3
NWS = 4
WSW = 1024
FBLK = 510
POOL_CONV_CHUNKS = 0

V_BADA = 0
V_N1G = 48
V_N2G = 56
V_CDW = 64
V_CDB = V_CDW + 4 * CK
V_LNG = V_CDB + 4
V_LNB = V_LNG + 4
V_FDW = V_LNB + 4
V_FDB = V_FDW + 44 * 3
V_FNG = V_FDB + 44
NV = V_FNG + 8

ENGS = ("pe", "act", "dve", "pool", "sp")


class _Op:
    __slots__ = ("eng", "fn", "deps", "signal", "ticket", "dma_key", "idx")


class Prog:
    def __init__(self, nc):
        self.nc = nc
        self.ops = []
        self.last_w = {}
        self.readers = {}
        self.barrier_deps = set()

    def add(self, eng, fn, reads=(), writes=(), dma_key=None):
        op = _Op()
        op.eng, op.fn, op.dma_key = eng, fn, dma_key
        op.signal, op.ticket = False, None
        op.idx = len(self.ops)
        deps = set(self.barrier_deps)
        for r in reads:
            w = self.last_w.get(r)
            if w is not None:
                deps.add(w)
        for w_ in writes:
            w = self.last_w.get(w_)
            if w is not None:
                deps.add(w)
            deps.update(self.readers.get(w_, ()))
        if dma_key is not None:
            k = ("__dk", dma_key)
            w = self.last_w.get(k)
            if w is not None:
                deps.add(w)
            self.last_w[k] = op.idx
        for r in reads:
            self.readers.setdefault(r, []).append(op.idx)
        for w_ in writes:
            self.last_w[w_] = op.idx
            self.readers[w_] = []
        fin = set()
        for d in deps:
            dop = self.ops[d]
            if eng == "pe" and dop.eng == "pe" and dop.dma_key is None and dma_key is None:
                continue
            fin.add(d)
        op.deps = fin
        self.ops.append(op)
        return op.idx

    def barrier(self):
        last = {}
        for op in self.ops:
            key = ("d", op.dma_key) if op.dma_key is not None else ("e", op.eng)
            last[key] = op.idx
        self.barrier_deps = set(last.values())

    def emit(self, final_wait_keys=()):
        nc, ops = self.nc, self.ops
        for op in ops:
            for d in op.deps:
                ops[d].signal = True
        dma_keys = []
        seen = set()
        for op in ops:
            if op.dma_key is not None and op.dma_key not in seen:
                seen.add(op.dma_key)
                dma_keys.append(op.dma_key)
        cnt = {e: 0 for e in ENGS}
        dcnt = {k: 0 for k in dma_keys}
        for op in ops:
            if op.dma_key is not None:
                dcnt[op.dma_key] += 16
                op.ticket = ("d", op.dma_key, dcnt[op.dma_key])
            elif op.signal:
                cnt[op.eng] += 1
                op.ticket = ("e", op.eng, cnt[op.eng])
        per_eng = {e: [op for op in ops if op.eng == e] for e in ENGS}
        with contextlib.ExitStack() as st:
            esem = {e: st.enter_context(nc.semaphore("s_" + e)) for e in ENGS}
            dsem = {k: st.enter_context(nc.semaphore("d_%d" % i)) for i, k in enumerate(dma_keys)}
            block = st.enter_context(nc.Block())

            def run(name, e):
                waited = {}
                for op in per_eng[name]:
                    need = {}
                    for d in op.deps:
                        t = ops[d].ticket
                        key = (t[0], t[1])
                        if waited.get(key, 0) >= t[2]:
                            continue
                        if need.get(key, 0) < t[2]:
                            need[key] = t[2]
                    for key, v in need.items():
                        e.wait_ge(esem[key[1]] if key[0] == "e" else dsem[key[1]], v)
                        waited[key] = v
                    ins = op.fn(e)
                    if op.dma_key is not None:
                        ins.then_inc(dsem[op.dma_key], 16)
                    elif op.signal:
                        ins.then_inc(esem[name], 1)
                if name == "sp":
                    for k in final_wait_keys:
                        e.wait_ge(dsem[k], dcnt[k])

            block.tensor(lambda e: run("pe", e))
            block.scalar(lambda e: run("act", e))
            block.vector(lambda e: run("dve", e))
            block.gpsimd(lambda e: run("pool", e))
            block.sync(lambda e: run("sp", e))


def layer_cfg(big, l, last):
    if big:
        return dict(l=l, last=last, KA=-5, KB=20, TA=-3, TB=19, F0=-3 * 128 + 64, F1=18 * 128)
    return dict(l=l, last=last, KA=-3, KB=18, TA=-1, TB=17, F0=0, F1=OWN * 128)


def make_cfg(mode):
    if mode == "fused":
        layers = [layer_cfg(True, 0, False), layer_cfg(False, 1, True)]
    elif mode == "l0":
        layers = [layer_cfg(False, 0, False)]
    else:
        layers = [layer_cfg(False, 1, True)]
    XA = min(c["TA"] for c in layers)
    XB = max(c["TB"] for c in layers)
    XIA = layers[0]["KA"]
    XIB = layers[0]["KB"]
    import os
    return dict(mode=mode, layers=layers, XA=XA, XB=XB, XIA=XIA, XIB=XIB, final=layers[-1]["last"],
                ndbg=int(os.environ.get("KDBG", "0")))


def build(cfg):
    nc = bass.Bass("TRN2", target_bir_lowering=False)
    layers = cfg["layers"]
    XA, XB, XIA, XIB = cfg["XA"], cfg["XB"], cfg["XIA"], cfg["XIB"]
    NXT = (XB - XA) * 128
    NIT = (XIB - XIA) * 128

    def din(name, shape, dt=F32):
        return nc.dram_tensor(name, list(shape), dt, kind="ExternalInput").ap()

    xT_d = din("xT", [NCH, 128, NIT])
    ctxT_d = din("ctxT", [NCH, 128, CTX])
    cin_d = din("cin", [128, NCH * 2])
    tokm_d = din("tokm", [128, NIT])
    ropeC_d = din("ropeC", [128, NIT])
    ropeS_d = din("ropeS", [128, NIT])
    rm_d = din("rm", [128, 48])
    W = {}
    for c in layers:
        l = c["l"]
        W[l] = dict(
            w_in=din("w_in%d" % l, [D, IN_DIM]), w_rot=din("w_rot%d" % l, [D, 1024]),
            w_co=din("w_co%d" % l, [CDIM, D]), w_no=din("w_no%d" % l, [CDIM, D]),
            w_out=din("w_out%d" % l, [D, D]), w_up=din("w_up%d" % l, [D, 2 * FFN]),
            w_down=din("w_down%d" % l, [FFN, D]), w_ada=din("w_ada%d" % l, [D, 6 * D]),
            vec=din("vec%d" % l, [128, NV]), bias=din("bias%d" % l, [128, NH * 8 * 64]))
        for k in ("w_in", "w_rot", "w_co", "w_no", "w_out", "w_up", "w_down"):
            W[l][k + "_b"] = nc.dram_tensor("%s%d_bf" % (k, l), list(W[l][k].shape), BF16).ap()
    outT_d = nc.dram_tensor("outT", [NCH, 128, OWN * 128], F32, kind="ExternalOutput").ap()
    xcT_d = None
    if not cfg["final"]:
        xcT_d = nc.dram_tensor("xcT", [NCH, 128, CTX], F32, kind="ExternalOutput").ap()

    NDBG = cfg.get("ndbg", 0)
    dbg_d = nc.dram_tensor("dbg", [max(NDBG, 1), 128, 512], F32, kind="ExternalOutput").ap() if NDBG else None
    dbgc = {"n": 0}
    st = contextlib.ExitStack()
    with st:
        ASZ = 53200
        arena_t = st.enter_context(nc.sbuf_tensor("arena", [128, ASZ], F32))
        psum_t = st.enter_context(nc.psum_tensor("psum", [128, 4096], F32))
        AR = {"top": 0}

        def af(n):
            o = AR["top"]
            AR["top"] += n
            assert AR["top"] <= ASZ, ("SBUF arena overflow", AR["top"])
            return arena_t[:, o:o + n]

        def ab(n):
            return af((n + 1) // 2).bitcast(BF16)

        P = Prog(nc)
        add = P.add

        def A(eng, method, *args, reads=(), writes=(), dma_key=None, **kw):
            return add(eng, lambda e: getattr(e, method)(*args, **kw), reads=reads, writes=writes, dma_key=dma_key)

        def bank(k):
            return psum_t[:, 512 * k:512 * (k + 1)]

        def dbg(ap, reads, label):
            if not NDBG or dbgc["n"] >= NDBG:
                return
            i = dbgc["n"]
            dbgc["n"] += 1
            pr, w = ap.shape[0], ap.shape[1]
            print("DBG", i, label, ap.shape)
            A("pool", "dma_start", out=dbg_d[i, 0:pr, 0:w], in_=ap, reads=reads, dma_key="dbg")

        x_res = af(NCH * NXT).rearrange("p (c t) -> p c t", c=NCH)
        xc_res = af(NCH * CTX).rearrange("p (c t) -> p c t", c=NCH)
        wslot = [af(WSW) for _ in range(NWS)]
        ones_f = af(128)
        ident_f = af(128)
        ident = ab(128)
        epsT = af(1)
        sT = af(NCH * 2)
        cinT = af(NCH * 2)
        rmT = af(48)
        vecT = {c["l"]: af(NV) for c in layers}
        modT = {c["l"]: af(96).rearrange("p (c s) -> p c s", s=2) for c in layers}
        modv = {c["l"]: af(2 * 6 * 8).rearrange("p (s k c) -> p s k c", s=2, k=6) for c in layers}
        Etab = ab(NH * 8 * 64).rearrange("p (h i q) -> p h i q", h=NH, i=8)
        KcT = ab(4 * CTX).rearrange("p (c t) -> p c t", c=4)
        Vc = ab(2 * NH * 65).rearrange("p (t h d) -> p t h d", t=2, h=NH)
        mark_persist = AR["top"]

        HW_ = 64 + TPB * 128
        hbuf = [ab(NCH * HW_).rearrange("p (c t) -> p c t", c=NCH) for _ in range(2)]
        hbuf.append(ab(NCH * (64 + 128)).rearrange("p (c t) -> p c t", c=NCH))
        KW_ = 64 + RING * 128
        Kring = ab(4 * KW_).rearrange("p (c t) -> p c t", c=4)
        Vev = ab(RING * NH * 65).rearrange("p (t h d) -> p t h d", t=RING, h=NH)
        Vod = ab(RING * NH * 65).rearrange("p (t h d) -> p t h d", t=RING, h=NH)
        UW_ = 16 + URING * TPB * 128 + 16
        uring = ab(4 * UW_).rearrange("p (c t) -> p c t", c=4)
        ucx = ab(4 * (16 + CTX + 16)).rearrange("p (c t) -> p c t", c=4)
        NB = TPB * 128
        sq = [af(NB) for _ in range(2)]
        rstd = af(NB)
        tt = [af(NB) for _ in range(2)]
        rC = [af(NB) for _ in range(2)] + [af(128)]
        rS = [af(NB) for _ in range(2)] + [af(128)]
        tkm = af(NB)
        r1 = [af(NB) for _ in range(2)]
        QT = ab(4 * NB).rearrange("p (c t) -> p c t", c=4)
        Pb = [ab(512) for _ in range(3)]
        Otok = [ab(512) for _ in range(2)]
        rcp = [af(8) for _ in range(2)]
        OT = ab(4 * NB).rearrange("p (c t) -> p c t", c=4)
        cacc_raw = af(4 * NB)
        cacc = cacc_raw.rearrange("p (c t) -> p c t", c=4)
        xk = cacc_raw.rearrange("p (c t) -> p c t", c=NCH)
        stg = cacc_raw[:, 0:512]
        CACC = ["cacc0", "cacc1", "cacc2", "cacc3"]
        lsq = sq
        lmu = rstd
        lrs = af(NB)
        cT = ab(4 * NB).rearrange("p (c t) -> p c t", c=4)
        dg = [ab(128) for _ in range(4)]
        gt = [ab(NB) for _ in range(2)]
        m12 = [af(NB) for _ in range(2)]
        sig = m12
        mrg = ab(NCH * NB).rearrange("p (c t) -> p c t", c=NCH)
        mark_tm = AR["top"]

        AR["top"] = mark_persist
        FW_ = FBLK + 2
        h2 = ab(NCH * FW_).rearrange("p (c t) -> p c t", c=NCH)
        hid = ab(NJ * FBLK).rearrange("p (c t) -> p c t", c=NJ)
        fsq = [af(FW_) for _ in range(2)]
        frs = af(FW_)
        ftt = [af(FW_) for _ in range(2)]
        fta = [af(FBLK) for _ in range(2)]
        ftg = [af(FBLK) for _ in range(2)]
        fsg = [af(FBLK) for _ in range(2)]
        ftm = af(FW_)
        hstash = ab(16).rearrange("p (c t) -> p c t", c=NCH)
        fo = [af(512) for _ in range(2)]
        mark_ffn = AR["top"]
        AR["top"] = max(mark_tm, mark_ffn)
        print("ARENA words: persist", mark_persist, "tm", mark_tm, "ffn", mark_ffn, "of", ASZ, "(%.1f KB)" % (AR["top"] * 4 / 1024))

        wctr = {"n": 0}

        def wload(dram_ap, view_fn, dt_bf=True, reads=()):
            s = wctr["n"] % NWS
            wctr["n"] += 1
            raw = wslot[s]
            v = view_fn(raw.bitcast(BF16) if dt_bf else raw)
            res = "ws%d" % s
            A("sp", "dma_start", out=v, in_=dram_ap, reads=list(reads), writes=[res], dma_key=res)
            return v, res

        def kgroup(wb, c0, ncols, kchunks):
            src = wb[:, c0:c0 + ncols].rearrange("(kc p) n -> p kc n", p=128)
            return wload(src, lambda raw: raw[:, 0:kchunks * ncols].rearrange("p (kc n) -> p kc n", kc=kchunks),
                         reads=CASTRES[id(wb)])

        if NDBG == 20:
            pass
            dbg(Kring.rearrange("p c t -> p (c t)")[:, 0:512], ["ARENA0"], "Kring")
            dbg(Vod.rearrange("p t h d -> p (t h d)")[:, 0:512], ["ARENA0"], "Vod")
            dbg(Pb[2][:, 0:512], ["ARENA0"], "Pb2")
            dbg(Otok[1][:, 0:512], ["ARENA0"], "Otok1")
            dbg(mrg.rearrange("p c t -> p (c t)")[:, 0:512], ["ARENA0"], "mrg")
            dbg(stg[:, 0:512], ["ARENA0"], "stg")
            dbg(x_res[:, 0, 0:512], ["ARENA0"], "xres(poison expected)")
        A("pool", "memset", ones_f, 1.0, writes=["ones"])
        A("pool", "memset", epsT, EPS, writes=["eps"])
        A("pool", "memset", ident_f, 0.0, writes=["identf"])
        A("pool", "affine_select", out=ident_f, in_=ident_f, pattern=[[-1, 128]], compare_op=ALU.not_equal,
                                              fill=1.0, base=0, channel_multiplier=1, reads=["identf"], writes=["identf"])
        A("dve", "tensor_copy", out=ident, in_=ident_f, reads=["identf"], writes=["ident"])
        A("pool", "memset", Vc[:, :, :, 64:65], 1.0, writes=["Vc"])
        CASTRES = {}
        for c in layers:
            l = c["l"]
            for k in ("w_in", "w_rot", "w_co", "w_no", "w_out", "w_up", "w_down"):
                src, dst = W[l][k], W[l][k + "_b"]
                rows = src.shape[0]
                step = 512
                lst = []
                for r0 in range(0, rows, step):
                    r1_ = min(rows, r0 + step)
                    res = "cast_%s%d_%d" % (k, l, r0 // step)
                    A("pool", "dma_start", out=dst[r0:r1_, :], in_=src[r0:r1_, :], writes=[res], dma_key=res)
                    lst.append(res)
                CASTRES[id(dst)] = lst

        A("sp", "dma_start", out=cinT, in_=cin_d, writes=["cin"], dma_key="misc")
        A("sp", "dma_start", out=rmT, in_=rm_d, writes=["rm"], dma_key="misc")
        for c in layers:
            l = c["l"]
            A("sp", "dma_start", out=vecT[l], in_=W[l]["vec"], writes=["vec%d" % l], dma_key="misc")
        for ch in range(NCH):
            A("sp", "dma_start", out=x_res[:, ch, :], in_=xT_d[ch, :, (XA - XIA) * 128:(XB - XIA) * 128],
                writes=["x%d" % ch], dma_key="xin%d" % (ch % 2))
        A("sp", "dma_start", out=xc_res, in_=ctxT_d.rearrange("c p t -> p c t"), writes=["xc"], dma_key="misc")
        A("act", "activation", out=sT, in_=cinT, func=AF.Silu, reads=["cin"], writes=["sT"])

        def emit_mod(l):
            wa = W[l]["w_ada"]
            pm = bank(7)
            for oc in range(48):
                src = wa[:, oc * 128:(oc + 1) * 128].rearrange("(kc p) n -> p kc n", p=128)
                v, res = wload(src, lambda raw: raw[:, 0:1024].rearrange("p (kc n) -> p kc n", kc=8), dt_bf=False)
                for kc in range(NCH):
                    A("pe", "matmul", pm[:, oc * 2:oc * 2 + 2], lhsT=v[:, kc, :],
                                                                    rhs=sT[:, kc * 2:kc * 2 + 2], start=(kc == 0), stop=(kc == 7),
                        reads=[res, "sT"], writes=["B7"])
            pmv = pm[:, 0:96].rearrange("p (c s) -> p c s", s=2)
            for s in range(2):
                A("dve", "tensor_tensor", out=modT[l][:, :, s], in0=pmv[:, :, s], in1=vecT[l][:, V_BADA:V_BADA + 48],
                                                          op=ALU.add, reads=["B7", "vec%d" % l], writes=["modT%d" % l])
            for s in range(2):
                m = modT[l]
                A("dve", "scalar_tensor_tensor", out=modv[l][:, s, 0, :], in0=m[:, 8:16, s], scalar=1.0,
                                                                      in1=vecT[l][:, V_N1G:V_N1G + 8], op0=ALU.add, op1=ALU.mult,
                    reads=["modT%d" % l, "vec%d" % l], writes=["modv%d" % l])
                A("dve", "scalar_tensor_tensor", out=modv[l][:, s, 3, :], in0=m[:, 32:40, s], scalar=1.0,
                                                                      in1=vecT[l][:, V_N2G:V_N2G + 8], op0=ALU.add, op1=ALU.mult,
                    reads=["modT%d" % l, "vec%d" % l], writes=["modv%d" % l])
                for kind, c0 in ((1, 0), (2, 16), (4, 24), (5, 40)):
                    A("dve", "tensor_copy", out=modv[l][:, s, kind, :], in_=m[:, c0:c0 + 8, s],
                        reads=["modT%d" % l], writes=["modv%d" % l])

        def emit_etab(l):
            bsrc = W[l]["bias"].rearrange("p (h n) -> p h n", h=NH)
            for h in range(NH):
                A("sp", "dma_start", out=stg, in_=bsrc[:, h, :], writes=["stg"] + CACC, dma_key="stg")
                A("act", "activation", out=Etab[:, h, :, :].rearrange("p i q -> p (i q)"), in_=stg, func=AF.Exp,
                    reads=["stg"] + CACC, writes=["Etab"])

        def emit_norm_mod(xsrc, n, mv, kinds, sqb, rsb, ttb, out_fn, psb, xres, tag):
            ps = bank(psb)[:, 0:n]
            for c in range(NCH):
                b = sqb[c % 2][:, 0:n]
                A("pool", "tensor_tensor", out=b, in0=xsrc[:, c, :], in1=xsrc[:, c, :], op=ALU.mult,
                    reads=xres(c), writes=[tag + "sq%d" % (c % 2)])
                A("pe", "matmul", ps, lhsT=ones_f, rhs=b, start=(c == 0), stop=(c == 7),
                    reads=[tag + "sq%d" % (c % 2), "ones"], writes=["B%d" % psb])
            rs = rsb[:, 0:n]
            A("act", "activation", out=rs, in_=ps, func=AF.Sqrt, bias=epsT, scale=1.0 / D,
                reads=["B%d" % psb, "eps"], writes=[tag + "rs"])
            A("dve", "reciprocal", out=rs, in_=rs, reads=[tag + "rs"], writes=[tag + "rs"])
            for c in range(NCH):
                t = ttb[c % 2][:, 0:n]
                A("dve", "tensor_tensor", out=t, in0=xsrc[:, c, :], in1=rs, op=ALU.mult,
                    reads=xres(c) + [tag + "rs"], writes=[tag + "tt%d" % (c % 2)])
                o, ores = out_fn(c)
                A("act", "activation", out=o, in_=t, func=AF.Identity, bias=mv[:, kinds[1], c:c + 1],
                                                                 scale=mv[:, kinds[0], c:c + 1],
                    reads=[tag + "tt%d" % (c % 2), "modv"], writes=ores)

        gctr = {"n": 0}

        def gslot(n):
            k = (5, 6, 0, 1, 2)[gctr["n"] % 5]
            gctr["n"] += 1
            return bank(k)[:, 0:n], "B%d" % k

        def proj(wview, wres, col0, ncol, rhs_fn, nk, n, rres):
            ps, pres = gslot(n)
            for k in range(nk):
                A("pe", "matmul", ps[0:ncol, :], lhsT=wview[:, k, col0:col0 + ncol], rhs=rhs_fn(k),
                                                  start=(k == 0), stop=(k == nk - 1),
                    reads=[wres] + rres, writes=[pres])
            return ps, pres

        def phase_A(c, stream, blk):
            l = c["l"]
            mv = modv[l][:, stream]
            lat = stream == 0
            if lat:
                lt0, nt, hs, xsrc, xres, need_mask = blk["lt0"], blk["nt"], blk["hs"], blk["xsrc"], blk["xres"], blk["mask"]
                n = nt * 128
                hb = hbuf[hs]
                hview = hb[:, :, 64:64 + n]
                if blk["prev_hs"] is not None:
                    pb = hbuf[blk["prev_hs"]]
                    pn = blk["prev_n"]
                    A("pool", "tensor_copy", out=hb[:, :, 0:64], in_=pb[:, :, 64 + pn - 64:64 + pn],
                        reads=["h%d" % blk["prev_hs"]], writes=["h%d" % hs])
                hres = "h%d" % hs
            else:
                n = CTX
                xsrc, xres = xc_res, (lambda cc: ["xc"])
                hb = hbuf[0]
                hview = hb[:, :, 64:64 + n]
                hres = "h0"
                need_mask = False
            emit_norm_mod(xsrc, n, mv, (0, 1), sq, rstd, tt, lambda cc: (hview[:, cc, :], [hres]), 7, xres, "A")
            wb = W[l]
            if lat:
                col = (lt0 - XIA) * 128
                bi = blk["hs"]
                A("sp", "dma_start", out=rC[bi][:, 0:n], in_=ropeC_d[:, col:col + n], writes=["rC%d" % bi], dma_key="rC%d" % bi)
                A("sp", "dma_start", out=rS[bi][:, 0:n], in_=ropeS_d[:, col:col + n], writes=["rS%d" % bi], dma_key="rS%d" % bi)
                if need_mask:
                    A("sp", "dma_start", out=tkm[:, 0:n], in_=tokm_d[:, col:col + n], writes=["tkm"], dma_key="tkm")
            for g in range(2):
                wv, wr = kgroup(wb["w_in_b"], 1536 + g * 256, 256, 8)
                if lat:
                    wv2, wr2 = kgroup(wb["w_rot_b"], 512 + g * 256, 256, 8)
                for j in range(2):
                    hp = g * 2 + j
                    ps, pres = proj(wv, wr, j * 128, 128, lambda k: hview[:, k, :], 8, n, [hres])
                    if lat:
                        ps2, pres2 = proj(wv2, wr2, j * 128, 128, lambda k: hview[:, k, :], 8, n, [hres])
                        a, b = r1[0][:, 0:n], r1[1][:, 0:n]
                        A("dve", "tensor_tensor", out=a, in0=ps, in1=rC[bi][:, 0:n], op=ALU.mult,
                            reads=[pres, "rC%d" % bi], writes=["r1a"])
                        A("dve", "tensor_tensor", out=b, in0=ps2, in1=rS[bi][:, 0:n], op=ALU.mult,
                            reads=[pres2, "rS%d" % bi], writes=["r1b"])
                        for t in range(nt):
                            sl = (lt0 + t - c["KA"]) % RING
                            A("pool", "tensor_tensor",
                                out=Kring[:, hp, 64 + sl * 128:64 + (sl + 1) * 128], in0=a[:, t * 128:(t + 1) * 128],
                                in1=b[:, t * 128:(t + 1) * 128], op=ALU.add, reads=["r1a", "r1b"], writes=["K%d" % sl])
                            if sl == RING - 1:
                                A("pool", "tensor_copy", out=Kring[:, hp, 0:64],
                                                                                 in_=Kring[:, hp, 64 + sl * 128 + 64:64 + (sl + 1) * 128],
                                    reads=["K%d" % sl], writes=["Kmar"])
                    else:
                        A("act", "copy", out=KcT[:, hp, :], in_=ps, reads=[pres], writes=["KcT"])
            wv0, wr0 = kgroup(wb["w_in_b"], 2048, 256, 8)
            wv1, wr1 = kgroup(wb["w_in_b"], 2304, 256, 8)

            def vtile(col_lo, dst, dres):
                for half, (wv, wr) in enumerate(((wv0, wr0), (wv1, wr1))):
                    ps, pres = gslot(256)
                    for k in range(NCH):
                        A("pe", "matmul", ps, lhsT=hb[:, k, col_lo:col_lo + 128], rhs=wv[:, k, :],
                                                                        start=(k == 0), stop=(k == 7),
                            reads=[wr, hres], writes=[pres])
                    A("act", "copy", out=dst[:, half * 4:(half + 1) * 4, 0:64],
                                                                  in_=ps.rearrange("p (h d) -> p h d", h=4),
                        reads=[pres], writes=[dres])
            if lat:
                for t in range(nt):
                    lt = lt0 + t
                    sl = (lt - c["KA"]) % RING
                    vtile(64 + t * 128, Vev[:, sl], "Ve%d" % sl)
                    if lt - 1 >= c["KA"]:
                        so = (lt - 1 - c["KA"]) % RING
                        vtile(64 + t * 128 - 64, Vod[:, so], "Vo%d" % so)
            else:
                for t in range(2):
                    vtile(64 + t * 128, Vc[:, t], "Vc")
            if lat or not c["last"]:
                for g in range(2):
                    wa_, wra = kgroup(wb["w_in_b"], g * 256, 256, 8)
                    wg_, wrg = kgroup(wb["w_in_b"], 512 + g * 256, 256, 8)
                    for j in range(2):
                        cc = g * 2 + j
                        pa, pra = proj(wa_, wra, j * 128, 128, lambda k: hview[:, k, :], 8, n, [hres])
                        pg, prg = proj(wg_, wrg, j * 128, 128, lambda k: hview[:, k, :], 8, n, [hres])
                        sg = sig[cc % 2][:, 0:n]
                        A("act", "activation", out=sg, in_=pg, func=AF.Sigmoid, reads=[prg],
                            writes=["m%d" % (cc % 2)])
                        if need_mask:
                            A("pool", "tensor_tensor", out=sg, in0=sg, in1=tkm[:, 0:n], op=ALU.mult,
                                reads=["m%d" % (cc % 2), "tkm"], writes=["m%d" % (cc % 2)])
                        if lat:
                            up = blk["upos"]
                            dst = uring[:, cc, 16 + up:16 + up + n]
                            ures = ["u%d" % (up // 128 + q_) for q_ in range(nt)]
                        else:
                            dst = ucx[:, cc, 16:16 + CTX]
                            ures = ["ucx"]
                        A("dve", "tensor_tensor", out=dst, in0=pa, in1=sg, op=ALU.mult,
                            reads=[pra, "m%d" % (cc % 2)], writes=ures)
                if lat:
                    up = blk["upos"]
                    URT = URING * NB
                    if up + n == URT:
                        A("pool", "tensor_copy", out=uring[:, :, 0:16], in_=uring[:, :, 16 + URT - 16:16 + URT],
                            reads=["u%d" % (URT // 128 - 1)], writes=["umarF"])
                    if up == 0:
                        A("pool", "tensor_copy", out=uring[:, :, 16 + URT:16 + URT + 16], in_=uring[:, :, 16:32],
                            reads=["u0"], writes=["umarB"])

        sctr = {"n": 0}
        actr = {"n": 0}
        dscr = af(512) if NDBG == 30 else None

        def attention(c, qT_fn, nq, chunks, ores, out_rows, fill=None):
            import os
            SER = ["ATTSER"] if os.environ.get("KSER") else []
            actr["n"] += 1
            DBGA = NDBG == 30 and actr["n"] == 1
            nch = len(chunks)
            width = nch * nq
            obank = psum_t[:, 3 * 512:5 * 512]
            ov = obank[0:nq, :].rearrange("p (h d) -> p h d", h=NH)
            pvq = []
            for h in range(NH):
                sb = sctr["n"] % 3
                sctr["n"] += 1
                sps = bank(sb)[:, 0:width]
                pb = Pb[sb][:, 0:width]
                hp, po = h // 2, (h % 2) * 64
                for i, ch in enumerate(chunks):
                    A("pe", "matmul",
                        sps[:, i * nq:(i + 1) * nq], lhsT=ch["k"](hp, po), rhs=qT_fn(hp, po), start=True, stop=True,
                        reads=ch["kr"] + ["QT"], writes=["B%d" % sb] + SER)
                A("act", "activation", out=pb, in_=sps, func=AF.Exp, scale=DH ** -0.5,
                    reads=["B%d" % sb], writes=["P%d" % sb] + SER)
                if DBGA and h < 4:
                    dbg(pb, ["P%d" % sb], "Pexp h%d" % h)
                i = 0
                while i < nch:
                    ch = chunks[i]
                    if ch["e"] is None:
                        i += 1
                        continue
                    if ch["rm"] is None:
                        j = i
                        while j + 1 < nch and chunks[j + 1]["e"] is not None and chunks[j + 1]["rm"] is None \
                                and chunks[j + 1]["ei"] == chunks[j]["ei"] + 1:
                            j += 1
                        e0 = ch["ei"]
                        ev = Etab[:, h, e0:e0 + (j - i + 1), :].rearrange("p i q -> p (i q)")
                        A("dve", "tensor_tensor", out=pb[:, i * nq:(j + 1) * nq],
                                                                                   in0=pb[:, i * nq:(j + 1) * nq], in1=ev, op=ALU.mult,
                            reads=["P%d" % sb, "Etab"], writes=["P%d" % sb] + SER)
                        i = j + 1
                    else:
                        A("dve", "scalar_tensor_tensor",
                            out=pb[:, i * nq:(i + 1) * nq], in0=pb[:, i * nq:(i + 1) * nq], scalar=ch["rm"],
                            in1=Etab[:, h, ch["ei"], :], op0=ALU.mult, op1=ALU.mult,
                            reads=["P%d" % sb, "Etab", "rm"], writes=["P%d" % sb] + SER)
                        i += 1
                if fill is not None:
                    fill["f"](fill["k"])

                def _pv(h=h, pb=pb, sb=sb):
                    for i, ch in enumerate(chunks):
                        A("pe", "matmul", ov[:, h, 0:65], lhsT=pb[:, i * nq:(i + 1) * nq],
                          rhs=ch["v"](h), start=(i == 0), stop=(i == nch - 1),
                          reads=["P%d" % sb] + ch["vr"], writes=["OB"] + SER)
                if pvq:
                    pvq.pop(0)()
                pvq.append(_pv)
            while pvq:
                pvq.pop(0)()
            ob = out_rows["otok"]
            rc = rcp[ob]
            A("dve", "reciprocal", out=rc[0:nq, :], in_=ov[:, :, 64], reads=["OB"], writes=["rcp%d" % ob] + SER)
            ot = Otok[ob][0:nq, :].rearrange("p (h d) -> p h d", h=NH)
            A("dve", "tensor_tensor", out=ot, in0=ov[:, :, 0:64], in1=rc[0:nq, :].unsqueeze(2).broadcast_to([nq, NH, 64]),
                                                 op=ALU.mult, reads=["OB", "rcp%d" % ob], writes=["Otok%d" % ob] + SER)
            if DBGA:
                dbg(rc[0:nq, :], ["rcp%d" % ob], "rc")
                dbg(Otok[ob][0:nq, :], ["Otok%d" % ob], "Otok")
            tp = bank(7).bitcast(BF16)
            for f in range(4):
                A("pe", "transpose", out=tp[:, f * nq:(f + 1) * nq], in_=Otok[ob][0:nq, f * 128:(f + 1) * 128],
                                                     identity=ident[0:nq, 0:nq], reads=["Otok%d" % ob, "ident"], writes=["B7"] + SER)
            c0 = out_rows["col"]
            A("act", "copy", out=OT[:, :, c0:c0 + nq], in_=tp[:, 0:4 * nq].rearrange("p (f q) -> p f q", f=4),
                reads=["B7"], writes=[ores] + SER)

        def phase_B(c, stream, blk):
            l = c["l"]
            wb = W[l]
            mv = modv[l][:, stream]
            lat = stream == 0
            if lat:
                lt0, nt, hs = blk["lt0"], blk["nt"], blk["hs"]
                n = nt * 128
                hview = hbuf[hs][:, :, 64:64 + n]
                hres = "h%d" % hs
                bi = blk["hs"]
                xcol = (lt0 - XA) * 128
                xv = x_res[:, :, xcol:xcol + n]
                xres = lambda cc: ["x%d" % cc]
            else:
                n = CTX
                hview = hbuf[0][:, :, 64:64 + n]
                hres = "h0"
                xv = xc_res
                xres = lambda cc: ["xc"]
            for g in range(2):
                wv, wr = kgroup(wb["w_in_b"], 1024 + g * 256, 256, 8)
                if lat:
                    wv2, wr2 = kgroup(wb["w_rot_b"], g * 256, 256, 8)
                for j in range(2):
                    hp = g * 2 + j
                    ps, pres = proj(wv, wr, j * 128, 128, lambda k: hview[:, k, :], 8, n, [hres])
                    if lat:
                        ps2, pres2 = proj(wv2, wr2, j * 128, 128, lambda k: hview[:, k, :], 8, n, [hres])
                        a, b = r1[0][:, 0:n], r1[1][:, 0:n]
                        A("dve", "tensor_tensor", out=a, in0=ps, in1=rC[bi][:, 0:n], op=ALU.mult,
                            reads=[pres, "rC%d" % bi], writes=["r1a"])
                        A("dve", "tensor_tensor", out=b, in0=ps2, in1=rS[bi][:, 0:n], op=ALU.mult,
                            reads=[pres2, "rS%d" % bi], writes=["r1b"])
                        A("pool", "tensor_tensor", out=QT[:, hp, 0:n], in0=a, in1=b, op=ALU.add,
                            reads=["r1a", "r1b"], writes=["QT"])
                    else:
                        A("act", "copy", out=QT[:, hp, 0:n], in_=ps, reads=[pres], writes=["QT"])
            if not lat and NDBG == 50:
                for hp_ in range(4):
                    dbg(QT[:, hp_, 0:n], ["QT"], "QT%d" % hp_)
                for hp_ in range(4):
                    dbg(KcT[:, hp_, :], ["KcT"], "KcT%d" % hp_)
                dbg(hview[:, 0, :], [hres], "hc0")
                dbg(hview[:, 7, :], [hres], "hc7")
            if not lat and False:
                dbg(hview[:, 0, :], [hres], "hc")
                dbg(KcT[:, 0, :], ["KcT"], "KcT")
                dbg(Vc[:, 0].rearrange("p h d -> p (h d)")[:, 0:512], ["Vc"], "Vc")
                dbg(QT[:, 0, 0:n], ["QT"], "QT")
            vec = vecT[l]

            def conv_gen():
              dctr = 0
              for cc in range(4):
                  if lat:
                      up = blk["upos"]
                      base = 16 + up
                      usrc = uring
                      nsl = URING * TPB
                      ur = ["u%d" % ((up // 128 + q_) % nsl) for q_ in (-1, 0, 1, 2)] + ["umarF", "umarB"]
                  else:
                      base = 16
                      usrc = ucx
                      ur = ["ucx"]
                  bk = 5 + cc // 2
                  cps = bank(bk)[:, (cc % 2) * 256:(cc % 2) * 256 + n]
                  for k in range(CK):
                      src = usrc[:, cc, base + k - 15:base + k - 15 + n]
                      wk = vec[:, V_CDW + cc * CK + k:V_CDW + cc * CK + k + 1]
                      d_ = dg[dctr % 4]
                      dres = "dg%d" % (dctr % 4)
                      dctr += 1
                      A("dve", "tensor_scalar", out=d_, in0=ident, scalar1=wk, scalar2=None, op0=ALU.mult,
                        reads=["ident", "vec%d" % l], writes=[dres])
                      A("pe", "matmul", cps, lhsT=d_, rhs=src, start=(k == 0), stop=(k == CK - 1),
                        reads=ur + [dres], writes=["B%d" % bk])
                      yield

            cgen = conv_gen()

            def filler(k):
                for _ in range(k):
                    try:
                        next(cgen)
                    except StopIteration:
                        return
            nheads_total = (nt * 2 * NH) if lat else (4 * NH)
            per_head = -(-4 * CK // nheads_total)
            FILL = {"f": filler, "k": per_head}
            if lat:
                for t in range(nt):
                    lt = lt0 + t
                    for rr in range(2):
                        r = 2 * lt + rr
                        qc0 = t * 128 + rr * 64
                        special = None
                        if lt in (0, 1):
                            special = ("top", r)
                        elif lt in (OWN - 2, OWN - 1):
                            special = ("bot", r - (2 * OWN - 4))
                        if special is None:
                            cis = [0, 1, 2, 3]
                        elif special[0] == "top":
                            cis = [0, 1, 2, 3, 4, 5]
                        else:
                            cis = [-2, -1, 0, 1, 2, 3]
                        chunks = []
                        for ii, ci in enumerate(cis):
                            kr0 = r - 4 + 2 * ci
                            pos = (kr0 * 64 - c["KA"] * 128)
                            rp = pos % (RING * 128)
                            if rp + 128 <= RING * 128:
                                ka = 64 + rp
                                kres = ["K%d" % (rp // 128)] + (["K%d" % ((rp // 128 + 1) % RING)] if rp % 128 else [])
                            else:
                                ka = 0
                                kres = ["Kmar", "K0"]
                            if kr0 % 2 == 0:
                                vs = ((kr0 // 2) - c["KA"]) % RING
                                vfn = (lambda h, vs=vs: Vev[:, vs, h, :])
                                vres = ["Ve%d" % vs]
                            else:
                                vs = (((kr0 - 1) // 2) - c["KA"]) % RING
                                vfn = (lambda h, vs=vs: Vod[:, vs, h, :])
                                vres = ["Vo%d" % vs]
                            rmap = None
                            if special is not None:
                                sidx = (special[1] + (0 if special[0] == "top" else 4)) * 6 + ii
                                rmap = rmT[:, sidx:sidx + 1]
                            chunks.append(dict(k=(lambda hp, po, ka=ka: Kring[po:po + 64, hp, ka:ka + 128]), kr=kres,
                                               v=vfn, vr=vres, e=True, ei=ci + 2, rm=rmap))
                        for t2 in range(2):
                            chunks.append(dict(k=(lambda hp, po, t2=t2: KcT[po:po + 64, hp, t2 * 128:(t2 + 1) * 128]), kr=["KcT"],
                                               v=(lambda h, t2=t2: Vc[:, t2, h, :]), vr=["Vc"], e=None, ei=None, rm=None))
                        attention(c, lambda hp, po, qc0=qc0: QT[po:po + 64, hp, qc0:qc0 + 64], 64, chunks, "OT",
                                  dict(otok=(t * 2 + rr) % 2, col=qc0), fill=FILL)
            else:
                for t in range(2):
                    for hh in range(2):
                        qc0 = t * 128 + hh * 64
                        chunks = [dict(k=(lambda hp, po, t2=t2: KcT[po:po + 64, hp, t2 * 128:(t2 + 1) * 128]), kr=["KcT"],
                                       v=(lambda h, t2=t2: Vc[:, t2, h, :]), vr=["Vc"], e=None, ei=None, rm=None) for t2 in range(2)]
                        attention(c, lambda hp, po, qc0=qc0: QT[po:po + 64, hp, qc0:qc0 + 64], 64, chunks, "OT",
                                  dict(otok=(t * 2 + hh) % 2, col=qc0), fill=FILL)
            filler(10 ** 6)
            for cc in range(4):
                bk = 5 + cc // 2
                cps = bank(bk)[:, (cc % 2) * 256:(cc % 2) * 256 + n]
                A("act", "activation", out=cacc[:, cc, 0:n], in_=cps, func=AF.Identity, bias=vec[:, V_CDB + cc:V_CDB + cc + 1], scale=1.0,
                  reads=["B%d" % bk, "vec%d" % l], writes=["cacc%d" % cc])
            pmu = bank(5)[:, 0:n]

            pm2 = bank(6)[:, 0:n]
            for cc in range(4):
                b = lsq[cc % 2][:, 0:n]
                A("pool", "tensor_tensor", out=b, in0=cacc[:, cc, 0:n], in1=cacc[:, cc, 0:n], op=ALU.mult,
                    reads=["cacc%d" % cc], writes=["Asq%d" % (cc % 2)])
                A("pe", "matmul", pmu, lhsT=ones_f, rhs=cacc[:, cc, 0:n], start=(cc == 0), stop=(cc == 3),
                    reads=["cacc%d" % cc, "ones"], writes=["B5"])
                A("pe", "matmul", pm2, lhsT=ones_f, rhs=b, start=(cc == 0), stop=(cc == 3),
                    reads=["Asq%d" % (cc % 2), "ones"], writes=["B6"])
            gctr["n"] = 0
            mu, rs = lmu[:, 0:n], lrs[:, 0:n]
            A("act", "activation", out=mu, in_=pmu, func=AF.Identity, scale=1.0 / CDIM, reads=["B5"], writes=["Ars"])
            A("dve", "tensor_tensor", out=rs, in0=mu, in1=mu, op=ALU.mult, reads=["Ars"], writes=["lrs"])
            A("dve", "scalar_tensor_tensor", out=rs, in0=pm2, scalar=1.0 / CDIM, in1=rs, op0=ALU.mult, op1=ALU.subtract,
                reads=["B6", "lrs"], writes=["lrs"])
            A("act", "activation", out=rs, in_=rs, func=AF.Sqrt, bias=epsT, scale=1.0, reads=["lrs", "eps"], writes=["lrs"])
            A("dve", "reciprocal", out=rs, in_=rs, reads=["lrs"], writes=["lrs"])
            for cc in range(4):
                acc = cacc[:, cc, 0:n]
                A("dve", "tensor_tensor", out=acc, in0=acc, in1=mu, op=ALU.subtract,
                    reads=["cacc%d" % cc, "Ars"], writes=["cacc%d" % cc])
                A("pool", "tensor_tensor", out=acc, in0=acc, in1=rs, op=ALU.mult,
                    reads=["cacc%d" % cc, "lrs"], writes=["cacc%d" % cc])
                A("act", "activation", out=cT[:, cc, 0:n], in_=acc, func=AF.Silu,
                                                                  bias=vec[:, V_LNB + cc:V_LNB + cc + 1], scale=vec[:, V_LNG + cc:V_LNG + cc + 1],
                    reads=["cacc%d" % cc, "vec%d" % l], writes=["cT"])
            gctr["n"] = 0
            if NDBG == 40 and not lat:
                for f_ in range(4):
                    dbg(OT[:, f_, 0:n], ["OT"], "cOT%d" % f_)
            if NDBG == 10 and lat and blk["lt0"] == -1:
                dbg(QT[:, 0, 0:n], ["QT"], "QT")
                dbg(cT[:, 0, 0:n], ["cT"], "cT")
                for f_ in range(4):
                    dbg(OT[:, f_, 0:n], ["OT"], "OT%d" % f_)
                dbg(Kring[:, 0, 0:512], ["K0"], "Kring0")
                dbg(Vev.rearrange("p t h d -> p (t h d)")[:, 0:512], ["Ve0"], "Vev0")
                dbg(Vod.rearrange("p t h d -> p (t h d)")[:, 0:512], ["Vo0"], "Vod0")
            if not lat and False:
                dbg(cT[:, 0, 0:n], ["cT"], "cT")
                dbg(OT[:, 0, 0:n], ["OT"], "OT")
                dbg(Otok[0][0:64, :], ["Otok0"], "Otok0")
                dbg(Pb[0][:, 0:128], ["P0"], "P0")
            for g in range(4):
                wcv, wcr = kgroup(wb["w_co_b"], g * 256, 256, 4)
                wnv, wnr = kgroup(wb["w_no_b"], g * 256, 256, 4)
                wgc, wgcr = kgroup(wb["w_in_b"], 2560 + g * 256, 256, 8)
                wga, wgar = kgroup(wb["w_in_b"], 3584 + g * 256, 256, 8)
                for j in range(2):
                    oc = g * 2 + j
                    pg1, pg1r = proj(wgc, wgcr, j * 128, 128, lambda k: hview[:, k, :], 8, n, [hres])
                    g1_ = gt[0][:, 0:n]
                    A("act", "activation", out=g1_, in_=pg1, func=AF.Sigmoid, reads=[pg1r], writes=["gt0"])
                    py1, py1r = proj(wcv, wcr, j * 128, 128, lambda k: cT[:, k, 0:n], 4, n, ["cT"])
                    ma = m12[0][:, 0:n]
                    A("dve", "tensor_tensor", out=ma, in0=py1, in1=g1_, op=ALU.mult,
                        reads=[py1r, "gt0"], writes=["m0"])
                    pg2, pg2r = proj(wga, wgar, j * 128, 128, lambda k: hview[:, k, :], 8, n, [hres])
                    g2_ = gt[1][:, 0:n]
                    A("act", "activation", out=g2_, in_=pg2, func=AF.Sigmoid, reads=[pg2r], writes=["gt1"])
                    py2, py2r = proj(wnv, wnr, j * 128, 128, lambda k: OT[:, k, 0:n], 4, n, ["OT"])
                    mb = m12[1][:, 0:n]
                    A("dve", "tensor_tensor", out=mb, in0=py2, in1=g2_, op=ALU.mult,
                        reads=[py2r, "gt1"], writes=["m1"])
                    A("pool", "tensor_tensor", out=mrg[:, oc, 0:n], in0=ma, in1=mb, op=ALU.add,
                        reads=["m0", "m1"], writes=["mrg"])
            if lat and blk["lt0"] == -1:
                dbg(mrg[:, 0, 0:n], ["mrg"], "mrg")
            for g in range(4):
                wov, wor = kgroup(wb["w_out_b"], g * 256, 256, 8)
                for j in range(2):
                    oc = g * 2 + j
                    po, por = proj(wov, wor, j * 128, 128, lambda k: mrg[:, k, 0:n], 8, n, ["mrg"])
                    A("dve", "scalar_tensor_tensor", out=xv[:, oc, :], in0=po, scalar=mv[:, 2, oc:oc + 1],
                                                                            in1=xv[:, oc, :], op0=ALU.mult, op1=ALU.add,
                        reads=[por, "modv"] + xres(oc), writes=xres(oc))

        fprev = {"n": None}

        def ffn_block(c, stream, t0, n, need_mask):
            l = c["l"]
            wb = W[l]
            vec = vecT[l]
            mv = modv[l][:, stream]
            lat = stream == 0
            if lat:
                xin = x_res[:, :, t0 - 1:t0 + n + 1]
                xo = x_res[:, :, t0:t0 + n]
                xres = lambda cc: ["x%d" % cc]
                hv = h2[:, :, 0:n + 2]
                nprev = fprev["n"]
                if nprev is not None:
                    A("pool", "tensor_copy", out=hstash[:, :, 0:1], in_=h2[:, :, nprev:nprev + 1], reads=["h2"], writes=["hstash"])
                emit_norm_mod(xin, n + 2, mv, (3, 4), fsq, frs, ftt, lambda cc: (hv[:, cc, :], ["h2"]), 7, xres, "F")
                if nprev is not None:
                    A("pool", "tensor_copy", out=h2[:, :, 0:1], in_=hstash[:, :, 0:1], reads=["hstash"], writes=["h2"])
                fprev["n"] = n
                if need_mask:
                    col = t0 - 1 + (XA - XIA) * 128
                    A("sp", "dma_start", out=ftm[:, 0:n + 2], in_=tokm_d[:, col:col + n + 2], writes=["ftm"], dma_key="ftm")
                    for cc in range(NCH):
                        A("pool", "tensor_tensor", out=hv[:, cc, :], in0=hv[:, cc, :], in1=ftm[:, 0:n + 2], op=ALU.mult,
                            reads=["h2", "ftm"], writes=["h2"])
            else:
                xo = xc_res
                xres = lambda cc: ["xc"]
                hv = h2[:, :, 0:n + 2]
                A("pool", "memset", h2[:, :, 0:1], 0.0, writes=["h2"])
                A("pool", "memset", h2[:, :, n + 1:n + 2], 0.0, writes=["h2"])
                emit_norm_mod(xc_res, n, mv, (3, 4), fsq, frs, ftt, lambda cc: (h2[:, cc, 1:n + 1], ["h2"]), 7, xres, "F")
            for j in range(NJ):
                wv, wr = kgroup(wb["w_up_b"], j * 256, 256, 8)
                outs = []
                for half in range(2):
                    k_ = 1 + 2 * (j % 2) + half
                    ps = bank(k_)[:, 0:n + 2]
                    pres = "B%d" % k_
                    for k in range(NCH):
                        A("pe", "matmul", ps, lhsT=wv[:, k, half * 128:(half + 1) * 128],
                                                                                 rhs=hv[:, k, :], start=(k == 0), stop=(k == 7),
                            reads=[wr, "h2"], writes=[pres])
                    ch = 2 * j + half
                    w0 = vec[:, V_FDW + ch * 3 + 0:V_FDW + ch * 3 + 1]
                    w1 = vec[:, V_FDW + ch * 3 + 1:V_FDW + ch * 3 + 2]
                    w2 = vec[:, V_FDW + ch * 3 + 2:V_FDW + ch * 3 + 3]
                    bb = vec[:, V_FDB + ch:V_FDB + ch + 1]
                    tb = (fta if half == 0 else ftg)[j % 2][:, 0:n]
                    tres = "ft%d%d" % (half, j % 2)
                    A("act", "activation", out=tb, in_=ps[:, 1:n + 1], func=AF.Identity, bias=bb, scale=w1,
                        reads=[pres, "vec%d" % l], writes=[tres])
                    A("dve", "scalar_tensor_tensor", out=tb, in0=ps[:, 0:n], scalar=w0, in1=tb,
                                                                                   op0=ALU.mult, op1=ALU.add,
                        reads=[pres, tres, "vec%d" % l], writes=[tres])
                    A("dve", "scalar_tensor_tensor", out=tb, in0=ps[:, 2:n + 2], scalar=w2, in1=tb,
                                                                                   op0=ALU.mult, op1=ALU.add,
                        reads=[pres, tres, "vec%d" % l], writes=[tres])
                    outs.append((tb, tres))
                (ta, tar), (tg, tgr) = outs
                sg = fsg[j % 2][:, 0:n]
                A("act", "activation", out=sg, in_=tg, func=AF.Silu, reads=[tgr], writes=["fsg%d" % (j % 2)])
                A("pool", "tensor_tensor", out=hid[:, j, 0:n], in0=ta, in1=sg, op=ALU.mult,
                    reads=[tar, "fsg%d" % (j % 2)], writes=["hid"])
            for oc in range(NCH):
                halves = []
                for hf in range(2):
                    src = wb["w_down_b"][hf * 11 * 128:(hf + 1) * 11 * 128, oc * 128:(oc + 1) * 128].rearrange("(j p) n -> p j n", p=128)
                    halves.append(wload(src, lambda raw: raw[:, 0:11 * 128].rearrange("p (j n) -> p j n", j=11),
                                        reads=CASTRES[id(wb["w_down_b"])]))
                k_ = 5 + (oc % 2)
                ps = bank(k_)[:, 0:n]
                for j in range(NJ):
                    wv, wr = halves[j // 11]
                    A("pe", "matmul", ps, lhsT=wv[:, j % 11, :], rhs=hid[:, j, 0:n], start=(j == 0), stop=(j == NJ - 1),
                        reads=[wr, "hid"], writes=["B%d" % k_])
                A("dve", "scalar_tensor_tensor", out=xo[:, oc, :], in0=ps, scalar=mv[:, 5, oc:oc + 1], in1=xo[:, oc, :],
                                                                        op0=ALU.mult, op1=ALU.add,
                    reads=["B%d" % k_, "modv"] + xres(oc), writes=xres(oc))

        def final_out(c):
            l = c["l"]
            vec = vecT[l]
            col0 = (0 - XA) * 128
            for bi in range(OWN * 128 // 512):
                t0 = col0 + bi * 512
                n = 512
                xin = x_res[:, :, t0:t0 + n]
                ps = bank(7)[:, 0:n]
                for cc in range(NCH):
                    b = fsq[cc % 2][:, 0:n]
                    A("pool", "tensor_tensor", out=b, in0=xin[:, cc, :], in1=xin[:, cc, :], op=ALU.mult,
                        reads=["x%d" % cc], writes=["Fsq%d" % (cc % 2)])
                    A("pe", "matmul", ps, lhsT=ones_f, rhs=b, start=(cc == 0), stop=(cc == 7),
                        reads=["Fsq%d" % (cc % 2), "ones"], writes=["B7"])
                rs = frs[:, 0:n]
                A("act", "activation", out=rs, in_=ps, func=AF.Sqrt, bias=epsT, scale=1.0 / D, reads=["B7", "eps"], writes=["Frs"])
                A("dve", "reciprocal", out=rs, in_=rs, reads=["Frs"], writes=["Frs"])
                for cc in range(NCH):
                    o = fo[cc % 2][:, 0:n]
                    A("dve", "scalar_tensor_tensor", out=o, in0=xin[:, cc, :], scalar=vec[:, V_FNG + cc:V_FNG + cc + 1],
                                                                          in1=rs, op0=ALU.mult, op1=ALU.mult,
                        reads=["x%d" % cc, "Frs", "vec%d" % l], writes=["fo%d" % (cc % 2)])
                    A("sp", "dma_start", out=outT_d[cc, :, bi * 512:(bi + 1) * 512], in_=o,
                        reads=["fo%d" % (cc % 2)], dma_key="out%d" % (cc % 2))

        UR = URING * NB
        for c in layers:
            l = c["l"]
            first_layer = c is layers[0]
            half = (mark_persist + AR["top"]) // 2
            A("pool", "memset", arena_t[:, mark_persist:half], 0.0, writes=["ARENA0"])
            A("dve", "memset", arena_t[:, half:AR["top"]], 0.0, writes=["ARENA1"])
            P.barrier()
            A("pool", "memset", Vev[:, :, :, 64:65], 1.0, writes=["Vev"])
            A("pool", "memset", Vod[:, :, :, 64:65], 1.0, writes=["Vod"])
            emit_mod(l)
            A("dve", "tensor_copy", out=modv[l][:, 0, 0, 0:1], in_=modv[l][:, 0, 0, 0:1],
                reads=["modv%d" % l], writes=["modv"])
            emit_etab(l)
            if NDBG == 60 and not first_layer:
                dbg(x_res[:, 0, 0:512], ["x0"], "x1 cols0-512")
                dbg(x_res[:, 0, 1000:1512], ["x0"], "x1 cols1000-1512")
                dbg(x_res[:, 7, NXT - 512:NXT], ["x7"], "x1 last512 ch7")
                dbg(xc_res[:, 0, :], ["xc"], "xc1")
                dbg(modv[l].rearrange("p s k c -> p (s k c)"), ["modv"], "modv1")
                dbg(Etab[:, 0, :, :].rearrange("p i q -> p (i q)"), ["Etab"], "Etab1 h0")
            if NDBG == 40:
                for h_ in range(NH):
                    dbg(Etab[:, h_, :, :].rearrange("p i q -> p (i q)"), ["Etab"], "Etab%d" % h_)
            phase_A(c, 1, None)
            if not c["last"]:
                phase_B(c, 1, None)
            blocks = []
            lt, idx = c["KA"], 0
            while lt < c["KB"]:
                tm = c["TA"] <= lt < c["TB"]
                nt = 1
                if tm and lt % 2 == 0 and lt + 1 < c["TB"]:
                    nt = 2
                blocks.append(dict(lt0=lt, nt=nt, tm=tm, idx=idx))
                lt += nt
                idx += 1
            KA0 = c["KA"] - (c["KA"] % 2)
            prev, tmc = None, 0
            for b in blocks:
                if b["tm"]:
                    b["hs"] = tmc % 2
                    tmc += 1
                else:
                    b["hs"] = 2
                b["upos"] = ((b["lt0"] - KA0) * 128) % UR
                b["prev_hs"] = prev["hs"] if prev else None
                b["prev_n"] = prev["nt"] * 128 if prev else None
                b["mask"] = (b["lt0"] < 0) or (b["lt0"] + b["nt"] > OWN)
                b["need"] = min(b["lt0"] + b["nt"] - 1 + 2, c["KB"] - 1)
                prev = b
            pend = []
            for b in blocks:
                resident = XA <= b["lt0"] and b["lt0"] + b["nt"] <= XB
                if resident:
                    xcol = (b["lt0"] - XA) * 128
                    b["xsrc"] = x_res[:, :, xcol:xcol + b["nt"] * 128]
                    b["xres"] = lambda cc: ["x%d" % cc]
                else:
                    assert first_layer and b["nt"] == 1 and not b["tm"]
                    col = (b["lt0"] - XIA) * 128
                    A("sp", "dma_start", out=xk, in_=xT_d[:, :, col:col + 128].rearrange("c p t -> p c t"), writes=["xk"] + CACC, dma_key="xk")
                    b["xsrc"] = xk
                    b["xres"] = lambda cc: ["xk"] + CACC
                phase_A(c, 0, b)
                covered = b["lt0"] + b["nt"] - 1
                if b["tm"]:
                    pend.append(b)
                while pend and pend[0]["need"] <= covered:
                    phase_B(c, 0, pend.pop(0))
            assert not pend
            if NDBG == 61 and not first_layer:
                for q_ in range(4):
                    dbg(x_res[:, 0, 384 + q_ * 512:384 + (q_ + 1) * 512], ["x0"], "xmid own q%d" % q_)
            P.barrier()
            if not c["last"]:
                ffn_block(c, 1, 0, CTX, False)
            f0 = c["F0"] - XA * 128
            f1 = c["F1"] - XA * 128
            fprev["n"] = None
            t0 = f0
            while t0 < f1:
                n = min(FBLK, f1 - t0)
                lo_t = (t0 - 1) // 128 + XA
                hi_t = (t0 + n) // 128 + XA
                ffn_block(c, 0, t0, n, lo_t < 0 or hi_t >= OWN)
                t0 += n
            if NDBG == 61 and not first_layer:
                for q_ in range(4):
                    dbg(x_res[:, 0, 384 + q_ * 512:384 + (q_ + 1) * 512], ["x0"], "x2 own q%d" % q_)
            if c["last"]:
                final_out(c)
            P.barrier()
        outs = ["out0", "out1"]
        if not cfg["final"]:
            col0 = (0 - XA) * 128
            for cc in range(NCH):
                A("sp", "dma_start", out=outT_d[cc], in_=x_res[:, cc, col0:col0 + OWN * 128], reads=["x%d" % cc],
                    dma_key="out%d" % (cc % 2))
            A("sp", "dma_start", out=xcT_d.rearrange("c p t -> p c t"), in_=xc_res, reads=["xc"], dma_key="out0")
        P.emit(final_wait_keys=outs + (["dbg"] if dbgc["n"] else []))
    return nc


ROPE_THETA = 10000.0


def _rope_tables(g_tiles):
    half = DH // 2
    inv_freq = (ROPE_THETA ** (-np.arange(0, half, 2, dtype=np.float32) / half)).astype(np.float32)
    p = np.arange(128)
    d = p % 64
    f = d % 16
    first = (d % 32) < 16
    use_row = d < 32
    C = np.zeros((128, len(g_tiles) * 128), np.float32)
    S = np.zeros_like(C)
    i = np.arange(128)
    for k, g in enumerate(g_tiles):
        row = (2 * g + i // 64).astype(np.float32)
        col = (i % 64).astype(np.float32)
        pos = np.where(use_row[:, None], row[None, :], col[None, :]).astype(np.float32)
        ang = (pos * inv_freq[f][:, None]).astype(np.float32)
        C[:, k * 128:(k + 1) * 128] = np.cos(ang)
        sn = np.sin(ang)
        S[:, k * 128:(k + 1) * 128] = np.where(first[:, None], -sn, sn)
    return C, S


def _bias_table(rpb_l):
    p = np.arange(128)
    kr2 = p // 64
    kc = p % 64
    qc = np.arange(64)
    cs = np.clip(qc - 8, 0, 48)
    out = np.full((128, NH, 8, 64), NEG, np.float32)
    for ei in range(8):
        ci = ei - 2
        dr = -4 + 2 * ci + kr2
        dc = kc[:, None] - qc[None, :]
        ok = (kc[:, None] >= cs[None, :]) & (kc[:, None] < cs[None, :] + 16) & (np.abs(dr)[:, None] <= 7)
        dri = np.clip(dr + 7, 0, 14)
        dci = np.clip(dc + 15, 0, 30)
        vals = rpb_l[:, dri[:, None], dci]
        out[:, :, ei, :] = np.where(ok[:, None, :], vals.transpose(1, 0, 2), np.float32(NEG))
    return out.reshape(128, NH * 8 * 64)


def _row_masks(ci_core):
    p = np.arange(128)
    kr2 = p // 64
    rm = np.zeros((128, 48), np.float32)
    for sidx in range(8):
        if sidx < 4:
            r = sidx
            cis = range(0, 6)
        else:
            r = 28 + (sidx - 4)
            cis = range(-2, 4)
        R = ci_core * 32 + r
        rs = min(max(R - 4, 0), 120)
        for ii, ci in enumerate(cis):
            kr = R - 4 + 2 * ci + kr2
            rm[:, sidx * 6 + ii] = ((kr >= rs) & (kr < rs + 8)).astype(np.float32)
    return rm


def _pack_vec(inp, l):
    v = np.zeros((128, NV), np.float32)
    fm = lambda a: np.ascontiguousarray(np.asarray(a, np.float32).reshape(-1, 128).T)
    v[:, V_BADA:V_BADA + 48] = fm(inp["b_ada"][l])
    v[:, V_N1G:V_N1G + 8] = fm(inp["norm1_g"][l])
    v[:, V_N2G:V_N2G + 8] = fm(inp["norm2_g"][l])
    cdw = np.asarray(inp["conv_dw"][l], np.float32)
    for cc in range(4):
        v[:, V_CDW + cc * CK:V_CDW + (cc + 1) * CK] = cdw[:, cc * 128:(cc + 1) * 128].T
    v[:, V_CDB:V_CDB + 4] = fm(inp["conv_dw_b"][l])
    v[:, V_LNG:V_LNG + 4] = fm(inp["conv_ln_g"][l])
    v[:, V_LNB:V_LNB + 4] = fm(inp["conv_ln_b"][l])
    fdw = np.asarray(inp["ffn_dw"][l], np.float32)
    fdb = np.asarray(inp["ffn_dw_b"][l], np.float32)
    for j in range(NJ):
        for half in range(2):
            ch = 2 * j + half
            c0 = half * FFN + j * 128
            v[:, V_FDW + ch * 3:V_FDW + ch * 3 + 3] = fdw[:, c0:c0 + 128].T
            v[:, V_FDB + ch] = fdb[c0:c0 + 128]
    v[:, V_FNG:V_FNG + 8] = fm(inp["final_norm_g"])
    return v


def _weights_for_layer(inp, l):
    w_in = np.asarray(inp["w_in"][l], np.float32)
    d = np.arange(64)
    partner = np.where((d % 32) < 16, d + 16, d - 16)
    qcols = np.concatenate([1024 + h * 64 + partner for h in range(NH)])
    kcols = np.concatenate([1536 + h * 64 + partner for h in range(NH)])
    w_rot = np.ascontiguousarray(w_in[:, np.concatenate([qcols, kcols])])
    w_up = np.asarray(inp["w_up"][l], np.float32)
    perm = np.concatenate([np.concatenate([np.arange(j * 128, (j + 1) * 128), FFN + np.arange(j * 128, (j + 1) * 128)])
                           for j in range(NJ)])
    return {
        "w_in%d" % l: np.ascontiguousarray(w_in), "w_rot%d" % l: w_rot,
        "w_co%d" % l: np.ascontiguousarray(np.asarray(inp["w_conv_out"][l], np.float32)),
        "w_no%d" % l: np.ascontiguousarray(np.asarray(inp["w_na_out"][l], np.float32)),
        "w_out%d" % l: np.ascontiguousarray(np.asarray(inp["w_out"][l], np.float32)),
        "w_up%d" % l: np.ascontiguousarray(w_up[:, perm]),
        "w_down%d" % l: np.ascontiguousarray(np.asarray(inp["w_down"][l], np.float32)),
        "w_ada%d" % l: np.ascontiguousarray(np.asarray(inp["w_ada"][l], np.float32)),
        "vec%d" % l: _pack_vec(inp, l),
        "bias%d" % l: _bias_table(np.asarray(inp["na_rpb"][l], np.float32)),
    }


def _core_inputs(cfg, inp, x, ctx, shared):
    XIA, XIB = cfg["XIA"], cfg["XIB"]
    maps = []
    for core in range(8):
        b, ci = core // 4, core % 4
        tiles = [ci * OWN + lt for lt in range(XIA, XIB)]
        nit = len(tiles) * 128
        xr = np.zeros((nit, D), np.float32)
        tm = np.zeros((nit,), np.float32)
        for k, g in enumerate(tiles):
            if 0 <= g < SEQ // 128:
                xr[k * 128:(k + 1) * 128] = x[b, g * 128:(g + 1) * 128]
                tm[k * 128:(k + 1) * 128] = 1.0
        C, S = _rope_tables(tiles)
        cin = np.zeros((128, NCH * 2), np.float32)
        cin[:, 0::2] = np.asarray(inp["c"], np.float32)[b].reshape(NCH, 128).T
        cin[:, 1::2] = np.asarray(inp["c_ctx"], np.float32).reshape(NCH, 128).T
        m = {
            "xT": np.ascontiguousarray(xr.T.reshape(NCH, 128, nit)),
            "ctxT": np.ascontiguousarray(ctx[b].T.reshape(NCH, 128, CTX)),
            "cin": cin,
            "tokm": np.ascontiguousarray(np.broadcast_to(tm[None, :], (128, nit))),
            "ropeC": C, "ropeS": S,
            "rm": _row_masks(ci),
        }
        m.update(shared)
        maps.append(m)
    return maps


_NC_CACHE = {}
MODE = "fused"


def _run(mode, inp, x, ctx):
    cfg = make_cfg(mode)
    if mode not in _NC_CACHE:
        _NC_CACHE[mode] = build(cfg)
    nc = _NC_CACHE[mode]
    shared = {}
    for c in cfg["layers"]:
        shared.update(_weights_for_layer(inp, c["l"]))
    maps = _core_inputs(cfg, inp, x, ctx, shared)
    res = run_bass_kernel_spmd(nc, maps, core_ids=list(range(8)))
    if cfg.get("ndbg"):
        np.save("_dbg.npy", np.asarray(res.results[0]["dbg"]))
    xo = np.zeros((2, SEQ, D), np.float32)
    xc = None if cfg["final"] else np.zeros((2, CTX, D), np.float32)
    for core in range(8):
        b, ci = core // 4, core % 4
        r = res.results[core]
        xo[b, ci * OWN * 128:(ci + 1) * OWN * 128] = np.asarray(r["outT"]).reshape(D, OWN * 128).T
        if xc is not None:
            xc[b] = np.asarray(r["xcT"]).reshape(D, CTX).T
    return xo, xc


def kernel(**inputs):
    x = np.asarray(inputs["x"], np.float32)
    ctx = np.asarray(inputs["ctx"], np.float32)
    if MODE == "fused":
        out, _ = _run("fused", inputs, x, ctx)
        return out
    x1, xc1 = _run("l0", inputs, x, ctx)
    out, _ = _run("l1", inputs, x1, xc1)
    return out
```

```python
import contextlib
import numpy as np
import concourse.bass as bass
import concourse.mybir as mybir
from concourse.bass_utils import run_bass_kernel_spmd

F32 = mybir.dt.float32
BF16 = mybir.dt.bfloat16
AF = mybir.ActivationFunctionType
ALU = mybir.AluOpType

D = 1024
NCH = 8
SEQ = 8192
GW = 64
NH = 8
DH = 64
CDIM = 512
CK = 31
FFN = 2816
NJ = 22
CTX = 256
IN_DIM = 4608
EPS = 1e-6
NEG = -30000.0
OWN = 16
TPB = 2
RING = 6
URING = 3
NWS = 4
WSW = 1024
FBLK = 510
POOL_CONV_CHUNKS = 0

V_BADA = 0
V_N1G = 48
V_N2G = 56
V_CDW = 64
V_CDB = V_CDW + 4 * CK
V_LNG = V_CDB + 4
V_LNB = V_LNG + 4
V_FDW = V_LNB + 4
V_FDB = V_FDW + 44 * 3
V_FNG = V_FDB + 44
NV = V_FNG + 8

ENGS = ("pe", "act", "dve", "pool", "sp")


class _Op:
    __slots__ = ("eng", "fn", "deps", "signal", "ticket", "dma_key", "idx")


class Prog:
    def __init__(self, nc):
        self.nc = nc
        self.ops = []
        self.last_w = {}
        self.readers = {}
        self.barrier_deps = set()

    def add(self, eng, fn, reads=(), writes=(), dma_key=None):
        op = _Op()
        op.eng, op.fn, op.dma_key = eng, fn, dma_key
        op.signal, op.ticket = False, None
        op.idx = len(self.ops)
        deps = set(self.barrier_deps)
        for r in reads:
            w = self.last_w.get(r)
            if w is not None:
                deps.add(w)
        for w_ in writes:
            w = self.last_w.get(w_)
            if w is not None:
                deps.add(w)
            deps.update(self.readers.get(w_, ()))
        if dma_key is not None:
            k = ("__dk", dma_key)
            w = self.last_w.get(k)
            if w is not None:
                deps.add(w)
            self.last_w[k] = op.idx
        for r in reads:
            self.readers.setdefault(r, []).append(op.idx)
        for w_ in writes:
            self.last_w[w_] = op.idx
            self.readers[w_] = []
        fin = set()
        for d in deps:
            dop = self.ops[d]
            if eng == "pe" and dop.eng == "pe" and dop.dma_key is None and dma_key is None:
                continue
            fin.add(d)
        op.deps = fin
        self.ops.append(op)
        return op.idx

    def barrier(self):
        last = {}
        for op in self.ops:
            key = ("d", op.dma_key) if op.dma_key is not None else ("e", op.eng)
            last[key] = op.idx
        self.barrier_deps = set(last.values())

    def emit(self, final_wait_keys=()):
        nc, ops = self.nc, self.ops
        for op in ops:
            for d in op.deps:
                ops[d].signal = True
        dma_keys = []
        seen = set()
        for op in ops:
            if op.dma_key is not None and op.dma_key not in seen:
                seen.add(op.dma_key)
                dma_keys.append(op.dma_key)
        cnt = {e: 0 for e in ENGS}
        dcnt = {k: 0 for k in dma_keys}
        for op in ops:
            if op.dma_key is not None:
                dcnt[op.dma_key] += 16
                op.ticket = ("d", op.dma_key, dcnt[op.dma_key])
            elif op.signal:
                cnt[op.eng] += 1
                op.ticket = ("e", op.eng, cnt[op.eng])
        per_eng = {e: [op for op in ops if op.eng == e] for e in ENGS}
        with contextlib.ExitStack() as st:
            esem = {e: st.enter_context(nc.semaphore("s_" + e)) for e in ENGS}
            dsem = {k: st.enter_context(nc.semaphore("d_%d" % i)) for i, k in enumerate(dma_keys)}
            block = st.enter_context(nc.Block())

            def run(name, e):
                waited = {}
                for op in per_eng[name]:
                    need = {}
                    for d in op.deps:
                        t = ops[d].ticket
                        key = (t[0], t[1])
                        if waited.get(key, 0) >= t[2]:
                            continue
                        if need.get(key, 0) < t[2]:
                            need[key] = t[2]
                    for key, v in need.items():
                        e.wait_ge(esem[key[1]] if key[0] == "e" else dsem[key[1]], v)
                        waited[key] = v
                    ins = op.fn(e)
                    if op.dma_key is not None:
                        ins.then_inc(dsem[op.dma_key], 16)
                    elif op.signal:
                        ins.then_inc(esem[name], 1)
                if name == "sp":
                    for k in final_wait_keys:
                        e.wait_ge(dsem[k], dcnt[k])

            block.tensor(lambda e: run("pe", e))
            block.scalar(lambda e: run("act", e))
            block.vector(lambda e: run("dve", e))
            block.gpsimd(lambda e: run("pool", e))
            block.sync(lambda e: run("sp", e))


def layer_cfg(big, l, last):
    if big:
        return dict(l=l, last=last, KA=-5, KB=20, TA=-3, TB=19, F0=-3 * 128 + 64, F1=18 * 128)
    return dict(l=l, last=last, KA=-3, KB=18, TA=-1, TB=17, F0=0, F1=OWN * 128)


def make_cfg(mode):
    if mode == "fused":
        layers = [layer_cfg(True, 0, False), layer_cfg(False, 1, True)]
    elif mode == "l0":
        layers = [layer_cfg(False, 0, False)]
    else:
        layers = [layer_cfg(False, 1, True)]
    XA = min(c["TA"] for c in layers)
    XB = max(c["TB"] for c in layers)
    XIA = layers[0]["KA"]
    XIB = layers[0]["KB"]
    import os
    return dict(mode=mode, layers=layers, XA=XA, XB=XB, XIA=XIA, XIB=XIB, final=layers[-1]["last"],
                ndbg=int(os.environ.get("KDBG", "0")))


def build(cfg):
    nc = bass.Bass("TRN2", target_bir_lowering=False)
    layers = cfg["layers"]
    XA, XB, XIA, XIB = cfg["XA"], cfg["XB"], cfg["XIA"], cfg["XIB"]
    NXT = (XB - XA) * 128
    NIT = (XIB - XIA) * 128

    def din(name, shape, dt=F32):
        return nc.dram_tensor(name, list(shape), dt, kind="ExternalInput").ap()

    xT_d = din("xT", [NCH, 128, NIT])
    ctxT_d = din("ctxT", [NCH, 128, CTX])
    cin_d = din("cin", [128, NCH * 2])
    tokm_d = din("tokm", [128, NIT])
    ropeC_d = din("ropeC", [128, NIT])
    ropeS_d = din("ropeS", [128, NIT])
    rm_d = din("rm", [128, 48])
    W = {}
    for c in layers:
        l = c["l"]
        W[l] = dict(
            w_in=din("w_in%d" % l, [D, IN_DIM]), w_rot=din("w_rot%d" % l, [D, 1024]),
            w_co=din("w_co%d" % l, [CDIM, D]), w_no=din("w_no%d" % l, [CDIM, D]),
            w_out=din("w_out%d" % l, [D, D]), w_up=din("w_up%d" % l, [D, 2 * FFN]),
            w_down=din("w_down%d" % l, [FFN, D]), w_ada=din("w_ada%d" % l, [D, 6 * D]),
            vec=din("vec%d" % l, [128, NV]), bias=din("bias%d" % l, [128, NH * 8 * 64]))
        for k in ("w_in", "w_rot", "w_co", "w_no", "w_out", "w_up", "w_down"):
            W[l][k + "_b"] = nc.dram_tensor("%s%d_bf" % (k, l), list(W[l][k].shape), BF16).ap()
    outT_d = nc.dram_tensor("outT", [NCH, 128, OWN * 128], F32, kind="ExternalOutput").ap()
    xcT_d = None
    if not cfg["final"]:
        xcT_d = nc.dram_tensor("xcT", [NCH, 128, CTX], F32, kind="ExternalOutput").ap()

    NDBG = cfg.get("ndbg", 0)
    dbg_d = nc.dram_tensor("dbg", [max(NDBG, 1), 128, 512], F32, kind="ExternalOutput").ap() if NDBG else None
    dbgc = {"n": 0}
    st = contextlib.ExitStack()
    with st:
        ASZ = 53200
        arena_t = st.enter_context(nc.sbuf_tensor("arena", [128, ASZ], F32))
        psum_t = st.enter_context(nc.psum_tensor("psum", [128, 4096], F32))
        AR = {"top": 0}

        def af(n):
            o = AR["top"]
            AR["top"] += n
            assert AR["top"] <= ASZ, ("SBUF arena overflow", AR["top"])
            return arena_t[:, o:o + n]

        def ab(n):
            return af((n + 1) // 2).bitcast(BF16)

        P = Prog(nc)
        add = P.add

        def A(eng, method, *args, reads=(), writes=(), dma_key=None, **kw):
            return add(eng, lambda e: getattr(e, method)(*args, **kw), reads=reads, writes=writes, dma_key=dma_key)

        def bank(k):
            return psum_t[:, 512 * k:512 * (k + 1)]

        def dbg(ap, reads, label):
            if not NDBG or dbgc["n"] >= NDBG:
                return
            i = dbgc["n"]
            dbgc["n"] += 1
            pr, w = ap.shape[0], ap.shape[1]
            print("DBG", i, label, ap.shape)
            A("pool", "dma_start", out=dbg_d[i, 0:pr, 0:w], in_=ap, reads=reads, dma_key="dbg")

        x_res = af(NCH * NXT).rearrange("p (c t) -> p c t", c=NCH)
        xc_res = af(NCH * CTX).rearrange("p (c t) -> p c t", c=NCH)
        wslot = [af(WSW) for _ in range(NWS)]
        ones_f = af(128)
        ident_f = af(128)
        ident = ab(128)
        epsT = af(1)
        sT = af(NCH * 2)
        cinT = af(NCH * 2)
        rmT = af(48)
        vecT = {c["l"]: af(NV) for c in layers}
        modT = {c["l"]: af(96).rearrange("p (c s) -> p c s", s=2) for c in layers}
        modv = {c["l"]: af(2 * 6 * 8).rearrange("p (s k c) -> p s k c", s=2, k=6) for c in layers}
        Etab = ab(NH * 8 * 64).rearrange("p (h i q) -> p h i q", h=NH, i=8)
        KcT = ab(4 * CTX).rearrange("p (c t) -> p c t", c=4)
        Vc = ab(2 * NH * 65).rearrange("p (t h d) -> p t h d", t=2, h=NH)
        mark_persist = AR["top"]

        HW_ = 64 + TPB * 128
        hbuf = [ab(NCH * HW_).rearrange("p (c t) -> p c t", c=NCH) for _ in range(2)]
        hbuf.append(ab(NCH * (64 + 128)).rearrange("p (c t) -> p c t", c=NCH))
        KW_ = 64 + RING * 128
        Kring = ab(4 * KW_).rearrange("p (c t) -> p c t", c=4)
        Vev = ab(RING * NH * 65).rearrange("p (t h d) -> p t h d", t=RING, h=NH)
        Vod = ab(RING * NH * 65).rearrange("p (t h d) -> p t h d", t=RING, h=NH)
        UW_ = 16 + URING * TPB * 128 + 16
        uring = ab(4 * UW_).rearrange("p (c t) -> p c t", c=4)
        ucx = ab(4 * (16 + CTX + 16)).rearrange("p (c t) -> p c t", c=4)
        NB = TPB * 128
        sq = [af(NB) for _ in range(2)]
        rstd = af(NB)
        tt = [af(NB) for _ in range(2)]
        rC = [af(NB) for _ in range(2)] + [af(128)]
        rS = [af(NB) for _ in range(2)] + [af(128)]
        tkm = af(NB)
        r1 = [af(NB) for _ in range(2)]
        QT = ab(4 * NB).rearrange("p (c t) -> p c t", c=4)
        Pb = [ab(512) for _ in range(3)]
        Otok = [ab(512) for _ in range(2)]
        rcp = [af(8) for _ in range(2)]
        OT = ab(4 * NB).rearrange("p (c t) -> p c t", c=4)
        cacc_raw = af(4 * NB)
        cacc = cacc_raw.rearrange("p (c t) -> p c t", c=4)
        xk = cacc_raw.rearrange("p (c t) -> p c t", c=NCH)
        stg = cacc_raw[:, 0:512]
        CACC = ["cacc0", "cacc1", "cacc2", "cacc3"]
        lsq = sq
        lmu = rstd
        lrs = af(NB)
        cT = ab(4 * NB).rearrange("p (c t) -> p c t", c=4)
        dg = ab(4 * 128).rearrange("p (h j c) -> p h j c", h=2, j=2)
        gt = [ab(NB) for _ in range(2)]
        m12 = [af(NB) for _ in range(2)]
        sig = m12
        mrg = ab(NCH * NB).rearrange("p (c t) -> p c t", c=NCH)
        mark_tm = AR["top"]

        AR["top"] = mark_persist
        FW_ = FBLK + 2
        h2 = ab(NCH * FW_).rearrange("p (c t) -> p c t", c=NCH)
        hid = ab(NJ * FBLK).rearrange("p (c t) -> p c t", c=NJ)
        fsq = [af(FW_) for _ in range(2)]
        frs = af(FW_)
        ftt = [af(FW_) for _ in range(2)]
        fta = [af(FBLK) for _ in range(2)]
        ftg = [af(FBLK) for _ in range(2)]
        fsg = [af(FBLK) for _ in range(2)]
        ftm = af(FW_)
        hstash = ab(16).rearrange("p (c t) -> p c t", c=NCH)
        fo = [af(512) for _ in range(2)]
        mark_ffn = AR["top"]
        AR["top"] = max(mark_tm, mark_ffn)
        print("ARENA words: persist", mark_persist, "tm", mark_tm, "ffn", mark_ffn, "of", ASZ, "(%.1f KB)" % (AR["top"] * 4 / 1024))

        wctr = {"n": 0}

        def wload(dram_ap, view_fn, dt_bf=True, reads=()):
            s = wctr["n"] % NWS
            wctr["n"] += 1
            raw = wslot[s]
            v = view_fn(raw.bitcast(BF16) if dt_bf else raw)
            res = "ws%d" % s
            A("sp", "dma_start", out=v, in_=dram_ap, reads=list(reads), writes=[res], dma_key=res)
            return v, res

        def kgroup(wb, c0, ncols, kchunks):
            src = wb[:, c0:c0 + ncols].rearrange("(kc p) n -> p kc n", p=128)
            return wload(src, lambda raw: raw[:, 0:kchunks * ncols].rearrange("p (kc n) -> p kc n", kc=kchunks),
                         reads=CASTRES[id(wb)])

        if NDBG == 20:
            pass
            dbg(Kring.rearrange("p c t -> p (c t)")[:, 0:512], ["ARENA0"], "Kring")
            dbg(Vod.rearrange("p t h d -> p (t h d)")[:, 0:512], ["ARENA0"], "Vod")
            dbg(Pb[2][:, 0:512], ["ARENA0"], "Pb2")
            dbg(Otok[1][:, 0:512], ["ARENA0"], "Otok1")
            dbg(mrg.rearrange("p c t -> p (c t)")[:, 0:512], ["ARENA0"], "mrg")
            dbg(stg[:, 0:512], ["ARENA0"], "stg")
            dbg(x_res[:, 0, 0:512], ["ARENA0"], "xres(poison expected)")
        A("pool", "memset", ones_f, 1.0, writes=["ones"])
        A("pool", "memset", epsT, EPS, writes=["eps"])
        A("pool", "memset", ident_f, 0.0, writes=["identf"])
        A("pool", "affine_select", out=ident_f, in_=ident_f, pattern=[[-1, 128]], compare_op=ALU.not_equal,
                                              fill=1.0, base=0, channel_multiplier=1, reads=["identf"], writes=["identf"])
        A("dve", "tensor_copy", out=ident, in_=ident_f, reads=["identf"], writes=["ident"])
        A("pool", "memset", Vc[:, :, :, 64:65], 1.0, writes=["Vc"])
        CASTRES = {}
        for c in layers:
            l = c["l"]
            for k in ("w_in", "w_rot", "w_co", "w_no", "w_out", "w_up", "w_down"):
                src, dst = W[l][k], W[l][k + "_b"]
                rows = src.shape[0]
                step = 512
                lst = []
                for r0 in range(0, rows, step):
                    r1_ = min(rows, r0 + step)
                    res = "cast_%s%d_%d" % (k, l, r0 // step)
                    A("pool", "dma_start", out=dst[r0:r1_, :], in_=src[r0:r1_, :], writes=[res], dma_key=res)
                    lst.append(res)
                CASTRES[id(dst)] = lst

        A("sp", "dma_start", out=cinT, in_=cin_d, writes=["cin"], dma_key="misc")
        A("sp", "dma_start", out=rmT, in_=rm_d, writes=["rm"], dma_key="misc")
        for c in layers:
            l = c["l"]
            A("sp", "dma_start", out=vecT[l], in_=W[l]["vec"], writes=["vec%d" % l], dma_key="misc")
        for ch in range(NCH):
            A("sp", "dma_start", out=x_res[:, ch, :], in_=xT_d[ch, :, (XA - XIA) * 128:(XB - XIA) * 128],
                writes=["x%d" % ch], dma_key="xin%d" % (ch % 2))
        A("sp", "dma_start", out=xc_res, in_=ctxT_d.rearrange("c p t -> p c t"), writes=["xc"], dma_key="misc")
        A("act", "activation", out=sT, in_=cinT, func=AF.Silu, reads=["cin"], writes=["sT"])

        def emit_mod(l):
            wa = W[l]["w_ada"]
            pm = bank(7)
            for oc in range(48):
                src = wa[:, oc * 128:(oc + 1) * 128].rearrange("(kc p) n -> p kc n", p=128)
                v, res = wload(src, lambda raw: raw[:, 0:1024].rearrange("p (kc n) -> p kc n", kc=8), dt_bf=False)
                for kc in range(NCH):
                    A("pe", "matmul", pm[:, oc * 2:oc * 2 + 2], lhsT=v[:, kc, :],
                                                                    rhs=sT[:, kc * 2:kc * 2 + 2], start=(kc == 0), stop=(kc == 7),
                        reads=[res, "sT"], writes=["B7"])
            pmv = pm[:, 0:96].rearrange("p (c s) -> p c s", s=2)
            for s in range(2):
                A("dve", "tensor_tensor", out=modT[l][:, :, s], in0=pmv[:, :, s], in1=vecT[l][:, V_BADA:V_BADA + 48],
                                                          op=ALU.add, reads=["B7", "vec%d" % l], writes=["modT%d" % l])
            for s in range(2):
                m = modT[l]
                A("dve", "scalar_tensor_tensor", out=modv[l][:, s, 0, :], in0=m[:, 8:16, s], scalar=1.0,
                                                                      in1=vecT[l][:, V_N1G:V_N1G + 8], op0=ALU.add, op1=ALU.mult,
                    reads=["modT%d" % l, "vec%d" % l], writes=["modv%d" % l])
                A("dve", "scalar_tensor_tensor", out=modv[l][:, s, 3, :], in0=m[:, 32:40, s], scalar=1.0,
                                                                      in1=vecT[l][:, V_N2G:V_N2G + 8], op0=ALU.add, op1=ALU.mult,
                    reads=["modT%d" % l, "vec%d" % l], writes=["modv%d" % l])
                for kind, c0 in ((1, 0), (2, 16), (4, 24), (5, 40)):
                    A("dve", "tensor_copy", out=modv[l][:, s, kind, :], in_=m[:, c0:c0 + 8, s],
                        reads=["modT%d" % l], writes=["modv%d" % l])

        def emit_etab(l):
            bsrc = W[l]["bias"].rearrange("p (h n) -> p h n", h=NH)
            for h in range(NH):
                A("sp", "dma_start", out=stg, in_=bsrc[:, h, :], writes=["stg"] + CACC, dma_key="stg")
                A("act", "activation", out=Etab[:, h, :, :].rearrange("p i q -> p (i q)"), in_=stg, func=AF.Exp,
                    reads=["stg"] + CACC, writes=["Etab"])

        def emit_norm_mod(xsrc, n, mv, kinds, sqb, rsb, ttb, out_fn, psb, xres, tag):
            ps = bank(psb)[:, 0:n]
            for c in range(NCH):
                b = sqb[c % 2][:, 0:n]
                A("pool", "tensor_tensor", out=b, in0=xsrc[:, c, :], in1=xsrc[:, c, :], op=ALU.mult,
                    reads=xres(c), writes=[tag + "sq%d" % (c % 2)])
                A("pe", "matmul", ps, lhsT=ones_f, rhs=b, start=(c == 0), stop=(c == 7),
                    reads=[tag + "sq%d" % (c % 2), "ones"], writes=["B%d" % psb])
            rs = rsb[:, 0:n]
            A("act", "activation", out=rs, in_=ps, func=AF.Sqrt, bias=epsT, scale=1.0 / D,
                reads=["B%d" % psb, "eps"], writes=[tag + "rs"])
            A("dve", "reciprocal", out=rs, in_=rs, reads=[tag + "rs"], writes=[tag + "rs"])
            for c in range(NCH):
                t = ttb[c % 2][:, 0:n]
                A("dve", "tensor_tensor", out=t, in0=xsrc[:, c, :], in1=rs, op=ALU.mult,
                    reads=xres(c) + [tag + "rs"], writes=[tag + "tt%d" % (c % 2)])
                o, ores = out_fn(c)
                A("act", "activation", out=o, in_=t, func=AF.Identity, bias=mv[:, kinds[1], c:c + 1],
                                                                 scale=mv[:, kinds[0], c:c + 1],
                    reads=[tag + "tt%d" % (c % 2), "modv"], writes=ores)

        gctr = {"n": 0}

        def gslot(n):
            k = (5, 6, 0, 1, 2)[gctr["n"] % 5]
            gctr["n"] += 1
            return bank(k)[:, 0:n], "B%d" % k

        def proj(wview, wres, col0, ncol, rhs_fn, nk, n, rres):
            ps, pres = gslot(n)
            for k in range(nk):
                A("pe", "matmul", ps[0:ncol, :], lhsT=wview[:, k, col0:col0 + ncol], rhs=rhs_fn(k),
                                                  start=(k == 0), stop=(k == nk - 1),
                    reads=[wres] + rres, writes=[pres])
            return ps, pres

        def phase_A(c, stream, blk):
            l = c["l"]
            mv = modv[l][:, stream]
            lat = stream == 0
            if lat:
                lt0, nt, hs, xsrc, xres, need_mask = blk["lt0"], blk["nt"], blk["hs"], blk["xsrc"], blk["xres"], blk["mask"]
                n = nt * 128
                hb = hbuf[hs]
                hview = hb[:, :, 64:64 + n]
                if blk["prev_hs"] is not None:
                    pb = hbuf[blk["prev_hs"]]
                    pn = blk["prev_n"]
                    A("pool", "tensor_copy", out=hb[:, :, 0:64], in_=pb[:, :, 64 + pn - 64:64 + pn],
                        reads=["h%d" % blk["prev_hs"]], writes=["h%d" % hs])
                hres = "h%d" % hs
            else:
                n = CTX
                xsrc, xres = xc_res, (lambda cc: ["xc"])
                hb = hbuf[0]
                hview = hb[:, :, 64:64 + n]
                hres = "h0"
                need_mask = False
            emit_norm_mod(xsrc, n, mv, (0, 1), sq, rstd, tt, lambda cc: (hview[:, cc, :], [hres]), 7, xres, "A")
            wb = W[l]
            if lat:
                col = (lt0 - XIA) * 128
                bi = blk["hs"]
                A("sp", "dma_start", out=rC[bi][:, 0:n], in_=ropeC_d[:, col:col + n], writes=["rC%d" % bi], dma_key="rC%d" % bi)
                A("sp", "dma_start", out=rS[bi][:, 0:n], in_=ropeS_d[:, col:col + n], writes=["rS%d" % bi], dma_key="rS%d" % bi)
                if need_mask:
                    A("sp", "dma_start", out=tkm[:, 0:n], in_=tokm_d[:, col:col + n], writes=["tkm"], dma_key="tkm")
            for g in range(2):
                wv, wr = kgroup(wb["w_in_b"], 1536 + g * 256, 256, 8)
                if lat:
                    wv2, wr2 = kgroup(wb["w_rot_b"], 512 + g * 256, 256, 8)
                for j in range(2):
                    hp = g * 2 + j
                    ps, pres = proj(wv, wr, j * 128, 128, lambda k: hview[:, k, :], 8, n, [hres])
                    if lat:
                        ps2, pres2 = proj(wv2, wr2, j * 128, 128, lambda k: hview[:, k, :], 8, n, [hres])
                        a, b = r1[0][:, 0:n], r1[1][:, 0:n]
                        A("dve", "tensor_tensor", out=a, in0=ps, in1=rC[bi][:, 0:n], op=ALU.mult,
                            reads=[pres, "rC%d" % bi], writes=["r1a"])
                        A("dve", "tensor_tensor", out=b, in0=ps2, in1=rS[bi][:, 0:n], op=ALU.mult,
                            reads=[pres2, "rS%d" % bi], writes=["r1b"])
                        for t in range(nt):
                            sl = (lt0 + t - c["KA"]) % RING
                            A("pool", "tensor_tensor",
                                out=Kring[:, hp, 64 + sl * 128:64 + (sl + 1) * 128], in0=a[:, t * 128:(t + 1) * 128],
                                in1=b[:, t * 128:(t + 1) * 128], op=ALU.add, reads=["r1a", "r1b"], writes=["K%d" % sl])
                            if sl == RING - 1:
                                A("pool", "tensor_copy", out=Kring[:, hp, 0:64],
                                                                                 in_=Kring[:, hp, 64 + sl * 128 + 64:64 + (sl + 1) * 128],
                                    reads=["K%d" % sl], writes=["Kmar"])
                    else:
                        A("act", "copy", out=KcT[:, hp, :], in_=ps, reads=[pres], writes=["KcT"])
            wv0, wr0 = kgroup(wb["w_in_b"], 2048, 256, 8)
            wv1, wr1 = kgroup(wb["w_in_b"], 2304, 256, 8)

            def vtile(col_lo, dst, dres):
                for half, (wv, wr) in enumerate(((wv0, wr0), (wv1, wr1))):
                    ps, pres = gslot(256)
                    for k in range(NCH):
                        A("pe", "matmul", ps, lhsT=hb[:, k, col_lo:col_lo + 128], rhs=wv[:, k, :],
                                                                        start=(k == 0), stop=(k == 7),
                            reads=[wr, hres], writes=[pres])
                    A("act", "copy", out=dst[:, half * 4:(half + 1) * 4, 0:64],
                                                                  in_=ps.rearrange("p (h d) -> p h d", h=4),
                        reads=[pres], writes=[dres])
            if lat:
                for t in range(nt):
                    lt = lt0 + t
                    sl = (lt - c["KA"]) % RING
                    vtile(64 + t * 128, Vev[:, sl], "Ve%d" % sl)
                    if lt - 1 >= c["KA"]:
                        so = (lt - 1 - c["KA"]) % RING
                        vtile(64 + t * 128 - 64, Vod[:, so], "Vo%d" % so)
            else:
                for t in range(2):
                    vtile(64 + t * 128, Vc[:, t], "Vc")
            if lat or not c["last"]:
                for g in range(2):
                    wa_, wra = kgroup(wb["w_in_b"], g * 256, 256, 8)
                    wg_, wrg = kgroup(wb["w_in_b"], 512 + g * 256, 256, 8)
                    for j in range(2):
                        cc = g * 2 + j
                        pa, pra = proj(wa_, wra, j * 128, 128, lambda k: hview[:, k, :], 8, n, [hres])
                        pg, prg = proj(wg_, wrg, j * 128, 128, lambda k: hview[:, k, :], 8, n, [hres])
                        sg = sig[cc % 2][:, 0:n]
                        A("act", "activation", out=sg, in_=pg, func=AF.Sigmoid, reads=[prg],
                            writes=["m%d" % (cc % 2)])
                        if need_mask:
                            A("pool", "tensor_tensor", out=sg, in0=sg, in1=tkm[:, 0:n], op=ALU.mult,
                                reads=["m%d" % (cc % 2), "tkm"], writes=["m%d" % (cc % 2)])
                        if lat:
                            up = blk["upos"]
                            dst = uring[:, cc, 16 + up:16 + up + n]
                            ures = ["u%d" % (up // 128 + q_) for q_ in range(nt)]
                        else:
                            dst = ucx[:, cc, 16:16 + CTX]
                            ures = ["ucx"]
                        A("dve", "tensor_tensor", out=dst, in0=pa, in1=sg, op=ALU.mult,
                            reads=[pra, "m%d" % (cc % 2)], writes=ures)
                if lat:
                    up = blk["upos"]
                    URT = URING * NB
                    if up + n == URT:
                        A("pool", "tensor_copy", out=uring[:, :, 0:16], in_=uring[:, :, 16 + URT - 16:16 + URT],
                            reads=["u%d" % (URT // 128 - 1)], writes=["umarF"])
                    if up == 0:
                        A("pool", "tensor_copy", out=uring[:, :, 16 + URT:16 + URT + 16], in_=uring[:, :, 16:32],
                            reads=["u0"], writes=["umarB"])

        sctr = {"n": 0}
        actr = {"n": 0}
        dscr = af(512) if NDBG == 30 else None

        def attention(c, qT_fn, nq, chunks, ores, out_rows, fill=None):
            import os
            SER = ["ATTSER"] if os.environ.get("KSER") else []
            actr["n"] += 1
            DBGA = NDBG == 30 and actr["n"] == 1
            nch = len(chunks)
            width = nch * nq
            obank = psum_t[:, 3 * 512:5 * 512]
            ov = obank[0:nq, :].rearrange("p (h d) -> p h d", h=NH)
            pvq = []
            for h in range(NH):
                sb = sctr["n"] % 3
                sctr["n"] += 1
                sps = bank(sb)[:, 0:width]
                pb = Pb[sb][:, 0:width]
                hp, po = h // 2, (h % 2) * 64
                for i, ch in enumerate(chunks):
                    A("pe", "matmul",
                        sps[:, i * nq:(i + 1) * nq], lhsT=ch["k"](hp, po), rhs=qT_fn(hp, po), start=True, stop=True,
                        reads=ch["kr"] + ["QT"], writes=["B%d" % sb] + SER)
                A("act", "activation", out=pb, in_=sps, func=AF.Exp, scale=DH ** -0.5,
                    reads=["B%d" % sb], writes=["P%d" % sb] + SER)
                if DBGA and h < 4:
                    dbg(pb, ["P%d" % sb], "Pexp h%d" % h)
                i = 0
                while i < nch:
                    ch = chunks[i]
                    if ch["e"] is None:
                        i += 1
                        continue
                    if ch["rm"] is None:
                        j = i
                        while j + 1 < nch and chunks[j + 1]["e"] is not None and chunks[j + 1]["rm"] is None \
                                and chunks[j + 1]["ei"] == chunks[j]["ei"] + 1:
                            j += 1
                        e0 = ch["ei"]
                        ev = Etab[:, h, e0:e0 + (j - i + 1), :].rearrange("p i q -> p (i q)")
                        A("dve", "tensor_tensor", out=pb[:, i * nq:(j + 1) * nq],
                                                                                   in0=pb[:, i * nq:(j + 1) * nq], in1=ev, op=ALU.mult,
                            reads=["P%d" % sb, "Etab"], writes=["P%d" % sb] + SER)
                        i = j + 1
                    else:
                        A("dve", "scalar_tensor_tensor",
                            out=pb[:, i * nq:(i + 1) * nq], in0=pb[:, i * nq:(i + 1) * nq], scalar=ch["rm"],
                            in1=Etab[:, h, ch["ei"], :], op0=ALU.mult, op1=ALU.mult,
                            reads=["P%d" % sb, "Etab", "rm"], writes=["P%d" % sb] + SER)
                        i += 1
                if fill is not None:
                    fill["f"](fill["k"])

                def _pv(h=h, pb=pb, sb=sb):
                    for i, ch in enumerate(chunks):
                        A("pe", "matmul", ov[:, h, 0:65], lhsT=pb[:, i * nq:(i + 1) * nq],
                          rhs=ch["v"](h), start=(i == 0), stop=(i == nch - 1),
                          reads=["P%d" % sb] + ch["vr"], writes=["OB"] + SER)
                if pvq:
                    pvq.pop(0)()
                pvq.append(_pv)
            while pvq:
                pvq.pop(0)()
            ob = out_rows["otok"]
            rc = rcp[ob]
            A("dve", "reciprocal", out=rc[0:nq, :], in_=ov[:, :, 64], reads=["OB"], writes=["rcp%d" % ob] + SER)
            ot = Otok[ob][0:nq, :].rearrange("p (h d) -> p h d", h=NH)
            A("dve", "tensor_tensor", out=ot, in0=ov[:, :, 0:64], in1=rc[0:nq, :].unsqueeze(2).broadcast_to([nq, NH, 64]),
                                                 op=ALU.mult, reads=["OB", "rcp%d" % ob], writes=["Otok%d" % ob] + SER)
            if DBGA:
                dbg(rc[0:nq, :], ["rcp%d" % ob], "rc")
                dbg(Otok[ob][0:nq, :], ["Otok%d" % ob], "Otok")
            tp = bank(7).bitcast(BF16)
            for f in range(4):
                A("pe", "transpose", out=tp[:, f * nq:(f + 1) * nq], in_=Otok[ob][0:nq, f * 128:(f + 1) * 128],
                                                     identity=ident[0:nq, 0:nq], reads=["Otok%d" % ob, "ident"], writes=["B7"] + SER)
            c0 = out_rows["col"]
            A("act", "copy", out=OT[:, :, c0:c0 + nq], in_=tp[:, 0:4 * nq].rearrange("p (f q) -> p f q", f=4),
                reads=["B7"], writes=[ores] + SER)

        def phase_B(c, stream, blk):
            l = c["l"]
            wb = W[l]
            mv = modv[l][:, stream]
            lat = stream == 0
            if lat:
                lt0, nt, hs = blk["lt0"], blk["nt"], blk["hs"]
                n = nt * 128
                hview = hbuf[hs][:, :, 64:64 + n]
                hres = "h%d" % hs
                bi = blk["hs"]
                xcol = (lt0 - XA) * 128
                xv = x_res[:, :, xcol:xcol + n]
                xres = lambda cc: ["x%d" % cc]
            else:
                n = CTX
                hview = hbuf[0][:, :, 64:64 + n]
                hres = "h0"
                xv = xc_res
                xres = lambda cc: ["xc"]
            for g in range(2):
                wv, wr = kgroup(wb["w_in_b"], 1024 + g * 256, 256, 8)
                if lat:
                    wv2, wr2 = kgroup(wb["w_rot_b"], g * 256, 256, 8)
                for j in range(2):
                    hp = g * 2 + j
                    ps, pres = proj(wv, wr, j * 128, 128, lambda k: hview[:, k, :], 8, n, [hres])
                    if lat:
                        ps2, pres2 = proj(wv2, wr2, j * 128, 128, lambda k: hview[:, k, :], 8, n, [hres])
                        a, b = r1[0][:, 0:n], r1[1][:, 0:n]
                        A("dve", "tensor_tensor", out=a, in0=ps, in1=rC[bi][:, 0:n], op=ALU.mult,
                            reads=[pres, "rC%d" % bi], writes=["r1a"])
                        A("dve", "tensor_tensor", out=b, in0=ps2, in1=rS[bi][:, 0:n], op=ALU.mult,
                            reads=[pres2, "rS%d" % bi], writes=["r1b"])
                        A("pool", "tensor_tensor", out=QT[:, hp, 0:n], in0=a, in1=b, op=ALU.add,
                            reads=["r1a", "r1b"], writes=["QT"])
                    else:
                        A("act", "copy", out=QT[:, hp, 0:n], in_=ps, reads=[pres], writes=["QT"])
            if not lat and NDBG == 50:
                for hp_ in range(4):
                    dbg(QT[:, hp_, 0:n], ["QT"], "QT%d" % hp_)
                for hp_ in range(4):
                    dbg(KcT[:, hp_, :], ["KcT"], "KcT%d" % hp_)
                dbg(hview[:, 0, :], [hres], "hc0")
                dbg(hview[:, 7, :], [hres], "hc7")
            if not lat and False:
                dbg(hview[:, 0, :], [hres], "hc")
                dbg(KcT[:, 0, :], ["KcT"], "KcT")
                dbg(Vc[:, 0].rearrange("p h d -> p (h d)")[:, 0:512], ["Vc"], "Vc")
                dbg(QT[:, 0, 0:n], ["QT"], "QT")
            vec = vecT[l]

            def conv_gen():
              if lat:
                  up = blk["upos"]
                  base = 16 + up
                  usrc = uring
                  nsl = URING * TPB
                  ur = ["u%d" % ((up // 128 + q_) % nsl) for q_ in (-1, 0, 1, 2)] + ["umarF", "umarB"]
              else:
                  base = 16
                  usrc = ucx
                  ur = ["ucx"]
              ntap = 4 * CK
              nbat = ntap // 2

              def build(j):
                  hf = j % 2
                  A("dve", "tensor_tensor", out=dg[:, hf], in0=ident.unsqueeze(1).broadcast_to([128, 2, 128]),
                    in1=vec[:, V_CDW + 2 * j:V_CDW + 2 * j + 2].unsqueeze(2).broadcast_to([128, 2, 128]), op=ALU.mult,
                    reads=["ident", "vec%d" % l], writes=["dg%d" % hf])
              build(0)
              for j in range(nbat):
                  if j + 1 < nbat:
                      build(j + 1)
                  for q_ in range(2):
                      t_ = 2 * j + q_
                      cc, k = t_ // CK, t_ % CK
                      bk = 5 + cc // 2
                      cps = bank(bk)[:, (cc % 2) * 256:(cc % 2) * 256 + n]
                      src = usrc[:, cc, base + k - 15:base + k - 15 + n]
                      A("pe", "matmul", cps, lhsT=dg[:, j % 2, q_], rhs=src, start=(k == 0), stop=(k == CK - 1),
                        reads=ur + ["dg%d" % (j % 2)], writes=["B%d" % bk])
                      yield

            cgen = conv_gen()

            def filler(k):
                for _ in range(k):
                    try:
                        next(cgen)
                    except StopIteration:
                        return
            nheads_total = (nt * 2 * NH) if lat else (4 * NH)
            per_head = -(-4 * CK // nheads_total)
            FILL = {"f": filler, "k": per_head}
            if lat:
                for t in range(nt):
                    lt = lt0 + t
                    for rr in range(2):
                        r = 2 * lt + rr
                        qc0 = t * 128 + rr * 64
                        special = None
                        if lt in (0, 1):
                            special = ("top", r)
                        elif lt in (OWN - 2, OWN - 1):
                            special = ("bot", r - (2 * OWN - 4))
                        if special is None:
                            cis = [0, 1, 2, 3]
                        elif special[0] == "top":
                            cis = [0, 1, 2, 3, 4, 5]
                        else:
                            cis = [-2, -1, 0, 1, 2, 3]
                        chunks = []
                        for ii, ci in enumerate(cis):
                            kr0 = r - 4 + 2 * ci
                            pos = (kr0 * 64 - c["KA"] * 128)
                            rp = pos % (RING * 128)
                            if rp + 128 <= RING * 128:
                                ka = 64 + rp
                                kres = ["K%d" % (rp // 128)] + (["K%d" % ((rp // 128 + 1) % RING)] if rp % 128 else [])
                            else:
                                ka = 0
                                kres = ["Kmar", "K0"]
                            if kr0 % 2 == 0:
                                vs = ((kr0 // 2) - c["KA"]) % RING
                                vfn = (lambda h, vs=vs: Vev[:, vs, h, :])
                                vres = ["Ve%d" % vs]
                            else:
                                vs = (((kr0 - 1) // 2) - c["KA"]) % RING
                                vfn = (lambda h, vs=vs: Vod[:, vs, h, :])
                                vres = ["Vo%d" % vs]
                            rmap = None
                            if special is not None:
                                sidx = (special[1] + (0 if special[0] == "top" else 4)) * 6 + ii
                                rmap = rmT[:, sidx:sidx + 1]
                            chunks.append(dict(k=(lambda hp, po, ka=ka: Kring[po:po + 64, hp, ka:ka + 128]), kr=kres,
                                               v=vfn, vr=vres, e=True, ei=ci + 2, rm=rmap))
                        for t2 in range(2):
                            chunks.append(dict(k=(lambda hp, po, t2=t2: KcT[po:po + 64, hp, t2 * 128:(t2 + 1) * 128]), kr=["KcT"],
                                               v=(lambda h, t2=t2: Vc[:, t2, h, :]), vr=["Vc"], e=None, ei=None, rm=None))
                        attention(c, lambda hp, po, qc0=qc0: QT[po:po + 64, hp, qc0:qc0 + 64], 64, chunks, "OT",
                                  dict(otok=(t * 2 + rr) % 2, col=qc0), fill=FILL)
            else:
                for t in range(2):
                    for hh in range(2):
                        qc0 = t * 128 + hh * 64
                        chunks = [dict(k=(lambda hp, po, t2=t2: KcT[po:po + 64, hp, t2 * 128:(t2 + 1) * 128]), kr=["KcT"],
                                       v=(lambda h, t2=t2: Vc[:, t2, h, :]), vr=["Vc"], e=None, ei=None, rm=None) for t2 in range(2)]
                        attention(c, lambda hp, po, qc0=qc0: QT[po:po + 64, hp, qc0:qc0 + 64], 64, chunks, "OT",
                                  dict(otok=(t * 2 + hh) % 2, col=qc0), fill=FILL)
            filler(10 ** 6)
            for cc in range(4):
                bk = 5 + cc // 2
                cps = bank(bk)[:, (cc % 2) * 256:(cc % 2) * 256 + n]
                A("act", "activation", out=cacc[:, cc, 0:n], in_=cps, func=AF.Identity, bias=vec[:, V_CDB + cc:V_CDB + cc + 1], scale=1.0,
                  reads=["B%d" % bk, "vec%d" % l], writes=["cacc%d" % cc])
            pmu = bank(5)[:, 0:n]

            pm2 = bank(6)[:, 0:n]
            for cc in range(4):
                b = lsq[cc % 2][:, 0:n]
                A("pool", "tensor_tensor", out=b, in0=cacc[:, cc, 0:n], in1=cacc[:, cc, 0:n], op=ALU.mult,
                    reads=["cacc%d" % cc], writes=["Asq%d" % (cc % 2)])
                A("pe", "matmul", pmu, lhsT=ones_f, rhs=cacc[:, cc, 0:n], start=(cc == 0), stop=(cc == 3),
                    reads=["cacc%d" % cc, "ones"], writes=["B5"])
                A("pe", "matmul", pm2, lhsT=ones_f, rhs=b, start=(cc == 0), stop=(cc == 3),
                    reads=["Asq%d" % (cc % 2), "ones"], writes=["B6"])
            gctr["n"] = 0
            mu, rs = lmu[:, 0:n], lrs[:, 0:n]
            A("act", "activation", out=mu, in_=pmu, func=AF.Identity, scale=1.0 / CDIM, reads=["B5"], writes=["Ars"])
            A("dve", "tensor_tensor", out=rs, in0=mu, in1=mu, op=ALU.mult, reads=["Ars"], writes=["lrs"])
            A("dve", "scalar_tensor_tensor", out=rs, in0=pm2, scalar=1.0 / CDIM, in1=rs, op0=ALU.mult, op1=ALU.subtract,
                reads=["B6", "lrs"], writes=["lrs"])
            A("act", "activation", out=rs, in_=rs, func=AF.Sqrt, bias=epsT, scale=1.0, reads=["lrs", "eps"], writes=["lrs"])
            A("dve", "reciprocal", out=rs, in_=rs, reads=["lrs"], writes=["lrs"])
            for cc in range(4):
                acc = cacc[:, cc, 0:n]
                A("dve", "tensor_tensor", out=acc, in0=acc, in1=mu, op=ALU.subtract,
                    reads=["cacc%d" % cc, "Ars"], writes=["cacc%d" % cc])
                A("pool", "tensor_tensor", out=acc, in0=acc, in1=rs, op=ALU.mult,
                    reads=["cacc%d" % cc, "lrs"], writes=["cacc%d" % cc])
                A("act", "activation", out=cT[:, cc, 0:n], in_=acc, func=AF.Silu,
                                                                  bias=vec[:, V_LNB + cc:V_LNB + cc + 1], scale=vec[:, V_LNG + cc:V_LNG + cc + 1],
                    reads=["cacc%d" % cc, "vec%d" % l], writes=["cT"])
            gctr["n"] = 0
            if NDBG == 40 and not lat:
                for f_ in range(4):
                    dbg(OT[:, f_, 0:n], ["OT"], "cOT%d" % f_)
            if NDBG == 10 and lat and blk["lt0"] == -1:
                dbg(QT[:, 0, 0:n], ["QT"], "QT")
                dbg(cT[:, 0, 0:n], ["cT"], "cT")
                for f_ in range(4):
                    dbg(OT[:, f_, 0:n], ["OT"], "OT%d" % f_)
                dbg(Kring[:, 0, 0:512], ["K0"], "Kring0")
                dbg(Vev.rearrange("p t h d -> p (t h d)")[:, 0:512], ["Ve0"], "Vev0")
                dbg(Vod.rearrange("p t h d -> p (t h d)")[:, 0:512], ["Vo0"], "Vod0")
            if not lat and False:
                dbg(cT[:, 0, 0:n], ["cT"], "cT")
                dbg(OT[:, 0, 0:n], ["OT"], "OT")
                dbg(Otok[0][0:64, :], ["Otok0"], "Otok0")
                dbg(Pb[0][:, 0:128], ["P0"], "P0")
            for g in range(4):
                wcv, wcr = kgroup(wb["w_co_b"], g * 256, 256, 4)
                wnv, wnr = kgroup(wb["w_no_b"], g * 256, 256, 4)
                wgc, wgcr = kgroup(wb["w_in_b"], 2560 + g * 256, 256, 8)
                wga, wgar = kgroup(wb["w_in_b"], 3584 + g * 256, 256, 8)
                for j in range(2):
                    oc = g * 2 + j
                    pg1, pg1r = proj(wgc, wgcr, j * 128, 128, lambda k: hview[:, k, :], 8, n, [hres])
                    g1_ = gt[0][:, 0:n]
                    A("act", "activation", out=g1_, in_=pg1, func=AF.Sigmoid, reads=[pg1r], writes=["gt0"])
                    py1, py1r = proj(wcv, wcr, j * 128, 128, lambda k: cT[:, k, 0:n], 4, n, ["cT"])
                    ma = m12[0][:, 0:n]
                    A("dve", "tensor_tensor", out=ma, in0=py1, in1=g1_, op=ALU.mult,
                        reads=[py1r, "gt0"], writes=["m0"])
                    pg2, pg2r = proj(wga, wgar, j * 128, 128, lambda k: hview[:, k, :], 8, n, [hres])
                    g2_ = gt[1][:, 0:n]
                    A("act", "activation", out=g2_, in_=pg2, func=AF.Sigmoid, reads=[pg2r], writes=["gt1"])
                    py2, py2r = proj(wnv, wnr, j * 128, 128, lambda k: OT[:, k, 0:n], 4, n, ["OT"])
                    mb = m12[1][:, 0:n]
                    A("dve", "tensor_tensor", out=mb, in0=py2, in1=g2_, op=ALU.mult,
                        reads=[py2r, "gt1"], writes=["m1"])
                    A("pool", "tensor_tensor", out=mrg[:, oc, 0:n], in0=ma, in1=mb, op=ALU.add,
                        reads=["m0", "m1"], writes=["mrg"])
            if lat and blk["lt0"] == -1:
                dbg(mrg[:, 0, 0:n], ["mrg"], "mrg")
            for g in range(4):
                wov, wor = kgroup(wb["w_out_b"], g * 256, 256, 8)
                for j in range(2):
                    oc = g * 2 + j
                    po, por = proj(wov, wor, j * 128, 128, lambda k: mrg[:, k, 0:n], 8, n, ["mrg"])
                    A("dve", "scalar_tensor_tensor", out=xv[:, oc, :], in0=po, scalar=mv[:, 2, oc:oc + 1],
                                                                            in1=xv[:, oc, :], op0=ALU.mult, op1=ALU.add,
                        reads=[por, "modv"] + xres(oc), writes=xres(oc))

        fprev = {"n": None}

        def ffn_block(c, stream, t0, n, need_mask):
            l = c["l"]
            wb = W[l]
            vec = vecT[l]
            mv = modv[l][:, stream]
            lat = stream == 0
            if lat:
                xin = x_res[:, :, t0 - 1:t0 + n + 1]
                xo = x_res[:, :, t0:t0 + n]
                xres = lambda cc: ["x%d" % cc]
                hv = h2[:, :, 0:n + 2]
                nprev = fprev["n"]
                if nprev is not None:
                    A("pool", "tensor_copy", out=hstash[:, :, 0:1], in_=h2[:, :, nprev:nprev + 1], reads=["h2"], writes=["hstash"])
                emit_norm_mod(xin, n + 2, mv, (3, 4), fsq, frs, ftt, lambda cc: (hv[:, cc, :], ["h2"]), 7, xres, "F")
                if nprev is not None:
                    A("pool", "tensor_copy", out=h2[:, :, 0:1], in_=hstash[:, :, 0:1], reads=["hstash"], writes=["h2"])
                fprev["n"] = n
                if need_mask:
                    col = t0 - 1 + (XA - XIA) * 128
                    A("sp", "dma_start", out=ftm[:, 0:n + 2], in_=tokm_d[:, col:col + n + 2], writes=["ftm"], dma_key="ftm")
                    for cc in range(NCH):
                        A("pool", "tensor_tensor", out=hv[:, cc, :], in0=hv[:, cc, :], in1=ftm[:, 0:n + 2], op=ALU.mult,
                            reads=["h2", "ftm"], writes=["h2"])
            else:
                xo = xc_res
                xres = lambda cc: ["xc"]
                hv = h2[:, :, 0:n + 2]
                A("pool", "memset", h2[:, :, 0:1], 0.0, writes=["h2"])
                A("pool", "memset", h2[:, :, n + 1:n + 2], 0.0, writes=["h2"])
                emit_norm_mod(xc_res, n, mv, (3, 4), fsq, frs, ftt, lambda cc: (h2[:, cc, 1:n + 1], ["h2"]), 7, xres, "F")
            for j in range(NJ):
                wv, wr = kgroup(wb["w_up_b"], j * 256, 256, 8)
                outs = []
                for half in range(2):
                    k_ = 1 + 2 * (j % 2) + half
                    ps = bank(k_)[:, 0:n + 2]
                    pres = "B%d" % k_
                    for k in range(NCH):
                        A("pe", "matmul", ps, lhsT=wv[:, k, half * 128:(half + 1) * 128],
                                                                                 rhs=hv[:, k, :], start=(k == 0), stop=(k == 7),
                            reads=[wr, "h2"], writes=[pres])
                    ch = 2 * j + half
                    w0 = vec[:, V_FDW + ch * 3 + 0:V_FDW + ch * 3 + 1]
                    w1 = vec[:, V_FDW + ch * 3 + 1:V_FDW + ch * 3 + 2]
                    w2 = vec[:, V_FDW + ch * 3 + 2:V_FDW + ch * 3 + 3]
                    bb = vec[:, V_FDB + ch:V_FDB + ch + 1]
                    tb = (fta if half == 0 else ftg)[j % 2][:, 0:n]
                    tres = "ft%d%d" % (half, j % 2)
                    A("act", "activation", out=tb, in_=ps[:, 1:n + 1], func=AF.Identity, bias=bb, scale=w1,
                        reads=[pres, "vec%d" % l], writes=[tres])
                    A("dve", "scalar_tensor_tensor", out=tb, in0=ps[:, 0:n], scalar=w0, in1=tb,
                                                                                   op0=ALU.mult, op1=ALU.add,
                        reads=[pres, tres, "vec%d" % l], writes=[tres])
                    A("dve", "scalar_tensor_tensor", out=tb, in0=ps[:, 2:n + 2], scalar=w2, in1=tb,
                                                                                   op0=ALU.mult, op1=ALU.add,
                        reads=[pres, tres, "vec%d" % l], writes=[tres])
                    outs.append((tb, tres))
                (ta, tar), (tg, tgr) = outs
                sg = fsg[j % 2][:, 0:n]
                A("act", "activation", out=sg, in_=tg, func=AF.Silu, reads=[tgr], writes=["fsg%d" % (j % 2)])
                A("pool", "tensor_tensor", out=hid[:, j, 0:n], in0=ta, in1=sg, op=ALU.mult,
                    reads=[tar, "fsg%d" % (j % 2)], writes=["hid"])
            for oc in range(NCH):
                halves = []
                for hf in range(2):
                    src = wb["w_down_b"][hf * 11 * 128:(hf + 1) * 11 * 128, oc * 128:(oc + 1) * 128].rearrange("(j p) n -> p j n", p=128)
                    halves.append(wload(src, lambda raw: raw[:, 0:11 * 128].rearrange("p (j n) -> p j n", j=11),
                                        reads=CASTRES[id(wb["w_down_b"])]))
                k_ = 5 + (oc % 2)
                ps = bank(k_)[:, 0:n]
                for j in range(NJ):
                    wv, wr = halves[j // 11]
                    A("pe", "matmul", ps, lhsT=wv[:, j % 11, :], rhs=hid[:, j, 0:n], start=(j == 0), stop=(j == NJ - 1),
                        reads=[wr, "hid"], writes=["B%d" % k_])
                A("dve", "scalar_tensor_tensor", out=xo[:, oc, :], in0=ps, scalar=mv[:, 5, oc:oc + 1], in1=xo[:, oc, :],
                                                                        op0=ALU.mult, op1=ALU.add,
                    reads=["B%d" % k_, "modv"] + xres(oc), writes=xres(oc))

        def final_out(c):
            l = c["l"]
            vec = vecT[l]
            col0 = (0 - XA) * 128
            for bi in range(OWN * 128 // 512):
                t0 = col0 + bi * 512
                n = 512
                xin = x_res[:, :, t0:t0 + n]
                ps = bank(7)[:, 0:n]
                for cc in range(NCH):
                    b = fsq[cc % 2][:, 0:n]
                    A("pool", "tensor_tensor", out=b, in0=xin[:, cc, :], in1=xin[:, cc, :], op=ALU.mult,
                        reads=["x%d" % cc], writes=["Fsq%d" % (cc % 2)])
                    A("pe", "matmul", ps, lhsT=ones_f, rhs=b, start=(cc == 0), stop=(cc == 7),
                        reads=["Fsq%d" % (cc % 2), "ones"], writes=["B7"])
                rs = frs[:, 0:n]
                A("act", "activation", out=rs, in_=ps, func=AF.Sqrt, bias=epsT, scale=1.0 / D, reads=["B7", "eps"], writes=["Frs"])
                A("dve", "reciprocal", out=rs, in_=rs, reads=["Frs"], writes=["Frs"])
                for cc in range(NCH):
                    o = fo[cc % 2][:, 0:n]
                    A("dve", "scalar_tensor_tensor", out=o, in0=xin[:, cc, :], scalar=vec[:, V_FNG + cc:V_FNG + cc + 1],
                                                                          in1=rs, op0=ALU.mult, op1=ALU.mult,
                        reads=["x%d" % cc, "Frs", "vec%d" % l], writes=["fo%d" % (cc % 2)])
                    A("sp", "dma_start", out=outT_d[cc, :, bi * 512:(bi + 1) * 512], in_=o,
                        reads=["fo%d" % (cc % 2)], dma_key="out%d" % (cc % 2))

        UR = URING * NB
        for c in layers:
            l = c["l"]
            first_layer = c is layers[0]
            half = (mark_persist + AR["top"]) // 2
            A("pool", "memset", arena_t[:, mark_persist:half], 0.0, writes=["ARENA0"])
            A("dve", "memset", arena_t[:, half:AR["top"]], 0.0, writes=["ARENA1"])
            P.barrier()
            A("pool", "memset", Vev[:, :, :, 64:65], 1.0, writes=["Vev"])
            A("pool", "memset", Vod[:, :, :, 64:65], 1.0, writes=["Vod"])
            emit_mod(l)
            A("dve", "tensor_copy", out=modv[l][:, 0, 0, 0:1], in_=modv[l][:, 0, 0, 0:1],
                reads=["modv%d" % l], writes=["modv"])
            emit_etab(l)
            if NDBG == 60 and not first_layer:
                dbg(x_res[:, 0, 0:512], ["x0"], "x1 cols0-512")
                dbg(x_res[:, 0, 1000:1512], ["x0"], "x1 cols1000-1512")
                dbg(x_res[:, 7, NXT - 512:NXT], ["x7"], "x1 last512 ch7")
                dbg(xc_res[:, 0, :], ["xc"], "xc1")
                dbg(modv[l].rearrange("p s k c -> p (s k c)"), ["modv"], "modv1")
                dbg(Etab[:, 0, :, :].rearrange("p i q -> p (i q)"), ["Etab"], "Etab1 h0")
            if NDBG == 40:
                for h_ in range(NH):
                    dbg(Etab[:, h_, :, :].rearrange("p i q -> p (i q)"), ["Etab"], "Etab%d" % h_)
            phase_A(c, 1, None)
            if not c["last"]:
                phase_B(c, 1, None)
            blocks = []
            lt, idx = c["KA"], 0
            while lt < c["KB"]:
                tm = c["TA"] <= lt < c["TB"]
                nt = 1
                if tm and lt % 2 == 0 and lt + 1 < c["TB"]:
                    nt = 2
                blocks.append(dict(lt0=lt, nt=nt, tm=tm, idx=idx))
                lt += nt
                idx += 1
            KA0 = c["KA"] - (c["KA"] % 2)
            prev, tmc = None, 0
            for b in blocks:
                if b["tm"]:
                    b["hs"] = tmc % 2
                    tmc += 1
                else:
                    b["hs"] = 2
                b["upos"] = ((b["lt0"] - KA0) * 128) % UR
                b["prev_hs"] = prev["hs"] if prev else None
                b["prev_n"] = prev["nt"] * 128 if prev else None
                b["mask"] = (b["lt0"] < 0) or (b["lt0"] + b["nt"] > OWN)
                b["need"] = min(b["lt0"] + b["nt"] - 1 + 2, c["KB"] - 1)
                prev = b
            pend = []
            for b in blocks:
                resident = XA <= b["lt0"] and b["lt0"] + b["nt"] <= XB
                if resident:
                    xcol = (b["lt0"] - XA) * 128
                    b["xsrc"] = x_res[:, :, xcol:xcol + b["nt"] * 128]
                    b["xres"] = lambda cc: ["x%d" % cc]
                else:
                    assert first_layer and b["nt"] == 1 and not b["tm"]
                    col = (b["lt0"] - XIA) * 128
                    A("sp", "dma_start", out=xk, in_=xT_d[:, :, col:col + 128].rearrange("c p t -> p c t"), writes=["xk"] + CACC, dma_key="xk")
                    b["xsrc"] = xk
                    b["xres"] = lambda cc: ["xk"] + CACC
                phase_A(c, 0, b)
                covered = b["lt0"] + b["nt"] - 1
                if b["tm"]:
                    pend.append(b)
                while pend and pend[0]["need"] <= covered:
                    phase_B(c, 0, pend.pop(0))
            assert not pend
            if NDBG == 61 and not first_layer:
                for q_ in range(4):
                    dbg(x_res[:, 0, 384 + q_ * 512:384 + (q_ + 1) * 512], ["x0"], "xmid own q%d" % q_)
            P.barrier()
            if not c["last"]:
                ffn_block(c, 1, 0, CTX, False)
            f0 = c["F0"] - XA * 128
            f1 = c["F1"] - XA * 128
            fprev["n"] = None
            t0 = f0
            while t0 < f1:
                n = min(FBLK, f1 - t0)
                lo_t = (t0 - 1) // 128 + XA
                hi_t = (t0 + n) // 128 + XA
                ffn_block(c, 0, t0, n, lo_t < 0 or hi_t >= OWN)
                t0 += n
            if NDBG == 61 and not first_layer:
                for q_ in range(4):
                    dbg(x_res[:, 0, 384 + q_ * 512:384 + (q_ + 1) * 512], ["x0"], "x2 own q%d" % q_)
            if c["last"]:
                final_out(c)
            P.barrier()
        outs = ["out0", "out1"]
        if not cfg["final"]:
            col0 = (0 - XA) * 128
            for cc in range(NCH):
                A("sp", "dma_start", out=outT_d[cc], in_=x_res[:, cc, col0:col0 + OWN * 128], reads=["x%d" % cc],
                    dma_key="out%d" % (cc % 2))
            A("sp", "dma_start", out=xcT_d.rearrange("c p t -> p c t"), in_=xc_res, reads=["xc"], dma_key="out0")
        P.emit(final_wait_keys=outs + (["dbg"] if dbgc["n"] else []))
    return nc


ROPE_THETA = 10000.0


def _rope_tables(g_tiles):
    half = DH // 2
    inv_freq = (ROPE_THETA ** (-np.arange(0, half, 2, dtype=np.float32) / half)).astype(np.float32)
    p = np.arange(128)
    d = p % 64
    f = d % 16
    first = (d % 32) < 16
    use_row = d < 32
    C = np.zeros((128, len(g_tiles) * 128), np.float32)
    S = np.zeros_like(C)
    i = np.arange(128)
    for k, g in enumerate(g_tiles):
        row = (2 * g + i // 64).astype(np.float32)
        col = (i % 64).astype(np.float32)
        pos = np.where(use_row[:, None], row[None, :], col[None, :]).astype(np.float32)
        ang = (pos * inv_freq[f][:, None]).astype(np.float32)
        C[:, k * 128:(k + 1) * 128] = np.cos(ang)
        sn = np.sin(ang)
        S[:, k * 128:(k + 1) * 128] = np.where(first[:, None], -sn, sn)
    return C, S


def _bias_table(rpb_l):
    p = np.arange(128)
    kr2 = p // 64
    kc = p % 64
    qc = np.arange(64)
    cs = np.clip(qc - 8, 0, 48)
    out = np.full((128, NH, 8, 64), NEG, np.float32)
    for ei in range(8):
        ci = ei - 2
        dr = -4 + 2 * ci + kr2
        dc = kc[:, None] - qc[None, :]
        ok = (kc[:, None] >= cs[None, :]) & (kc[:, None] < cs[None, :] + 16) & (np.abs(dr)[:, None] <= 7)
        dri = np.clip(dr + 7, 0, 14)
        dci = np.clip(dc + 15, 0, 30)
        vals = rpb_l[:, dri[:, None], dci]
        out[:, :, ei, :] = np.where(ok[:, None, :], vals.transpose(1, 0, 2), np.float32(NEG))
    return out.reshape(128, NH * 8 * 64)


def _row_masks(ci_core):
    p = np.arange(128)
    kr2 = p // 64
    rm = np.zeros((128, 48), np.float32)
    for sidx in range(8):
        if sidx < 4:
            r = sidx
            cis = range(0, 6)
        else:
            r = 28 + (sidx - 4)
            cis = range(-2, 4)
        R = ci_core * 32 + r
        rs = min(max(R - 4, 0), 120)
        for ii, ci in enumerate(cis):
            kr = R - 4 + 2 * ci + kr2
            rm[:, sidx * 6 + ii] = ((kr >= rs) & (kr < rs + 8)).astype(np.float32)
    return rm


def _pack_vec(inp, l):
    v = np.zeros((128, NV), np.float32)
    fm = lambda a: np.ascontiguousarray(np.asarray(a, np.float32).reshape(-1, 128).T)
    v[:, V_BADA:V_BADA + 48] = fm(inp["b_ada"][l])
    v[:, V_N1G:V_N1G + 8] = fm(inp["norm1_g"][l])
    v[:, V_N2G:V_N2G + 8] = fm(inp["norm2_g"][l])
    cdw = np.asarray(inp["conv_dw"][l], np.float32)
    for cc in range(4):
        v[:, V_CDW + cc * CK:V_CDW + (cc + 1) * CK] = cdw[:, cc * 128:(cc + 1) * 128].T
    v[:, V_CDB:V_CDB + 4] = fm(inp["conv_dw_b"][l])
    v[:, V_LNG:V_LNG + 4] = fm(inp["conv_ln_g"][l])
    v[:, V_LNB:V_LNB + 4] = fm(inp["conv_ln_b"][l])
    fdw = np.asarray(inp["ffn_dw"][l], np.float32)
    fdb = np.asarray(inp["ffn_dw_b"][l], np.float32)
    for j in range(NJ):
        for half in range(2):
            ch = 2 * j + half
            c0 = half * FFN + j * 128
            v[:, V_FDW + ch * 3:V_FDW + ch * 3 + 3] = fdw[:, c0:c0 + 128].T
            v[:, V_FDB + ch] = fdb[c0:c0 + 128]
    v[:, V_FNG:V_FNG + 8] = fm(inp["final_norm_g"])
    return v


def _weights_for_layer(inp, l):
    w_in = np.asarray(inp["w_in"][l], np.float32)
    d = np.arange(64)
    partner = np.where((d % 32) < 16, d + 16, d - 16)
    qcols = np.concatenate([1024 + h * 64 + partner for h in range(NH)])
    kcols = np.concatenate([1536 + h * 64 + partner for h in range(NH)])
    w_rot = np.ascontiguousarray(w_in[:, np.concatenate([qcols, kcols])])
    w_up = np.asarray(inp["w_up"][l], np.float32)
    perm = np.concatenate([np.concatenate([np.arange(j * 128, (j + 1) * 128), FFN + np.arange(j * 128, (j + 1) * 128)])
                           for j in range(NJ)])
    return {
        "w_in%d" % l: np.ascontiguousarray(w_in), "w_rot%d" % l: w_rot,
        "w_co%d" % l: np.ascontiguousarray(np.asarray(inp["w_conv_out"][l], np.float32)),
        "w_no%d" % l: np.ascontiguousarray(np.asarray(inp["w_na_out"][l], np.float32)),
        "w_out%d" % l: np.ascontiguousarray(np.asarray(inp["w_out"][l], np.float32)),
        "w_up%d" % l: np.ascontiguousarray(w_up[:, perm]),
        "w_down%d" % l: np.ascontiguousarray(np.asarray(inp["w_down"][l], np.float32)),
        "w_ada%d" % l: np.ascontiguousarray(np.asarray(inp["w_ada"][l], np.float32)),
        "vec%d" % l: _pack_vec(inp, l),
        "bias%d" % l: _bias_table(np.asarray(inp["na_rpb"][l], np.float32)),
    }


def _core_inputs(cfg, inp, x, ctx, shared):
    XIA, XIB = cfg["XIA"], cfg["XIB"]
    maps = []
    for core in range(8):
        b, ci = core // 4, core % 4
        tiles = [ci * OWN + lt for lt in range(XIA, XIB)]
        nit = len(tiles) * 128
        xr = np.zeros((nit, D), np.float32)
        tm = np.zeros((nit,), np.float32)
        for k, g in enumerate(tiles):
            if 0 <= g < SEQ // 128:
                xr[k * 128:(k + 1) * 128] = x[b, g * 128:(g + 1) * 128]
                tm[k * 128:(k + 1) * 128] = 1.0
        C, S = _rope_tables(tiles)
        cin = np.zeros((128, NCH * 2), np.float32)
        cin[:, 0::2] = np.asarray(inp["c"], np.float32)[b].reshape(NCH, 128).T
        cin[:, 1::2] = np.asarray(inp["c_ctx"], np.float32).reshape(NCH, 128).T
        m = {
            "xT": np.ascontiguousarray(xr.T.reshape(NCH, 128, nit)),
            "ctxT": np.ascontiguousarray(ctx[b].T.reshape(NCH, 128, CTX)),
            "cin": cin,
            "tokm": np.ascontiguousarray(np.broadcast_to(tm[None, :], (128, nit))),
            "ropeC": C, "ropeS": S,
            "rm": _row_masks(ci),
        }
        m.update(shared)
        maps.append(m)
    return maps


_NC_CACHE = {}
MODE = "fused"


def _run(mode, inp, x, ctx):
    cfg = make_cfg(mode)
    if mode not in _NC_CACHE:
        _NC_CACHE[mode] = build(cfg)
    nc = _NC_CACHE[mode]
    shared = {}
    for c in cfg["layers"]:
        shared.update(_weights_for_layer(inp, c["l"]))
    maps = _core_inputs(cfg, inp, x, ctx, shared)
    res = run_bass_kernel_spmd(nc, maps, core_ids=list(range(8)))
    if cfg.get("ndbg"):
        np.save("_dbg.npy", np.asarray(res.results[0]["dbg"]))
    xo = np.zeros((2, SEQ, D), np.float32)
    xc = None if cfg["final"] else np.zeros((2, CTX, D), np.float32)
    for core in range(8):
        b, ci = core // 4, core % 4
        r = res.results[core]
        xo[b, ci * OWN * 128:(ci + 1) * OWN * 128] = np.asarray(r["outT"]).reshape(D, OWN * 128).T
        if xc is not None:
            xc[b] = np.asarray(r["xcT"]).reshape(D, CTX).T
    return xo, xc


def kernel(**inputs):
    x = np.asarray(inputs["x"], np.float32)
    ctx = np.asarray(inputs["ctx"], np.float32)
    if MODE == "fused":
        out, _ = _run("fused", inputs, x, ctx)
        return out
    x1, xc1 = _run("l0", inputs, x, ctx)
    out, _ = _run("l1", inputs, x1, xc1)
    return out
```

```python
import contextlib
import numpy as np
import concourse.bass as bass
import concourse.mybir as mybir
from concourse.bass_utils import run_bass_kernel_spmd

F32 = mybir.dt.float32
BF16 = mybir.dt.bfloat16
AF = mybir.ActivationFunctionType
ALU = mybir.AluOpType

D = 1024
NCH = 8
SEQ = 8192
GW = 64
NH = 8
DH = 64
CDIM = 512
CK = 31
FFN = 2816
NJ = 22
CTX = 256
IN_DIM = 4608
EPS = 1e-6
NEG = -30000.0
OWN = 16
TPB = 2
RING = 6
URING = 3
NWS = 4
WSW = 1024
FBLK = 510
POOL_CONV_CHUNKS = 0

V_BADA = 0
V_N1G = 48
V_N2G = 56
V_CDW = 64
V_CDB = V_CDW + 4 * CK
V_LNG = V_CDB + 4
V_LNB = V_LNG + 4
V_FDW = V_LNB + 4
V_FDB = V_FDW + 44 * 3
V_FNG = V_FDB + 44
NV = V_FNG + 8

ENGS = ("pe", "act", "dve", "pool", "sp")


class _Op:
    __slots__ = ("eng", "fn", "deps", "signal", "ticket", "dma_key", "idx")


class Prog:
    def __init__(self, nc):
        self.nc = nc
        self.ops = []
        self.last_w = {}
        self.readers = {}
        self.barrier_deps = set()

    def add(self, eng, fn, reads=(), writes=(), dma_key=None):
        op = _Op()
        op.eng, op.fn, op.dma_key = eng, fn, dma_key
        op.signal, op.ticket = False, None
        op.idx = len(self.ops)
        deps = set(self.barrier_deps)
        for r in reads:
            w = self.last_w.get(r)
            if w is not None:
                deps.add(w)
        for w_ in writes:
            w = self.last_w.get(w_)
            if w is not None:
                deps.add(w)
            deps.update(self.readers.get(w_, ()))
        if dma_key is not None:
            k = ("__dk", dma_key)
            w = self.last_w.get(k)
            if w is not None:
                deps.add(w)
            self.last_w[k] = op.idx
        for r in reads:
            self.readers.setdefault(r, []).append(op.idx)
        for w_ in writes:
            self.last_w[w_] = op.idx
            self.readers[w_] = []
        fin = set()
        for d in deps:
            dop = self.ops[d]
            if eng == "pe" and dop.eng == "pe" and dop.dma_key is None and dma_key is None:
                continue
            fin.add(d)
        op.deps = fin
        self.ops.append(op)
        return op.idx

    def barrier(self):
        last = {}
        for op in self.ops:
            key = ("d", op.dma_key) if op.dma_key is not None else ("e", op.eng)
            last[key] = op.idx
        self.barrier_deps = set(last.values())

    def emit(self, final_wait_keys=()):
        nc, ops = self.nc, self.ops
        for op in ops:
            for d in op.deps:
                ops[d].signal = True
        dma_keys = []
        seen = set()
        for op in ops:
            if op.dma_key is not None and op.dma_key not in seen:
                seen.add(op.dma_key)
                dma_keys.append(op.dma_key)
        cnt = {e: 0 for e in ENGS}
        dcnt = {k: 0 for k in dma_keys}
        for op in ops:
            if op.dma_key is not None:
                dcnt[op.dma_key] += 16
                op.ticket = ("d", op.dma_key, dcnt[op.dma_key])
            elif op.signal:
                cnt[op.eng] += 1
                op.ticket = ("e", op.eng, cnt[op.eng])
        per_eng = {e: [op for op in ops if op.eng == e] for e in ENGS}
        with contextlib.ExitStack() as st:
            esem = {e: st.enter_context(nc.semaphore("s_" + e)) for e in ENGS}
            dsem = {k: st.enter_context(nc.semaphore("d_%d" % i)) for i, k in enumerate(dma_keys)}
            block = st.enter_context(nc.Block())

            def run(name, e):
                waited = {}
                for op in per_eng[name]:
                    need = {}
                    for d in op.deps:
                        t = ops[d].ticket
                        key = (t[0], t[1])
                        if waited.get(key, 0) >= t[2]:
                            continue
                        if need.get(key, 0) < t[2]:
                            need[key] = t[2]
                    for key, v in need.items():
                        e.wait_ge(esem[key[1]] if key[0] == "e" else dsem[key[1]], v)
                        waited[key] = v
                    ins = op.fn(e)
                    if op.dma_key is not None:
                        ins.then_inc(dsem[op.dma_key], 16)
                    elif op.signal:
                        ins.then_inc(esem[name], 1)
                if name == "sp":
                    for k in final_wait_keys:
                        e.wait_ge(dsem[k], dcnt[k])

            block.tensor(lambda e: run("pe", e))
            block.scalar(lambda e: run("act", e))
            block.vector(lambda e: run("dve", e))
            block.gpsimd(lambda e: run("pool", e))
            block.sync(lambda e: run("sp", e))


def layer_cfg(big, l, last):
    if big:
        return dict(l=l, last=last, KA=-5, KB=20, TA=-3, TB=19, F0=-3 * 128 + 64, F1=18 * 128)
    return dict(l=l, last=last, KA=-3, KB=18, TA=-1, TB=17, F0=0, F1=OWN * 128)


def make_cfg(mode):
    if mode == "fused":
        layers = [layer_cfg(True, 0, False), layer_cfg(False, 1, True)]
    elif mode == "l0":
        layers = [layer_cfg(False, 0, False)]
    else:
        layers = [layer_cfg(False, 1, True)]
    XA = min(c["TA"] for c in layers)
    XB = max(c["TB"] for c in layers)
    XIA = layers[0]["KA"]
    XIB = layers[0]["KB"]
    import os
    return dict(mode=mode, layers=layers, XA=XA, XB=XB, XIA=XIA, XIB=XIB, final=layers[-1]["last"],
                ndbg=int(os.environ.get("KDBG", "0")))


def build(cfg):
    nc = bass.Bass("TRN2", target_bir_lowering=False)
    layers = cfg["layers"]
    XA, XB, XIA, XIB = cfg["XA"], cfg["XB"], cfg["XIA"], cfg["XIB"]
    NXT = (XB - XA) * 128
    NIT = (XIB - XIA) * 128

    def din(name, shape, dt=F32):
        return nc.dram_tensor(name, list(shape), dt, kind="ExternalInput").ap()

    xT_d = din("xT", [NCH, 128, NIT])
    ctxT_d = din("ctxT", [NCH, 128, CTX])
    cin_d = din("cin", [128, NCH * 2])
    tokm_d = din("tokm", [128, NIT])
    ropeC_d = din("ropeC", [128, NIT])
    ropeS_d = din("ropeS", [128, NIT])
    rm_d = din("rm", [128, 48])
    W = {}
    for c in layers:
        l = c["l"]
        W[l] = dict(
            w_in=din("w_in%d" % l, [D, IN_DIM]), w_rot=din("w_rot%d" % l, [D, 1024]),
            w_co=din("w_co%d" % l, [CDIM, D]), w_no=din("w_no%d" % l, [CDIM, D]),
            w_out=din("w_out%d" % l, [D, D]), w_up=din("w_up%d" % l, [D, 2 * FFN]),
            w_down=din("w_down%d" % l, [FFN, D]), w_ada=din("w_ada%d" % l, [D, 6 * D]),
            vec=din("vec%d" % l, [128, NV]), bias=din("bias%d" % l, [128, NH * 8 * 64]))
        for k in ("w_in", "w_rot", "w_co", "w_no", "w_out", "w_up", "w_down"):
            W[l][k + "_b"] = nc.dram_tensor("%s%d_bf" % (k, l), list(W[l][k].shape), BF16).ap()
    outT_d = nc.dram_tensor("outT", [NCH, 128, OWN * 128], F32, kind="ExternalOutput").ap()
    xcT_d = None
    if not cfg["final"]:
        xcT_d = nc.dram_tensor("xcT", [NCH, 128, CTX], F32, kind="ExternalOutput").ap()

    NDBG = cfg.get("ndbg", 0)
    dbg_d = nc.dram_tensor("dbg", [max(NDBG, 1), 128, 512], F32, kind="ExternalOutput").ap() if NDBG else None
    dbgc = {"n": 0}
    st = contextlib.ExitStack()
    with st:
        ASZ = 53200
        arena_t = st.enter_context(nc.sbuf_tensor("arena", [128, ASZ], F32))
        psum_t = st.enter_context(nc.psum_tensor("psum", [128, 4096], F32))
        AR = {"top": 0}

        def af(n):
            o = AR["top"]
            AR["top"] += n
            assert AR["top"] <= ASZ, ("SBUF arena overflow", AR["top"])
            return arena_t[:, o:o + n]

        def ab(n):
            return af((n + 1) // 2).bitcast(BF16)

        P = Prog(nc)
        add = P.add

        def A(eng, method, *args, reads=(), writes=(), dma_key=None, **kw):
            return add(eng, lambda e: getattr(e, method)(*args, **kw), reads=reads, writes=writes, dma_key=dma_key)

        def bank(k):
            return psum_t[:, 512 * k:512 * (k + 1)]

        def dbg(ap, reads, label):
            if not NDBG or dbgc["n"] >= NDBG:
                return
            i = dbgc["n"]
            dbgc["n"] += 1
            pr, w = ap.shape[0], ap.shape[1]
            print("DBG", i, label, ap.shape)
            A("pool", "dma_start", out=dbg_d[i, 0:pr, 0:w], in_=ap, reads=reads, dma_key="dbg")

        x_res = af(NCH * NXT).rearrange("p (c t) -> p c t", c=NCH)
        xc_res = af(NCH * CTX).rearrange("p (c t) -> p c t", c=NCH)
        wslot = [af(WSW) for _ in range(NWS)]
        ones_f = af(128)
        ident_f = af(128)
        ident = ab(128)
        epsT = af(1)
        sT = af(NCH * 2)
        cinT = af(NCH * 2)
        rmT = af(48)
        vecT = {c["l"]: af(NV) for c in layers}
        modT = {c["l"]: af(96).rearrange("p (c s) -> p c s", s=2) for c in layers}
        modv = {c["l"]: af(2 * 6 * 8).rearrange("p (s k c) -> p s k c", s=2, k=6) for c in layers}
        Etab = ab(NH * 8 * 64).rearrange("p (h i q) -> p h i q", h=NH, i=8)
        KcT = ab(4 * CTX).rearrange("p (c t) -> p c t", c=4)
        Vc = ab(2 * NH * 65).rearrange("p (t h d) -> p t h d", t=2, h=NH)
        mark_persist = AR["top"]

        HW_ = 64 + TPB * 128
        hbuf = [ab(NCH * HW_).rearrange("p (c t) -> p c t", c=NCH) for _ in range(2)]
        hbuf.append(ab(NCH * (64 + 128)).rearrange("p (c t) -> p c t", c=NCH))
        KW_ = 64 + RING * 128
        Kring = ab(4 * KW_).rearrange("p (c t) -> p c t", c=4)
        Vev = ab(RING * NH * 65).rearrange("p (t h d) -> p t h d", t=RING, h=NH)
        Vod = ab(RING * NH * 65).rearrange("p (t h d) -> p t h d", t=RING, h=NH)
        UW_ = 16 + URING * TPB * 128 + 16
        uring = ab(4 * UW_).rearrange("p (c t) -> p c t", c=4)
        ucx = ab(4 * (16 + CTX + 16)).rearrange("p (c t) -> p c t", c=4)
        NB = TPB * 128
        sq = [af(NB) for _ in range(2)]
        rstd = af(NB)
        tt = [af(NB) for _ in range(2)]
        rC = [af(NB) for _ in range(2)] + [af(128)]
        rS = [af(NB) for _ in range(2)] + [af(128)]
        tkm = af(NB)
        r1 = [af(NB) for _ in range(2)]
        QT = ab(4 * NB).rearrange("p (c t) -> p c t", c=4)
        Pb = [ab(512) for _ in range(3)]
        Otok = [ab(512) for _ in range(2)]
        rcp = [af(8) for _ in range(2)]
        OT = ab(4 * NB).rearrange("p (c t) -> p c t", c=4)
        cacc_raw = af(4 * NB)
        cacc = cacc_raw.rearrange("p (c t) -> p c t", c=4)
        xk = cacc_raw.rearrange("p (c t) -> p c t", c=NCH)
        stg = cacc_raw[:, 0:512]
        CACC = ["cacc0", "cacc1", "cacc2", "cacc3"]
        lsq = sq
        lmu = rstd
        lrs = af(NB)
        cT = ab(4 * NB).rearrange("p (c t) -> p c t", c=4)
        dg = ab(4 * 128).rearrange("p (h j c) -> p h j c", h=2, j=2)
        gt = [ab(NB) for _ in range(2)]
        m12 = [af(NB) for _ in range(2)]
        sig = m12
        mrg = ab(NCH * NB).rearrange("p (c t) -> p c t", c=NCH)
        mark_tm = AR["top"]

        AR["top"] = mark_persist
        FW_ = FBLK + 2
        h2 = ab(NCH * FW_).rearrange("p (c t) -> p c t", c=NCH)
        hid = ab(NJ * FBLK).rearrange("p (c t) -> p c t", c=NJ)
        fsq = [af(FW_) for _ in range(2)]
        frs = af(FW_)
        ftt = [af(FW_) for _ in range(2)]
        fta = [af(FBLK) for _ in range(2)]
        ftg = [af(FBLK) for _ in range(2)]
        fsg = [af(FBLK) for _ in range(2)]
        ftm = af(FW_)
        hstash = ab(16).rearrange("p (c t) -> p c t", c=NCH)
        fo = [af(512) for _ in range(2)]
        mark_ffn = AR["top"]
        AR["top"] = max(mark_tm, mark_ffn)
        print("ARENA words: persist", mark_persist, "tm", mark_tm, "ffn", mark_ffn, "of", ASZ, "(%.1f KB)" % (AR["top"] * 4 / 1024))

        wctr = {"n": 0}

        def wload(dram_ap, view_fn, dt_bf=True, reads=()):
            s = wctr["n"] % NWS
            wctr["n"] += 1
            raw = wslot[s]
            v = view_fn(raw.bitcast(BF16) if dt_bf else raw)
            res = "ws%d" % s
            A("sp", "dma_start", out=v, in_=dram_ap, reads=list(reads), writes=[res], dma_key=res)
            return v, res

        def kgroup(wb, c0, ncols, kchunks):
            src = wb[:, c0:c0 + ncols].rearrange("(kc p) n -> p kc n", p=128)
            return wload(src, lambda raw: raw[:, 0:kchunks * ncols].rearrange("p (kc n) -> p kc n", kc=kchunks),
                         reads=CASTRES[id(wb)])

        if NDBG == 20:
            pass
            dbg(Kring.rearrange("p c t -> p (c t)")[:, 0:512], ["ARENA0"], "Kring")
            dbg(Vod.rearrange("p t h d -> p (t h d)")[:, 0:512], ["ARENA0"], "Vod")
            dbg(Pb[2][:, 0:512], ["ARENA0"], "Pb2")
            dbg(Otok[1][:, 0:512], ["ARENA0"], "Otok1")
            dbg(mrg.rearrange("p c t -> p (c t)")[:, 0:512], ["ARENA0"], "mrg")
            dbg(stg[:, 0:512], ["ARENA0"], "stg")
            dbg(x_res[:, 0, 0:512], ["ARENA0"], "xres(poison expected)")
        A("pool", "memset", ones_f, 1.0, writes=["ones"])
        A("pool", "memset", epsT, EPS, writes=["eps"])
        A("pool", "memset", ident_f, 0.0, writes=["identf"])
        A("pool", "affine_select", out=ident_f, in_=ident_f, pattern=[[-1, 128]], compare_op=ALU.not_equal,
                                              fill=1.0, base=0, channel_multiplier=1, reads=["identf"], writes=["identf"])
        A("dve", "tensor_copy", out=ident, in_=ident_f, reads=["identf"], writes=["ident"])
        A("pool", "memset", Vc[:, :, :, 64:65], 1.0, writes=["Vc"])
        CASTRES = {}
        for c in layers:
            l = c["l"]
            for k in ("w_in", "w_rot", "w_co", "w_no", "w_out", "w_up", "w_down"):
                src, dst = W[l][k], W[l][k + "_b"]
                rows = src.shape[0]
                step = 512
                lst = []
                for r0 in range(0, rows, step):
                    r1_ = min(rows, r0 + step)
                    res = "cast_%s%d_%d" % (k, l, r0 // step)
                    A("pool", "dma_start", out=dst[r0:r1_, :], in_=src[r0:r1_, :], writes=[res], dma_key=res)
                    lst.append(res)
                CASTRES[id(dst)] = lst

        A("sp", "dma_start", out=cinT, in_=cin_d, writes=["cin"], dma_key="misc")
        A("sp", "dma_start", out=rmT, in_=rm_d, writes=["rm"], dma_key="misc")
        for c in layers:
            l = c["l"]
            A("sp", "dma_start", out=vecT[l], in_=W[l]["vec"], writes=["vec%d" % l], dma_key="misc")
        for ch in range(NCH):
            A("sp", "dma_start", out=x_res[:, ch, :], in_=xT_d[ch, :, (XA - XIA) * 128:(XB - XIA) * 128],
                writes=["x%d" % ch], dma_key="xin%d" % (ch % 2))
        A("sp", "dma_start", out=xc_res, in_=ctxT_d.rearrange("c p t -> p c t"), writes=["xc"], dma_key="misc")
        A("act", "activation", out=sT, in_=cinT, func=AF.Silu, reads=["cin"], writes=["sT"])

        def emit_mod(l):
            wa = W[l]["w_ada"]
            pm = bank(7)
            for oc in range(48):
                src = wa[:, oc * 128:(oc + 1) * 128].rearrange("(kc p) n -> p kc n", p=128)
                v, res = wload(src, lambda raw: raw[:, 0:1024].rearrange("p (kc n) -> p kc n", kc=8), dt_bf=False)
                for kc in range(NCH):
                    A("pe", "matmul", pm[:, oc * 2:oc * 2 + 2], lhsT=v[:, kc, :],
                                                                    rhs=sT[:, kc * 2:kc * 2 + 2], start=(kc == 0), stop=(kc == 7),
                        reads=[res, "sT"], writes=["B7"])
            pmv = pm[:, 0:96].rearrange("p (c s) -> p c s", s=2)
            for s in range(2):
                A("dve", "tensor_tensor", out=modT[l][:, :, s], in0=pmv[:, :, s], in1=vecT[l][:, V_BADA:V_BADA + 48],
                                                          op=ALU.add, reads=["B7", "vec%d" % l], writes=["modT%d" % l])
            for s in range(2):
                m = modT[l]
                A("dve", "scalar_tensor_tensor", out=modv[l][:, s, 0, :], in0=m[:, 8:16, s], scalar=1.0,
                                                                      in1=vecT[l][:, V_N1G:V_N1G + 8], op0=ALU.add, op1=ALU.mult,
                    reads=["modT%d" % l, "vec%d" % l], writes=["modv%d" % l])
                A("dve", "scalar_tensor_tensor", out=modv[l][:, s, 3, :], in0=m[:, 32:40, s], scalar=1.0,
                                                                      in1=vecT[l][:, V_N2G:V_N2G + 8], op0=ALU.add, op1=ALU.mult,
                    reads=["modT%d" % l, "vec%d" % l], writes=["modv%d" % l])
                for kind, c0 in ((1, 0), (2, 16), (4, 24), (5, 40)):
                    A("dve", "tensor_copy", out=modv[l][:, s, kind, :], in_=m[:, c0:c0 + 8, s],
                        reads=["modT%d" % l], writes=["modv%d" % l])

        def emit_etab(l):
            bsrc = W[l]["bias"].rearrange("p (h n) -> p h n", h=NH)
            for h in range(NH):
                A("sp", "dma_start", out=stg, in_=bsrc[:, h, :], writes=["stg"] + CACC, dma_key="stg")
                A("act", "activation", out=Etab[:, h, :, :].rearrange("p i q -> p (i q)"), in_=stg, func=AF.Exp,
                    reads=["stg"] + CACC, writes=["Etab"])

        def emit_norm_mod(xsrc, n, mv, kinds, sqb, rsb, ttb, out_fn, psb, xres, tag):
            ps = bank(psb)[:, 0:n]
            for c in range(NCH):
                b = sqb[c % 2][:, 0:n]
                A("pool", "tensor_tensor", out=b, in0=xsrc[:, c, :], in1=xsrc[:, c, :], op=ALU.mult,
                    reads=xres(c), writes=[tag + "sq%d" % (c % 2)])
                A("pe", "matmul", ps, lhsT=ones_f, rhs=b, start=(c == 0), stop=(c == 7),
                    reads=[tag + "sq%d" % (c % 2), "ones"], writes=["B%d" % psb])
            rs = rsb[:, 0:n]
            A("act", "activation", out=rs, in_=ps, func=AF.Sqrt, bias=epsT, scale=1.0 / D,
                reads=["B%d" % psb, "eps"], writes=[tag + "rs"])
            A("dve", "reciprocal", out=rs, in_=rs, reads=[tag + "rs"], writes=[tag + "rs"])
            for c in range(NCH):
                t = ttb[c % 2][:, 0:n]
                A("dve", "tensor_tensor", out=t, in0=xsrc[:, c, :], in1=rs, op=ALU.mult,
                    reads=xres(c) + [tag + "rs"], writes=[tag + "tt%d" % (c % 2)])
                o, ores = out_fn(c)
                A("act", "activation", out=o, in_=t, func=AF.Identity, bias=mv[:, kinds[1], c:c + 1],
                                                                 scale=mv[:, kinds[0], c:c + 1],
                    reads=[tag + "tt%d" % (c % 2), "modv"], writes=ores)

        gctr = {"n": 0}

        def gslot(n):
            k = (5, 6, 0, 1, 2)[gctr["n"] % 5]
            gctr["n"] += 1
            return bank(k)[:, 0:n], "B%d" % k

        def proj(wview, wres, col0, ncol, rhs_fn, nk, n, rres):
            ps, pres = gslot(n)
            for k in range(nk):
                A("pe", "matmul", ps[0:ncol, :], lhsT=wview[:, k, col0:col0 + ncol], rhs=rhs_fn(k),
                                                  start=(k == 0), stop=(k == nk - 1),
                    reads=[wres] + rres, writes=[pres])
            return ps, pres

        def phase_A(c, stream, blk):
            l = c["l"]
            mv = modv[l][:, stream]
            lat = stream == 0
            if lat:
                lt0, nt, hs, xsrc, xres, need_mask = blk["lt0"], blk["nt"], blk["hs"], blk["xsrc"], blk["xres"], blk["mask"]
                n = nt * 128
                hb = hbuf[hs]
                hview = hb[:, :, 64:64 + n]
                if blk["prev_hs"] is not None:
                    pb = hbuf[blk["prev_hs"]]
                    pn = blk["prev_n"]
                    A("pool", "tensor_copy", out=hb[:, :, 0:64], in_=pb[:, :, 64 + pn - 64:64 + pn],
                        reads=["h%d" % blk["prev_hs"]], writes=["h%d" % hs])
                hres = "h%d" % hs
            else:
                n = CTX
                xsrc, xres = xc_res, (lambda cc: ["xc"])
                hb = hbuf[0]
                hview = hb[:, :, 64:64 + n]
                hres = "h0"
                need_mask = False
            emit_norm_mod(xsrc, n, mv, (0, 1), sq, rstd, tt, lambda cc: (hview[:, cc, :], [hres]), 7, xres, "A")
            wb = W[l]
            if lat:
                col = (lt0 - XIA) * 128
                bi = blk["hs"]
                A("sp", "dma_start", out=rC[bi][:, 0:n], in_=ropeC_d[:, col:col + n], writes=["rC%d" % bi], dma_key="rC%d" % bi)
                A("sp", "dma_start", out=rS[bi][:, 0:n], in_=ropeS_d[:, col:col + n], writes=["rS%d" % bi], dma_key="rS%d" % bi)
                if need_mask:
                    A("sp", "dma_start", out=tkm[:, 0:n], in_=tokm_d[:, col:col + n], writes=["tkm"], dma_key="tkm")
            for g in range(2):
                wv, wr = kgroup(wb["w_in_b"], 1536 + g * 256, 256, 8)
                if lat:
                    wv2, wr2 = kgroup(wb["w_rot_b"], 512 + g * 256, 256, 8)
                for j in range(2):
                    hp = g * 2 + j
                    ps, pres = proj(wv, wr, j * 128, 128, lambda k: hview[:, k, :], 8, n, [hres])
                    if lat:
                        ps2, pres2 = proj(wv2, wr2, j * 128, 128, lambda k: hview[:, k, :], 8, n, [hres])
                        a, b = r1[0][:, 0:n], r1[1][:, 0:n]
                        A("dve", "tensor_tensor", out=a, in0=ps, in1=rC[bi][:, 0:n], op=ALU.mult,
                            reads=[pres, "rC%d" % bi], writes=["r1a"])
                        A("dve", "tensor_tensor", out=b, in0=ps2, in1=rS[bi][:, 0:n], op=ALU.mult,
                            reads=[pres2, "rS%d" % bi], writes=["r1b"])
                        for t in range(nt):
                            sl = (lt0 + t - c["KA"]) % RING
                            A("pool", "tensor_tensor",
                                out=Kring[:, hp, 64 + sl * 128:64 + (sl + 1) * 128], in0=a[:, t * 128:(t + 1) * 128],
                                in1=b[:, t * 128:(t + 1) * 128], op=ALU.add, reads=["r1a", "r1b"], writes=["K%d" % sl])
                            if sl == RING - 1:
                                A("pool", "tensor_copy", out=Kring[:, hp, 0:64],
                                                                                 in_=Kring[:, hp, 64 + sl * 128 + 64:64 + (sl + 1) * 128],
                                    reads=["K%d" % sl], writes=["Kmar"])
                    else:
                        A("act", "copy", out=KcT[:, hp, :], in_=ps, reads=[pres], writes=["KcT"])
            wv0, wr0 = kgroup(wb["w_in_b"], 2048, 256, 8)
            wv1, wr1 = kgroup(wb["w_in_b"], 2304, 256, 8)

            def vtile(col_lo, dst, dres):
                for half, (wv, wr) in enumerate(((wv0, wr0), (wv1, wr1))):
                    ps, pres = gslot(256)
                    for k in range(NCH):
                        A("pe", "matmul", ps, lhsT=hb[:, k, col_lo:col_lo + 128], rhs=wv[:, k, :],
                                                                        start=(k == 0), stop=(k == 7),
                            reads=[wr, hres], writes=[pres])
                    A("act", "copy", out=dst[:, half * 4:(half + 1) * 4, 0:64],
                                                                  in_=ps.rearrange("p (h d) -> p h d", h=4),
                        reads=[pres], writes=[dres])
            if lat:
                for t in range(nt):
                    lt = lt0 + t
                    sl = (lt - c["KA"]) % RING
                    vtile(64 + t * 128, Vev[:, sl], "Ve%d" % sl)
                    if lt - 1 >= c["KA"]:
                        so = (lt - 1 - c["KA"]) % RING
                        vtile(64 + t * 128 - 64, Vod[:, so], "Vo%d" % so)
            else:
                for t in range(2):
                    vtile(64 + t * 128, Vc[:, t], "Vc")
            if lat or not c["last"]:
                for g in range(2):
                    wa_, wra = kgroup(wb["w_in_b"], g * 256, 256, 8)
                    wg_, wrg = kgroup(wb["w_in_b"], 512 + g * 256, 256, 8)
                    for j in range(2):
                        cc = g * 2 + j
                        pa, pra = proj(wa_, wra, j * 128, 128, lambda k: hview[:, k, :], 8, n, [hres])
                        pg, prg = proj(wg_, wrg, j * 128, 128, lambda k: hview[:, k, :], 8, n, [hres])
                        sg = sig[cc % 2][:, 0:n]
                        A("act", "activation", out=sg, in_=pg, func=AF.Sigmoid, reads=[prg],
                            writes=["m%d" % (cc % 2)])
                        if need_mask:
                            A("pool", "tensor_tensor", out=sg, in0=sg, in1=tkm[:, 0:n], op=ALU.mult,
                                reads=["m%d" % (cc % 2), "tkm"], writes=["m%d" % (cc % 2)])
                        if lat:
                            up = blk["upos"]
                            dst = uring[:, cc, 16 + up:16 + up + n]
                            ures = ["u%d" % (up // 128 + q_) for q_ in range(nt)]
                        else:
                            dst = ucx[:, cc, 16:16 + CTX]
                            ures = ["ucx"]
                        A("dve", "tensor_tensor", out=dst, in0=pa, in1=sg, op=ALU.mult,
                            reads=[pra, "m%d" % (cc % 2)], writes=ures)
                if lat:
                    up = blk["upos"]
                    URT = URING * NB
                    if up + n == URT:
                        A("pool", "tensor_copy", out=uring[:, :, 0:16], in_=uring[:, :, 16 + URT - 16:16 + URT],
                            reads=["u%d" % (URT // 128 - 1)], writes=["umarF"])
                    if up == 0:
                        A("pool", "tensor_copy", out=uring[:, :, 16 + URT:16 + URT + 16], in_=uring[:, :, 16:32],
                            reads=["u0"], writes=["umarB"])

        sctr = {"n": 0}
        actr = {"n": 0}
        dscr = af(512) if NDBG == 30 else None

        def attention(c, qT_fn, nq, chunks, ores, out_rows, fill=None):
            import os
            SER = ["ATTSER"] if os.environ.get("KSER") else []
            actr["n"] += 1
            DBGA = NDBG == 30 and actr["n"] == 1
            nch = len(chunks)
            width = nch * nq
            obank = psum_t[:, 3 * 512:5 * 512]
            ov = obank[0:nq, :].rearrange("p (h d) -> p h d", h=NH)
            pvq = []
            for h in range(NH):
                sb = sctr["n"] % 3
                sctr["n"] += 1
                sps = bank(sb)[:, 0:width]
                pb = Pb[sb][:, 0:width]
                hp, po = h // 2, (h % 2) * 64
                for i, ch in enumerate(chunks):
                    A("pe", "matmul",
                        sps[:, i * nq:(i + 1) * nq], lhsT=ch["k"](hp, po), rhs=qT_fn(hp, po), start=True, stop=True,
                        reads=ch["kr"] + ["QT"], writes=["B%d" % sb] + SER)
                A("act", "activation", out=pb, in_=sps, func=AF.Exp, scale=DH ** -0.5,
                    reads=["B%d" % sb], writes=["P%d" % sb] + SER)
                if DBGA and h < 4:
                    dbg(pb, ["P%d" % sb], "Pexp h%d" % h)
                i = 0
                while i < nch:
                    ch = chunks[i]
                    if ch["e"] is None:
                        i += 1
                        continue
                    if ch["rm"] is None:
                        j = i
                        while j + 1 < nch and chunks[j + 1]["e"] is not None and chunks[j + 1]["rm"] is None \
                                and chunks[j + 1]["ei"] == chunks[j]["ei"] + 1:
                            j += 1
                        e0 = ch["ei"]
                        ev = Etab[:, h, e0:e0 + (j - i + 1), :].rearrange("p i q -> p (i q)")
                        A("dve", "tensor_tensor", out=pb[:, i * nq:(j + 1) * nq],
                                                                                   in0=pb[:, i * nq:(j + 1) * nq], in1=ev, op=ALU.mult,
                            reads=["P%d" % sb, "Etab"], writes=["P%d" % sb] + SER)
                        i = j + 1
                    else:
                        A("dve", "scalar_tensor_tensor",
                            out=pb[:, i * nq:(i + 1) * nq], in0=pb[:, i * nq:(i + 1) * nq], scalar=ch["rm"],
                            in1=Etab[:, h, ch["ei"], :], op0=ALU.mult, op1=ALU.mult,
                            reads=["P%d" % sb, "Etab", "rm"], writes=["P%d" % sb] + SER)
                        i += 1
                if fill is not None:
                    fill["f"](fill["k"])

                def _pv(h=h, pb=pb, sb=sb):
                    for i, ch in enumerate(chunks):
                        A("pe", "matmul", ov[:, h, 0:65], lhsT=pb[:, i * nq:(i + 1) * nq],
                          rhs=ch["v"](h), start=(i == 0), stop=(i == nch - 1),
                          reads=["P%d" % sb] + ch["vr"], writes=["OB"] + SER)
                if pvq:
                    pvq.pop(0)()
                pvq.append(_pv)
            while pvq:
                pvq.pop(0)()
            ob = out_rows["otok"]
            rc = rcp[ob]
            A("dve", "reciprocal", out=rc[0:nq, :], in_=ov[:, :, 64], reads=["OB"], writes=["rcp%d" % ob] + SER)
            ot = Otok[ob][0:nq, :].rearrange("p (h d) -> p h d", h=NH)
            A("dve", "tensor_tensor", out=ot, in0=ov[:, :, 0:64], in1=rc[0:nq, :].unsqueeze(2).broadcast_to([nq, NH, 64]),
                                                 op=ALU.mult, reads=["OB", "rcp%d" % ob], writes=["Otok%d" % ob] + SER)
            if DBGA:
                dbg(rc[0:nq, :], ["rcp%d" % ob], "rc")
                dbg(Otok[ob][0:nq, :], ["Otok%d" % ob], "Otok")
            tp = bank(7).bitcast(BF16)
            for f in range(4):
                A("pe", "transpose", out=tp[:, f * nq:(f + 1) * nq], in_=Otok[ob][0:nq, f * 128:(f + 1) * 128],
                                                     identity=ident[0:nq, 0:nq], reads=["Otok%d" % ob, "ident"], writes=["B7"] + SER)
            c0 = out_rows["col"]
            A("act", "copy", out=OT[:, :, c0:c0 + nq], in_=tp[:, 0:4 * nq].rearrange("p (f q) -> p f q", f=4),
                reads=["B7"], writes=[ores] + SER)

        def phase_B(c, stream, blk):
            l = c["l"]
            wb = W[l]
            mv = modv[l][:, stream]
            lat = stream == 0
            if lat:
                lt0, nt, hs = blk["lt0"], blk["nt"], blk["hs"]
                n = nt * 128
                hview = hbuf[hs][:, :, 64:64 + n]
                hres = "h%d" % hs
                bi = blk["hs"]
                xcol = (lt0 - XA) * 128
                xv = x_res[:, :, xcol:xcol + n]
                xres = lambda cc: ["x%d" % cc]
            else:
                n = CTX
                hview = hbuf[0][:, :, 64:64 + n]
                hres = "h0"
                xv = xc_res
                xres = lambda cc: ["xc"]
            for g in range(2):
                wv, wr = kgroup(wb["w_in_b"], 1024 + g * 256, 256, 8)
                if lat:
                    wv2, wr2 = kgroup(wb["w_rot_b"], g * 256, 256, 8)
                for j in range(2):
                    hp = g * 2 + j
                    ps, pres = proj(wv, wr, j * 128, 128, lambda k: hview[:, k, :], 8, n, [hres])
                    if lat:
                        ps2, pres2 = proj(wv2, wr2, j * 128, 128, lambda k: hview[:, k, :], 8, n, [hres])
                        a, b = r1[0][:, 0:n], r1[1][:, 0:n]
                        A("dve", "tensor_tensor", out=a, in0=ps, in1=rC[bi][:, 0:n], op=ALU.mult,
                            reads=[pres, "rC%d" % bi], writes=["r1a"])
                        A("dve", "tensor_tensor", out=b, in0=ps2, in1=rS[bi][:, 0:n], op=ALU.mult,
                            reads=[pres2, "rS%d" % bi], writes=["r1b"])
                        A("pool", "tensor_tensor", out=QT[:, hp, 0:n], in0=a, in1=b, op=ALU.add,
                            reads=["r1a", "r1b"], writes=["QT"])
                    else:
                        A("act", "copy", out=QT[:, hp, 0:n], in_=ps, reads=[pres], writes=["QT"])
            if not lat and NDBG == 50:
                for hp_ in range(4):
                    dbg(QT[:, hp_, 0:n], ["QT"], "QT%d" % hp_)
                for hp_ in range(4):
                    dbg(KcT[:, hp_, :], ["KcT"], "KcT%d" % hp_)
                dbg(hview[:, 0, :], [hres], "hc0")
                dbg(hview[:, 7, :], [hres], "hc7")
            if not lat and False:
                dbg(hview[:, 0, :], [hres], "hc")
                dbg(KcT[:, 0, :], ["KcT"], "KcT")
                dbg(Vc[:, 0].rearrange("p h d -> p (h d)")[:, 0:512], ["Vc"], "Vc")
                dbg(QT[:, 0, 0:n], ["QT"], "QT")
            vec = vecT[l]

            def conv_gen():
              if lat:
                  up = blk["upos"]
                  base = 16 + up
                  usrc = uring
                  nsl = URING * TPB
                  ur = ["u%d" % ((up // 128 + q_) % nsl) for q_ in (-1, 0, 1, 2)] + ["umarF", "umarB"]
              else:
                  base = 16
                  usrc = ucx
                  ur = ["ucx"]
              nstep = CK

              def build(j):
                  hf = j % 2
                  c0 = V_CDW + 2 * CK + 2 * j
                  A("dve", "tensor_tensor", out=dg[:, hf], in0=ident.unsqueeze(1).broadcast_to([128, 2, 128]),
                    in1=vec[:, c0:c0 + 2].unsqueeze(2).broadcast_to([128, 2, 128]), op=ALU.mult,
                    reads=["ident", "vec%d" % l], writes=["dg%d" % hf])
              build(0)
              for j in range(nstep):
                  if j + 1 < nstep:
                      build(j + 1)
                  for q_ in range(2):
                      t_ = 2 * j + q_
                      cc, k = 2 + t_ // CK, t_ % CK
                      cps = bank(6)[:, (cc % 2) * 256:(cc % 2) * 256 + n]
                      src = usrc[:, cc, base + k - 15:base + k - 15 + n]
                      A("pe", "matmul", cps, lhsT=dg[:, j % 2, q_], rhs=src, start=(k == 0), stop=(k == CK - 1),
                        reads=ur + ["dg%d" % (j % 2)], writes=["B6"])
                  for q_ in range(2):
                      t_ = 2 * j + q_
                      cc, k = t_ // CK, t_ % CK
                      acc = cacc[:, cc, 0:n]
                      src = usrc[:, cc, base + k - 15:base + k - 15 + n]
                      wk = vec[:, V_CDW + cc * CK + k:V_CDW + cc * CK + k + 1]
                      if k == 0:
                          A("dve", "tensor_scalar", out=acc, in0=src, scalar1=wk, scalar2=vec[:, V_CDB + cc:V_CDB + cc + 1],
                            op0=ALU.mult, op1=ALU.add, reads=ur + ["vec%d" % l], writes=["cacc%d" % cc])
                      else:
                          A("dve", "scalar_tensor_tensor", out=acc, in0=src, scalar=wk, in1=acc, op0=ALU.mult, op1=ALU.add,
                            reads=ur + ["vec%d" % l, "cacc%d" % cc], writes=["cacc%d" % cc])
                  yield

            cgen = conv_gen()

            def filler(k):
                for _ in range(k):
                    try:
                        next(cgen)
                    except StopIteration:
                        return
            nheads_total = (nt * 2 * NH) if lat else (4 * NH)
            per_head = -(-CK // nheads_total)
            FILL = {"f": filler, "k": per_head}
            if lat:
                for t in range(nt):
                    lt = lt0 + t
                    for rr in range(2):
                        r = 2 * lt + rr
                        qc0 = t * 128 + rr * 64
                        special = None
                        if lt in (0, 1):
                            special = ("top", r)
                        elif lt in (OWN - 2, OWN - 1):
                            special = ("bot", r - (2 * OWN - 4))
                        if special is None:
                            cis = [0, 1, 2, 3]
                        elif special[0] == "top":
                            cis = [0, 1, 2, 3, 4, 5]
                        else:
                            cis = [-2, -1, 0, 1, 2, 3]
                        chunks = []
                        for ii, ci in enumerate(cis):
                            kr0 = r - 4 + 2 * ci
                            pos = (kr0 * 64 - c["KA"] * 128)
                            rp = pos % (RING * 128)
                            if rp + 128 <= RING * 128:
                                ka = 64 + rp
                                kres = ["K%d" % (rp // 128)] + (["K%d" % ((rp // 128 + 1) % RING)] if rp % 128 else [])
                            else:
                                ka = 0
                                kres = ["Kmar", "K0"]
                            if kr0 % 2 == 0:
                                vs = ((kr0 // 2) - c["KA"]) % RING
                                vfn = (lambda h, vs=vs: Vev[:, vs, h, :])
                                vres = ["Ve%d" % vs]
                            else:
                                vs = (((kr0 - 1) // 2) - c["KA"]) % RING
                                vfn = (lambda h, vs=vs: Vod[:, vs, h, :])
                                vres = ["Vo%d" % vs]
                            rmap = None
                            if special is not None:
                                sidx = (special[1] + (0 if special[0] == "top" else 4)) * 6 + ii
                                rmap = rmT[:, sidx:sidx + 1]
                            chunks.append(dict(k=(lambda hp, po, ka=ka: Kring[po:po + 64, hp, ka:ka + 128]), kr=kres,
                                               v=vfn, vr=vres, e=True, ei=ci + 2, rm=rmap))
                        for t2 in range(2):
                            chunks.append(dict(k=(lambda hp, po, t2=t2: KcT[po:po + 64, hp, t2 * 128:(t2 + 1) * 128]), kr=["KcT"],
                                               v=(lambda h, t2=t2: Vc[:, t2, h, :]), vr=["Vc"], e=None, ei=None, rm=None))
                        attention(c, lambda hp, po, qc0=qc0: QT[po:po + 64, hp, qc0:qc0 + 64], 64, chunks, "OT",
                                  dict(otok=(t * 2 + rr) % 2, col=qc0), fill=FILL)
            else:
                for t in range(2):
                    for hh in range(2):
                        qc0 = t * 128 + hh * 64
                        chunks = [dict(k=(lambda hp, po, t2=t2: KcT[po:po + 64, hp, t2 * 128:(t2 + 1) * 128]), kr=["KcT"],
                                       v=(lambda h, t2=t2: Vc[:, t2, h, :]), vr=["Vc"], e=None, ei=None, rm=None) for t2 in range(2)]
                        attention(c, lambda hp, po, qc0=qc0: QT[po:po + 64, hp, qc0:qc0 + 64], 64, chunks, "OT",
                                  dict(otok=(t * 2 + hh) % 2, col=qc0), fill=FILL)
            filler(10 ** 6)
            for cc in (2, 3):
                cps = bank(6)[:, (cc % 2) * 256:(cc % 2) * 256 + n]
                A("act", "activation", out=cacc[:, cc, 0:n], in_=cps, func=AF.Identity, bias=vec[:, V_CDB + cc:V_CDB + cc + 1], scale=1.0,
                  reads=["B6", "vec%d" % l], writes=["cacc%d" % cc])
            pmu = bank(5)[:, 0:n]

            pm2 = bank(6)[:, 0:n]
            for cc in range(4):
                b = lsq[cc % 2][:, 0:n]
                A("pool", "tensor_tensor", out=b, in0=cacc[:, cc, 0:n], in1=cacc[:, cc, 0:n], op=ALU.mult,
                    reads=["cacc%d" % cc], writes=["Asq%d" % (cc % 2)])
                A("pe", "matmul", pmu, lhsT=ones_f, rhs=cacc[:, cc, 0:n], start=(cc == 0), stop=(cc == 3),
                    reads=["cacc%d" % cc, "ones"], writes=["B5"])
                A("pe", "matmul", pm2, lhsT=ones_f, rhs=b, start=(cc == 0), stop=(cc == 3),
                    reads=["Asq%d" % (cc % 2), "ones"], writes=["B6"])
            gctr["n"] = 0
            mu, rs = lmu[:, 0:n], lrs[:, 0:n]
            A("act", "activation", out=mu, in_=pmu, func=AF.Identity, scale=1.0 / CDIM, reads=["B5"], writes=["Ars"])
            A("dve", "tensor_tensor", out=rs, in0=mu, in1=mu, op=ALU.mult, reads=["Ars"], writes=["lrs"])
            A("dve", "scalar_tensor_tensor", out=rs, in0=pm2, scalar=1.0 / CDIM, in1=rs, op0=ALU.mult, op1=ALU.subtract,
                reads=["B6", "lrs"], writes=["lrs"])
            A("act", "activation", out=rs, in_=rs, func=AF.Sqrt, bias=epsT, scale=1.0, reads=["lrs", "eps"], writes=["lrs"])
            A("dve", "reciprocal", out=rs, in_=rs, reads=["lrs"], writes=["lrs"])
            for cc in range(4):
                acc = cacc[:, cc, 0:n]
                A("dve", "tensor_tensor", out=acc, in0=acc, in1=mu, op=ALU.subtract,
                    reads=["cacc%d" % cc, "Ars"], writes=["cacc%d" % cc])
                A("pool", "tensor_tensor", out=acc, in0=acc, in1=rs, op=ALU.mult,
                    reads=["cacc%d" % cc, "lrs"], writes=["cacc%d" % cc])
                A("act", "activation", out=cT[:, cc, 0:n], in_=acc, func=AF.Silu,
                                                                  bias=vec[:, V_LNB + cc:V_LNB + cc + 1], scale=vec[:, V_LNG + cc:V_LNG + cc + 1],
                    reads=["cacc%d" % cc, "vec%d" % l], writes=["cT"])
            gctr["n"] = 0
            if NDBG == 40 and not lat:
                for f_ in range(4):
                    dbg(OT[:, f_, 0:n], ["OT"], "cOT%d" % f_)
            if NDBG == 10 and lat and blk["lt0"] == -1:
                dbg(QT[:, 0, 0:n], ["QT"], "QT")
                dbg(cT[:, 0, 0:n], ["cT"], "cT")
                for f_ in range(4):
                    dbg(OT[:, f_, 0:n], ["OT"], "OT%d" % f_)
                dbg(Kring[:, 0, 0:512], ["K0"], "Kring0")
                dbg(Vev.rearrange("p t h d -> p (t h d)")[:, 0:512], ["Ve0"], "Vev0")
                dbg(Vod.rearrange("p t h d -> p (t h d)")[:, 0:512], ["Vo0"], "Vod0")
            if not lat and False:
                dbg(cT[:, 0, 0:n], ["cT"], "cT")
                dbg(OT[:, 0, 0:n], ["OT"], "OT")
                dbg(Otok[0][0:64, :], ["Otok0"], "Otok0")
                dbg(Pb[0][:, 0:128], ["P0"], "P0")
            for g in range(4):
                wcv, wcr = kgroup(wb["w_co_b"], g * 256, 256, 4)
                wnv, wnr = kgroup(wb["w_no_b"], g * 256, 256, 4)
                wgc, wgcr = kgroup(wb["w_in_b"], 2560 + g * 256, 256, 8)
                wga, wgar = kgroup(wb["w_in_b"], 3584 + g * 256, 256, 8)
                for j in range(2):
                    oc = g * 2 + j
                    pg1, pg1r = proj(wgc, wgcr, j * 128, 128, lambda k: hview[:, k, :], 8, n, [hres])
                    g1_ = gt[0][:, 0:n]
                    A("act", "activation", out=g1_, in_=pg1, func=AF.Sigmoid, reads=[pg1r], writes=["gt0"])
                    py1, py1r = proj(wcv, wcr, j * 128, 128, lambda k: cT[:, k, 0:n], 4, n, ["cT"])
                    ma = m12[0][:, 0:n]
                    A("dve", "tensor_tensor", out=ma, in0=py1, in1=g1_, op=ALU.mult,
                        reads=[py1r, "gt0"], writes=["m0"])
                    pg2, pg2r = proj(wga, wgar, j * 128, 128, lambda k: hview[:, k, :], 8, n, [hres])
                    g2_ = gt[1][:, 0:n]
                    A("act", "activation", out=g2_, in_=pg2, func=AF.Sigmoid, reads=[pg2r], writes=["gt1"])
                    py2, py2r = proj(wnv, wnr, j * 128, 128, lambda k: OT[:, k, 0:n], 4, n, ["OT"])
                    mb = m12[1][:, 0:n]
                    A("dve", "tensor_tensor", out=mb, in0=py2, in1=g2_, op=ALU.mult,
                        reads=[py2r, "gt1"], writes=["m1"])
                    A("pool", "tensor_tensor", out=mrg[:, oc, 0:n], in0=ma, in1=mb, op=ALU.add,
                        reads=["m0", "m1"], writes=["mrg"])
            if lat and blk["lt0"] == -1:
                dbg(mrg[:, 0, 0:n], ["mrg"], "mrg")
            for g in range(4):
                wov, wor = kgroup(wb["w_out_b"], g * 256, 256, 8)
                for j in range(2):
                    oc = g * 2 + j
                    po, por = proj(wov, wor, j * 128, 128, lambda k: mrg[:, k, 0:n], 8, n, ["mrg"])
                    A("dve", "scalar_tensor_tensor", out=xv[:, oc, :], in0=po, scalar=mv[:, 2, oc:oc + 1],
                                                                            in1=xv[:, oc, :], op0=ALU.mult, op1=ALU.add,
                        reads=[por, "modv"] + xres(oc), writes=xres(oc))

        fprev = {"n": None}

        def ffn_block(c, stream, t0, n, need_mask):
            l = c["l"]
            wb = W[l]
            vec = vecT[l]
            mv = modv[l][:, stream]
            lat = stream == 0
            if lat:
                xin = x_res[:, :, t0 - 1:t0 + n + 1]
                xo = x_res[:, :, t0:t0 + n]
                xres = lambda cc: ["x%d" % cc]
                hv = h2[:, :, 0:n + 2]
                nprev = fprev["n"]
                if nprev is not None:
                    A("pool", "tensor_copy", out=hstash[:, :, 0:1], in_=h2[:, :, nprev:nprev + 1], reads=["h2"], writes=["hstash"])
                emit_norm_mod(xin, n + 2, mv, (3, 4), fsq, frs, ftt, lambda cc: (hv[:, cc, :], ["h2"]), 7, xres, "F")
                if nprev is not None:
                    A("pool", "tensor_copy", out=h2[:, :, 0:1], in_=hstash[:, :, 0:1], reads=["hstash"], writes=["h2"])
                fprev["n"] = n
                if need_mask:
                    col = t0 - 1 + (XA - XIA) * 128
                    A("sp", "dma_start", out=ftm[:, 0:n + 2], in_=tokm_d[:, col:col + n + 2], writes=["ftm"], dma_key="ftm")
                    for cc in range(NCH):
                        A("pool", "tensor_tensor", out=hv[:, cc, :], in0=hv[:, cc, :], in1=ftm[:, 0:n + 2], op=ALU.mult,
                            reads=["h2", "ftm"], writes=["h2"])
            else:
                xo = xc_res
                xres = lambda cc: ["xc"]
                hv = h2[:, :, 0:n + 2]
                A("pool", "memset", h2[:, :, 0:1], 0.0, writes=["h2"])
                A("pool", "memset", h2[:, :, n + 1:n + 2], 0.0, writes=["h2"])
                emit_norm_mod(xc_res, n, mv, (3, 4), fsq, frs, ftt, lambda cc: (h2[:, cc, 1:n + 1], ["h2"]), 7, xres, "F")
            for j in range(NJ):
                wv, wr = kgroup(wb["w_up_b"], j * 256, 256, 8)
                outs = []
                for half in range(2):
                    k_ = 1 + 2 * (j % 2) + half
                    ps = bank(k_)[:, 0:n + 2]
                    pres = "B%d" % k_
                    for k in range(NCH):
                        A("pe", "matmul", ps, lhsT=wv[:, k, half * 128:(half + 1) * 128],
                                                                                 rhs=hv[:, k, :], start=(k == 0), stop=(k == 7),
                            reads=[wr, "h2"], writes=[pres])
                    ch = 2 * j + half
                    w0 = vec[:, V_FDW + ch * 3 + 0:V_FDW + ch * 3 + 1]
                    w1 = vec[:, V_FDW + ch * 3 + 1:V_FDW + ch * 3 + 2]
                    w2 = vec[:, V_FDW + ch * 3 + 2:V_FDW + ch * 3 + 3]
                    bb = vec[:, V_FDB + ch:V_FDB + ch + 1]
                    tb = (fta if half == 0 else ftg)[j % 2][:, 0:n]
                    tres = "ft%d%d" % (half, j % 2)
                    A("act", "activation", out=tb, in_=ps[:, 1:n + 1], func=AF.Identity, bias=bb, scale=w1,
                        reads=[pres, "vec%d" % l], writes=[tres])
                    A("dve", "scalar_tensor_tensor", out=tb, in0=ps[:, 0:n], scalar=w0, in1=tb,
                                                                                   op0=ALU.mult, op1=ALU.add,
                        reads=[pres, tres, "vec%d" % l], writes=[tres])
                    A("dve", "scalar_tensor_tensor", out=tb, in0=ps[:, 2:n + 2], scalar=w2, in1=tb,
                                                                                   op0=ALU.mult, op1=ALU.add,
                        reads=[pres, tres, "vec%d" % l], writes=[tres])
                    outs.append((tb, tres))
                (ta, tar), (tg, tgr) = outs
                sg = fsg[j % 2][:, 0:n]
                A("act", "activation", out=sg, in_=tg, func=AF.Silu, reads=[tgr], writes=["fsg%d" % (j % 2)])
                A("pool", "tensor_tensor", out=hid[:, j, 0:n], in0=ta, in1=sg, op=ALU.mult,
                    reads=[tar, "fsg%d" % (j % 2)], writes=["hid"])
            for oc in range(NCH):
                halves = []
                for hf in range(2):
                    src = wb["w_down_b"][hf * 11 * 128:(hf + 1) * 11 * 128, oc * 128:(oc + 1) * 128].rearrange("(j p) n -> p j n", p=128)
                    halves.append(wload(src, lambda raw: raw[:, 0:11 * 128].rearrange("p (j n) -> p j n", j=11),
                                        reads=CASTRES[id(wb["w_down_b"])]))
                k_ = 5 + (oc % 2)
                ps = bank(k_)[:, 0:n]
                for j in range(NJ):
                    wv, wr = halves[j // 11]
                    A("pe", "matmul", ps, lhsT=wv[:, j % 11, :], rhs=hid[:, j, 0:n], start=(j == 0), stop=(j == NJ - 1),
                        reads=[wr, "hid"], writes=["B%d" % k_])
                A("dve", "scalar_tensor_tensor", out=xo[:, oc, :], in0=ps, scalar=mv[:, 5, oc:oc + 1], in1=xo[:, oc, :],
                                                                        op0=ALU.mult, op1=ALU.add,
                    reads=["B%d" % k_, "modv"] + xres(oc), writes=xres(oc))

        def final_out(c):
            l = c["l"]
            vec = vecT[l]
            col0 = (0 - XA) * 128
            for bi in range(OWN * 128 // 512):
                t0 = col0 + bi * 512
                n = 512
                xin = x_res[:, :, t0:t0 + n]
                ps = bank(7)[:, 0:n]
                for cc in range(NCH):
                    b = fsq[cc % 2][:, 0:n]
                    A("pool", "tensor_tensor", out=b, in0=xin[:, cc, :], in1=xin[:, cc, :], op=ALU.mult,
                        reads=["x%d" % cc], writes=["Fsq%d" % (cc % 2)])
                    A("pe", "matmul", ps, lhsT=ones_f, rhs=b, start=(cc == 0), stop=(cc == 7),
                        reads=["Fsq%d" % (cc % 2), "ones"], writes=["B7"])
                rs = frs[:, 0:n]
                A("act", "activation", out=rs, in_=ps, func=AF.Sqrt, bias=epsT, scale=1.0 / D, reads=["B7", "eps"], writes=["Frs"])
                A("dve", "reciprocal", out=rs, in_=rs, reads=["Frs"], writes=["Frs"])
                for cc in range(NCH):
                    o = fo[cc % 2][:, 0:n]
                    A("dve", "scalar_tensor_tensor", out=o, in0=xin[:, cc, :], scalar=vec[:, V_FNG + cc:V_FNG + cc + 1],
                                                                          in1=rs, op0=ALU.mult, op1=ALU.mult,
                        reads=["x%d" % cc, "Frs", "vec%d" % l], writes=["fo%d" % (cc % 2)])
                    A("sp", "dma_start", out=outT_d[cc, :, bi * 512:(bi + 1) * 512], in_=o,
                        reads=["fo%d" % (cc % 2)], dma_key="out%d" % (cc % 2))

        UR = URING * NB
        for c in layers:
            l = c["l"]
            first_layer = c is layers[0]
            half = (mark_persist + AR["top"]) // 2
            A("pool", "memset", arena_t[:, mark_persist:half], 0.0, writes=["ARENA0"])
            A("dve", "memset", arena_t[:, half:AR["top"]], 0.0, writes=["ARENA1"])
            P.barrier()
            A("pool", "memset", Vev[:, :, :, 64:65], 1.0, writes=["Vev"])
            A("pool", "memset", Vod[:, :, :, 64:65], 1.0, writes=["Vod"])
            emit_mod(l)
            A("dve", "tensor_copy", out=modv[l][:, 0, 0, 0:1], in_=modv[l][:, 0, 0, 0:1],
                reads=["modv%d" % l], writes=["modv"])
            emit_etab(l)
            if NDBG == 60 and not first_layer:
                dbg(x_res[:, 0, 0:512], ["x0"], "x1 cols0-512")
                dbg(x_res[:, 0, 1000:1512], ["x0"], "x1 cols1000-1512")
                dbg(x_res[:, 7, NXT - 512:NXT], ["x7"], "x1 last512 ch7")
                dbg(xc_res[:, 0, :], ["xc"], "xc1")
                dbg(modv[l].rearrange("p s k c -> p (s k c)"), ["modv"], "modv1")
                dbg(Etab[:, 0, :, :].rearrange("p i q -> p (i q)"), ["Etab"], "Etab1 h0")
            if NDBG == 40:
                for h_ in range(NH):
                    dbg(Etab[:, h_, :, :].rearrange("p i q -> p (i q)"), ["Etab"], "Etab%d" % h_)
            phase_A(c, 1, None)
            if not c["last"]:
                phase_B(c, 1, None)
            blocks = []
            lt, idx = c["KA"], 0
            while lt < c["KB"]:
                tm = c["TA"] <= lt < c["TB"]
                nt = 1
                if tm and lt % 2 == 0 and lt + 1 < c["TB"]:
                    nt = 2
                blocks.append(dict(lt0=lt, nt=nt, tm=tm, idx=idx))
                lt += nt
                idx += 1
            KA0 = c["KA"] - (c["KA"] % 2)
            prev, tmc = None, 0
            for b in blocks:
                if b["tm"]:
                    b["hs"] = tmc % 2
                    tmc += 1
                else:
                    b["hs"] = 2
                b["upos"] = ((b["lt0"] - KA0) * 128) % UR
                b["prev_hs"] = prev["hs"] if prev else None
                b["prev_n"] = prev["nt"] * 128 if prev else None
                b["mask"] = (b["lt0"] < 0) or (b["lt0"] + b["nt"] > OWN)
                b["need"] = min(b["lt0"] + b["nt"] - 1 + 2, c["KB"] - 1)
                prev = b
            pend = []
            for b in blocks:
                resident = XA <= b["lt0"] and b["lt0"] + b["nt"] <= XB
                if resident:
                    xcol = (b["lt0"] - XA) * 128
                    b["xsrc"] = x_res[:, :, xcol:xcol + b["nt"] * 128]
                    b["xres"] = lambda cc: ["x%d" % cc]
                else:
                    assert first_layer and b["nt"] == 1 and not b["tm"]
                    col = (b["lt0"] - XIA) * 128
                    A("sp", "dma_start", out=xk, in_=xT_d[:, :, col:col + 128].rearrange("c p t -> p c t"), writes=["xk"] + CACC, dma_key="xk")
                    b["xsrc"] = xk
                    b["xres"] = lambda cc: ["xk"] + CACC
                phase_A(c, 0, b)
                covered = b["lt0"] + b["nt"] - 1
                if b["tm"]:
                    pend.append(b)
                while pend and pend[0]["need"] <= covered:
                    phase_B(c, 0, pend.pop(0))
            assert not pend
            if NDBG == 61 and not first_layer:
                for q_ in range(4):
                    dbg(x_res[:, 0, 384 + q_ * 512:384 + (q_ + 1) * 512], ["x0"], "xmid own q%d" % q_)
            P.barrier()
            if not c["last"]:
                ffn_block(c, 1, 0, CTX, False)
            f0 = c["F0"] - XA * 128
            f1 = c["F1"] - XA * 128
            fprev["n"] = None
            t0 = f0
            while t0 < f1:
                n = min(FBLK, f1 - t0)
                lo_t = (t0 - 1) // 128 + XA
                hi_t = (t0 + n) // 128 + XA
                ffn_block(c, 0, t0, n, lo_t < 0 or hi_t >= OWN)
                t0 += n
            if NDBG == 61 and not first_layer:
                for q_ in range(4):
                    dbg(x_res[:, 0, 384 + q_ * 512:384 + (q_ + 1) * 512], ["x0"], "x2 own q%d" % q_)
            if c["last"]:
                final_out(c)
            P.barrier()
        outs = ["out0", "out1"]
        if not cfg["final"]:
            col0 = (0 - XA) * 128
            for cc in range(NCH):
                A("sp", "dma_start", out=outT_d[cc], in_=x_res[:, cc, col0:col0 + OWN * 128], reads=["x%d" % cc],
                    dma_key="out%d" % (cc % 2))
            A("sp", "dma_start", out=xcT_d.rearrange("c p t -> p c t"), in_=xc_res, reads=["xc"], dma_key="out0")
        P.emit(final_wait_keys=outs + (["dbg"] if dbgc["n"] else []))
    return nc


ROPE_THETA = 10000.0


def _rope_tables(g_tiles):
    half = DH // 2
    inv_freq = (ROPE_THETA ** (-np.arange(0, half, 2, dtype=np.float32) / half)).astype(np.float32)
    p = np.arange(128)
    d = p % 64
    f = d % 16
    first = (d % 32) < 16
    use_row = d < 32
    C = np.zeros((128, len(g_tiles) * 128), np.float32)
    S = np.zeros_like(C)
    i = np.arange(128)
    for k, g in enumerate(g_tiles):
        row = (2 * g + i // 64).astype(np.float32)
        col = (i % 64).astype(np.float32)
        pos = np.where(use_row[:, None], row[None, :], col[None, :]).astype(np.float32)
        ang = (pos * inv_freq[f][:, None]).astype(np.float32)
        C[:, k * 128:(k + 1) * 128] = np.cos(ang)
        sn = np.sin(ang)
        S[:, k * 128:(k + 1) * 128] = np.where(first[:, None], -sn, sn)
    return C, S


def _bias_table(rpb_l):
    p = np.arange(128)
    kr2 = p // 64
    kc = p % 64
    qc = np.arange(64)
    cs = np.clip(qc - 8, 0, 48)
    out = np.full((128, NH, 8, 64), NEG, np.float32)
    for ei in range(8):
        ci = ei - 2
        dr = -4 + 2 * ci + kr2
        dc = kc[:, None] - qc[None, :]
        ok = (kc[:, None] >= cs[None, :]) & (kc[:, None] < cs[None, :] + 16) & (np.abs(dr)[:, None] <= 7)
        dri = np.clip(dr + 7, 0, 14)
        dci = np.clip(dc + 15, 0, 30)
        vals = rpb_l[:, dri[:, None], dci]
        out[:, :, ei, :] = np.where(ok[:, None, :], vals.transpose(1, 0, 2), np.float32(NEG))
    return out.reshape(128, NH * 8 * 64)


def _row_masks(ci_core):
    p = np.arange(128)
    kr2 = p // 64
    rm = np.zeros((128, 48), np.float32)
    for sidx in range(8):
        if sidx < 4:
            r = sidx
            cis = range(0, 6)
        else:
            r = 28 + (sidx - 4)
            cis = range(-2, 4)
        R = ci_core * 32 + r
        rs = min(max(R - 4, 0), 120)
        for ii, ci in enumerate(cis):
            kr = R - 4 + 2 * ci + kr2
            rm[:, sidx * 6 + ii] = ((kr >= rs) & (kr < rs + 8)).astype(np.float32)
    return rm


def _pack_vec(inp, l):
    v = np.zeros((128, NV), np.float32)
    fm = lambda a: np.ascontiguousarray(np.asarray(a, np.float32).reshape(-1, 128).T)
    v[:, V_BADA:V_BADA + 48] = fm(inp["b_ada"][l])
    v[:, V_N1G:V_N1G + 8] = fm(inp["norm1_g"][l])
    v[:, V_N2G:V_N2G + 8] = fm(inp["norm2_g"][l])
    cdw = np.asarray(inp["conv_dw"][l], np.float32)
    for cc in range(4):
        v[:, V_CDW + cc * CK:V_CDW + (cc + 1) * CK] = cdw[:, cc * 128:(cc + 1) * 128].T
    v[:, V_CDB:V_CDB + 4] = fm(inp["conv_dw_b"][l])
    v[:, V_LNG:V_LNG + 4] = fm(inp["conv_ln_g"][l])
    v[:, V_LNB:V_LNB + 4] = fm(inp["conv_ln_b"][l])
    fdw = np.asarray(inp["ffn_dw"][l], np.float32)
    fdb = np.asarray(inp["ffn_dw_b"][l], np.float32)
    for j in range(NJ):
        for half in range(2):
            ch = 2 * j + half
            c0 = half * FFN + j * 128
            v[:, V_FDW + ch * 3:V_FDW + ch * 3 + 3] = fdw[:, c0:c0 + 128].T
            v[:, V_FDB + ch] = fdb[c0:c0 + 128]
    v[:, V_FNG:V_FNG + 8] = fm(inp["final_norm_g"])
    return v


def _weights_for_layer(inp, l):
    w_in = np.asarray(inp["w_in"][l], np.float32)
    d = np.arange(64)
    partner = np.where((d % 32) < 16, d + 16, d - 16)
    qcols = np.concatenate([1024 + h * 64 + partner for h in range(NH)])
    kcols = np.concatenate([1536 + h * 64 + partner for h in range(NH)])
    w_rot = np.ascontiguousarray(w_in[:, np.concatenate([qcols, kcols])])
    w_up = np.asarray(inp["w_up"][l], np.float32)
    perm = np.concatenate([np.concatenate([np.arange(j * 128, (j + 1) * 128), FFN + np.arange(j * 128, (j + 1) * 128)])
                           for j in range(NJ)])
    return {
        "w_in%d" % l: np.ascontiguousarray(w_in), "w_rot%d" % l: w_rot,
        "w_co%d" % l: np.ascontiguousarray(np.asarray(inp["w_conv_out"][l], np.float32)),
        "w_no%d" % l: np.ascontiguousarray(np.asarray(inp["w_na_out"][l], np.float32)),
        "w_out%d" % l: np.ascontiguousarray(np.asarray(inp["w_out"][l], np.float32)),
        "w_up%d" % l: np.ascontiguousarray(w_up[:, perm]),
        "w_down%d" % l: np.ascontiguousarray(np.asarray(inp["w_down"][l], np.float32)),
        "w_ada%d" % l: np.ascontiguousarray(np.asarray(inp["w_ada"][l], np.float32)),
        "vec%d" % l: _pack_vec(inp, l),
        "bias%d" % l: _bias_table(np.asarray(inp["na_rpb"][l], np.float32)),
    }


def _core_inputs(cfg, inp, x, ctx, shared):
    XIA, XIB = cfg["XIA"], cfg["XIB"]
    maps = []
    for core in range(8):
        b, ci = core // 4, core % 4
        tiles = [ci * OWN + lt for lt in range(XIA, XIB)]
        nit = len(tiles) * 128
        xr = np.zeros((nit, D), np.float32)
        tm = np.zeros((nit,), np.float32)
        for k, g in enumerate(tiles):
            if 0 <= g < SEQ // 128:
                xr[k * 128:(k + 1) * 128] = x[b, g * 128:(g + 1) * 128]
                tm[k * 128:(k + 1) * 128] = 1.0
        C, S = _rope_tables(tiles)
        cin = np.zeros((128, NCH * 2), np.float32)
        cin[:, 0::2] = np.asarray(inp["c"], np.float32)[b].reshape(NCH, 128).T
        cin[:, 1::2] = np.asarray(inp["c_ctx"], np.float32).reshape(NCH, 128).T
        m = {
            "xT": np.ascontiguousarray(xr.T.reshape(NCH, 128, nit)),
            "ctxT": np.ascontiguousarray(ctx[b].T.reshape(NCH, 128, CTX)),
            "cin": cin,
            "tokm": np.ascontiguousarray(np.broadcast_to(tm[None, :], (128, nit))),
            "ropeC": C, "ropeS": S,
            "rm": _row_masks(ci),
        }
        m.update(shared)
        maps.append(m)
    return maps


_NC_CACHE = {}
MODE = "fused"


def _run(mode, inp, x, ctx):
    cfg = make_cfg(mode)
    if mode not in _NC_CACHE:
        _NC_CACHE[mode] = build(cfg)
    nc = _NC_CACHE[mode]
    shared = {}
    for c in cfg["layers"]:
        shared.update(_weights_for_layer(inp, c["l"]))
    maps = _core_inputs(cfg, inp, x, ctx, shared)
    res = run_bass_kernel_spmd(nc, maps, core_ids=list(range(8)))
    if cfg.get("ndbg"):
        np.save("_dbg.npy", np.asarray(res.results[0]["dbg"]))
    xo = np.zeros((2, SEQ, D), np.float32)
    xc = None if cfg["final"] else np.zeros((2, CTX, D), np.float32)
    for core in range(8):
        b, ci = core // 4, core % 4
        r = res.results[core]
        xo[b, ci * OWN * 128:(ci + 1) * OWN * 128] = np.asarray(r["outT"]).reshape(D, OWN * 128).T
        if xc is not None:
            xc[b] = np.asarray(r["xcT"]).reshape(D, CTX).T
    return xo, xc


def kernel(**inputs):
    x = np.asarray(inputs["x"], np.float32)
    ctx = np.asarray(inputs["ctx"], np.float32)
    if MODE == "fused":
        out, _ = _run("fused", inputs, x, ctx)
        return out
    x1, xc1 = _run("l0", inputs, x, ctx)
    out, _ = _run("l1", inputs, x1, xc1)
    return out
```

```python
import contextlib
import numpy as np
import concourse.bass as bass
import concourse.mybir as mybir
from concourse.bass_utils import run_bass_kernel_spmd

F32 = mybir.dt.float32
BF16 = mybir.dt.bfloat16
AF = mybir.ActivationFunctionType
ALU = mybir.AluOpType

D = 1024
NCH = 8
SEQ = 8192
GW = 64
NH = 8
DH = 64
CDIM = 512
CK = 31
FFN = 2816
NJ = 22
CTX = 256
IN_DIM = 4608
EPS = 1e-6
NEG = -30000.0
OWN = 16
TPB = 2
RING = 6
URING = 3
NWS = 4
WSW = 1024
FBLK = 510
POOL_CONV_CHUNKS = 0

V_BADA = 0
V_N1G = 48
V_N2G = 56
V_CDW = 64
V_CDB = V_CDW + 4 * CK
V_LNG = V_CDB + 4
V_LNB = V_LNG + 4
V_FDW = V_LNB + 4
V_FDB = V_FDW + 44 * 3
V_FNG = V_FDB + 44
NV = V_FNG + 8

ENGS = ("pe", "act", "dve", "pool", "sp")


class _Op:
    __slots__ = ("eng", "fn", "deps", "signal", "ticket", "dma_key", "idx")


class Prog:
    def __init__(self, nc):
        self.nc = nc
        self.ops = []
        self.last_w = {}
        self.readers = {}
        self.barrier_deps = set()

    def add(self, eng, fn, reads=(), writes=(), dma_key=None):
        op = _Op()
        op.eng, op.fn, op.dma_key = eng, fn, dma_key
        op.signal, op.ticket = False, None
        op.idx = len(self.ops)
        deps = set(self.barrier_deps)
        for r in reads:
            w = self.last_w.get(r)
            if w is not None:
                deps.add(w)
        for w_ in writes:
            w = self.last_w.get(w_)
            if w is not None:
                deps.add(w)
            deps.update(self.readers.get(w_, ()))
        if dma_key is not None:
            k = ("__dk", dma_key)
            w = self.last_w.get(k)
            if w is not None:
                deps.add(w)
            self.last_w[k] = op.idx
        for r in reads:
            self.readers.setdefault(r, []).append(op.idx)
        for w_ in writes:
            self.last_w[w_] = op.idx
            self.readers[w_] = []
        fin = set()
        for d in deps:
            dop = self.ops[d]
            if eng == "pe" and dop.eng == "pe" and dop.dma_key is None and dma_key is None:
                continue
            fin.add(d)
        op.deps = fin
        self.ops.append(op)
        return op.idx

    def barrier(self):
        last = {}
        for op in self.ops:
            key = ("d", op.dma_key) if op.dma_key is not None else ("e", op.eng)
            last[key] = op.idx
        self.barrier_deps = set(last.values())

    def emit(self, final_wait_keys=()):
        nc, ops = self.nc, self.ops
        for op in ops:
            for d in op.deps:
                ops[d].signal = True
        dma_keys = []
        seen = set()
        for op in ops:
            if op.dma_key is not None and op.dma_key not in seen:
                seen.add(op.dma_key)
                dma_keys.append(op.dma_key)
        cnt = {e: 0 for e in ENGS}
        dcnt = {k: 0 for k in dma_keys}
        for op in ops:
            if op.dma_key is not None:
                dcnt[op.dma_key] += 16
                op.ticket = ("d", op.dma_key, dcnt[op.dma_key])
            elif op.signal:
                cnt[op.eng] += 1
                op.ticket = ("e", op.eng, cnt[op.eng])
        per_eng = {e: [op for op in ops if op.eng == e] for e in ENGS}
        with contextlib.ExitStack() as st:
            esem = {e: st.enter_context(nc.semaphore("s_" + e)) for e in ENGS}
            dsem = {k: st.enter_context(nc.semaphore("d_%d" % i)) for i, k in enumerate(dma_keys)}
            block = st.enter_context(nc.Block())

            def run(name, e):
                waited = {}
                for op in per_eng[name]:
                    need = {}
                    for d in op.deps:
                        t = ops[d].ticket
                        key = (t[0], t[1])
                        if waited.get(key, 0) >= t[2]:
                            continue
                        if need.get(key, 0) < t[2]:
                            need[key] = t[2]
                    for key, v in need.items():
                        e.wait_ge(esem[key[1]] if key[0] == "e" else dsem[key[1]], v)
                        waited[key] = v
                    ins = op.fn(e)
                    if op.dma_key is not None:
                        ins.then_inc(dsem[op.dma_key], 16)
                    elif op.signal:
                        ins.then_inc(esem[name], 1)
                if name == "sp":
                    for k in final_wait_keys:
                        e.wait_ge(dsem[k], dcnt[k])

            block.tensor(lambda e: run("pe", e))
            block.scalar(lambda e: run("act", e))
            block.vector(lambda e: run("dve", e))
            block.gpsimd(lambda e: run("pool", e))
            block.sync(lambda e: run("sp", e))


def layer_cfg(big, l, last):
    if big:
        return dict(l=l, last=last, KA=-5, KB=20, TA=-3, TB=19, F0=-3 * 128 + 64, F1=18 * 128)
    return dict(l=l, last=last, KA=-3, KB=18, TA=-1, TB=17, F0=0, F1=OWN * 128)


def make_cfg(mode):
    if mode == "fused":
        layers = [layer_cfg(True, 0, False), layer_cfg(False, 1, True)]
    elif mode == "l0":
        layers = [layer_cfg(False, 0, False)]
    else:
        layers = [layer_cfg(False, 1, True)]
    XA = min(c["TA"] for c in layers)
    XB = max(c["TB"] for c in layers)
    XIA = layers[0]["KA"]
    XIB = layers[0]["KB"]
    import os
    return dict(mode=mode, layers=layers, XA=XA, XB=XB, XIA=XIA, XIB=XIB, final=layers[-1]["last"],
                ndbg=int(os.environ.get("KDBG", "0")))


def build(cfg):
    nc = bass.Bass("TRN2", target_bir_lowering=False)
    layers = cfg["layers"]
    XA, XB, XIA, XIB = cfg["XA"], cfg["XB"], cfg["XIA"], cfg["XIB"]
    NXT = (XB - XA) * 128
    NIT = (XIB - XIA) * 128

    def din(name, shape, dt=F32):
        return nc.dram_tensor(name, list(shape), dt, kind="ExternalInput").ap()

    xT_d = din("xT", [NCH, 128, NIT])
    ctxT_d = din("ctxT", [NCH, 128, CTX])
    cin_d = din("cin", [128, NCH * 2])
    tokm_d = din("tokm", [128, NIT])
    ropeC_d = din("ropeC", [128, NIT])
    ropeS_d = din("ropeS", [128, NIT])
    rm_d = din("rm", [128, 48])
    W = {}
    for c in layers:
        l = c["l"]
        W[l] = dict(
            w_in=din("w_in%d" % l, [D, IN_DIM]), w_rot=din("w_rot%d" % l, [D, 1024]),
            w_co=din("w_co%d" % l, [CDIM, D]), w_no=din("w_no%d" % l, [CDIM, D]),
            w_out=din("w_out%d" % l, [D, D]), w_up=din("w_up%d" % l, [D, 2 * FFN]),
            w_down=din("w_down%d" % l, [FFN, D]), w_ada=din("w_ada%d" % l, [D, 6 * D]),
            vec=din("vec%d" % l, [128, NV]), bias=din("bias%d" % l, [128, NH * 8 * 64]))
        for k in ("w_in", "w_rot", "w_co", "w_no", "w_out", "w_up", "w_down"):
            W[l][k + "_b"] = nc.dram_tensor("%s%d_bf" % (k, l), list(W[l][k].shape), BF16).ap()
    outT_d = nc.dram_tensor("outT", [NCH, 128, OWN * 128], F32, kind="ExternalOutput").ap()
    xcT_d = None
    if not cfg["final"]:
        xcT_d = nc.dram_tensor("xcT", [NCH, 128, CTX], F32, kind="ExternalOutput").ap()

    NDBG = cfg.get("ndbg", 0)
    dbg_d = nc.dram_tensor("dbg", [max(NDBG, 1), 128, 512], F32, kind="ExternalOutput").ap() if NDBG else None
    dbgc = {"n": 0}
    st = contextlib.ExitStack()
    with st:
        ASZ = 53200
        arena_t = st.enter_context(nc.sbuf_tensor("arena", [128, ASZ], F32))
        psum_t = st.enter_context(nc.psum_tensor("psum", [128, 4096], F32))
        AR = {"top": 0}

        def af(n):
            o = AR["top"]
            AR["top"] += n
            assert AR["top"] <= ASZ, ("SBUF arena overflow", AR["top"])
            return arena_t[:, o:o + n]

        def ab(n):
            return af((n + 1) // 2).bitcast(BF16)

        P = Prog(nc)
        add = P.add

        def A(eng, method, *args, reads=(), writes=(), dma_key=None, **kw):
            return add(eng, lambda e: getattr(e, method)(*args, **kw), reads=reads, writes=writes, dma_key=dma_key)

        def bank(k):
            return psum_t[:, 512 * k:512 * (k + 1)]

        def dbg(ap, reads, label):
            if not NDBG or dbgc["n"] >= NDBG:
                return
            i = dbgc["n"]
            dbgc["n"] += 1
            pr, w = ap.shape[0], ap.shape[1]
            print("DBG", i, label, ap.shape)
            A("pool", "dma_start", out=dbg_d[i, 0:pr, 0:w], in_=ap, reads=reads, dma_key="dbg")

        x_res = af(NCH * NXT).rearrange("p (c t) -> p c t", c=NCH)
        xc_res = af(NCH * CTX).rearrange("p (c t) -> p c t", c=NCH)
        wslot = [af(WSW) for _ in range(NWS)]
        ones_f = af(128)
        ident_f = af(128)
        ident = ab(128)
        epsT = af(1)
        sT = af(NCH * 2)
        cinT = af(NCH * 2)
        rmT = af(48)
        vecT = {c["l"]: af(NV) for c in layers}
        modT = {c["l"]: af(96).rearrange("p (c s) -> p c s", s=2) for c in layers}
        modv = {c["l"]: af(2 * 6 * 8).rearrange("p (s k c) -> p s k c", s=2, k=6) for c in layers}
        Etab = ab(NH * 8 * 64).rearrange("p (h i q) -> p h i q", h=NH, i=8)
        KcT = ab(4 * CTX).rearrange("p (c t) -> p c t", c=4)
        Vc = ab(2 * NH * 65).rearrange("p (t h d) -> p t h d", t=2, h=NH)
        mark_persist = AR["top"]

        HW_ = 64 + TPB * 128
        hbuf = [ab(NCH * HW_).rearrange("p (c t) -> p c t", c=NCH) for _ in range(2)]
        hbuf.append(ab(NCH * (64 + 128)).rearrange("p (c t) -> p c t", c=NCH))
        KW_ = 64 + RING * 128
        Kring = ab(4 * KW_).rearrange("p (c t) -> p c t", c=4)
        Vev = ab(RING * NH * 65).rearrange("p (t h d) -> p t h d", t=RING, h=NH)
        Vod = ab(RING * NH * 65).rearrange("p (t h d) -> p t h d", t=RING, h=NH)
        UW_ = 16 + URING * TPB * 128 + 16
        uring = ab(4 * UW_).rearrange("p (c t) -> p c t", c=4)
        ucx = ab(4 * (16 + CTX + 16)).rearrange("p (c t) -> p c t", c=4)
        NB = TPB * 128
        sq = [af(NB) for _ in range(2)]
        rstd = af(NB)
        tt = [af(NB) for _ in range(2)]
        rC = [af(NB) for _ in range(2)] + [af(128)]
        rS = [af(NB) for _ in range(2)] + [af(128)]
        tkm = af(NB)
        r1 = [af(NB) for _ in range(2)]
        QT = ab(4 * NB).rearrange("p (c t) -> p c t", c=4)
        Pb = [ab(512) for _ in range(3)]
        Otok = [ab(512) for _ in range(2)]
        rcp = [af(8) for _ in range(2)]
        OT = ab(4 * NB).rearrange("p (c t) -> p c t", c=4)
        cacc_raw = af(4 * NB)
        cacc = cacc_raw.rearrange("p (c t) -> p c t", c=4)
        xk = cacc_raw.rearrange("p (c t) -> p c t", c=NCH)
        stg = cacc_raw[:, 0:512]
        CACC = ["cacc0", "cacc1", "cacc2", "cacc3"]
        lsq = sq
        lmu = rstd
        lrs = af(NB)
        cT = ab(4 * NB).rearrange("p (c t) -> p c t", c=4)
        dg = ab(4 * 128).rearrange("p (h j c) -> p h j c", h=2, j=2)
        gt = [ab(NB) for _ in range(2)]
        m12 = [af(NB) for _ in range(2)]
        sig = m12
        mrg = ab(NCH * NB).rearrange("p (c t) -> p c t", c=NCH)
        mark_tm = AR["top"]

        AR["top"] = mark_persist
        FW_ = FBLK + 2
        h2 = ab(NCH * FW_).rearrange("p (c t) -> p c t", c=NCH)
        hid = ab(NJ * FBLK).rearrange("p (c t) -> p c t", c=NJ)
        fsq = [af(FW_) for _ in range(2)]
        frs = af(FW_)
        ftt = [af(FW_) for _ in range(2)]
        fta = [af(FBLK) for _ in range(2)]
        ftg = [af(FBLK) for _ in range(2)]
        fsg = [af(FBLK) for _ in range(2)]
        ftm = af(FW_)
        hstash = ab(16).rearrange("p (c t) -> p c t", c=NCH)
        fo = [af(512) for _ in range(2)]
        mark_ffn = AR["top"]
        AR["top"] = max(mark_tm, mark_ffn)
        print("ARENA words: persist", mark_persist, "tm", mark_tm, "ffn", mark_ffn, "of", ASZ, "(%.1f KB)" % (AR["top"] * 4 / 1024))

        wctr = {"n": 0}

        def wload(dram_ap, view_fn, dt_bf=True, reads=()):
            s = wctr["n"] % NWS
            wctr["n"] += 1
            raw = wslot[s]
            v = view_fn(raw.bitcast(BF16) if dt_bf else raw)
            res = "ws%d" % s
            A("sp", "dma_start", out=v, in_=dram_ap, reads=list(reads), writes=[res], dma_key=res)
            return v, res

        def kgroup(wb, c0, ncols, kchunks):
            src = wb[:, c0:c0 + ncols].rearrange("(kc p) n -> p kc n", p=128)
            return wload(src, lambda raw: raw[:, 0:kchunks * ncols].rearrange("p (kc n) -> p kc n", kc=kchunks),
                         reads=CASTRES[id(wb)])

        if NDBG == 20:
            pass
            dbg(Kring.rearrange("p c t -> p (c t)")[:, 0:512], ["ARENA0"], "Kring")
            dbg(Vod.rearrange("p t h d -> p (t h d)")[:, 0:512], ["ARENA0"], "Vod")
            dbg(Pb[2][:, 0:512], ["ARENA0"], "Pb2")
            dbg(Otok[1][:, 0:512], ["ARENA0"], "Otok1")
            dbg(mrg.rearrange("p c t -> p (c t)")[:, 0:512], ["ARENA0"], "mrg")
            dbg(stg[:, 0:512], ["ARENA0"], "stg")
            dbg(x_res[:, 0, 0:512], ["ARENA0"], "xres(poison expected)")
        A("pool", "memset", ones_f, 1.0, writes=["ones"])
        A("pool", "memset", epsT, EPS, writes=["eps"])
        A("pool", "memset", ident_f, 0.0, writes=["identf"])
        A("pool", "affine_select", out=ident_f, in_=ident_f, pattern=[[-1, 128]], compare_op=ALU.not_equal,
                                              fill=1.0, base=0, channel_multiplier=1, reads=["identf"], writes=["identf"])
        A("dve", "tensor_copy", out=ident, in_=ident_f, reads=["identf"], writes=["ident"])
        A("pool", "memset", Vc[:, :, :, 64:65], 1.0, writes=["Vc"])
        CASTRES = {}
        for c in layers:
            l = c["l"]
            for k in ("w_in", "w_rot", "w_co", "w_no", "w_out", "w_up", "w_down"):
                src, dst = W[l][k], W[l][k + "_b"]
                rows = src.shape[0]
                step = 512
                lst = []
                for r0 in range(0, rows, step):
                    r1_ = min(rows, r0 + step)
                    res = "cast_%s%d_%d" % (k, l, r0 // step)
                    A("pool", "dma_start", out=dst[r0:r1_, :], in_=src[r0:r1_, :], writes=[res], dma_key=res)
                    lst.append(res)
                CASTRES[id(dst)] = lst

        A("sp", "dma_start", out=cinT, in_=cin_d, writes=["cin"], dma_key="misc")
        A("sp", "dma_start", out=rmT, in_=rm_d, writes=["rm"], dma_key="misc")
        for c in layers:
            l = c["l"]
            A("sp", "dma_start", out=vecT[l], in_=W[l]["vec"], writes=["vec%d" % l], dma_key="misc")
        for ch in range(NCH):
            A("sp", "dma_start", out=x_res[:, ch, :], in_=xT_d[ch, :, (XA - XIA) * 128:(XB - XIA) * 128],
                writes=["x%d" % ch], dma_key="xin%d" % (ch % 2))
        A("sp", "dma_start", out=xc_res, in_=ctxT_d.rearrange("c p t -> p c t"), writes=["xc"], dma_key="misc")
        A("act", "activation", out=sT, in_=cinT, func=AF.Silu, reads=["cin"], writes=["sT"])

        def emit_mod(l):
            wa = W[l]["w_ada"]
            pm = bank(7)
            for oc in range(48):
                src = wa[:, oc * 128:(oc + 1) * 128].rearrange("(kc p) n -> p kc n", p=128)
                v, res = wload(src, lambda raw: raw[:, 0:1024].rearrange("p (kc n) -> p kc n", kc=8), dt_bf=False)
                for kc in range(NCH):
                    A("pe", "matmul", pm[:, oc * 2:oc * 2 + 2], lhsT=v[:, kc, :],
                                                                    rhs=sT[:, kc * 2:kc * 2 + 2], start=(kc == 0), stop=(kc == 7),
                        reads=[res, "sT"], writes=["B7"])
            pmv = pm[:, 0:96].rearrange("p (c s) -> p c s", s=2)
            for s in range(2):
                A("dve", "tensor_tensor", out=modT[l][:, :, s], in0=pmv[:, :, s], in1=vecT[l][:, V_BADA:V_BADA + 48],
                                                          op=ALU.add, reads=["B7", "vec%d" % l], writes=["modT%d" % l])
            for s in range(2):
                m = modT[l]
                A("dve", "scalar_tensor_tensor", out=modv[l][:, s, 0, :], in0=m[:, 8:16, s], scalar=1.0,
                                                                      in1=vecT[l][:, V_N1G:V_N1G + 8], op0=ALU.add, op1=ALU.mult,
                    reads=["modT%d" % l, "vec%d" % l], writes=["modv%d" % l])
                A("dve", "scalar_tensor_tensor", out=modv[l][:, s, 3, :], in0=m[:, 32:40, s], scalar=1.0,
                                                                      in1=vecT[l][:, V_N2G:V_N2G + 8], op0=ALU.add, op1=ALU.mult,
                    reads=["modT%d" % l, "vec%d" % l], writes=["modv%d" % l])
                for kind, c0 in ((1, 0), (2, 16), (4, 24), (5, 40)):
                    A("dve", "tensor_copy", out=modv[l][:, s, kind, :], in_=m[:, c0:c0 + 8, s],
                        reads=["modT%d" % l], writes=["modv%d" % l])

        def emit_etab(l):
            bsrc = W[l]["bias"].rearrange("p (h n) -> p h n", h=NH)
            for h in range(NH):
                A("sp", "dma_start", out=stg, in_=bsrc[:, h, :], writes=["stg"] + CACC, dma_key="stg")
                A("act", "activation", out=Etab[:, h, :, :].rearrange("p i q -> p (i q)"), in_=stg, func=AF.Exp,
                    reads=["stg"] + CACC, writes=["Etab"])

        def emit_norm_mod(xsrc, n, mv, kinds, sqb, rsb, ttb, out_fn, psb, xres, tag):
            ps = bank(psb)[:, 0:n]
            for c in range(NCH):
                b = sqb[c % 2][:, 0:n]
                A("act", "activation", out=b, in_=xsrc[:, c, :], func=AF.Square,
                    reads=xres(c), writes=[tag + "sq%d" % (c % 2)])
                A("pe", "matmul", ps, lhsT=ones_f, rhs=b, start=(c == 0), stop=(c == 7),
                    reads=[tag + "sq%d" % (c % 2), "ones"], writes=["B%d" % psb])
            rs = rsb[:, 0:n]
            A("act", "activation", out=rs, in_=ps, func=AF.Sqrt, bias=epsT, scale=1.0 / D,
                reads=["B%d" % psb, "eps"], writes=[tag + "rs"])
            A("dve", "reciprocal", out=rs, in_=rs, reads=[tag + "rs"], writes=[tag + "rs"])
            for c in range(NCH):
                t = ttb[c % 2][:, 0:n]
                A("dve", "tensor_tensor", out=t, in0=xsrc[:, c, :], in1=rs, op=ALU.mult,
                    reads=xres(c) + [tag + "rs"], writes=[tag + "tt%d" % (c % 2)])
                o, ores = out_fn(c)
                A("act", "activation", out=o, in_=t, func=AF.Identity, bias=mv[:, kinds[1], c:c + 1],
                                                                 scale=mv[:, kinds[0], c:c + 1],
                    reads=[tag + "tt%d" % (c % 2), "modv"], writes=ores)

        gctr = {"n": 0}

        def gslot(n):
            k = (5, 6, 0, 1, 2)[gctr["n"] % 5]
            gctr["n"] += 1
            return bank(k)[:, 0:n], "B%d" % k

        def proj(wview, wres, col0, ncol, rhs_fn, nk, n, rres):
            ps, pres = gslot(n)
            for k in range(nk):
                A("pe", "matmul", ps[0:ncol, :], lhsT=wview[:, k, col0:col0 + ncol], rhs=rhs_fn(k),
                                                  start=(k == 0), stop=(k == nk - 1),
                    reads=[wres] + rres, writes=[pres])
            return ps, pres

        def phase_A(c, stream, blk):
            l = c["l"]
            mv = modv[l][:, stream]
            lat = stream == 0
            if lat:
                lt0, nt, hs, xsrc, xres, need_mask = blk["lt0"], blk["nt"], blk["hs"], blk["xsrc"], blk["xres"], blk["mask"]
                n = nt * 128
                hb = hbuf[hs]
                hview = hb[:, :, 64:64 + n]
                if blk["prev_hs"] is not None:
                    pb = hbuf[blk["prev_hs"]]
                    pn = blk["prev_n"]
                    A("pool", "tensor_copy", out=hb[:, :, 0:64], in_=pb[:, :, 64 + pn - 64:64 + pn],
                        reads=["h%d" % blk["prev_hs"]], writes=["h%d" % hs])
                hres = "h%d" % hs
            else:
                n = CTX
                xsrc, xres = xc_res, (lambda cc: ["xc"])
                hb = hbuf[0]
                hview = hb[:, :, 64:64 + n]
                hres = "h0"
                need_mask = False
            emit_norm_mod(xsrc, n, mv, (0, 1), sq, rstd, tt, lambda cc: (hview[:, cc, :], [hres]), 7, xres, "A")
            wb = W[l]
            if lat:
                col = (lt0 - XIA) * 128
                bi = blk["hs"]
                A("sp", "dma_start", out=rC[bi][:, 0:n], in_=ropeC_d[:, col:col + n], writes=["rC%d" % bi], dma_key="rC%d" % bi)
                A("sp", "dma_start", out=rS[bi][:, 0:n], in_=ropeS_d[:, col:col + n], writes=["rS%d" % bi], dma_key="rS%d" % bi)
                if need_mask:
                    A("sp", "dma_start", out=tkm[:, 0:n], in_=tokm_d[:, col:col + n], writes=["tkm"], dma_key="tkm")
            for g in range(2):
                wv, wr = kgroup(wb["w_in_b"], 1536 + g * 256, 256, 8)
                if lat:
                    wv2, wr2 = kgroup(wb["w_rot_b"], 512 + g * 256, 256, 8)
                for j in range(2):
                    hp = g * 2 + j
                    ps, pres = proj(wv, wr, j * 128, 128, lambda k: hview[:, k, :], 8, n, [hres])
                    if lat:
                        ps2, pres2 = proj(wv2, wr2, j * 128, 128, lambda k: hview[:, k, :], 8, n, [hres])
                        a, b = r1[0][:, 0:n], r1[1][:, 0:n]
                        A("dve", "tensor_tensor", out=a, in0=ps, in1=rC[bi][:, 0:n], op=ALU.mult,
                            reads=[pres, "rC%d" % bi], writes=["r1a"])
                        A("dve", "tensor_tensor", out=b, in0=ps2, in1=rS[bi][:, 0:n], op=ALU.mult,
                            reads=[pres2, "rS%d" % bi], writes=["r1b"])
                        for t in range(nt):
                            sl = (lt0 + t - c["KA"]) % RING
                            A("dve", "tensor_tensor",
                                out=Kring[:, hp, 64 + sl * 128:64 + (sl + 1) * 128], in0=a[:, t * 128:(t + 1) * 128],
                                in1=b[:, t * 128:(t + 1) * 128], op=ALU.add, reads=["r1a", "r1b"], writes=["K%d" % sl])
                            if sl == RING - 1:
                                A("pool", "tensor_copy", out=Kring[:, hp, 0:64],
                                                                                 in_=Kring[:, hp, 64 + sl * 128 + 64:64 + (sl + 1) * 128],
                                    reads=["K%d" % sl], writes=["Kmar"])
                    else:
                        A("act", "copy", out=KcT[:, hp, :], in_=ps, reads=[pres], writes=["KcT"])
            wv0, wr0 = kgroup(wb["w_in_b"], 2048, 256, 8)
            wv1, wr1 = kgroup(wb["w_in_b"], 2304, 256, 8)

            def vtile(col_lo, dst, dres):
                for half, (wv, wr) in enumerate(((wv0, wr0), (wv1, wr1))):
                    ps, pres = gslot(256)
                    for k in range(NCH):
                        A("pe", "matmul", ps, lhsT=hb[:, k, col_lo:col_lo + 128], rhs=wv[:, k, :],
                                                                        start=(k == 0), stop=(k == 7),
                            reads=[wr, hres], writes=[pres])
                    A("act", "copy", out=dst[:, half * 4:(half + 1) * 4, 0:64],
                                                                  in_=ps.rearrange("p (h d) -> p h d", h=4),
                        reads=[pres], writes=[dres])
            if lat:
                for t in range(nt):
                    lt = lt0 + t
                    sl = (lt - c["KA"]) % RING
                    vtile(64 + t * 128, Vev[:, sl], "Ve%d" % sl)
                    if lt - 1 >= c["KA"]:
                        so = (lt - 1 - c["KA"]) % RING
                        vtile(64 + t * 128 - 64, Vod[:, so], "Vo%d" % so)
            else:
                for t in range(2):
                    vtile(64 + t * 128, Vc[:, t], "Vc")
            if lat or not c["last"]:
                for g in range(2):
                    wa_, wra = kgroup(wb["w_in_b"], g * 256, 256, 8)
                    wg_, wrg = kgroup(wb["w_in_b"], 512 + g * 256, 256, 8)
                    for j in range(2):
                        cc = g * 2 + j
                        pa, pra = proj(wa_, wra, j * 128, 128, lambda k: hview[:, k, :], 8, n, [hres])
                        pg, prg = proj(wg_, wrg, j * 128, 128, lambda k: hview[:, k, :], 8, n, [hres])
                        sg = sig[cc % 2][:, 0:n]
                        A("act", "activation", out=sg, in_=pg, func=AF.Sigmoid, reads=[prg],
                            writes=["m%d" % (cc % 2)])
                        if need_mask:
                            A("pool", "tensor_tensor", out=sg, in0=sg, in1=tkm[:, 0:n], op=ALU.mult,
                                reads=["m%d" % (cc % 2), "tkm"], writes=["m%d" % (cc % 2)])
                        if lat:
                            up = blk["upos"]
                            dst = uring[:, cc, 16 + up:16 + up + n]
                            ures = ["u%d" % (up // 128 + q_) for q_ in range(nt)]
                        else:
                            dst = ucx[:, cc, 16:16 + CTX]
                            ures = ["ucx"]
                        A("dve", "tensor_tensor", out=dst, in0=pa, in1=sg, op=ALU.mult,
                            reads=[pra, "m%d" % (cc % 2)], writes=ures)
                if lat:
                    up = blk["upos"]
                    URT = URING * NB
                    if up + n == URT:
                        A("pool", "tensor_copy", out=uring[:, :, 0:16], in_=uring[:, :, 16 + URT - 16:16 + URT],
                            reads=["u%d" % (URT // 128 - 1)], writes=["umarF"])
                    if up == 0:
                        A("pool", "tensor_copy", out=uring[:, :, 16 + URT:16 + URT + 16], in_=uring[:, :, 16:32],
                            reads=["u0"], writes=["umarB"])

        sctr = {"n": 0}
        actr = {"n": 0}
        dscr = af(512) if NDBG == 30 else None

        def attention(c, qT_fn, nq, chunks, ores, out_rows, fill=None):
            import os
            SER = ["ATTSER"] if os.environ.get("KSER") else []
            actr["n"] += 1
            DBGA = NDBG == 30 and actr["n"] == 1
            nch = len(chunks)
            width = nch * nq
            obank = psum_t[:, 3 * 512:5 * 512]
            ov = obank[0:nq, :].rearrange("p (h d) -> p h d", h=NH)
            pvq = []
            for h in range(NH):
                sb = sctr["n"] % 3
                sctr["n"] += 1
                sps = bank(sb)[:, 0:width]
                pb = Pb[sb][:, 0:width]
                hp, po = h // 2, (h % 2) * 64
                for i, ch in enumerate(chunks):
                    A("pe", "matmul",
                        sps[:, i * nq:(i + 1) * nq], lhsT=ch["k"](hp, po), rhs=qT_fn(hp, po), start=True, stop=True,
                        reads=ch["kr"] + ["QT"], writes=["B%d" % sb] + SER)
                A("act", "activation", out=pb, in_=sps, func=AF.Exp, scale=DH ** -0.5,
                    reads=["B%d" % sb], writes=["P%d" % sb] + SER)
                if DBGA and h < 4:
                    dbg(pb, ["P%d" % sb], "Pexp h%d" % h)
                i = 0
                while i < nch:
                    ch = chunks[i]
                    if ch["e"] is None:
                        i += 1
                        continue
                    if ch["rm"] is None:
                        j = i
                        while j + 1 < nch and chunks[j + 1]["e"] is not None and chunks[j + 1]["rm"] is None \
                                and chunks[j + 1]["ei"] == chunks[j]["ei"] + 1:
                            j += 1
                        e0 = ch["ei"]
                        ev = Etab[:, h, e0:e0 + (j - i + 1), :].rearrange("p i q -> p (i q)")
                        A("dve", "tensor_tensor", out=pb[:, i * nq:(j + 1) * nq],
                                                                                   in0=pb[:, i * nq:(j + 1) * nq], in1=ev, op=ALU.mult,
                            reads=["P%d" % sb, "Etab"], writes=["P%d" % sb] + SER)
                        i = j + 1
                    else:
                        A("dve", "scalar_tensor_tensor",
                            out=pb[:, i * nq:(i + 1) * nq], in0=pb[:, i * nq:(i + 1) * nq], scalar=ch["rm"],
                            in1=Etab[:, h, ch["ei"], :], op0=ALU.mult, op1=ALU.mult,
                            reads=["P%d" % sb, "Etab", "rm"], writes=["P%d" % sb] + SER)
                        i += 1
                if fill is not None:
                    fill["f"](fill["k"])

                def _pv(h=h, pb=pb, sb=sb):
                    for i, ch in enumerate(chunks):
                        A("pe", "matmul", ov[:, h, 0:65], lhsT=pb[:, i * nq:(i + 1) * nq],
                          rhs=ch["v"](h), start=(i == 0), stop=(i == nch - 1),
                          reads=["P%d" % sb] + ch["vr"], writes=["OB"] + SER)
                if pvq:
                    pvq.pop(0)()
                pvq.append(_pv)
            while pvq:
                pvq.pop(0)()
            ob = out_rows["otok"]
            rc = rcp[ob]
            A("dve", "reciprocal", out=rc[0:nq, :], in_=ov[:, :, 64], reads=["OB"], writes=["rcp%d" % ob] + SER)
            ot = Otok[ob][0:nq, :].rearrange("p (h d) -> p h d", h=NH)
            A("dve", "tensor_tensor", out=ot, in0=ov[:, :, 0:64], in1=rc[0:nq, :].unsqueeze(2).broadcast_to([nq, NH, 64]),
                                                 op=ALU.mult, reads=["OB", "rcp%d" % ob], writes=["Otok%d" % ob] + SER)
            if DBGA:
                dbg(rc[0:nq, :], ["rcp%d" % ob], "rc")
                dbg(Otok[ob][0:nq, :], ["Otok%d" % ob], "Otok")
            tp = bank(7).bitcast(BF16)
            for f in range(4):
                A("pe", "transpose", out=tp[:, f * nq:(f + 1) * nq], in_=Otok[ob][0:nq, f * 128:(f + 1) * 128],
                                                     identity=ident[0:nq, 0:nq], reads=["Otok%d" % ob, "ident"], writes=["B7"] + SER)
            c0 = out_rows["col"]
            A("act", "copy", out=OT[:, :, c0:c0 + nq], in_=tp[:, 0:4 * nq].rearrange("p (f q) -> p f q", f=4),
                reads=["B7"], writes=[ores] + SER)

        def phase_B(c, stream, blk):
            l = c["l"]
            wb = W[l]
            mv = modv[l][:, stream]
            lat = stream == 0
            if lat:
                lt0, nt, hs = blk["lt0"], blk["nt"], blk["hs"]
                n = nt * 128
                hview = hbuf[hs][:, :, 64:64 + n]
                hres = "h%d" % hs
                bi = blk["hs"]
                xcol = (lt0 - XA) * 128
                xv = x_res[:, :, xcol:xcol + n]
                xres = lambda cc: ["x%d" % cc]
            else:
                n = CTX
                hview = hbuf[0][:, :, 64:64 + n]
                hres = "h0"
                xv = xc_res
                xres = lambda cc: ["xc"]
            for g in range(2):
                wv, wr = kgroup(wb["w_in_b"], 1024 + g * 256, 256, 8)
                if lat:
                    wv2, wr2 = kgroup(wb["w_rot_b"], g * 256, 256, 8)
                for j in range(2):
                    hp = g * 2 + j
                    ps, pres = proj(wv, wr, j * 128, 128, lambda k: hview[:, k, :], 8, n, [hres])
                    if lat:
                        ps2, pres2 = proj(wv2, wr2, j * 128, 128, lambda k: hview[:, k, :], 8, n, [hres])
                        a, b = r1[0][:, 0:n], r1[1][:, 0:n]
                        A("dve", "tensor_tensor", out=a, in0=ps, in1=rC[bi][:, 0:n], op=ALU.mult,
                            reads=[pres, "rC%d" % bi], writes=["r1a"])
                        A("dve", "tensor_tensor", out=b, in0=ps2, in1=rS[bi][:, 0:n], op=ALU.mult,
                            reads=[pres2, "rS%d" % bi], writes=["r1b"])
                        A("dve", "tensor_tensor", out=QT[:, hp, 0:n], in0=a, in1=b, op=ALU.add,
                            reads=["r1a", "r1b"], writes=["QT"])
                    else:
                        A("act", "copy", out=QT[:, hp, 0:n], in_=ps, reads=[pres], writes=["QT"])
            if not lat and NDBG == 50:
                for hp_ in range(4):
                    dbg(QT[:, hp_, 0:n], ["QT"], "QT%d" % hp_)
                for hp_ in range(4):
                    dbg(KcT[:, hp_, :], ["KcT"], "KcT%d" % hp_)
                dbg(hview[:, 0, :], [hres], "hc0")
                dbg(hview[:, 7, :], [hres], "hc7")
            if not lat and False:
                dbg(hview[:, 0, :], [hres], "hc")
                dbg(KcT[:, 0, :], ["KcT"], "KcT")
                dbg(Vc[:, 0].rearrange("p h d -> p (h d)")[:, 0:512], ["Vc"], "Vc")
                dbg(QT[:, 0, 0:n], ["QT"], "QT")
            vec = vecT[l]

            def conv_gen():
              if lat:
                  up = blk["upos"]
                  base = 16 + up
                  usrc = uring
                  nsl = URING * TPB
                  ur = ["u%d" % ((up // 128 + q_) % nsl) for q_ in (-1, 0, 1, 2)] + ["umarF", "umarB"]
              else:
                  base = 16
                  usrc = ucx
                  ur = ["ucx"]
              nstep = CK

              def build(j):
                  hf = j % 2
                  c0 = V_CDW + 2 * CK + 2 * j
                  A("dve", "tensor_tensor", out=dg[:, hf], in0=ident.unsqueeze(1).broadcast_to([128, 2, 128]),
                    in1=vec[:, c0:c0 + 2].unsqueeze(2).broadcast_to([128, 2, 128]), op=ALU.mult,
                    reads=["ident", "vec%d" % l], writes=["dg%d" % hf])
              build(0)
              for j in range(nstep):
                  if j + 1 < nstep:
                      build(j + 1)
                  for q_ in range(2):
                      t_ = 2 * j + q_
                      cc, k = 2 + t_ // CK, t_ % CK
                      cps = bank(6)[:, (cc % 2) * 256:(cc % 2) * 256 + n]
                      src = usrc[:, cc, base + k - 15:base + k - 15 + n]
                      A("pe", "matmul", cps, lhsT=dg[:, j % 2, q_], rhs=src, start=(k == 0), stop=(k == CK - 1),
                        reads=ur + ["dg%d" % (j % 2)], writes=["B6"])
                  for q_ in range(2):
                      t_ = 2 * j + q_
                      cc, k = t_ // CK, t_ % CK
                      acc = cacc[:, cc, 0:n]
                      src = usrc[:, cc, base + k - 15:base + k - 15 + n]
                      wk = vec[:, V_CDW + cc * CK + k:V_CDW + cc * CK + k + 1]
                      if k == 0:
                          A("dve", "tensor_scalar", out=acc, in0=src, scalar1=wk, scalar2=vec[:, V_CDB + cc:V_CDB + cc + 1],
                            op0=ALU.mult, op1=ALU.add, reads=ur + ["vec%d" % l], writes=["cacc%d" % cc])
                      else:
                          A("dve", "scalar_tensor_tensor", out=acc, in0=src, scalar=wk, in1=acc, op0=ALU.mult, op1=ALU.add,
                            reads=ur + ["vec%d" % l, "cacc%d" % cc], writes=["cacc%d" % cc])
                  yield

            cgen = conv_gen()

            def filler(k):
                for _ in range(k):
                    try:
                        next(cgen)
                    except StopIteration:
                        return
            nheads_total = (nt * 2 * NH) if lat else (4 * NH)
            per_head = -(-CK // nheads_total)
            FILL = {"f": filler, "k": per_head}
            if lat:
                for t in range(nt):
                    lt = lt0 + t
                    for rr in range(2):
                        r = 2 * lt + rr
                        qc0 = t * 128 + rr * 64
                        special = None
                        if lt in (0, 1):
                            special = ("top", r)
                        elif lt in (OWN - 2, OWN - 1):
                            special = ("bot", r - (2 * OWN - 4))
                        if special is None:
                            cis = [0, 1, 2, 3]
                        elif special[0] == "top":
                            cis = [0, 1, 2, 3, 4, 5]
                        else:
                            cis = [-2, -1, 0, 1, 2, 3]
                        chunks = []
                        for ii, ci in enumerate(cis):
                            kr0 = r - 4 + 2 * ci
                            pos = (kr0 * 64 - c["KA"] * 128)
                            rp = pos % (RING * 128)
                            if rp + 128 <= RING * 128:
                                ka = 64 + rp
                                kres = ["K%d" % (rp // 128)] + (["K%d" % ((rp // 128 + 1) % RING)] if rp % 128 else [])
                            else:
                                ka = 0
                                kres = ["Kmar", "K0"]
                            if kr0 % 2 == 0:
                                vs = ((kr0 // 2) - c["KA"]) % RING
                                vfn = (lambda h, vs=vs: Vev[:, vs, h, :])
                                vres = ["Ve%d" % vs]
                            else:
                                vs = (((kr0 - 1) // 2) - c["KA"]) % RING
                                vfn = (lambda h, vs=vs: Vod[:, vs, h, :])
                                vres = ["Vo%d" % vs]
                            rmap = None
                            if special is not None:
                                sidx = (special[1] + (0 if special[0] == "top" else 4)) * 6 + ii
                                rmap = rmT[:, sidx:sidx + 1]
                            chunks.append(dict(k=(lambda hp, po, ka=ka: Kring[po:po + 64, hp, ka:ka + 128]), kr=kres,
                                               v=vfn, vr=vres, e=True, ei=ci + 2, rm=rmap))
                        for t2 in range(2):
                            chunks.append(dict(k=(lambda hp, po, t2=t2: KcT[po:po + 64, hp, t2 * 128:(t2 + 1) * 128]), kr=["KcT"],
                                               v=(lambda h, t2=t2: Vc[:, t2, h, :]), vr=["Vc"], e=None, ei=None, rm=None))
                        attention(c, lambda hp, po, qc0=qc0: QT[po:po + 64, hp, qc0:qc0 + 64], 64, chunks, "OT",
                                  dict(otok=(t * 2 + rr) % 2, col=qc0), fill=FILL)
            else:
                for t in range(2):
                    for hh in range(2):
                        qc0 = t * 128 + hh * 64
                        chunks = [dict(k=(lambda hp, po, t2=t2: KcT[po:po + 64, hp, t2 * 128:(t2 + 1) * 128]), kr=["KcT"],
                                       v=(lambda h, t2=t2: Vc[:, t2, h, :]), vr=["Vc"], e=None, ei=None, rm=None) for t2 in range(2)]
                        attention(c, lambda hp, po, qc0=qc0: QT[po:po + 64, hp, qc0:qc0 + 64], 64, chunks, "OT",
                                  dict(otok=(t * 2 + hh) % 2, col=qc0), fill=FILL)
            filler(10 ** 6)
            for cc in (2, 3):
                cps = bank(6)[:, (cc % 2) * 256:(cc % 2) * 256 + n]
                A("act", "activation", out=cacc[:, cc, 0:n], in_=cps, func=AF.Identity, bias=vec[:, V_CDB + cc:V_CDB + cc + 1], scale=1.0,
                  reads=["B6", "vec%d" % l], writes=["cacc%d" % cc])
            pmu = bank(5)[:, 0:n]

            pm2 = bank(6)[:, 0:n]
            for cc in range(4):
                b = lsq[cc % 2][:, 0:n]
                A("act", "activation", out=b, in_=cacc[:, cc, 0:n], func=AF.Square,
                    reads=["cacc%d" % cc], writes=["Asq%d" % (cc % 2)])
                A("pe", "matmul", pmu, lhsT=ones_f, rhs=cacc[:, cc, 0:n], start=(cc == 0), stop=(cc == 3),
                    reads=["cacc%d" % cc, "ones"], writes=["B5"])
                A("pe", "matmul", pm2, lhsT=ones_f, rhs=b, start=(cc == 0), stop=(cc == 3),
                    reads=["Asq%d" % (cc % 2), "ones"], writes=["B6"])
            gctr["n"] = 0
            mu, rs = lmu[:, 0:n], lrs[:, 0:n]
            A("act", "activation", out=mu, in_=pmu, func=AF.Identity, scale=1.0 / CDIM, reads=["B5"], writes=["Ars"])
            A("dve", "tensor_tensor", out=rs, in0=mu, in1=mu, op=ALU.mult, reads=["Ars"], writes=["lrs"])
            A("dve", "scalar_tensor_tensor", out=rs, in0=pm2, scalar=1.0 / CDIM, in1=rs, op0=ALU.mult, op1=ALU.subtract,
                reads=["B6", "lrs"], writes=["lrs"])
            A("act", "activation", out=rs, in_=rs, func=AF.Sqrt, bias=epsT, scale=1.0, reads=["lrs", "eps"], writes=["lrs"])
            A("dve", "reciprocal", out=rs, in_=rs, reads=["lrs"], writes=["lrs"])
            for cc in range(4):
                acc = cacc[:, cc, 0:n]
                A("dve", "tensor_tensor", out=acc, in0=acc, in1=mu, op=ALU.subtract,
                    reads=["cacc%d" % cc, "Ars"], writes=["cacc%d" % cc])
                A("dve", "tensor_tensor", out=acc, in0=acc, in1=rs, op=ALU.mult,
                    reads=["cacc%d" % cc, "lrs"], writes=["cacc%d" % cc])
                A("act", "activation", out=cT[:, cc, 0:n], in_=acc, func=AF.Silu,
                                                                  bias=vec[:, V_LNB + cc:V_LNB + cc + 1], scale=vec[:, V_LNG + cc:V_LNG + cc + 1],
                    reads=["cacc%d" % cc, "vec%d" % l], writes=["cT"])
            gctr["n"] = 0
            if NDBG == 40 and not lat:
                for f_ in range(4):
                    dbg(OT[:, f_, 0:n], ["OT"], "cOT%d" % f_)
            if NDBG == 10 and lat and blk["lt0"] == -1:
                dbg(QT[:, 0, 0:n], ["QT"], "QT")
                dbg(cT[:, 0, 0:n], ["cT"], "cT")
                for f_ in range(4):
                    dbg(OT[:, f_, 0:n], ["OT"], "OT%d" % f_)
                dbg(Kring[:, 0, 0:512], ["K0"], "Kring0")
                dbg(Vev.rearrange("p t h d -> p (t h d)")[:, 0:512], ["Ve0"], "Vev0")
                dbg(Vod.rearrange("p t h d -> p (t h d)")[:, 0:512], ["Vo0"], "Vod0")
            if not lat and False:
                dbg(cT[:, 0, 0:n], ["cT"], "cT")
                dbg(OT[:, 0, 0:n], ["OT"], "OT")
                dbg(Otok[0][0:64, :], ["Otok0"], "Otok0")
                dbg(Pb[0][:, 0:128], ["P0"], "P0")
            for g in range(4):
                wcv, wcr = kgroup(wb["w_co_b"], g * 256, 256, 4)
                wnv, wnr = kgroup(wb["w_no_b"], g * 256, 256, 4)
                wgc, wgcr = kgroup(wb["w_in_b"], 2560 + g * 256, 256, 8)
                wga, wgar = kgroup(wb["w_in_b"], 3584 + g * 256, 256, 8)
                for j in range(2):
                    oc = g * 2 + j
                    pg1, pg1r = proj(wgc, wgcr, j * 128, 128, lambda k: hview[:, k, :], 8, n, [hres])
                    g1_ = gt[0][:, 0:n]
                    A("act", "activation", out=g1_, in_=pg1, func=AF.Sigmoid, reads=[pg1r], writes=["gt0"])
                    py1, py1r = proj(wcv, wcr, j * 128, 128, lambda k: cT[:, k, 0:n], 4, n, ["cT"])
                    ma = m12[0][:, 0:n]
                    A("dve", "tensor_tensor", out=ma, in0=py1, in1=g1_, op=ALU.mult,
                        reads=[py1r, "gt0"], writes=["m0"])
                    pg2, pg2r = proj(wga, wgar, j * 128, 128, lambda k: hview[:, k, :], 8, n, [hres])
                    g2_ = gt[1][:, 0:n]
                    A("act", "activation", out=g2_, in_=pg2, func=AF.Sigmoid, reads=[pg2r], writes=["gt1"])
                    py2, py2r = proj(wnv, wnr, j * 128, 128, lambda k: OT[:, k, 0:n], 4, n, ["OT"])
                    mb = m12[1][:, 0:n]
                    A("dve", "tensor_tensor", out=mb, in0=py2, in1=g2_, op=ALU.mult,
                        reads=[py2r, "gt1"], writes=["m1"])
                    A("dve", "tensor_tensor", out=mrg[:, oc, 0:n], in0=ma, in1=mb, op=ALU.add,
                        reads=["m0", "m1"], writes=["mrg"])
            if lat and blk["lt0"] == -1:
                dbg(mrg[:, 0, 0:n], ["mrg"], "mrg")
            for g in range(4):
                wov, wor = kgroup(wb["w_out_b"], g * 256, 256, 8)
                for j in range(2):
                    oc = g * 2 + j
                    po, por = proj(wov, wor, j * 128, 128, lambda k: mrg[:, k, 0:n], 8, n, ["mrg"])
                    A("dve", "scalar_tensor_tensor", out=xv[:, oc, :], in0=po, scalar=mv[:, 2, oc:oc + 1],
                                                                            in1=xv[:, oc, :], op0=ALU.mult, op1=ALU.add,
                        reads=[por, "modv"] + xres(oc), writes=xres(oc))

        fprev = {"n": None}

        def ffn_block(c, stream, t0, n, need_mask):
            l = c["l"]
            wb = W[l]
            vec = vecT[l]
            mv = modv[l][:, stream]
            lat = stream == 0
            if lat:
                xin = x_res[:, :, t0 - 1:t0 + n + 1]
                xo = x_res[:, :, t0:t0 + n]
                xres = lambda cc: ["x%d" % cc]
                hv = h2[:, :, 0:n + 2]
                nprev = fprev["n"]
                if nprev is not None:
                    A("pool", "tensor_copy", out=hstash[:, :, 0:1], in_=h2[:, :, nprev:nprev + 1], reads=["h2"], writes=["hstash"])
                emit_norm_mod(xin, n + 2, mv, (3, 4), fsq, frs, ftt, lambda cc: (hv[:, cc, :], ["h2"]), 7, xres, "F")
                if nprev is not None:
                    A("pool", "tensor_copy", out=h2[:, :, 0:1], in_=hstash[:, :, 0:1], reads=["hstash"], writes=["h2"])
                fprev["n"] = n
                if need_mask:
                    col = t0 - 1 + (XA - XIA) * 128
                    A("sp", "dma_start", out=ftm[:, 0:n + 2], in_=tokm_d[:, col:col + n + 2], writes=["ftm"], dma_key="ftm")
                    for cc in range(NCH):
                        A("pool", "tensor_tensor", out=hv[:, cc, :], in0=hv[:, cc, :], in1=ftm[:, 0:n + 2], op=ALU.mult,
                            reads=["h2", "ftm"], writes=["h2"])
            else:
                xo = xc_res
                xres = lambda cc: ["xc"]
                hv = h2[:, :, 0:n + 2]
                A("pool", "memset", h2[:, :, 0:1], 0.0, writes=["h2"])
                A("pool", "memset", h2[:, :, n + 1:n + 2], 0.0, writes=["h2"])
                emit_norm_mod(xc_res, n, mv, (3, 4), fsq, frs, ftt, lambda cc: (h2[:, cc, 1:n + 1], ["h2"]), 7, xres, "F")
            for j in range(NJ):
                wv, wr = kgroup(wb["w_up_b"], j * 256, 256, 8)
                outs = []
                for half in range(2):
                    k_ = 1 + 2 * (j % 2) + half
                    ps = bank(k_)[:, 0:n + 2]
                    pres = "B%d" % k_
                    for k in range(NCH):
                        A("pe", "matmul", ps, lhsT=wv[:, k, half * 128:(half + 1) * 128],
                                                                                 rhs=hv[:, k, :], start=(k == 0), stop=(k == 7),
                            reads=[wr, "h2"], writes=[pres])
                    ch = 2 * j + half
                    w0 = vec[:, V_FDW + ch * 3 + 0:V_FDW + ch * 3 + 1]
                    w1 = vec[:, V_FDW + ch * 3 + 1:V_FDW + ch * 3 + 2]
                    w2 = vec[:, V_FDW + ch * 3 + 2:V_FDW + ch * 3 + 3]
                    bb = vec[:, V_FDB + ch:V_FDB + ch + 1]
                    tb = (fta if half == 0 else ftg)[j % 2][:, 0:n]
                    tres = "ft%d%d" % (half, j % 2)
                    A("act", "activation", out=tb, in_=ps[:, 1:n + 1], func=AF.Identity, bias=bb, scale=w1,
                        reads=[pres, "vec%d" % l], writes=[tres])
                    A("dve", "scalar_tensor_tensor", out=tb, in0=ps[:, 0:n], scalar=w0, in1=tb,
                                                                                   op0=ALU.mult, op1=ALU.add,
                        reads=[pres, tres, "vec%d" % l], writes=[tres])
                    A("dve", "scalar_tensor_tensor", out=tb, in0=ps[:, 2:n + 2], scalar=w2, in1=tb,
                                                                                   op0=ALU.mult, op1=ALU.add,
                        reads=[pres, tres, "vec%d" % l], writes=[tres])
                    outs.append((tb, tres))
                (ta, tar), (tg, tgr) = outs
                sg = fsg[j % 2][:, 0:n]
                A("act", "activation", out=sg, in_=tg, func=AF.Silu, reads=[tgr], writes=["fsg%d" % (j % 2)])
                A("dve", "tensor_tensor", out=hid[:, j, 0:n], in0=ta, in1=sg, op=ALU.mult,
                    reads=[tar, "fsg%d" % (j % 2)], writes=["hid"])
            for oc in range(NCH):
                halves = []
                for hf in range(2):
                    src = wb["w_down_b"][hf * 11 * 128:(hf + 1) * 11 * 128, oc * 128:(oc + 1) * 128].rearrange("(j p) n -> p j n", p=128)
                    halves.append(wload(src, lambda raw: raw[:, 0:11 * 128].rearrange("p (j n) -> p j n", j=11),
                                        reads=CASTRES[id(wb["w_down_b"])]))
                k_ = 5 + (oc % 2)
                ps = bank(k_)[:, 0:n]
                for j in range(NJ):
                    wv, wr = halves[j // 11]
                    A("pe", "matmul", ps, lhsT=wv[:, j % 11, :], rhs=hid[:, j, 0:n], start=(j == 0), stop=(j == NJ - 1),
                        reads=[wr, "hid"], writes=["B%d" % k_])
                A("dve", "scalar_tensor_tensor", out=xo[:, oc, :], in0=ps, scalar=mv[:, 5, oc:oc + 1], in1=xo[:, oc, :],
                                                                        op0=ALU.mult, op1=ALU.add,
                    reads=["B%d" % k_, "modv"] + xres(oc), writes=xres(oc))

        def final_out(c):
            l = c["l"]
            vec = vecT[l]
            col0 = (0 - XA) * 128
            for bi in range(OWN * 128 // 512):
                t0 = col0 + bi * 512
                n = 512
                xin = x_res[:, :, t0:t0 + n]
                ps = bank(7)[:, 0:n]
                for cc in range(NCH):
                    b = fsq[cc % 2][:, 0:n]
                    A("act", "activation", out=b, in_=xin[:, cc, :], func=AF.Square,
                        reads=["x%d" % cc], writes=["Fsq%d" % (cc % 2)])
                    A("pe", "matmul", ps, lhsT=ones_f, rhs=b, start=(cc == 0), stop=(cc == 7),
                        reads=["Fsq%d" % (cc % 2), "ones"], writes=["B7"])
                rs = frs[:, 0:n]
                A("act", "activation", out=rs, in_=ps, func=AF.Sqrt, bias=epsT, scale=1.0 / D, reads=["B7", "eps"], writes=["Frs"])
                A("dve", "reciprocal", out=rs, in_=rs, reads=["Frs"], writes=["Frs"])
                for cc in range(NCH):
                    o = fo[cc % 2][:, 0:n]
                    A("dve", "scalar_tensor_tensor", out=o, in0=xin[:, cc, :], scalar=vec[:, V_FNG + cc:V_FNG + cc + 1],
                                                                          in1=rs, op0=ALU.mult, op1=ALU.mult,
                        reads=["x%d" % cc, "Frs", "vec%d" % l], writes=["fo%d" % (cc % 2)])
                    A("sp", "dma_start", out=outT_d[cc, :, bi * 512:(bi + 1) * 512], in_=o,
                        reads=["fo%d" % (cc % 2)], dma_key="out%d" % (cc % 2))

        UR = URING * NB
        for c in layers:
            l = c["l"]
            first_layer = c is layers[0]
            half = (mark_persist + AR["top"]) // 2
            A("pool", "memset", arena_t[:, mark_persist:half], 0.0, writes=["ARENA0"])
            A("dve", "memset", arena_t[:, half:AR["top"]], 0.0, writes=["ARENA1"])
            P.barrier()
            A("pool", "memset", Vev[:, :, :, 64:65], 1.0, writes=["Vev"])
            A("pool", "memset", Vod[:, :, :, 64:65], 1.0, writes=["Vod"])
            emit_mod(l)
            A("dve", "tensor_copy", out=modv[l][:, 0, 0, 0:1], in_=modv[l][:, 0, 0, 0:1],
                reads=["modv%d" % l], writes=["modv"])
            emit_etab(l)
            if NDBG == 60 and not first_layer:
                dbg(x_res[:, 0, 0:512], ["x0"], "x1 cols0-512")
                dbg(x_res[:, 0, 1000:1512], ["x0"], "x1 cols1000-1512")
                dbg(x_res[:, 7, NXT - 512:NXT], ["x7"], "x1 last512 ch7")
                dbg(xc_res[:, 0, :], ["xc"], "xc1")
                dbg(modv[l].rearrange("p s k c -> p (s k c)"), ["modv"], "modv1")
                dbg(Etab[:, 0, :, :].rearrange("p i q -> p (i q)"), ["Etab"], "Etab1 h0")
            if NDBG == 40:
                for h_ in range(NH):
                    dbg(Etab[:, h_, :, :].rearrange("p i q -> p (i q)"), ["Etab"], "Etab%d" % h_)
            phase_A(c, 1, None)
            if not c["last"]:
                phase_B(c, 1, None)
            blocks = []
            lt, idx = c["KA"], 0
            while lt < c["KB"]:
                tm = c["TA"] <= lt < c["TB"]
                nt = 1
                if tm and lt % 2 == 0 and lt + 1 < c["TB"]:
                    nt = 2
                blocks.append(dict(lt0=lt, nt=nt, tm=tm, idx=idx))
                lt += nt
                idx += 1
            KA0 = c["KA"] - (c["KA"] % 2)
            prev, tmc = None, 0
            for b in blocks:
                if b["tm"]:
                    b["hs"] = tmc % 2
                    tmc += 1
                else:
                    b["hs"] = 2
                b["upos"] = ((b["lt0"] - KA0) * 128) % UR
                b["prev_hs"] = prev["hs"] if prev else None
                b["prev_n"] = prev["nt"] * 128 if prev else None
                b["mask"] = (b["lt0"] < 0) or (b["lt0"] + b["nt"] > OWN)
                b["need"] = min(b["lt0"] + b["nt"] - 1 + 2, c["KB"] - 1)
                prev = b
            pend = []
            for b in blocks:
                resident = XA <= b["lt0"] and b["lt0"] + b["nt"] <= XB
                if resident:
                    xcol = (b["lt0"] - XA) * 128
                    b["xsrc"] = x_res[:, :, xcol:xcol + b["nt"] * 128]
                    b["xres"] = lambda cc: ["x%d" % cc]
                else:
                    assert first_layer and b["nt"] == 1 and not b["tm"]
                    col = (b["lt0"] - XIA) * 128
                    A("sp", "dma_start", out=xk, in_=xT_d[:, :, col:col + 128].rearrange("c p t -> p c t"), writes=["xk"] + CACC, dma_key="xk")
                    b["xsrc"] = xk
                    b["xres"] = lambda cc: ["xk"] + CACC
                phase_A(c, 0, b)
                covered = b["lt0"] + b["nt"] - 1
                if b["tm"]:
                    pend.append(b)
                while pend and pend[0]["need"] <= covered:
                    phase_B(c, 0, pend.pop(0))
            assert not pend
            if NDBG == 61 and not first_layer:
                for q_ in range(4):
                    dbg(x_res[:, 0, 384 + q_ * 512:384 + (q_ + 1) * 512], ["x0"], "xmid own q%d" % q_)
            P.barrier()
            if not c["last"]:
                ffn_block(c, 1, 0, CTX, False)
            f0 = c["F0"] - XA * 128
            f1 = c["F1"] - XA * 128
            fprev["n"] = None
            t0 = f0
            while t0 < f1:
                n = min(FBLK, f1 - t0)
                lo_t = (t0 - 1) // 128 + XA
                hi_t = (t0 + n) // 128 + XA
                ffn_block(c, 0, t0, n, lo_t < 0 or hi_t >= OWN)
                t0 += n
            if NDBG == 61 and not first_layer:
                for q_ in range(4):
                    dbg(x_res[:, 0, 384 + q_ * 512:384 + (q_ + 1) * 512], ["x0"], "x2 own q%d" % q_)
            if c["last"]:
                final_out(c)
            P.barrier()
        outs = ["out0", "out1"]
        if not cfg["final"]:
            col0 = (0 - XA) * 128
            for cc in range(NCH):
                A("sp", "dma_start", out=outT_d[cc], in_=x_res[:, cc, col0:col0 + OWN * 128], reads=["x%d" % cc],
                    dma_key="out%d" % (cc % 2))
            A("sp", "dma_start", out=xcT_d.rearrange("c p t -> p c t"), in_=xc_res, reads=["xc"], dma_key="out0")
        P.emit(final_wait_keys=outs + (["dbg"] if dbgc["n"] else []))
    return nc


ROPE_THETA = 10000.0


def _rope_tables(g_tiles):
    half = DH // 2
    inv_freq = (ROPE_THETA ** (-np.arange(0, half, 2, dtype=np.float32) / half)).astype(np.float32)
    p = np.arange(128)
    d = p % 64
    f = d % 16
    first = (d % 32) < 16
    use_row = d < 32
    C = np.zeros((128, len(g_tiles) * 128), np.float32)
    S = np.zeros_like(C)
    i = np.arange(128)
    for k, g in enumerate(g_tiles):
        row = (2 * g + i // 64).astype(np.float32)
        col = (i % 64).astype(np.float32)
        pos = np.where(use_row[:, None], row[None, :], col[None, :]).astype(np.float32)
        ang = (pos * inv_freq[f][:, None]).astype(np.float32)
        C[:, k * 128:(k + 1) * 128] = np.cos(ang)
        sn = np.sin(ang)
        S[:, k * 128:(k + 1) * 128] = np.where(first[:, None], -sn, sn)
    return C, S


def _bias_table(rpb_l):
    p = np.arange(128)
    kr2 = p // 64
    kc = p % 64
    qc = np.arange(64)
    cs = np.clip(qc - 8, 0, 48)
    out = np.full((128, NH, 8, 64), NEG, np.float32)
    for ei in range(8):
        ci = ei - 2
        dr = -4 + 2 * ci + kr2
        dc = kc[:, None] - qc[None, :]
        ok = (kc[:, None] >= cs[None, :]) & (kc[:, None] < cs[None, :] + 16) & (np.abs(dr)[:, None] <= 7)
        dri = np.clip(dr + 7, 0, 14)
        dci = np.clip(dc + 15, 0, 30)
        vals = rpb_l[:, dri[:, None], dci]
        out[:, :, ei, :] = np.where(ok[:, None, :], vals.transpose(1, 0, 2), np.float32(NEG))
    return out.reshape(128, NH * 8 * 64)


def _row_masks(ci_core):
    p = np.arange(128)
    kr2 = p // 64
    rm = np.zeros((128, 48), np.float32)
    for sidx in range(8):
        if sidx < 4:
            r = sidx
            cis = range(0, 6)
        else:
            r = 28 + (sidx - 4)
            cis = range(-2, 4)
        R = ci_core * 32 + r
        rs = min(max(R - 4, 0), 120)
        for ii, ci in enumerate(cis):
            kr = R - 4 + 2 * ci + kr2
            rm[:, sidx * 6 + ii] = ((kr >= rs) & (kr < rs + 8)).astype(np.float32)
    return rm


def _pack_vec(inp, l):
    v = np.zeros((128, NV), np.float32)
    fm = lambda a: np.ascontiguousarray(np.asarray(a, np.float32).reshape(-1, 128).T)
    v[:, V_BADA:V_BADA + 48] = fm(inp["b_ada"][l])
    v[:, V_N1G:V_N1G + 8] = fm(inp["norm1_g"][l])
    v[:, V_N2G:V_N2G + 8] = fm(inp["norm2_g"][l])
    cdw = np.asarray(inp["conv_dw"][l], np.float32)
    for cc in range(4):
        v[:, V_CDW + cc * CK:V_CDW + (cc + 1) * CK] = cdw[:, cc * 128:(cc + 1) * 128].T
    v[:, V_CDB:V_CDB + 4] = fm(inp["conv_dw_b"][l])
    v[:, V_LNG:V_LNG + 4] = fm(inp["conv_ln_g"][l])
    v[:, V_LNB:V_LNB + 4] = fm(inp["conv_ln_b"][l])
    fdw = np.asarray(inp["ffn_dw"][l], np.float32)
    fdb = np.asarray(inp["ffn_dw_b"][l], np.float32)
    for j in range(NJ):
        for half in range(2):
            ch = 2 * j + half
            c0 = half * FFN + j * 128
            v[:, V_FDW + ch * 3:V_FDW + ch * 3 + 3] = fdw[:, c0:c0 + 128].T
            v[:, V_FDB + ch] = fdb[c0:c0 + 128]
    v[:, V_FNG:V_FNG + 8] = fm(inp["final_norm_g"])
    return v


def _weights_for_layer(inp, l):
    w_in = np.asarray(inp["w_in"][l], np.float32)
    d = np.arange(64)
    partner = np.where((d % 32) < 16, d + 16, d - 16)
    qcols = np.concatenate([1024 + h * 64 + partner for h in range(NH)])
    kcols = np.concatenate([1536 + h * 64 + partner for h in range(NH)])
    w_rot = np.ascontiguousarray(w_in[:, np.concatenate([qcols, kcols])])
    w_up = np.asarray(inp["w_up"][l], np.float32)
    perm = np.concatenate([np.concatenate([np.arange(j * 128, (j + 1) * 128), FFN + np.arange(j * 128, (j + 1) * 128)])
                           for j in range(NJ)])
    return {
        "w_in%d" % l: np.ascontiguousarray(w_in), "w_rot%d" % l: w_rot,
        "w_co%d" % l: np.ascontiguousarray(np.asarray(inp["w_conv_out"][l], np.float32)),
        "w_no%d" % l: np.ascontiguousarray(np.asarray(inp["w_na_out"][l], np.float32)),
        "w_out%d" % l: np.ascontiguousarray(np.asarray(inp["w_out"][l], np.float32)),
        "w_up%d" % l: np.ascontiguousarray(w_up[:, perm]),
        "w_down%d" % l: np.ascontiguousarray(np.asarray(inp["w_down"][l], np.float32)),
        "w_ada%d" % l: np.ascontiguousarray(np.asarray(inp["w_ada"][l], np.float32)),
        "vec%d" % l: _pack_vec(inp, l),
        "bias%d" % l: _bias_table(np.asarray(inp["na_rpb"][l], np.float32)),
    }


def _core_inputs(cfg, inp, x, ctx, shared):
    XIA, XIB = cfg["XIA"], cfg["XIB"]
    maps = []
    for core in range(8):
        b, ci = core // 4, core % 4
        tiles = [ci * OWN + lt for lt in range(XIA, XIB)]
        nit = len(tiles) * 128
        xr = np.zeros((nit, D), np.float32)
        tm = np.zeros((nit,), np.float32)
        for k, g in enumerate(tiles):
            if 0 <= g < SEQ // 128:
                xr[k * 128:(k + 1) * 128] = x[b, g * 128:(g + 1) * 128]
                tm[k * 128:(k + 1) * 128] = 1.0
        C, S = _rope_tables(tiles)
        cin = np.zeros((128, NCH * 2), np.float32)
        cin[:, 0::2] = np.asarray(inp["c"], np.float32)[b].reshape(NCH, 128).T
        cin[:, 1::2] = np.asarray(inp["c_ctx"], np.float32).reshape(NCH, 128).T
        m = {
            "xT": np.ascontiguousarray(xr.T.reshape(NCH, 128, nit)),
            "ctxT": np.ascontiguousarray(ctx[b].T.reshape(NCH, 128, CTX)),
            "cin": cin,
            "tokm": np.ascontiguousarray(np.broadcast_to(tm[None, :], (128, nit))),
            "ropeC": C, "ropeS": S,
            "rm": _row_masks(ci),
        }
        m.update(shared)
        maps.append(m)
    return maps


_NC_CACHE = {}
MODE = "fused"


def _run(mode, inp, x, ctx):
    cfg = make_cfg(mode)
    if mode not in _NC_CACHE:
        _NC_CACHE[mode] = build(cfg)
    nc = _NC_CACHE[mode]
    shared = {}
    for c in cfg["layers"]:
        shared.update(_weights_for_layer(inp, c["l"]))
    maps = _core_inputs(cfg, inp, x, ctx, shared)
    res = run_bass_kernel_spmd(nc, maps, core_ids=list(range(8)))
    if cfg.get("ndbg"):
        np.save("_dbg.npy", np.asarray(res.results[0]["dbg"]))
    xo = np.zeros((2, SEQ, D), np.float32)
    xc = None if cfg["final"] else np.zeros((2, CTX, D), np.float32)
    for core in range(8):
        b, ci = core // 4, core % 4
        r = res.results[core]
        xo[b, ci * OWN * 128:(ci + 1) * OWN * 128] = np.asarray(r["outT"]).reshape(D, OWN * 128).T
        if xc is not None:
            xc[b] = np.asarray(r["xcT"]).reshape(D, CTX).T
    return xo, xc


def kernel(**inputs):
    x = np.asarray(inputs["x"], np.float32)
    ctx = np.asarray(inputs["ctx"], np.float32)
    if MODE == "fused":
        out, _ = _run("fused", inputs, x, ctx)
        return out
    x1, xc1 = _run("l0", inputs, x, ctx)
    out, _ = _run("l1", inputs, x1, xc1)
    return out
```

```python
import contextlib
import numpy as np
import concourse.bass as bass
import concourse.mybir as mybir
from concourse.bass_utils import run_bass_kernel_spmd

F32 = mybir.dt.float32
BF16 = mybir.dt.bfloat16
AF = mybir.ActivationFunctionType
ALU = mybir.AluOpType

D = 1024
NCH = 8
SEQ = 8192
GW = 64
NH = 8
DH = 64
CDIM = 512
CK = 31
FFN = 2816
NJ = 22
CTX = 256
IN_DIM = 4608
EPS = 1e-6
NEG = -30000.0
OWN = 16
TPB = 2
RING = 6
URING = 3
NWS = 4
WSW = 1024
FBLK = 510
POOL_CONV_CHUNKS = 0

V_BADA = 0
V_N1G = 48
V_N2G = 56
V_CDW = 64
V_CDB = V_CDW + 4 * CK
V_LNG = V_CDB + 4
V_LNB = V_LNG + 4
V_FDW = V_LNB + 4
V_FDB = V_FDW + 44 * 3
V_FNG = V_FDB + 44
NV = V_FNG + 8

ENGS = ("pe", "act", "dve", "pool", "sp")


class _Op:
    __slots__ = ("eng", "fn", "deps", "signal", "ticket", "dma_key", "idx")


class Prog:
    def __init__(self, nc):
        self.nc = nc
        self.ops = []
        self.last_w = {}
        self.readers = {}
        self.barrier_deps = set()

    def add(self, eng, fn, reads=(), writes=(), dma_key=None):
        op = _Op()
        op.eng, op.fn, op.dma_key = eng, fn, dma_key
        op.signal, op.ticket = False, None
        op.idx = len(self.ops)
        deps = set(self.barrier_deps)
        for r in reads:
            w = self.last_w.get(r)
            if w is not None:
                deps.add(w)
        for w_ in writes:
            w = self.last_w.get(w_)
            if w is not None:
                deps.add(w)
            deps.update(self.readers.get(w_, ()))
        if dma_key is not None:
            k = ("__dk", dma_key)
            w = self.last_w.get(k)
            if w is not None:
                deps.add(w)
            self.last_w[k] = op.idx
        for r in reads:
            self.readers.setdefault(r, []).append(op.idx)
        for w_ in writes:
            self.last_w[w_] = op.idx
            self.readers[w_] = []
        fin = set()
        for d in deps:
            dop = self.ops[d]
            if eng == "pe" and dop.eng == "pe" and dop.dma_key is None and dma_key is None:
                continue
            fin.add(d)
        op.deps = fin
        self.ops.append(op)
        return op.idx

    def barrier(self):
        last = {}
        for op in self.ops:
            key = ("d", op.dma_key) if op.dma_key is not None else ("e", op.eng)
            last[key] = op.idx
        self.barrier_deps = set(last.values())

    def emit(self, final_wait_keys=()):
        nc, ops = self.nc, self.ops
        for op in ops:
            for d in op.deps:
                ops[d].signal = True
        dma_keys = []
        seen = set()
        for op in ops:
            if op.dma_key is not None and op.dma_key not in seen:
                seen.add(op.dma_key)
                dma_keys.append(op.dma_key)
        cnt = {e: 0 for e in ENGS}
        dcnt = {k: 0 for k in dma_keys}
        for op in ops:
            if op.dma_key is not None:
                dcnt[op.dma_key] += 16
                op.ticket = ("d", op.dma_key, dcnt[op.dma_key])
            elif op.signal:
                cnt[op.eng] += 1
                op.ticket = ("e", op.eng, cnt[op.eng])
        per_eng = {e: [op for op in ops if op.eng == e] for e in ENGS}
        with contextlib.ExitStack() as st:
            esem = {e: st.enter_context(nc.semaphore("s_" + e)) for e in ENGS}
            dsem = {k: st.enter_context(nc.semaphore("d_%d" % i)) for i, k in enumerate(dma_keys)}
            block = st.enter_context(nc.Block())

            def run(name, e):
                waited = {}
                for op in per_eng[name]:
                    need = {}
                    for d in op.deps:
                        t = ops[d].ticket
                        key = (t[0], t[1])
                        if waited.get(key, 0) >= t[2]:
                            continue
                        if need.get(key, 0) < t[2]:
                            need[key] = t[2]
                    for key, v in need.items():
                        e.wait_ge(esem[key[1]] if key[0] == "e" else dsem[key[1]], v)
                        waited[key] = v
                    ins = op.fn(e)
                    if op.dma_key is not None:
                        ins.then_inc(dsem[op.dma_key], 16)
                    elif op.signal:
                        ins.then_inc(esem[name], 1)
                if name == "sp":
                    for k in final_wait_keys:
                        e.wait_ge(dsem[k], dcnt[k])

            block.tensor(lambda e: run("pe", e))
            block.scalar(lambda e: run("act", e))
            block.vector(lambda e: run("dve", e))
            block.gpsimd(lambda e: run("pool", e))
            block.sync(lambda e: run("sp", e))


def layer_cfg(big, l, last):
    if big:
        return dict(l=l, last=last, KA=-5, KB=20, TA=-3, TB=19, F0=-3 * 128 + 64, F1=18 * 128)
    return dict(l=l, last=last, KA=-3, KB=18, TA=-1, TB=17, F0=0, F1=OWN * 128)


def make_cfg(mode):
    if mode == "fused":
        layers = [layer_cfg(True, 0, False), layer_cfg(False, 1, True)]
    elif mode == "l0":
        layers = [layer_cfg(False, 0, False)]
    else:
        layers = [layer_cfg(False, 1, True)]
    XA = min(c["TA"] for c in layers)
    XB = max(c["TB"] for c in layers)
    XIA = layers[0]["KA"]
    XIB = layers[0]["KB"]
    import os
    return dict(mode=mode, layers=layers, XA=XA, XB=XB, XIA=XIA, XIB=XIB, final=layers[-1]["last"],
                ndbg=int(os.environ.get("KDBG", "0")))


def build(cfg):
    nc = bass.Bass("TRN2", target_bir_lowering=False)
    layers = cfg["layers"]
    XA, XB, XIA, XIB = cfg["XA"], cfg["XB"], cfg["XIA"], cfg["XIB"]
    NXT = (XB - XA) * 128
    NIT = (XIB - XIA) * 128

    def din(name, shape, dt=F32):
        return nc.dram_tensor(name, list(shape), dt, kind="ExternalInput").ap()

    xT_d = din("xT", [NCH, 128, NIT])
    ctxT_d = din("ctxT", [NCH, 128, CTX])
    cin_d = din("cin", [128, NCH * 2])
    tokm_d = din("tokm", [128, NIT])
    ropeC_d = din("ropeC", [128, NIT])
    ropeS_d = din("ropeS", [128, NIT])
    rm_d = din("rm", [128, 48])
    W = {}
    for c in layers:
        l = c["l"]
        W[l] = dict(
            w_in=din("w_in%d" % l, [D, IN_DIM]), w_rot=din("w_rot%d" % l, [D, 1024]),
            w_co=din("w_co%d" % l, [CDIM, D]), w_no=din("w_no%d" % l, [CDIM, D]),
            w_out=din("w_out%d" % l, [D, D]), w_up=din("w_up%d" % l, [D, 2 * FFN]),
            w_down=din("w_down%d" % l, [FFN, D]), w_ada=din("w_ada%d" % l, [D, 6 * D]),
            vec=din("vec%d" % l, [128, NV]), bias=din("bias%d" % l, [128, NH * 8 * 64]))
        for k in ("w_in", "w_rot", "w_co", "w_no", "w_out", "w_up", "w_down"):
            W[l][k + "_b"] = nc.dram_tensor("%s%d_bf" % (k, l), list(W[l][k].shape), BF16).ap()
    outT_d = nc.dram_tensor("outT", [NCH, 128, OWN * 128], F32, kind="ExternalOutput").ap()
    xcT_d = None
    if not cfg["final"]:
        xcT_d = nc.dram_tensor("xcT", [NCH, 128, CTX], F32, kind="ExternalOutput").ap()

    NDBG = cfg.get("ndbg", 0)
    dbg_d = nc.dram_tensor("dbg", [max(NDBG, 1), 128, 512], F32, kind="ExternalOutput").ap() if NDBG else None
    dbgc = {"n": 0}
    st = contextlib.ExitStack()
    with st:
        ASZ = 53200
        arena_t = st.enter_context(nc.sbuf_tensor("arena", [128, ASZ], F32))
        psum_t = st.enter_context(nc.psum_tensor("psum", [128, 4096], F32))
        AR = {"top": 0}

        def af(n):
            o = AR["top"]
            AR["top"] += n
            assert AR["top"] <= ASZ, ("SBUF arena overflow", AR["top"])
            return arena_t[:, o:o + n]

        def ab(n):
            return af((n + 1) // 2).bitcast(BF16)

        P = Prog(nc)
        add = P.add

        def A(eng, method, *args, reads=(), writes=(), dma_key=None, **kw):
            return add(eng, lambda e: getattr(e, method)(*args, **kw), reads=reads, writes=writes, dma_key=dma_key)

        def bank(k):
            return psum_t[:, 512 * k:512 * (k + 1)]

        def dbg(ap, reads, label):
            if not NDBG or dbgc["n"] >= NDBG:
                return
            i = dbgc["n"]
            dbgc["n"] += 1
            pr, w = ap.shape[0], ap.shape[1]
            print("DBG", i, label, ap.shape)
            A("pool", "dma_start", out=dbg_d[i, 0:pr, 0:w], in_=ap, reads=reads, dma_key="dbg")

        x_res = af(NCH * NXT).rearrange("p (c t) -> p c t", c=NCH)
        xc_res = af(NCH * CTX).rearrange("p (c t) -> p c t", c=NCH)
        wslot = [af(WSW) for _ in range(NWS)]
        ones_f = af(128)
        ident_f = af(128)
        ident = ab(128)
        epsT = af(1)
        sT = af(NCH * 2)
        cinT = af(NCH * 2)
        rmT = af(48)
        vecT = {c["l"]: af(NV) for c in layers}
        modT = {c["l"]: af(96).rearrange("p (c s) -> p c s", s=2) for c in layers}
        modv = {c["l"]: af(2 * 6 * 8).rearrange("p (s k c) -> p s k c", s=2, k=6) for c in layers}
        Etab = ab(NH * 8 * 64).rearrange("p (h i q) -> p h i q", h=NH, i=8)
        KcT = ab(4 * CTX).rearrange("p (c t) -> p c t", c=4)
        Vc = ab(2 * NH * 65).rearrange("p (t h d) -> p t h d", t=2, h=NH)
        mark_persist = AR["top"]

        HW_ = 64 + TPB * 128
        hbuf = [ab(NCH * HW_).rearrange("p (c t) -> p c t", c=NCH) for _ in range(2)]
        hbuf.append(ab(NCH * (64 + 128)).rearrange("p (c t) -> p c t", c=NCH))
        KW_ = 64 + RING * 128
        Kring = ab(4 * KW_).rearrange("p (c t) -> p c t", c=4)
        Vev = ab(RING * NH * 65).rearrange("p (t h d) -> p t h d", t=RING, h=NH)
        Vod = ab(RING * NH * 65).rearrange("p (t h d) -> p t h d", t=RING, h=NH)
        UW_ = 16 + URING * TPB * 128 + 16
        uring = ab(4 * UW_).rearrange("p (c t) -> p c t", c=4)
        ucx = ab(4 * (16 + CTX + 16)).rearrange("p (c t) -> p c t", c=4)
        NB = TPB * 128
        sq = [af(NB) for _ in range(2)]
        rstd = af(NB)
        tt = [af(NB) for _ in range(2)]
        rC = [af(NB) for _ in range(2)] + [af(128)]
        rS = [af(NB) for _ in range(2)] + [af(128)]
        tkm = af(NB)
        r1 = [af(NB) for _ in range(2)]
        QT = ab(4 * NB).rearrange("p (c t) -> p c t", c=4)
        Pb = [ab(512) for _ in range(3)]
        Otok = [ab(512) for _ in range(2)]
        rcp = [af(8) for _ in range(2)]
        OT = ab(4 * NB).rearrange("p (c t) -> p c t", c=4)
        cacc_raw = af(4 * NB)
        cacc = cacc_raw.rearrange("p (c t) -> p c t", c=4)
        xk = cacc_raw.rearrange("p (c t) -> p c t", c=NCH)
        stg = cacc_raw[:, 0:512]
        CACC = ["cacc0", "cacc1", "cacc2", "cacc3"]
        lsq = sq
        lmu = rstd
        lrs = af(NB)
        cT = ab(4 * NB).rearrange("p (c t) -> p c t", c=4)
        dg = ab(4 * 128).rearrange("p (h j c) -> p h j c", h=2, j=2)
        gt = [ab(NB) for _ in range(2)]
        m12 = [af(NB) for _ in range(2)]
        sig = m12
        mrg = ab(NCH * NB).rearrange("p (c t) -> p c t", c=NCH)
        mark_tm = AR["top"]

        AR["top"] = mark_persist
        FW_ = FBLK + 2
        h2 = ab(NCH * FW_).rearrange("p (c t) -> p c t", c=NCH)
        hid = ab(NJ * FBLK).rearrange("p (c t) -> p c t", c=NJ)
        fsq = [af(FW_) for _ in range(2)]
        frs = af(FW_)
        ftt = [af(FW_) for _ in range(2)]
        fta = [af(FBLK) for _ in range(2)]
        ftg = [af(FBLK) for _ in range(2)]
        fsg = [af(FBLK) for _ in range(2)]
        ftm = af(FW_)
        hstash = ab(16).rearrange("p (c t) -> p c t", c=NCH)
        fo = [af(512) for _ in range(2)]
        mark_ffn = AR["top"]
        AR["top"] = max(mark_tm, mark_ffn)
        print("ARENA words: persist", mark_persist, "tm", mark_tm, "ffn", mark_ffn, "of", ASZ, "(%.1f KB)" % (AR["top"] * 4 / 1024))

        wctr = {"n": 0}

        def wload(dram_ap, view_fn, dt_bf=True, reads=()):
            s = wctr["n"] % NWS
            wctr["n"] += 1
            raw = wslot[s]
            v = view_fn(raw.bitcast(BF16) if dt_bf else raw)
            res = "ws%d" % s
            A("sp", "dma_start", out=v, in_=dram_ap, reads=list(reads), writes=[res], dma_key=res)
            return v, res

        def kgroup(wb, c0, ncols, kchunks):
            src = wb[:, c0:c0 + ncols].rearrange("(kc p) n -> p kc n", p=128)
            return wload(src, lambda raw: raw[:, 0:kchunks * ncols].rearrange("p (kc n) -> p kc n", kc=kchunks),
                         reads=CASTRES[id(wb)])

        if NDBG == 20:
            pass
            dbg(Kring.rearrange("p c t -> p (c t)")[:, 0:512], ["ARENA0"], "Kring")
            dbg(Vod.rearrange("p t h d -> p (t h d)")[:, 0:512], ["ARENA0"], "Vod")
            dbg(Pb[2][:, 0:512], ["ARENA0"], "Pb2")
            dbg(Otok[1][:, 0:512], ["ARENA0"], "Otok1")
            dbg(mrg.rearrange("p c t -> p (c t)")[:, 0:512], ["ARENA0"], "mrg")
            dbg(stg[:, 0:512], ["ARENA0"], "stg")
            dbg(x_res[:, 0, 0:512], ["ARENA0"], "xres(poison expected)")
        A("pool", "memset", ones_f, 1.0, writes=["ones"])
        A("pool", "memset", epsT, EPS, writes=["eps"])
        A("pool", "memset", ident_f, 0.0, writes=["identf"])
        A("pool", "affine_select", out=ident_f, in_=ident_f, pattern=[[-1, 128]], compare_op=ALU.not_equal,
                                              fill=1.0, base=0, channel_multiplier=1, reads=["identf"], writes=["identf"])
        A("dve", "tensor_copy", out=ident, in_=ident_f, reads=["identf"], writes=["ident"])
        A("pool", "memset", Vc[:, :, :, 64:65], 1.0, writes=["Vc"])
        CASTRES = {}

        def emit_casts(l, keys):
            for k in keys:
                src, dst = W[l][k], W[l][k + "_b"]
                rows = src.shape[0]
                step = 512
                lst = []
                for r0 in range(0, rows, step):
                    r1_ = min(rows, r0 + step)
                    res = "cast_%s%d_%d" % (k, l, r0 // step)
                    A("pool", "dma_start", out=dst[r0:r1_, :], in_=src[r0:r1_, :], writes=[res], dma_key=res)
                    lst.append(res)
                CASTRES[id(dst)] = lst
        TMK = ("w_in", "w_rot", "w_co", "w_no", "w_out")
        FFK = ("w_up", "w_down")
        emit_casts(layers[0]["l"], TMK)

        A("sp", "dma_start", out=cinT, in_=cin_d, writes=["cin"], dma_key="misc")
        A("sp", "dma_start", out=rmT, in_=rm_d, writes=["rm"], dma_key="misc")
        for c in layers:
            l = c["l"]
            A("sp", "dma_start", out=vecT[l], in_=W[l]["vec"], writes=["vec%d" % l], dma_key="misc")
        for ch in range(NCH):
            A("sp", "dma_start", out=x_res[:, ch, :], in_=xT_d[ch, :, (XA - XIA) * 128:(XB - XIA) * 128],
                writes=["x%d" % ch], dma_key="xin%d" % (ch % 2))
        A("sp", "dma_start", out=xc_res, in_=ctxT_d.rearrange("c p t -> p c t"), writes=["xc"], dma_key="misc")
        A("act", "activation", out=sT, in_=cinT, func=AF.Silu, reads=["cin"], writes=["sT"])

        def emit_mod(l):
            wa = W[l]["w_ada"]
            pm = bank(7)
            for oc in range(48):
                src = wa[:, oc * 128:(oc + 1) * 128].rearrange("(kc p) n -> p kc n", p=128)
                v, res = wload(src, lambda raw: raw[:, 0:1024].rearrange("p (kc n) -> p kc n", kc=8), dt_bf=False)
                for kc in range(NCH):
                    A("pe", "matmul", pm[:, oc * 2:oc * 2 + 2], lhsT=v[:, kc, :],
                                                                    rhs=sT[:, kc * 2:kc * 2 + 2], start=(kc == 0), stop=(kc == 7),
                        reads=[res, "sT"], writes=["B7"])
            pmv = pm[:, 0:96].rearrange("p (c s) -> p c s", s=2)
            for s in range(2):
                A("dve", "tensor_tensor", out=modT[l][:, :, s], in0=pmv[:, :, s], in1=vecT[l][:, V_BADA:V_BADA + 48],
                                                          op=ALU.add, reads=["B7", "vec%d" % l], writes=["modT%d" % l])
            for s in range(2):
                m = modT[l]
                A("dve", "scalar_tensor_tensor", out=modv[l][:, s, 0, :], in0=m[:, 8:16, s], scalar=1.0,
                                                                      in1=vecT[l][:, V_N1G:V_N1G + 8], op0=ALU.add, op1=ALU.mult,
                    reads=["modT%d" % l, "vec%d" % l], writes=["modv%d" % l])
                A("dve", "scalar_tensor_tensor", out=modv[l][:, s, 3, :], in0=m[:, 32:40, s], scalar=1.0,
                                                                      in1=vecT[l][:, V_N2G:V_N2G + 8], op0=ALU.add, op1=ALU.mult,
                    reads=["modT%d" % l, "vec%d" % l], writes=["modv%d" % l])
                for kind, c0 in ((1, 0), (2, 16), (4, 24), (5, 40)):
                    A("dve", "tensor_copy", out=modv[l][:, s, kind, :], in_=m[:, c0:c0 + 8, s],
                        reads=["modT%d" % l], writes=["modv%d" % l])

        def emit_etab(l):
            bsrc = W[l]["bias"].rearrange("p (h n) -> p h n", h=NH)
            for h in range(NH):
                A("sp", "dma_start", out=stg, in_=bsrc[:, h, :], writes=["stg"] + CACC, dma_key="stg")
                A("act", "activation", out=Etab[:, h, :, :].rearrange("p i q -> p (i q)"), in_=stg, func=AF.Exp,
                    reads=["stg"] + CACC, writes=["Etab"])

        def emit_norm_mod(xsrc, n, mv, kinds, sqb, rsb, ttb, out_fn, psb, xres, tag):
            ps = bank(psb)[:, 0:n]
            for c in range(NCH):
                b = sqb[c % 2][:, 0:n]
                A("act", "activation", out=b, in_=xsrc[:, c, :], func=AF.Square,
                    reads=xres(c), writes=[tag + "sq%d" % (c % 2)])
                A("pe", "matmul", ps, lhsT=ones_f, rhs=b, start=(c == 0), stop=(c == 7),
                    reads=[tag + "sq%d" % (c % 2), "ones"], writes=["B%d" % psb])
            rs = rsb[:, 0:n]
            A("act", "activation", out=rs, in_=ps, func=AF.Sqrt, bias=epsT, scale=1.0 / D,
                reads=["B%d" % psb, "eps"], writes=[tag + "rs"])
            A("dve", "reciprocal", out=rs, in_=rs, reads=[tag + "rs"], writes=[tag + "rs"])
            for c in range(NCH):
                t = ttb[c % 2][:, 0:n]
                A("dve", "tensor_tensor", out=t, in0=xsrc[:, c, :], in1=rs, op=ALU.mult,
                    reads=xres(c) + [tag + "rs"], writes=[tag + "tt%d" % (c % 2)])
                o, ores = out_fn(c)
                A("act", "activation", out=o, in_=t, func=AF.Identity, bias=mv[:, kinds[1], c:c + 1],
                                                                 scale=mv[:, kinds[0], c:c + 1],
                    reads=[tag + "tt%d" % (c % 2), "modv"], writes=ores)

        gctr = {"n": 0}

        def gslot(n):
            k = (5, 6, 0, 1, 2)[gctr["n"] % 5]
            gctr["n"] += 1
            return bank(k)[:, 0:n], "B%d" % k

        def proj(wview, wres, col0, ncol, rhs_fn, nk, n, rres):
            ps, pres = gslot(n)
            for k in range(nk):
                A("pe", "matmul", ps[0:ncol, :], lhsT=wview[:, k, col0:col0 + ncol], rhs=rhs_fn(k),
                                                  start=(k == 0), stop=(k == nk - 1),
                    reads=[wres] + rres, writes=[pres])
            return ps, pres

        def phase_A(c, stream, blk):
            l = c["l"]
            mv = modv[l][:, stream]
            lat = stream == 0
            if lat:
                lt0, nt, hs, xsrc, xres, need_mask = blk["lt0"], blk["nt"], blk["hs"], blk["xsrc"], blk["xres"], blk["mask"]
                n = nt * 128
                hb = hbuf[hs]
                hview = hb[:, :, 64:64 + n]
                if blk["prev_hs"] is not None:
                    pb = hbuf[blk["prev_hs"]]
                    pn = blk["prev_n"]
                    A("pool", "tensor_copy", out=hb[:, :, 0:64], in_=pb[:, :, 64 + pn - 64:64 + pn],
                        reads=["h%d" % blk["prev_hs"]], writes=["h%d" % hs])
                hres = "h%d" % hs
            else:
                n = CTX
                xsrc, xres = xc_res, (lambda cc: ["xc"])
                hb = hbuf[0]
                hview = hb[:, :, 64:64 + n]
                hres = "h0"
                need_mask = False
            emit_norm_mod(xsrc, n, mv, (0, 1), sq, rstd, tt, lambda cc: (hview[:, cc, :], [hres]), 7, xres, "A")
            wb = W[l]
            if lat:
                col = (lt0 - XIA) * 128
                bi = blk["hs"]
                A("sp", "dma_start", out=rC[bi][:, 0:n], in_=ropeC_d[:, col:col + n], writes=["rC%d" % bi], dma_key="rC%d" % bi)
                A("sp", "dma_start", out=rS[bi][:, 0:n], in_=ropeS_d[:, col:col + n], writes=["rS%d" % bi], dma_key="rS%d" % bi)
                if need_mask:
                    A("sp", "dma_start", out=tkm[:, 0:n], in_=tokm_d[:, col:col + n], writes=["tkm"], dma_key="tkm")
            for g in range(2):
                wv, wr = kgroup(wb["w_in_b"], 1536 + g * 256, 256, 8)
                if lat:
                    wv2, wr2 = kgroup(wb["w_rot_b"], 512 + g * 256, 256, 8)
                for j in range(2):
                    hp = g * 2 + j
                    ps, pres = proj(wv, wr, j * 128, 128, lambda k: hview[:, k, :], 8, n, [hres])
                    if lat:
                        ps2, pres2 = proj(wv2, wr2, j * 128, 128, lambda k: hview[:, k, :], 8, n, [hres])
                        a, b = r1[0][:, 0:n], r1[1][:, 0:n]
                        A("dve", "tensor_tensor", out=a, in0=ps, in1=rC[bi][:, 0:n], op=ALU.mult,
                            reads=[pres, "rC%d" % bi], writes=["r1a"])
                        A("dve", "tensor_tensor", out=b, in0=ps2, in1=rS[bi][:, 0:n], op=ALU.mult,
                            reads=[pres2, "rS%d" % bi], writes=["r1b"])
                        for t in range(nt):
                            sl = (lt0 + t - c["KA"]) % RING
                            A("dve", "tensor_tensor",
                                out=Kring[:, hp, 64 + sl * 128:64 + (sl + 1) * 128], in0=a[:, t * 128:(t + 1) * 128],
                                in1=b[:, t * 128:(t + 1) * 128], op=ALU.add, reads=["r1a", "r1b"], writes=["K%d" % sl])
                            if sl == RING - 1:
                                A("pool", "tensor_copy", out=Kring[:, hp, 0:64],
                                                                                 in_=Kring[:, hp, 64 + sl * 128 + 64:64 + (sl + 1) * 128],
                                    reads=["K%d" % sl], writes=["Kmar"])
                    else:
                        A("act", "copy", out=KcT[:, hp, :], in_=ps, reads=[pres], writes=["KcT"])
            wv0, wr0 = kgroup(wb["w_in_b"], 2048, 256, 8)
            wv1, wr1 = kgroup(wb["w_in_b"], 2304, 256, 8)

            def vtile(col_lo, dst, dres):
                for half, (wv, wr) in enumerate(((wv0, wr0), (wv1, wr1))):
                    ps, pres = gslot(256)
                    for k in range(NCH):
                        A("pe", "matmul", ps, lhsT=hb[:, k, col_lo:col_lo + 128], rhs=wv[:, k, :],
                                                                        start=(k == 0), stop=(k == 7),
                            reads=[wr, hres], writes=[pres])
                    A("act", "copy", out=dst[:, half * 4:(half + 1) * 4, 0:64],
                                                                  in_=ps.rearrange("p (h d) -> p h d", h=4),
                        reads=[pres], writes=[dres])
            if lat:
                for t in range(nt):
                    lt = lt0 + t
                    sl = (lt - c["KA"]) % RING
                    vtile(64 + t * 128, Vev[:, sl], "Ve%d" % sl)
                    if lt - 1 >= c["KA"]:
                        so = (lt - 1 - c["KA"]) % RING
                        vtile(64 + t * 128 - 64, Vod[:, so], "Vo%d" % so)
            else:
                for t in range(2):
                    vtile(64 + t * 128, Vc[:, t], "Vc")
            if lat or not c["last"]:
                for g in range(2):
                    wa_, wra = kgroup(wb["w_in_b"], g * 256, 256, 8)
                    wg_, wrg = kgroup(wb["w_in_b"], 512 + g * 256, 256, 8)
                    for j in range(2):
                        cc = g * 2 + j
                        pa, pra = proj(wa_, wra, j * 128, 128, lambda k: hview[:, k, :], 8, n, [hres])
                        pg, prg = proj(wg_, wrg, j * 128, 128, lambda k: hview[:, k, :], 8, n, [hres])
                        sg = sig[cc % 2][:, 0:n]
                        A("act", "activation", out=sg, in_=pg, func=AF.Sigmoid, reads=[prg],
                            writes=["m%d" % (cc % 2)])
                        if need_mask:
                            A("pool", "tensor_tensor", out=sg, in0=sg, in1=tkm[:, 0:n], op=ALU.mult,
                                reads=["m%d" % (cc % 2), "tkm"], writes=["m%d" % (cc % 2)])
                        if lat:
                            up = blk["upos"]
                            dst = uring[:, cc, 16 + up:16 + up + n]
                            ures = ["u%d" % (up // 128 + q_) for q_ in range(nt)]
                        else:
                            dst = ucx[:, cc, 16:16 + CTX]
                            ures = ["ucx"]
                        A("dve", "tensor_tensor", out=dst, in0=pa, in1=sg, op=ALU.mult,
                            reads=[pra, "m%d" % (cc % 2)], writes=ures)
                if lat:
                    up = blk["upos"]
                    URT = URING * NB
                    if up + n == URT:
                        A("pool", "tensor_copy", out=uring[:, :, 0:16], in_=uring[:, :, 16 + URT - 16:16 + URT],
                            reads=["u%d" % (URT // 128 - 1)], writes=["umarF"])
                    if up == 0:
                        A("pool", "tensor_copy", out=uring[:, :, 16 + URT:16 + URT + 16], in_=uring[:, :, 16:32],
                            reads=["u0"], writes=["umarB"])

        sctr = {"n": 0}
        actr = {"n": 0}
        dscr = af(512) if NDBG == 30 else None

        def attention(c, qT_fn, nq, chunks, ores, out_rows, fill=None):
            import os
            SER = ["ATTSER"] if os.environ.get("KSER") else []
            actr["n"] += 1
            DBGA = NDBG == 30 and actr["n"] == 1
            nch = len(chunks)
            width = nch * nq
            obank = psum_t[:, 3 * 512:5 * 512]
            ov = obank[0:nq, :].rearrange("p (h d) -> p h d", h=NH)
            pvq = []
            for h in range(NH):
                sb = sctr["n"] % 3
                sctr["n"] += 1
                sps = bank(sb)[:, 0:width]
                pb = Pb[sb][:, 0:width]
                hp, po = h // 2, (h % 2) * 64
                for i, ch in enumerate(chunks):
                    A("pe", "matmul",
                        sps[:, i * nq:(i + 1) * nq], lhsT=ch["k"](hp, po), rhs=qT_fn(hp, po), start=True, stop=True,
                        reads=ch["kr"] + ["QT"], writes=["B%d" % sb] + SER)
                A("act", "activation", out=pb, in_=sps, func=AF.Exp, scale=DH ** -0.5,
                    reads=["B%d" % sb], writes=["P%d" % sb] + SER)
                if DBGA and h < 4:
                    dbg(pb, ["P%d" % sb], "Pexp h%d" % h)
                i = 0
                while i < nch:
                    ch = chunks[i]
                    if ch["e"] is None:
                        i += 1
                        continue
                    if ch["rm"] is None:
                        j = i
                        while j + 1 < nch and chunks[j + 1]["e"] is not None and chunks[j + 1]["rm"] is None \
                                and chunks[j + 1]["ei"] == chunks[j]["ei"] + 1:
                            j += 1
                        e0 = ch["ei"]
                        ev = Etab[:, h, e0:e0 + (j - i + 1), :].rearrange("p i q -> p (i q)")
                        A("dve", "tensor_tensor", out=pb[:, i * nq:(j + 1) * nq],
                                                                                   in0=pb[:, i * nq:(j + 1) * nq], in1=ev, op=ALU.mult,
                            reads=["P%d" % sb, "Etab"], writes=["P%d" % sb] + SER)
                        i = j + 1
                    else:
                        A("dve", "scalar_tensor_tensor",
                            out=pb[:, i * nq:(i + 1) * nq], in0=pb[:, i * nq:(i + 1) * nq], scalar=ch["rm"],
                            in1=Etab[:, h, ch["ei"], :], op0=ALU.mult, op1=ALU.mult,
                            reads=["P%d" % sb, "Etab", "rm"], writes=["P%d" % sb] + SER)
                        i += 1
                if fill is not None:
                    fill["f"](fill["k"])

                def _pv(h=h, pb=pb, sb=sb):
                    for i, ch in enumerate(chunks):
                        A("pe", "matmul", ov[:, h, 0:65], lhsT=pb[:, i * nq:(i + 1) * nq],
                          rhs=ch["v"](h), start=(i == 0), stop=(i == nch - 1),
                          reads=["P%d" % sb] + ch["vr"], writes=["OB"] + SER)
                if pvq:
                    pvq.pop(0)()
                pvq.append(_pv)
            while pvq:
                pvq.pop(0)()
            ob = out_rows["otok"]
            rc = rcp[ob]
            A("dve", "reciprocal", out=rc[0:nq, :], in_=ov[:, :, 64], reads=["OB"], writes=["rcp%d" % ob] + SER)
            ot = Otok[ob][0:nq, :].rearrange("p (h d) -> p h d", h=NH)
            A("dve", "tensor_tensor", out=ot, in0=ov[:, :, 0:64], in1=rc[0:nq, :].unsqueeze(2).broadcast_to([nq, NH, 64]),
                                                 op=ALU.mult, reads=["OB", "rcp%d" % ob], writes=["Otok%d" % ob] + SER)
            if DBGA:
                dbg(rc[0:nq, :], ["rcp%d" % ob], "rc")
                dbg(Otok[ob][0:nq, :], ["Otok%d" % ob], "Otok")
            tp = bank(7).bitcast(BF16)
            for f in range(4):
                A("pe", "transpose", out=tp[:, f * nq:(f + 1) * nq], in_=Otok[ob][0:nq, f * 128:(f + 1) * 128],
                                                     identity=ident[0:nq, 0:nq], reads=["Otok%d" % ob, "ident"], writes=["B7"] + SER)
            c0 = out_rows["col"]
            A("act", "copy", out=OT[:, :, c0:c0 + nq], in_=tp[:, 0:4 * nq].rearrange("p (f q) -> p f q", f=4),
                reads=["B7"], writes=[ores] + SER)

        def phase_B(c, stream, blk):
            l = c["l"]
            wb = W[l]
            mv = modv[l][:, stream]
            lat = stream == 0
            if lat:
                lt0, nt, hs = blk["lt0"], blk["nt"], blk["hs"]
                n = nt * 128
                hview = hbuf[hs][:, :, 64:64 + n]
                hres = "h%d" % hs
                bi = blk["hs"]
                xcol = (lt0 - XA) * 128
                xv = x_res[:, :, xcol:xcol + n]
                xres = lambda cc: ["x%d" % cc]
            else:
                n = CTX
                hview = hbuf[0][:, :, 64:64 + n]
                hres = "h0"
                xv = xc_res
                xres = lambda cc: ["xc"]
            for g in range(2):
                wv, wr = kgroup(wb["w_in_b"], 1024 + g * 256, 256, 8)
                if lat:
                    wv2, wr2 = kgroup(wb["w_rot_b"], g * 256, 256, 8)
                for j in range(2):
                    hp = g * 2 + j
                    ps, pres = proj(wv, wr, j * 128, 128, lambda k: hview[:, k, :], 8, n, [hres])
                    if lat:
                        ps2, pres2 = proj(wv2, wr2, j * 128, 128, lambda k: hview[:, k, :], 8, n, [hres])
                        a, b = r1[0][:, 0:n], r1[1][:, 0:n]
                        A("dve", "tensor_tensor", out=a, in0=ps, in1=rC[bi][:, 0:n], op=ALU.mult,
                            reads=[pres, "rC%d" % bi], writes=["r1a"])
                        A("dve", "tensor_tensor", out=b, in0=ps2, in1=rS[bi][:, 0:n], op=ALU.mult,
                            reads=[pres2, "rS%d" % bi], writes=["r1b"])
                        A("dve", "tensor_tensor", out=QT[:, hp, 0:n], in0=a, in1=b, op=ALU.add,
                            reads=["r1a", "r1b"], writes=["QT"])
                    else:
                        A("act", "copy", out=QT[:, hp, 0:n], in_=ps, reads=[pres], writes=["QT"])
            if not lat and NDBG == 50:
                for hp_ in range(4):
                    dbg(QT[:, hp_, 0:n], ["QT"], "QT%d" % hp_)
                for hp_ in range(4):
                    dbg(KcT[:, hp_, :], ["KcT"], "KcT%d" % hp_)
                dbg(hview[:, 0, :], [hres], "hc0")
                dbg(hview[:, 7, :], [hres], "hc7")
            if not lat and False:
                dbg(hview[:, 0, :], [hres], "hc")
                dbg(KcT[:, 0, :], ["KcT"], "KcT")
                dbg(Vc[:, 0].rearrange("p h d -> p (h d)")[:, 0:512], ["Vc"], "Vc")
                dbg(QT[:, 0, 0:n], ["QT"], "QT")
            vec = vecT[l]

            def conv_gen():
              if lat:
                  up = blk["upos"]
                  base = 16 + up
                  usrc = uring
                  nsl = URING * TPB
                  ur = ["u%d" % ((up // 128 + q_) % nsl) for q_ in (-1, 0, 1, 2)] + ["umarF", "umarB"]
              else:
                  base = 16
                  usrc = ucx
                  ur = ["ucx"]
              nstep = CK

              def build(j):
                  hf = j % 2
                  c0 = V_CDW + 2 * CK + 2 * j
                  A("dve", "tensor_tensor", out=dg[:, hf], in0=ident.unsqueeze(1).broadcast_to([128, 2, 128]),
                    in1=vec[:, c0:c0 + 2].unsqueeze(2).broadcast_to([128, 2, 128]), op=ALU.mult,
                    reads=["ident", "vec%d" % l], writes=["dg%d" % hf])
              build(0)
              for j in range(nstep):
                  if j + 1 < nstep:
                      build(j + 1)
                  for q_ in range(2):
                      t_ = 2 * j + q_
                      cc, k = 2 + t_ // CK, t_ % CK
                      cps = bank(6)[:, (cc % 2) * 256:(cc % 2) * 256 + n]
                      src = usrc[:, cc, base + k - 15:base + k - 15 + n]
                      A("pe", "matmul", cps, lhsT=dg[:, j % 2, q_], rhs=src, start=(k == 0), stop=(k == CK - 1),
                        reads=ur + ["dg%d" % (j % 2)], writes=["B6"])
                  for q_ in range(2):
                      t_ = 2 * j + q_
                      cc, k = t_ // CK, t_ % CK
                      acc = cacc[:, cc, 0:n]
                      src = usrc[:, cc, base + k - 15:base + k - 15 + n]
                      wk = vec[:, V_CDW + cc * CK + k:V_CDW + cc * CK + k + 1]
                      if k == 0:
                          A("dve", "tensor_scalar", out=acc, in0=src, scalar1=wk, scalar2=vec[:, V_CDB + cc:V_CDB + cc + 1],
                            op0=ALU.mult, op1=ALU.add, reads=ur + ["vec%d" % l], writes=["cacc%d" % cc])
                      else:
                          A("dve", "scalar_tensor_tensor", out=acc, in0=src, scalar=wk, in1=acc, op0=ALU.mult, op1=ALU.add,
                            reads=ur + ["vec%d" % l, "cacc%d" % cc], writes=["cacc%d" % cc])
                  yield

            cgen = conv_gen()

            def filler(k):
                for _ in range(k):
                    try:
                        next(cgen)
                    except StopIteration:
                        return
            nheads_total = (nt * 2 * NH) if lat else (4 * NH)
            per_head = -(-CK // nheads_total)
            FILL = {"f": filler, "k": per_head}
            if lat:
                for t in range(nt):
                    lt = lt0 + t
                    for rr in range(2):
                        r = 2 * lt + rr
                        qc0 = t * 128 + rr * 64
                        special = None
                        if lt in (0, 1):
                            special = ("top", r)
                        elif lt in (OWN - 2, OWN - 1):
                            special = ("bot", r - (2 * OWN - 4))
                        if special is None:
                            cis = [0, 1, 2, 3]
                        elif special[0] == "top":
                            cis = [0, 1, 2, 3, 4, 5]
                        else:
                            cis = [-2, -1, 0, 1, 2, 3]
                        chunks = []
                        for ii, ci in enumerate(cis):
                            kr0 = r - 4 + 2 * ci
                            pos = (kr0 * 64 - c["KA"] * 128)
                            rp = pos % (RING * 128)
                            if rp + 128 <= RING * 128:
                                ka = 64 + rp
                                kres = ["K%d" % (rp // 128)] + (["K%d" % ((rp // 128 + 1) % RING)] if rp % 128 else [])
                            else:
                                ka = 0
                                kres = ["Kmar", "K0"]
                            if kr0 % 2 == 0:
                                vs = ((kr0 // 2) - c["KA"]) % RING
                                vfn = (lambda h, vs=vs: Vev[:, vs, h, :])
                                vres = ["Ve%d" % vs]
                            else:
                                vs = (((kr0 - 1) // 2) - c["KA"]) % RING
                                vfn = (lambda h, vs=vs: Vod[:, vs, h, :])
                                vres = ["Vo%d" % vs]
                            rmap = None
                            if special is not None:
                                sidx = (special[1] + (0 if special[0] == "top" else 4)) * 6 + ii
                                rmap = rmT[:, sidx:sidx + 1]
                            chunks.append(dict(k=(lambda hp, po, ka=ka: Kring[po:po + 64, hp, ka:ka + 128]), kr=kres,
                                               v=vfn, vr=vres, e=True, ei=ci + 2, rm=rmap))
                        for t2 in range(2):
                            chunks.append(dict(k=(lambda hp, po, t2=t2: KcT[po:po + 64, hp, t2 * 128:(t2 + 1) * 128]), kr=["KcT"],
                                               v=(lambda h, t2=t2: Vc[:, t2, h, :]), vr=["Vc"], e=None, ei=None, rm=None))
                        attention(c, lambda hp, po, qc0=qc0: QT[po:po + 64, hp, qc0:qc0 + 64], 64, chunks, "OT",
                                  dict(otok=(t * 2 + rr) % 2, col=qc0), fill=FILL)
            else:
                for t in range(2):
                    for hh in range(2):
                        qc0 = t * 128 + hh * 64
                        chunks = [dict(k=(lambda hp, po, t2=t2: KcT[po:po + 64, hp, t2 * 128:(t2 + 1) * 128]), kr=["KcT"],
                                       v=(lambda h, t2=t2: Vc[:, t2, h, :]), vr=["Vc"], e=None, ei=None, rm=None) for t2 in range(2)]
                        attention(c, lambda hp, po, qc0=qc0: QT[po:po + 64, hp, qc0:qc0 + 64], 64, chunks, "OT",
                                  dict(otok=(t * 2 + hh) % 2, col=qc0), fill=FILL)
            filler(10 ** 6)
            for cc in (2, 3):
                cps = bank(6)[:, (cc % 2) * 256:(cc % 2) * 256 + n]
                A("act", "activation", out=cacc[:, cc, 0:n], in_=cps, func=AF.Identity, bias=vec[:, V_CDB + cc:V_CDB + cc + 1], scale=1.0,
                  reads=["B6", "vec%d" % l], writes=["cacc%d" % cc])
            pmu = bank(5)[:, 0:n]

            pm2 = bank(6)[:, 0:n]
            for cc in range(4):
                b = lsq[cc % 2][:, 0:n]
                A("act", "activation", out=b, in_=cacc[:, cc, 0:n], func=AF.Square,
                    reads=["cacc%d" % cc], writes=["Asq%d" % (cc % 2)])
                A("pe", "matmul", pmu, lhsT=ones_f, rhs=cacc[:, cc, 0:n], start=(cc == 0), stop=(cc == 3),
                    reads=["cacc%d" % cc, "ones"], writes=["B5"])
                A("pe", "matmul", pm2, lhsT=ones_f, rhs=b, start=(cc == 0), stop=(cc == 3),
                    reads=["Asq%d" % (cc % 2), "ones"], writes=["B6"])
            gctr["n"] = 0
            mu, rs = lmu[:, 0:n], lrs[:, 0:n]
            A("act", "activation", out=mu, in_=pmu, func=AF.Identity, scale=1.0 / CDIM, reads=["B5"], writes=["Ars"])
            A("dve", "tensor_tensor", out=rs, in0=mu, in1=mu, op=ALU.mult, reads=["Ars"], writes=["lrs"])
            A("dve", "scalar_tensor_tensor", out=rs, in0=pm2, scalar=1.0 / CDIM, in1=rs, op0=ALU.mult, op1=ALU.subtract,
                reads=["B6", "lrs"], writes=["lrs"])
            A("act", "activation", out=rs, in_=rs, func=AF.Sqrt, bias=epsT, scale=1.0, reads=["lrs", "eps"], writes=["lrs"])
            A("dve", "reciprocal", out=rs, in_=rs, reads=["lrs"], writes=["lrs"])
            for cc in range(4):
                acc = cacc[:, cc, 0:n]
                A("dve", "tensor_tensor", out=acc, in0=acc, in1=mu, op=ALU.subtract,
                    reads=["cacc%d" % cc, "Ars"], writes=["cacc%d" % cc])
                A("dve", "tensor_tensor", out=acc, in0=acc, in1=rs, op=ALU.mult,
                    reads=["cacc%d" % cc, "lrs"], writes=["cacc%d" % cc])
                A("act", "activation", out=cT[:, cc, 0:n], in_=acc, func=AF.Silu,
                                                                  bias=vec[:, V_LNB + cc:V_LNB + cc + 1], scale=vec[:, V_LNG + cc:V_LNG + cc + 1],
                    reads=["cacc%d" % cc, "vec%d" % l], writes=["cT"])
            gctr["n"] = 0
            if NDBG == 40 and not lat:
                for f_ in range(4):
                    dbg(OT[:, f_, 0:n], ["OT"], "cOT%d" % f_)
            if NDBG == 10 and lat and blk["lt0"] == -1:
                dbg(QT[:, 0, 0:n], ["QT"], "QT")
                dbg(cT[:, 0, 0:n], ["cT"], "cT")
                for f_ in range(4):
                    dbg(OT[:, f_, 0:n], ["OT"], "OT%d" % f_)
                dbg(Kring[:, 0, 0:512], ["K0"], "Kring0")
                dbg(Vev.rearrange("p t h d -> p (t h d)")[:, 0:512], ["Ve0"], "Vev0")
                dbg(Vod.rearrange("p t h d -> p (t h d)")[:, 0:512], ["Vo0"], "Vod0")
            if not lat and False:
                dbg(cT[:, 0, 0:n], ["cT"], "cT")
                dbg(OT[:, 0, 0:n], ["OT"], "OT")
                dbg(Otok[0][0:64, :], ["Otok0"], "Otok0")
                dbg(Pb[0][:, 0:128], ["P0"], "P0")
            for g in range(4):
                wcv, wcr = kgroup(wb["w_co_b"], g * 256, 256, 4)
                wnv, wnr = kgroup(wb["w_no_b"], g * 256, 256, 4)
                wgc, wgcr = kgroup(wb["w_in_b"], 2560 + g * 256, 256, 8)
                wga, wgar = kgroup(wb["w_in_b"], 3584 + g * 256, 256, 8)
                for j in range(2):
                    oc = g * 2 + j
                    pg1, pg1r = proj(wgc, wgcr, j * 128, 128, lambda k: hview[:, k, :], 8, n, [hres])
                    g1_ = gt[0][:, 0:n]
                    A("act", "activation", out=g1_, in_=pg1, func=AF.Sigmoid, reads=[pg1r], writes=["gt0"])
                    py1, py1r = proj(wcv, wcr, j * 128, 128, lambda k: cT[:, k, 0:n], 4, n, ["cT"])
                    ma = m12[0][:, 0:n]
                    A("dve", "tensor_tensor", out=ma, in0=py1, in1=g1_, op=ALU.mult,
                        reads=[py1r, "gt0"], writes=["m0"])
                    pg2, pg2r = proj(wga, wgar, j * 128, 128, lambda k: hview[:, k, :], 8, n, [hres])
                    g2_ = gt[1][:, 0:n]
                    A("act", "activation", out=g2_, in_=pg2, func=AF.Sigmoid, reads=[pg2r], writes=["gt1"])
                    py2, py2r = proj(wnv, wnr, j * 128, 128, lambda k: OT[:, k, 0:n], 4, n, ["OT"])
                    mb = m12[1][:, 0:n]
                    A("dve", "tensor_tensor", out=mb, in0=py2, in1=g2_, op=ALU.mult,
                        reads=[py2r, "gt1"], writes=["m1"])
                    A("dve", "tensor_tensor", out=mrg[:, oc, 0:n], in0=ma, in1=mb, op=ALU.add,
                        reads=["m0", "m1"], writes=["mrg"])
            if lat and blk["lt0"] == -1:
                dbg(mrg[:, 0, 0:n], ["mrg"], "mrg")
            for g in range(4):
                wov, wor = kgroup(wb["w_out_b"], g * 256, 256, 8)
                for j in range(2):
                    oc = g * 2 + j
                    po, por = proj(wov, wor, j * 128, 128, lambda k: mrg[:, k, 0:n], 8, n, ["mrg"])
                    A("dve", "scalar_tensor_tensor", out=xv[:, oc, :], in0=po, scalar=mv[:, 2, oc:oc + 1],
                                                                            in1=xv[:, oc, :], op0=ALU.mult, op1=ALU.add,
                        reads=[por, "modv"] + xres(oc), writes=xres(oc))

        fprev = {"n": None}

        def ffn_block(c, stream, t0, n, need_mask):
            l = c["l"]
            wb = W[l]
            vec = vecT[l]
            mv = modv[l][:, stream]
            lat = stream == 0
            if lat:
                xin = x_res[:, :, t0 - 1:t0 + n + 1]
                xo = x_res[:, :, t0:t0 + n]
                xres = lambda cc: ["x%d" % cc]
                hv = h2[:, :, 0:n + 2]
                nprev = fprev["n"]
                if nprev is not None:
                    A("pool", "tensor_copy", out=hstash[:, :, 0:1], in_=h2[:, :, nprev:nprev + 1], reads=["h2"], writes=["hstash"])
                emit_norm_mod(xin, n + 2, mv, (3, 4), fsq, frs, ftt, lambda cc: (hv[:, cc, :], ["h2"]), 7, xres, "F")
                if nprev is not None:
                    A("pool", "tensor_copy", out=h2[:, :, 0:1], in_=hstash[:, :, 0:1], reads=["hstash"], writes=["h2"])
                fprev["n"] = n
                if need_mask:
                    col = t0 - 1 + (XA - XIA) * 128
                    A("sp", "dma_start", out=ftm[:, 0:n + 2], in_=tokm_d[:, col:col + n + 2], writes=["ftm"], dma_key="ftm")
                    for cc in range(NCH):
                        A("pool", "tensor_tensor", out=hv[:, cc, :], in0=hv[:, cc, :], in1=ftm[:, 0:n + 2], op=ALU.mult,
                            reads=["h2", "ftm"], writes=["h2"])
            else:
                xo = xc_res
                xres = lambda cc: ["xc"]
                hv = h2[:, :, 0:n + 2]
                A("pool", "memset", h2[:, :, 0:1], 0.0, writes=["h2"])
                A("pool", "memset", h2[:, :, n + 1:n + 2], 0.0, writes=["h2"])
                emit_norm_mod(xc_res, n, mv, (3, 4), fsq, frs, ftt, lambda cc: (h2[:, cc, 1:n + 1], ["h2"]), 7, xres, "F")
            for j in range(NJ):
                wv, wr = kgroup(wb["w_up_b"], j * 256, 256, 8)
                outs = []
                for half in range(2):
                    k_ = 1 + 2 * (j % 2) + half
                    ps = bank(k_)[:, 0:n + 2]
                    pres = "B%d" % k_
                    for k in range(NCH):
                        A("pe", "matmul", ps, lhsT=wv[:, k, half * 128:(half + 1) * 128],
                                                                                 rhs=hv[:, k, :], start=(k == 0), stop=(k == 7),
                            reads=[wr, "h2"], writes=[pres])
                    ch = 2 * j + half
                    w0 = vec[:, V_FDW + ch * 3 + 0:V_FDW + ch * 3 + 1]
                    w1 = vec[:, V_FDW + ch * 3 + 1:V_FDW + ch * 3 + 2]
                    w2 = vec[:, V_FDW + ch * 3 + 2:V_FDW + ch * 3 + 3]
                    bb = vec[:, V_FDB + ch:V_FDB + ch + 1]
                    tb = (fta if half == 0 else ftg)[j % 2][:, 0:n]
                    tres = "ft%d%d" % (half, j % 2)
                    A("act", "activation", out=tb, in_=ps[:, 1:n + 1], func=AF.Identity, bias=bb, scale=w1,
                        reads=[pres, "vec%d" % l], writes=[tres])
                    A("dve", "scalar_tensor_tensor", out=tb, in0=ps[:, 0:n], scalar=w0, in1=tb,
                                                                                   op0=ALU.mult, op1=ALU.add,
                        reads=[pres, tres, "vec%d" % l], writes=[tres])
                    A("dve", "scalar_tensor_tensor", out=tb, in0=ps[:, 2:n + 2], scalar=w2, in1=tb,
                                                                                   op0=ALU.mult, op1=ALU.add,
                        reads=[pres, tres, "vec%d" % l], writes=[tres])
                    outs.append((tb, tres))
                (ta, tar), (tg, tgr) = outs
                sg = fsg[j % 2][:, 0:n]
                A("act", "activation", out=sg, in_=tg, func=AF.Silu, reads=[tgr], writes=["fsg%d" % (j % 2)])
                A("dve", "tensor_tensor", out=hid[:, j, 0:n], in0=ta, in1=sg, op=ALU.mult,
                    reads=[tar, "fsg%d" % (j % 2)], writes=["hid"])
            for oc in range(NCH):
                halves = []
                for hf in range(2):
                    src = wb["w_down_b"][hf * 11 * 128:(hf + 1) * 11 * 128, oc * 128:(oc + 1) * 128].rearrange("(j p) n -> p j n", p=128)
                    halves.append(wload(src, lambda raw: raw[:, 0:11 * 128].rearrange("p (j n) -> p j n", j=11),
                                        reads=CASTRES[id(wb["w_down_b"])]))
                k_ = 5 + (oc % 2)
                ps = bank(k_)[:, 0:n]
                for j in range(NJ):
                    wv, wr = halves[j // 11]
                    A("pe", "matmul", ps, lhsT=wv[:, j % 11, :], rhs=hid[:, j, 0:n], start=(j == 0), stop=(j == NJ - 1),
                        reads=[wr, "hid"], writes=["B%d" % k_])
                A("dve", "scalar_tensor_tensor", out=xo[:, oc, :], in0=ps, scalar=mv[:, 5, oc:oc + 1], in1=xo[:, oc, :],
                                                                        op0=ALU.mult, op1=ALU.add,
                    reads=["B%d" % k_, "modv"] + xres(oc), writes=xres(oc))

        def final_out(c):
            l = c["l"]
            vec = vecT[l]
            col0 = (0 - XA) * 128
            for bi in range(OWN * 128 // 512):
                t0 = col0 + bi * 512
                n = 512
                xin = x_res[:, :, t0:t0 + n]
                ps = bank(7)[:, 0:n]
                for cc in range(NCH):
                    b = fsq[cc % 2][:, 0:n]
                    A("act", "activation", out=b, in_=xin[:, cc, :], func=AF.Square,
                        reads=["x%d" % cc], writes=["Fsq%d" % (cc % 2)])
                    A("pe", "matmul", ps, lhsT=ones_f, rhs=b, start=(cc == 0), stop=(cc == 7),
                        reads=["Fsq%d" % (cc % 2), "ones"], writes=["B7"])
                rs = frs[:, 0:n]
                A("act", "activation", out=rs, in_=ps, func=AF.Sqrt, bias=epsT, scale=1.0 / D, reads=["B7", "eps"], writes=["Frs"])
                A("dve", "reciprocal", out=rs, in_=rs, reads=["Frs"], writes=["Frs"])
                for cc in range(NCH):
                    o = fo[cc % 2][:, 0:n]
                    A("dve", "scalar_tensor_tensor", out=o, in0=xin[:, cc, :], scalar=vec[:, V_FNG + cc:V_FNG + cc + 1],
                                                                          in1=rs, op0=ALU.mult, op1=ALU.mult,
                        reads=["x%d" % cc, "Frs", "vec%d" % l], writes=["fo%d" % (cc % 2)])
                    A("sp", "dma_start", out=outT_d[cc, :, bi * 512:(bi + 1) * 512], in_=o,
                        reads=["fo%d" % (cc % 2)], dma_key="out%d" % (cc % 2))

        UR = URING * NB
        for c in layers:
            l = c["l"]
            first_layer = c is layers[0]
            half = (mark_persist + AR["top"]) // 2
            A("pool", "memset", arena_t[:, mark_persist:half], 0.0, writes=["ARENA0"])
            A("dve", "memset", arena_t[:, half:AR["top"]], 0.0, writes=["ARENA1"])
            P.barrier()
            A("pool", "memset", Vev[:, :, :, 64:65], 1.0, writes=["Vev"])
            A("pool", "memset", Vod[:, :, :, 64:65], 1.0, writes=["Vod"])
            emit_mod(l)
            A("dve", "tensor_copy", out=modv[l][:, 0, 0, 0:1], in_=modv[l][:, 0, 0, 0:1],
                reads=["modv%d" % l], writes=["modv"])
            emit_etab(l)
            if NDBG == 60 and not first_layer:
                dbg(x_res[:, 0, 0:512], ["x0"], "x1 cols0-512")
                dbg(x_res[:, 0, 1000:1512], ["x0"], "x1 cols1000-1512")
                dbg(x_res[:, 7, NXT - 512:NXT], ["x7"], "x1 last512 ch7")
                dbg(xc_res[:, 0, :], ["xc"], "xc1")
                dbg(modv[l].rearrange("p s k c -> p (s k c)"), ["modv"], "modv1")
                dbg(Etab[:, 0, :, :].rearrange("p i q -> p (i q)"), ["Etab"], "Etab1 h0")
            if NDBG == 40:
                for h_ in range(NH):
                    dbg(Etab[:, h_, :, :].rearrange("p i q -> p (i q)"), ["Etab"], "Etab%d" % h_)
            phase_A(c, 1, None)
            if not c["last"]:
                phase_B(c, 1, None)
            blocks = []
            lt, idx = c["KA"], 0
            while lt < c["KB"]:
                tm = c["TA"] <= lt < c["TB"]
                nt = 1
                if tm and lt % 2 == 0 and lt + 1 < c["TB"]:
                    nt = 2
                blocks.append(dict(lt0=lt, nt=nt, tm=tm, idx=idx))
                lt += nt
                idx += 1
            KA0 = c["KA"] - (c["KA"] % 2)
            prev, tmc = None, 0
            for b in blocks:
                if b["tm"]:
                    b["hs"] = tmc % 2
                    tmc += 1
                else:
                    b["hs"] = 2
                b["upos"] = ((b["lt0"] - KA0) * 128) % UR
                b["prev_hs"] = prev["hs"] if prev else None
                b["prev_n"] = prev["nt"] * 128 if prev else None
                b["mask"] = (b["lt0"] < 0) or (b["lt0"] + b["nt"] > OWN)
                b["need"] = min(b["lt0"] + b["nt"] - 1 + 2, c["KB"] - 1)
                prev = b
            pend = []
            for b in blocks:
                resident = XA <= b["lt0"] and b["lt0"] + b["nt"] <= XB
                if resident:
                    xcol = (b["lt0"] - XA) * 128
                    b["xsrc"] = x_res[:, :, xcol:xcol + b["nt"] * 128]
                    b["xres"] = lambda cc: ["x%d" % cc]
                else:
                    assert first_layer and b["nt"] == 1 and not b["tm"]
                    col = (b["lt0"] - XIA) * 128
                    A("sp", "dma_start", out=xk, in_=xT_d[:, :, col:col + 128].rearrange("c p t -> p c t"), writes=["xk"] + CACC, dma_key="xk")
                    b["xsrc"] = xk
                    b["xres"] = lambda cc: ["xk"] + CACC
                phase_A(c, 0, b)
                if first_layer and b["idx"] == 2:
                    emit_casts(l, FFK)
                covered = b["lt0"] + b["nt"] - 1
                if b["tm"]:
                    pend.append(b)
                while pend and pend[0]["need"] <= covered:
                    phase_B(c, 0, pend.pop(0))
            assert not pend
            li_ = layers.index(c)
            if li_ + 1 < len(layers):
                emit_casts(layers[li_ + 1]["l"], TMK + FFK)
            if NDBG == 61 and not first_layer:
                for q_ in range(4):
                    dbg(x_res[:, 0, 384 + q_ * 512:384 + (q_ + 1) * 512], ["x0"], "xmid own q%d" % q_)
            P.barrier()
            if not c["last"]:
                ffn_block(c, 1, 0, CTX, False)
            f0 = c["F0"] - XA * 128
            f1 = c["F1"] - XA * 128
            fprev["n"] = None
            t0 = f0
            while t0 < f1:
                n = min(FBLK, f1 - t0)
                lo_t = (t0 - 1) // 128 + XA
                hi_t = (t0 + n) // 128 + XA
                ffn_block(c, 0, t0, n, lo_t < 0 or hi_t >= OWN)
                t0 += n
            if NDBG == 61 and not first_layer:
                for q_ in range(4):
                    dbg(x_res[:, 0, 384 + q_ * 512:384 + (q_ + 1) * 512], ["x0"], "x2 own q%d" % q_)
            if c["last"]:
                final_out(c)
            P.barrier()
        outs = ["out0", "out1"]
        if not cfg["final"]:
            col0 = (0 - XA) * 128
            for cc in range(NCH):
                A("sp", "dma_start", out=outT_d[cc], in_=x_res[:, cc, col0:col0 + OWN * 128], reads=["x%d" % cc],
                    dma_key="out%d" % (cc % 2))
            A("sp", "dma_start", out=xcT_d.rearrange("c p t -> p c t"), in_=xc_res, reads=["xc"], dma_key="out0")
        P.emit(final_wait_keys=outs + (["dbg"] if dbgc["n"] else []))
    return nc


ROPE_THETA = 10000.0


def _rope_tables(g_tiles):
    half = DH // 2
    inv_freq = (ROPE_THETA ** (-np.arange(0, half, 2, dtype=np.float32) / half)).astype(np.float32)
    p = np.arange(128)
    d = p % 64
    f = d % 16
    first = (d % 32) < 16
    use_row = d < 32
    C = np.zeros((128, len(g_tiles) * 128), np.float32)
    S = np.zeros_like(C)
    i = np.arange(128)
    for k, g in enumerate(g_tiles):
        row = (2 * g + i // 64).astype(np.float32)
        col = (i % 64).astype(np.float32)
        pos = np.where(use_row[:, None], row[None, :], col[None, :]).astype(np.float32)
        ang = (pos * inv_freq[f][:, None]).astype(np.float32)
        C[:, k * 128:(k + 1) * 128] = np.cos(ang)
        sn = np.sin(ang)
        S[:, k * 128:(k + 1) * 128] = np.where(first[:, None], -sn, sn)
    return C, S


def _bias_table(rpb_l):
    p = np.arange(128)
    kr2 = p // 64
    kc = p % 64
    qc = np.arange(64)
    cs = np.clip(qc - 8, 0, 48)
    out = np.full((128, NH, 8, 64), NEG, np.float32)
    for ei in range(8):
        ci = ei - 2
        dr = -4 + 2 * ci + kr2
        dc = kc[:, None] - qc[None, :]
        ok = (kc[:, None] >= cs[None, :]) & (kc[:, None] < cs[None, :] + 16) & (np.abs(dr)[:, None] <= 7)
        dri = np.clip(dr + 7, 0, 14)
        dci = np.clip(dc + 15, 0, 30)
        vals = rpb_l[:, dri[:, None], dci]
        out[:, :, ei, :] = np.where(ok[:, None, :], vals.transpose(1, 0, 2), np.float32(NEG))
    return out.reshape(128, NH * 8 * 64)


def _row_masks(ci_core):
    p = np.arange(128)
    kr2 = p // 64
    rm = np.zeros((128, 48), np.float32)
    for sidx in range(8):
        if sidx < 4:
            r = sidx
            cis = range(0, 6)
        else:
            r = 28 + (sidx - 4)
            cis = range(-2, 4)
        R = ci_core * 32 + r
        rs = min(max(R - 4, 0), 120)
        for ii, ci in enumerate(cis):
            kr = R - 4 + 2 * ci + kr2
            rm[:, sidx * 6 + ii] = ((kr >= rs) & (kr < rs + 8)).astype(np.float32)
    return rm


def _pack_vec(inp, l):
    v = np.zeros((128, NV), np.float32)
    fm = lambda a: np.ascontiguousarray(np.asarray(a, np.float32).reshape(-1, 128).T)
    v[:, V_BADA:V_BADA + 48] = fm(inp["b_ada"][l])
    v[:, V_N1G:V_N1G + 8] = fm(inp["norm1_g"][l])
    v[:, V_N2G:V_N2G + 8] = fm(inp["norm2_g"][l])
    cdw = np.asarray(inp["conv_dw"][l], np.float32)
    for cc in range(4):
        v[:, V_CDW + cc * CK:V_CDW + (cc + 1) * CK] = cdw[:, cc * 128:(cc + 1) * 128].T
    v[:, V_CDB:V_CDB + 4] = fm(inp["conv_dw_b"][l])
    v[:, V_LNG:V_LNG + 4] = fm(inp["conv_ln_g"][l])
    v[:, V_LNB:V_LNB + 4] = fm(inp["conv_ln_b"][l])
    fdw = np.asarray(inp["ffn_dw"][l], np.float32)
    fdb = np.asarray(inp["ffn_dw_b"][l], np.float32)
    for j in range(NJ):
        for half in range(2):
            ch = 2 * j + half
            c0 = half * FFN + j * 128
            v[:, V_FDW + ch * 3:V_FDW + ch * 3 + 3] = fdw[:, c0:c0 + 128].T
            v[:, V_FDB + ch] = fdb[c0:c0 + 128]
    v[:, V_FNG:V_FNG + 8] = fm(inp["final_norm_g"])
    return v


def _weights_for_layer(inp, l):
    w_in = np.asarray(inp["w_in"][l], np.float32)
    d = np.arange(64)
    partner = np.where((d % 32) < 16, d + 16, d - 16)
    qcols = np.concatenate([1024 + h * 64 + partner for h in range(NH)])
    kcols = np.concatenate([1536 + h * 64 + partner for h in range(NH)])
    w_rot = np.ascontiguousarray(w_in[:, np.concatenate([qcols, kcols])])
    w_up = np.asarray(inp["w_up"][l], np.float32)
    perm = np.concatenate([np.concatenate([np.arange(j * 128, (j + 1) * 128), FFN + np.arange(j * 128, (j + 1) * 128)])
                           for j in range(NJ)])
    return {
        "w_in%d" % l: np.ascontiguousarray(w_in), "w_rot%d" % l: w_rot,
        "w_co%d" % l: np.ascontiguousarray(np.asarray(inp["w_conv_out"][l], np.float32)),
        "w_no%d" % l: np.ascontiguousarray(np.asarray(inp["w_na_out"][l], np.float32)),
        "w_out%d" % l: np.ascontiguousarray(np.asarray(inp["w_out"][l], np.float32)),
        "w_up%d" % l: np.ascontiguousarray(w_up[:, perm]),
        "w_down%d" % l: np.ascontiguousarray(np.asarray(inp["w_down"][l], np.float32)),
        "w_ada%d" % l: np.ascontiguousarray(np.asarray(inp["w_ada"][l], np.float32)),
        "vec%d" % l: _pack_vec(inp, l),
        "bias%d" % l: _bias_table(np.asarray(inp["na_rpb"][l], np.float32)),
    }


def _core_inputs(cfg, inp, x, ctx, shared):
    XIA, XIB = cfg["XIA"], cfg["XIB"]
    maps = []
    for core in range(8):
        b, ci = core // 4, core % 4
        tiles = [ci * OWN + lt for lt in range(XIA, XIB)]
        nit = len(tiles) * 128
        xr = np.zeros((nit, D), np.float32)
        tm = np.zeros((nit,), np.float32)
        for k, g in enumerate(tiles):
            if 0 <= g < SEQ // 128:
                xr[k * 128:(k + 1) * 128] = x[b, g * 128:(g + 1) * 128]
                tm[k * 128:(k + 1) * 128] = 1.0
        C, S = _rope_tables(tiles)
        cin = np.zeros((128, NCH * 2), np.float32)
        cin[:, 0::2] = np.asarray(inp["c"], np.float32)[b].reshape(NCH, 128).T
        cin[:, 1::2] = np.asarray(inp["c_ctx"], np.float32).reshape(NCH, 128).T
        m = {
            "xT": np.ascontiguousarray(xr.T.reshape(NCH, 128, nit)),
            "ctxT": np.ascontiguousarray(ctx[b].T.reshape(NCH, 128, CTX)),
            "cin": cin,
            "tokm": np.ascontiguousarray(np.broadcast_to(tm[None, :], (128, nit))),
            "ropeC": C, "ropeS": S,
            "rm": _row_masks(ci),
        }
        m.update(shared)
        maps.append(m)
    return maps


_NC_CACHE = {}
MODE = "fused"


def _run(mode, inp, x, ctx):
    cfg = make_cfg(mode)
    if mode not in _NC_CACHE:
        _NC_CACHE[mode] = build(cfg)
    nc = _NC_CACHE[mode]
    shared = {}
    for c in cfg["layers"]:
        shared.update(_weights_for_layer(inp, c["l"]))
    maps = _core_inputs(cfg, inp, x, ctx, shared)
    res = run_bass_kernel_spmd(nc, maps, core_ids=list(range(8)))
    if cfg.get("ndbg"):
        np.save("_dbg.npy", np.asarray(res.results[0]["dbg"]))
    xo = np.zeros((2, SEQ, D), np.float32)
    xc = None if cfg["final"] else np.zeros((2, CTX, D), np.float32)
    for core in range(8):
        b, ci = core // 4, core % 4
        r = res.results[core]
        xo[b, ci * OWN * 128:(ci + 1) * OWN * 128] = np.asarray(r["outT"]).reshape(D, OWN * 128).T
        if xc is not None:
            xc[b] = np.asarray(r["xcT"]).reshape(D, CTX).T
    return xo, xc


def kernel(**inputs):
    x = np.asarray(inputs["x"], np.float32)
    ctx = np.asarray(inputs["ctx"], np.float32)
    if MODE == "fused":
        out, _ = _run("fused", inputs, x, ctx)
        return out
    x1, xc1 = _run("l0", inputs, x, ctx)
    out, _ = _run("l1", inputs, x1, xc1)
    return out
```
